# Optimizing a Trainium2 kernel written in Bass

```python
import jax, jax.numpy as jnp
from jax import lax
import numpy as np

D_MODEL = 1024
BATCH = 4
SEQ = 8192
DEPTH = 1

CHUNK = 64
Q_BLOCK = 128
FOX_HEAD_DIM = 128
N_FOX_HEADS = D_MODEL // FOX_HEAD_DIM
FOX_WIDTH = N_FOX_HEADS * FOX_HEAD_DIM
SGU_GROUP_DIM = 128
N_SGU_GROUPS = D_MODEL // SGU_GROUP_DIM
SGU_WIDTH = N_SGU_GROUPS * SGU_GROUP_DIM
SGU_LEN = 128
D_FF = 4 * D_MODEL
RMS_EPS = 1e-6
LN_EPS = 1e-5

COL_Q = 0
COL_K = COL_Q + FOX_WIDTH
COL_V = COL_K + FOX_WIDTH
COL_F = COL_V + FOX_WIDTH
COL_U = COL_F + N_FOX_HEADS
COL_SV = COL_U + SGU_WIDTH
COL_GA = COL_SV + SGU_WIDTH
COL_GB = COL_GA + D_MODEL
IN_WIDTH = COL_GB + D_MODEL

kernel_name = "fox_gmlp_gated_macaron_block"


def rmsnorm(x, g):
    xf = x.astype(jnp.float32)
    y = xf * lax.rsqrt(jnp.mean(xf * xf, axis=-1, keepdims=True) + RMS_EPS)
    return (y * g.astype(jnp.float32)).astype(x.dtype)


def swiglu(h, w_gate, w_up, w_down):
    return (jax.nn.silu(h @ w_gate) * (h @ w_up)) @ w_down


def forgetting_attention(q, k, v, f_logit, b_forget):
    B, S, H, D = q.shape
    nb = S // Q_BLOCK
    scale = 1.0 / np.sqrt(D).astype(np.float32)
    log_f = jax.nn.log_sigmoid(f_logit.astype(jnp.float32) + b_forget.astype(jnp.float32))
    c = jnp.cumsum(log_f, axis=1).transpose(0, 2, 1)
    kh = k.transpose(0, 2, 1, 3)
    vh = v.transpose(0, 2, 1, 3)
    qb = q.transpose(0, 2, 1, 3).reshape(B, H, nb, Q_BLOCK, D).transpose(2, 0, 1, 3, 4)
    cb = c.reshape(B, H, nb, Q_BLOCK).transpose(2, 0, 1, 3)
    k_pos = jnp.arange(S)

    def block(args):
        q_blk, c_blk, idx = args
        s = jnp.einsum('bhqd,bhkd->bhqk', q_blk, kh).astype(jnp.float32) * scale
        s = s + c_blk[..., :, None] - c[:, :, None, :]
        q_pos = idx * Q_BLOCK + jnp.arange(Q_BLOCK)
        s = jnp.where(k_pos[None, :] <= q_pos[:, None], s, -jnp.inf)
        p = jax.nn.softmax(s, axis=-1).astype(vh.dtype)
        return jnp.einsum('bhqk,bhkd->bhqd', p, vh)

    out = lax.map(block, (qb, cb, jnp.arange(nb)))
    return out.transpose(1, 0, 3, 2, 4).reshape(B, S, H * D)


def spatial_gating(u, v, ln_g, ln_b, w_s, b_s):
    B, S, W = v.shape
    G, C, L = N_SGU_GROUPS, SGU_GROUP_DIM, SGU_LEN
    vf = v.astype(jnp.float32).reshape(B, S, G, C)
    mu = jnp.mean(vf, axis=-1, keepdims=True)
    var = jnp.mean(jnp.square(vf - mu), axis=-1, keepdims=True)
    vn = ((vf - mu) * lax.rsqrt(var + LN_EPS)).reshape(B, S, W)
    vn = (vn * ln_g.astype(jnp.float32) + ln_b.astype(jnp.float32)).reshape(B, S // L, L, G, C)
    pos = jnp.arange(L)
    mask = (pos[None, :] // CHUNK) <= (pos[:, None] // CHUNK)
    w = jnp.where(mask[None], w_s.astype(jnp.float32), 0.0)
    mixed = jnp.einsum('gts,bnsgc->bntgc', w, vn) + b_s.astype(jnp.float32).T[None, None, :, :, None]
    return u * mixed.reshape(B, S, W).astype(u.dtype)


def setup_inputs(seed: int = 0) -> dict:
    key = jax.random.key(seed)
    ks = jax.random.split(key, 24)
    L, D, F = DEPTH, D_MODEL, D_FF
    nrm = lambda k, shape, fan_in: jax.random.normal(k, shape, jnp.float32) * (fan_in ** -0.5)
    gain = lambda k, shape: 1.0 + 0.05 * jax.random.normal(k, shape, jnp.float32)
    return {
        "x": jax.random.normal(ks[0], (BATCH, SEQ, D), jnp.float32),
        "ffn1_pre_g": gain(ks[1], (L, D)),
        "ffn1_w_gate": nrm(ks[2], (L, D, F), D),
        "ffn1_w_up": nrm(ks[3], (L, D, F), D),
        "ffn1_w_down": nrm(ks[4], (L, F, D), F),
        "ffn1_post_g": gain(ks[5], (L, D)),
        "mix_pre_g": gain(ks[6], (L, D)),
        "w_in": nrm(ks[7], (L, D, IN_WIDTH), D),
        "b_forget": jax.random.uniform(ks[8], (L, N_FOX_HEADS), jnp.float32, 2.0, 6.0),
        "sgu_ln_g": gain(ks[9], (L, SGU_WIDTH)),
        "sgu_ln_b": 0.02 * jax.random.normal(ks[10], (L, SGU_WIDTH), jnp.float32),
        "sgu_w_s": nrm(ks[11], (L, N_SGU_GROUPS, SGU_LEN, SGU_LEN), SGU_LEN),
        "sgu_b_s": 1.0 + 0.02 * jax.random.normal(ks[12], (L, N_SGU_GROUPS, SGU_LEN), jnp.float32),
        "w_out": nrm(ks[13], (L, D, D), D),
        "mix_post_g": gain(ks[14], (L, D)),
        "ffn2_pre_g": gain(ks[15], (L, D)),
        "ffn2_w_gate": nrm(ks[16], (L, D, F), D),
        "ffn2_w_up": nrm(ks[17], (L, D, F), D),
        "ffn2_w_down": nrm(ks[18], (L, F, D), F),
        "ffn2_post_g": gain(ks[19], (L, D)),
    }


def reference(x, ffn1_pre_g, ffn1_w_gate, ffn1_w_up, ffn1_w_down, ffn1_post_g,
              mix_pre_g, w_in, b_forget, sgu_ln_g, sgu_ln_b, sgu_w_s, sgu_b_s,
              w_out, mix_post_g, ffn2_pre_g, ffn2_w_gate, ffn2_w_up, ffn2_w_down,
              ffn2_post_g):
    B, S, D = x.shape
    H, HD = N_FOX_HEADS, FOX_HEAD_DIM
    for l in range(DEPTH):
        h = rmsnorm(x, ffn1_pre_g[l])
        x = x + 0.5 * rmsnorm(swiglu(h, ffn1_w_gate[l], ffn1_w_up[l], ffn1_w_down[l]), ffn1_post_g[l])

        h = rmsnorm(x, mix_pre_g[l])
        z = h @ w_in[l]
        q = z[..., COL_Q:COL_K].reshape(B, S, H, HD)
        k = z[..., COL_K:COL_V].reshape(B, S, H, HD)
        v = z[..., COL_V:COL_F].reshape(B, S, H, HD)
        f_logit = z[..., COL_F:COL_U]
        u_s = jax.nn.gelu(z[..., COL_U:COL_SV], approximate=False)
        v_s = jax.nn.gelu(z[..., COL_SV:COL_GA], approximate=False)
        gate_a = jax.nn.sigmoid(z[..., COL_GA:COL_GB])
        gate_b = jax.nn.sigmoid(z[..., COL_GB:IN_WIDTH])

        o_a = forgetting_attention(q, k, v, f_logit, b_forget[l])
        o_b = spatial_gating(u_s, v_s, sgu_ln_g[l], sgu_ln_b[l], sgu_w_s[l], sgu_b_s[l])
        merged = gate_a * o_a + gate_b * o_b
        x = x + rmsnorm(merged @ w_out[l], mix_post_g[l])

        h = rmsnorm(x, ffn2_pre_g[l])
        x = x + 0.5 * rmsnorm(swiglu(h, ffn2_w_gate[l], ffn2_w_up[l], ffn2_w_down[l]), ffn2_post_g[l])
    return x
```

```python
import numpy as np
import concourse.bass as bass
import concourse.mybir as mybir
from concourse.bass_utils import run_bass_kernel_spmd

F32 = mybir.dt.float32
BF16 = mybir.dt.bfloat16
AF = mybir.ActivationFunctionType
ALU = mybir.AluOpType

D = 1024
DFF = 4096
SEQ = 8192
NB = 4
NCORE = 8
TT = 512
NT = 8
TPC = NT * TT
NLB = NT * 4
NIDX = 2 * NLB
H = 8
HD = 128
WIN = 7168
GT = [[0, 3, 4, 7, 8, 11, 12, 15], [1, 2, 5, 6, 9, 10, 13, 14]]


def _configure(nt, ncore):
    global NT, TPC, NLB, NIDX, NCORE, GT, SEQ, NB
    NT, NCORE = nt, ncore
    TPC, NLB, NIDX = NT * TT, NT * 4, 2 * NT * 4
    SEQ = 2 * TPC
    NB = ncore // 2
    g0 = [g for g in range(2 * NT) if g % 4 in (0, 3)]
    g1 = [g for g in range(2 * NT) if g % 4 in (1, 2)]
    GT = [g0, g1]
RMS_EPS = 1e-6
LN_EPS = 1e-5
BCLAMP = 64.0
NSLOT = 5
SAME_ENGINE_SYNC = True
STOP = 99


class _StopBuild(Exception):
    pass

WEIGHTS = [("wg1", D, DFF), ("wu1", D, DFF), ("wd1", DFF, D), ("win", D, WIN),
           ("wout", D, D), ("wg2", D, DFF), ("wu2", D, DFF), ("wd2", DFF, D)]


class Res:
    __slots__ = ("name", "w", "r")

    def __init__(self, name):
        self.name = name
        self.w = None
        self.r = {}


class Eng:
    def __init__(self, name):
        self.name = name
        self.sem = None
        self.cnt = 0
        self.waited = {}
        self.prog = []


class DSem:
    def __init__(self, name, step=16):
        self.name = name
        self.sem = None
        self.cnt = 0
        self.step = step
        self.last = None


class Tracker:
    def __init__(self):
        self.eng = {n: Eng(n) for n in ("pe", "act", "dve", "pool", "sp")}
        self.dsems = []
        self.nwait = 0

    def dsem(self, name, step=16):
        d = DSem(name, step)
        self.dsems.append(d)
        return d

    def _wait(self, E, deps):
        for tok in deps:
            if tok is None:
                continue
            key, val, en = tok
            if en == E.name:
                if E.name == "pe" or not SAME_ENGINE_SYNC:
                    continue
            if E.waited.get(id(key), 0) >= val:
                continue
            E.waited[id(key)] = val
            self.nwait += 1
            E.prog.append(("wait", key, val))

    def _deps(self, reads, writes):
        deps = []
        for r in reads:
            if r.w is not None:
                deps.append(r.w)
        for w in writes:
            if w.w is not None:
                deps.append(w.w)
            deps.extend(w.r.values())
        return deps

    def _reg(self, tok, reads, writes):
        for r in reads:
            r.r[id(tok[0])] = tok
        for w in writes:
            w.w = tok
            w.r = {}

    def op(self, en, fn, reads=(), writes=(), signal=True):
        E = self.eng[en]
        self._wait(E, self._deps(reads, writes))
        if signal:
            E.cnt += 1
            tok = (E, E.cnt, en)
        else:
            tok = (E, E.cnt + 1, en)
        E.prog.append(("op", fn, signal))
        self._reg(tok, reads, writes)
        return tok

    def dma(self, q, ds, pairs, reads=(), writes=()):
        E = self.eng[q]
        deps = self._deps(reads, writes)
        deps.append(ds.last)
        self._wait(E, deps)
        for (o, i) in pairs:
            E.prog.append(("dma", o, i, ds))
        ds.cnt += ds.step * len(pairs)
        tok = (ds, ds.cnt, None)
        ds.last = tok
        self._reg(tok, reads, writes)
        return tok

    def cc(self, ds, kind, groups, in_ap, out_ap, reads=(), writes=()):
        E = self.eng["pool"]
        deps = self._deps(reads, writes)
        deps.append(ds.last)
        self._wait(E, deps)
        E.prog.append(("cc", kind, groups, in_ap, out_ap, ds))
        ds.cnt += ds.step
        tok = (ds, ds.cnt, None)
        ds.last = tok
        self._reg(tok, reads, writes)
        return tok

    def barrier(self):
        toks = [(E, E.cnt, E.name) for E in self.eng.values() if E.cnt > 0]
        toks += [d.last for d in self.dsems if d.last is not None]
        for E in self.eng.values():
            self._wait(E, [t for t in toks if t[2] != E.name])

    def emit(self, nc, en, h):
        E = self.eng[en]
        for item in E.prog:
            k = item[0]
            if k == "wait":
                h.wait_ge(item[1].sem, item[2])
            elif k == "op":
                ins = item[1](h)
                if item[2]:
                    ins.then_inc(E.sem, 1)
            elif k == "dma":
                h.dma_start(out=item[1], in_=item[2]).then_inc(item[3].sem, 16)
            elif k == "cc":
                h.collective_compute(item[1], ALU.bypass, replica_groups=item[2],
                                     ins=[item[3]], outs=[item[4]]).then_inc(item[5].sem, item[5].step)


class Arena:
    def __init__(self, nc, base=16512, limit=229376):
        self.nc = nc
        self.base = base
        self.off = base
        self.limit = limit
        self.n = 0

    def alloc(self, shape, dt, name=None):
        esz = 4 if dt == F32 else 2
        nbytes = esz * int(np.prod(shape[1:]))
        nbytes = (nbytes + 31) // 32 * 32
        assert self.off + nbytes <= self.limit, f"SBUF overflow {self.off}+{nbytes}"
        self.n += 1
        t = self.nc.alloc_sbuf_tensor_at(f"{name or 't'}_{self.n}", list(shape), dt, offset=self.off)
        self.off += nbytes
        return t

    def mark(self):
        return self.off

    def reset(self, m):
        self.off = m


class Slot:
    def __init__(self, T, t, name):
        self.t = t
        self.res = Res(name)
        self.ds = T.dsem(name)


class Ring:
    def __init__(self, T, arena, n, shape, dt, name):
        self.slots = [Slot(T, arena.alloc(shape, dt, name), f"{name}{i}") for i in range(n)]
        self.i = 0

    def next(self):
        s = self.slots[self.i % len(self.slots)]
        self.i += 1
        return s


def build():
    nc = bass.Bass("TRN2", target_bir_lowering=False)
    T = Tracker()

    def body():
        def din(name, shape, dt=F32):
            return nc.dram_tensor(name, list(shape), dt, kind="ExternalInput")

        xT = din("xT", [D, TPC])
        xTp = din("xTp", [D, TPC])
        wfull = {n: din(n + "_f", [r, c]) for (n, r, c) in WEIGHTS}
        wf_in = din("wf", [D, H])
        gcols_in = din("gcols", [128, 48])
        lncols_in = din("lncols", [128, 16])
        wsT_in = din("wsT", [128, 8 * 128])
        bs_in = din("bs", [1, 8 * 128])
        bf_in = din("bfg", [1, H])
        flags_in = din("flags", [128, 4])
        outT = nc.dram_tensor("outT", [D, TPC], F32, kind="ExternalOutput")

        wbf = {n: nc.dram_tensor(n + "_bf", [r, c], BF16) for (n, r, c) in WEIGHTS}
        Qs = nc.dram_tensor("Qs", [H * 128, TPC], BF16)
        KG = nc.dram_tensor("KG", [2 * H * 128, TPC], BF16)
        VG = nc.dram_tensor("VG", [2 * H * 128, NLB * 128], BF16)
        LG = nc.dram_tensor("LG", [2 * TPC, H], F32)
        GAs = nc.dram_tensor("GAs", [D, TPC], F32)
        MBs = nc.dram_tensor("MBs", [D, TPC], F32)
        X1s = nc.dram_tensor("X1s", [D, TPC], F32)
        MGs = nc.dram_tensor("MGs", [D, TPC], BF16)

        ar = Arena(nc)
        psum = [nc.alloc_psum_tensor(f"ps{i}", [128, 512], F32) for i in range(8)]
        pres = [Res(f"ps{i}") for i in range(8)]

        gcols = ar.alloc([128, 6, 8], F32, "gcols")
        lncols = ar.alloc([128, 2, 8], F32, "lncols")
        flags = ar.alloc([128, 4], F32, "flags")
        ones_bf = ar.alloc([128, 128], BF16, "ones_bf")
        invd_bf = ar.alloc([128, 128], BF16, "invd_bf")
        ones_f = ar.alloc([128, 128], F32, "ones_f")
        tri_f = ar.alloc([128, 128], F32, "tri_f")
        diag = ar.alloc([128, 4, 512], BF16, "diag")
        eps_col = ar.alloc([128, 2], F32, "eps")
        r_const = Res("const")
        ds_const = T.dsem("const")
        cmark = ar.mark()

        def vop(en, fname, reads, writes, *args, **kw):
            return T.op(en, lambda e: getattr(e, fname)(*args, **kw), reads, writes)

        T.dma("sp", ds_const, [
            (gcols[:].rearrange("p a b -> p (a b)"), gcols_in.ap()),
            (lncols[:].rearrange("p a b -> p (a b)"), lncols_in.ap()),
            (flags[:], flags_in.ap()),
        ], writes=[r_const])
        if STOP == -3:
            raise _StopBuild
        r_pc = Res("poolconst")
        vop("pool", "memset", [], [r_pc], ones_bf[:], 1.0)
        vop("pool", "memset", [], [r_pc], invd_bf[:], 1.0 / D)
        vop("pool", "memset", [], [r_pc], ones_f[:], 1.0)
        vop("pool", "memset", [], [r_pc], eps_col[:, 0:1], RMS_EPS)
        vop("pool", "memset", [], [r_pc], eps_col[:, 1:2], LN_EPS)
        T.op("pool", lambda e: e.affine_select(tri_f[:], ones_f[:], [[1, 128]], ALU.is_ge, 0.0,
                                               base=0, channel_multiplier=-1), [r_pc], [r_pc])
        vop("pool", "memset", [], [r_pc], diag[:].rearrange("p a b -> p (a b)"), 1.0)
        for j in range(4):
            T.op("pool", lambda e, j=j: e.affine_select(diag[:, j, :], diag[:, j, :], [[1, 512]], ALU.is_ge, 0.0,
                                                        base=-128 * j, channel_multiplier=-1), [r_pc], [r_pc])
        if STOP == -2:
            raise _StopBuild
        for n in (1, 5):
            vop("dve", "tensor_scalar", [r_const], [r_const], gcols[:, n, :], gcols[:, n, :], 0.5, None, ALU.mult)

        if STOP == -1:
            raise _StopBuild
        ds_wc = [T.dsem(f"wcast{i}") for i in range(4)]
        r_w = {n: [Res(f"{n}_bf{b}") for b in range(c // 512)] for (n, r, c) in WEIGHTS}
        cast_order = []
        for f4 in range(8):
            cast_order += [("wg1", f4), ("wu1", f4)]
        cast_order += [("wd1", 0), ("wd1", 1)]
        cast_order += [("win", ci) for ci in (0, 1, 2, 3, 4, 5, 8, 9, 6, 12, 10, 7, 13, 11)]
        cast_order += [("wout", 0), ("wout", 1)]
        for f4 in range(8):
            cast_order += [("wg2", f4), ("wu2", f4)]
        cast_order += [("wd2", 0), ("wd2", 1)]
        cast_state = {"pos": 0}

        def cast_upto(pos):
            pos = min(pos, len(cast_order))
            while cast_state["pos"] < pos:
                n, b = cast_order[cast_state["pos"]]
                ds = ds_wc[cast_state["pos"] % 4]
                cast_state["pos"] += 1
                T.dma("pool", ds, [(wbf[n].ap()[:, b * 512:(b + 1) * 512], wfull[n].ap()[:, b * 512:(b + 1) * 512])],
                      writes=[r_w[n][b]])

        def need_cast(n, b):
            cast_upto(cast_order.index((n, b)) + 4)
            cast_upto(cast_state["pos"] + 1)

        if STOP == 0:
            raise _StopBuild
        wring = Ring(T, ar, NSLOT, [128, 4096], BF16, "w")

        wdirect = {"on": False}

        def wload(src_ap, view, rw, src32=None):
            s = wring.next()
            dst = view(s.t)
            if wdirect["on"] and src32 is not None:
                T.dma("pool", s.ds, [(dst, src32)], writes=[s.res])
            else:
                T.dma("pool", s.ds, [(dst, src_ap)], reads=[rw], writes=[s.res])
            return s, dst

        def w_gu(n, f0):
            need_cast(n, f0 // 512)
            src = wbf[n].ap().rearrange("(c p) f -> p c f", p=128)[:, :, f0:f0 + 512]
            src32 = wfull[n].ap().rearrange("(c p) f -> p c f", p=128)[:, :, f0:f0 + 512]
            return wload(src, lambda t: t[:].rearrange("p (c f) -> p c f", c=8), r_w[n][f0 // 512], src32)

        def w_dn(n, dc):
            need_cast(n, dc // 4)
            src = wbf[n].ap().rearrange("(c p) d -> p c d", p=128)[:, :, dc * 128:(dc + 1) * 128]
            src32 = wfull[n].ap().rearrange("(c p) d -> p c d", p=128)[:, :, dc * 128:(dc + 1) * 128]
            return wload(src, lambda t: t[:].rearrange("p (c d) -> p c d", c=32), r_w[n][dc // 4], src32)

        xbuf = [Slot(T, ar.alloc([128, 8, TT], F32, "x"), "x0")]
        hbuf = ar.alloc([128, 8, TT], BF16, "h")
        r_h = Res("h")
        abuf = ar.alloc([128, 32, TT], BF16, "a")
        r_a = [Res(f"a{i}") for i in range(32)]
        ybuf = ar.alloc([128, 8, TT], F32, "y")
        r_y = [Res(f"y{i}") for i in range(8)]
        sq = [ar.alloc([128, TT], BF16, "sq") for _ in range(2)]
        r_sq = [Res("sq0"), Res("sq1")]
        rstd = ar.alloc([128, TT], F32, "rstd")
        r_rstd = Res("rstd")
        sil = [ar.alloc([128, TT], F32, "sil") for _ in range(2)]
        r_sil = [Res("sil0"), Res("sil1")]
        tmpb = [ar.alloc([128, TT], F32, "tmp") for _ in range(2)]
        r_tmp = [Res("tmp0"), Res("tmp1")]
        a_base = ar.mark()
        xbuf.append(Slot(T, ar.alloc([128, 8, TT], F32, "x"), "x1"))
        ffn_mark = ar.mark()
        ar.reset(a_base)
        cnt = {"sq": 0, "sil": 0, "tmp": 0, "mm": 0}

        PS_G = [0, 1]
        PS_U = [2, 3]
        PS_MM = [4, 5, 6]
        PS_SS = 7

        def mm_group(bank, out_ap, pairs, reads):
            n = len(pairs)
            for i, (l, r) in enumerate(pairs):
                T.op("pe", lambda e, l=l, r=r, i=i: e.matmul(out_ap, l, r, start=(i == 0), stop=(i == n - 1)),
                     reads, [pres[bank]], signal=(i == n - 1))

        def next_mm():
            b = PS_MM[cnt["mm"] % 3]
            cnt["mm"] += 1
            return b

        def rmsnorm_stats(src, r_src_list):
            for dc in range(8):
                k = cnt["sq"] % 2
                cnt["sq"] += 1
                T.op("act", lambda e, dc=dc, k=k: e.activation(out=sq[k][:], in_=src[:, dc, :], func=AF.Square),
                     [r_src_list[dc]], [r_sq[k]])
                T.op("pe", lambda e, dc=dc, k=k: e.matmul(psum[PS_SS][:], invd_bf[:], sq[k][:],
                                                          start=(dc == 0), stop=(dc == 7)),
                     [r_sq[k], r_pc], [pres[PS_SS]], signal=True)
            T.op("act", lambda e: e.activation(out=rstd[:], in_=psum[PS_SS][:], func=AF.Sqrt, bias=eps_col[:, 0:1]),
                 [pres[PS_SS], r_pc], [r_rstd])
            T.op("dve", lambda e: e.reciprocal(rstd[:], rstd[:]), [r_rstd], [r_rstd])

        def prenorm(src, r_src_list, gidx):
            rmsnorm_stats(src, r_src_list)
            for dc in range(8):
                T.op("dve", lambda e, dc=dc: e.scalar_tensor_tensor(
                    out=hbuf[:, dc, :], in0=src[:, dc, :], scalar=gcols[:, gidx, dc:dc + 1], in1=rstd[:],
                    op0=ALU.mult, op1=ALU.mult), [r_src_list[dc], r_rstd, r_const], [r_h])

        def ffn(names, x_t, r_x_list, gpost):
            ng, nu, nd = names
            for f4 in range(8):
                sg, wg = w_gu(ng, f4 * 512)
                su, wu = w_gu(nu, f4 * 512)
                for fi in range(4):
                    fc = f4 * 4 + fi
                    bg = PS_G[fc % 2]
                    bu = PS_U[fc % 2]
                    mm_group(bg, psum[bg][:], [(wg[:, dc, fi * 128:(fi + 1) * 128], hbuf[:, dc, :]) for dc in range(8)],
                             [sg.res, r_h])
                    mm_group(bu, psum[bu][:], [(wu[:, dc, fi * 128:(fi + 1) * 128], hbuf[:, dc, :]) for dc in range(8)],
                             [su.res, r_h])
                    k = cnt["sil"] % 2
                    cnt["sil"] += 1
                    T.op("act", lambda e, bg=bg, k=k: e.activation(out=sil[k][:], in_=psum[bg][:], func=AF.Silu),
                         [pres[bg]], [r_sil[k]])
                    T.op("dve", lambda e, bu=bu, k=k, fc=fc: e.tensor_tensor(abuf[:, fc, :], psum[bu][:], sil[k][:], ALU.mult),
                         [pres[bu], r_sil[k]], [r_a[fc]])
            for dc in range(8):
                sd, wd = w_dn(nd, dc)
                b = next_mm()
                mm_group(b, psum[b][:], [(wd[:, fc, :], abuf[:, fc, :]) for fc in range(32)], [sd.res] + r_a)
                T.op("act", lambda e, dc=dc, b=b: e.activation(out=ybuf[:, dc, :], in_=psum[b][:], func=AF.Copy),
                     [pres[b]], [r_y[dc]])
            rmsnorm_stats(ybuf, r_y)
            for dc in range(8):
                k = cnt["tmp"] % 2
                cnt["tmp"] += 1
                T.op("dve", lambda e, dc=dc, k=k: e.scalar_tensor_tensor(
                    out=tmpb[k][:], in0=ybuf[:, dc, :], scalar=gcols[:, gpost, dc:dc + 1], in1=rstd[:],
                    op0=ALU.mult, op1=ALU.mult), [r_y[dc], r_rstd, r_const], [r_tmp[k]])
                T.op("dve", lambda e, dc=dc, k=k: e.tensor_tensor(x_t[:, dc, :], x_t[:, dc, :], tmpb[k][:], ALU.add),
                     [r_tmp[k], r_x_list[dc]], [r_x_list[dc]])

        wf_sb = ar.alloc([128, 8, H], BF16, "wf")
        r_wf = Res("wf")
        ds_wf = T.dsem("wf")
        T.dma("pool", ds_wf, [(wf_sb[:], wf_in.ap().rearrange("(c p) h -> p c h", p=128))], writes=[r_wf])
        wsT_f = ar.alloc([128, 8, 128], F32, "wsTf")
        wsT_b = ar.alloc([128, 8, 128], BF16, "wsTb")
        bs_bc = ar.alloc([128, 8, 128], F32, "bsbc")
        bf_bc = ar.alloc([128, H], F32, "bfbc")
        e_g = ar.alloc([128, 8, 128], F32, "eg")
        r_sgc = Res("sgc")
        T.dma("sp", ds_const, [
            (wsT_f[:].rearrange("p a b -> p (a b)"), wsT_in.ap()),
            (bs_bc[:].rearrange("p a b -> p (a b)"), bs_in.ap().partition_broadcast(128).rearrange("p a b -> p (a b)")),
            (bf_bc[:], bf_in.ap().partition_broadcast(128).rearrange("p a b -> p (a b)")),
        ], writes=[r_sgc])
        T.op("dve", lambda e: e.memset(wsT_f[64:128, :, 0:64], 0.0), [r_sgc], [r_sgc])
        T.op("dve", lambda e: e.tensor_copy(wsT_b[:], wsT_f[:]), [r_sgc], [r_sgc])
        for half in range(2):
            T.op("pe", lambda e, half=half: e.matmul(psum[half][:], ones_f[:],
                                                     wsT_f[:, half * 4:(half + 1) * 4, :].rearrange("p a b -> p (a b)"),
                                                     start=True, stop=True),
                 [r_sgc, r_pc], [pres[half]])
            for gi in range(4):
                g = half * 4 + gi
                T.op("dve", lambda e, half=half, gi=gi, g=g: e.scalar_tensor_tensor(
                    out=e_g[:, g, :], in0=psum[half][:, gi * 128:(gi + 1) * 128], scalar=lncols[:, 1, g:g + 1],
                    in1=bs_bc[:, g, :], op0=ALU.mult, op1=ALU.add), [pres[half], r_const, r_sgc], [r_sgc])

        a_mark = ar.mark()
        qk_ring = Ring(T, ar, 2, [128, 8, TT], BF16, "qkst")
        ybase = nc.lookup_mloc(ybuf).addr
        vst = Slot(T, nc.alloc_sbuf_tensor_at("vst_alias", [128, 4, D], BF16, offset=ybase), "vst")
        vn = nc.alloc_sbuf_tensor_at("vn_alias", [128, 4, D], BF16, offset=ybase + 4 * D * 2)
        r_vn = [r_y[4 + b] for b in range(4)]
        vs_ring = [ar.alloc([128, 512], F32, "vs") for _ in range(4)]
        r_vs = [Res(f"vs{i}") for i in range(4)]
        stats = ar.alloc([128, 16, 6], F32, "stats")
        mv = ar.alloc([128, 16, 2], F32, "mv")
        lrs = ar.alloc([128, 16], F32, "lrs")
        r_st = Res("stats")
        u_ring = [ar.alloc([128, TT], F32, "u") for _ in range(2)]
        r_u = [Res("u0"), Res("u1")]
        gb_ring = [ar.alloc([128, TT], F32, "gb") for _ in range(2)]
        r_gb = [Res("gb0"), Res("gb1")]
        ga_ring = Ring(T, ar, 2, [128, TT], F32, "gast")
        mb_ring = Ring(T, ar, 3, [128, TT], F32, "mbst")
        lf_z = ar.alloc([128, 4, H], F32, "lfz")
        lf_st = Slot(T, ar.alloc([128, 4, H], F32, "lfst"), "lfst")
        r_lfz = Res("lfz")

        r_Q = [Res(f"Q{i}") for i in range(NT)]
        r_KG = [[Res(f"KG{sl}_{i}") for i in range(NT)] for sl in range(2)]
        r_VG = [[Res(f"VG{sl}_{i}") for i in range(NT)] for sl in range(2)]
        r_LG = [[Res(f"LG{sl}_{i}") for i in range(NT)] for sl in range(2)]
        r_GA = [[Res(f"GA{i}_{g}") for g in range(8)] for i in range(NT)]
        r_MB = [[Res(f"MB{i}_{g}") for g in range(8)] for i in range(NT)]
        r_X1 = [Res(f"X1{i}") for i in range(NT)]
        r_MG = [[Res(f"MG{i}_{g}") for g in range(8)] for i in range(NT)]
        qk_cres = [[Res(f"qkc{i}_{c}") for c in range(8)] for i in range(2)]
        v_cres = [r_y[i % 4] for i in range(8)]

        xT_v = xT.ap().rearrange("(c p) t -> p c t", p=128)
        X1_v = X1s.ap().rearrange("(c p) t -> p c t", p=128)
        Q_v = Qs.ap().rearrange("(c p) t -> p c t", p=128)
        xTp_v = xTp.ap().rearrange("(c p) t -> p c t", p=128)
        KG_v = [KG.ap()[sl * 1024:(sl + 1) * 1024, :].rearrange("(c p) t -> p c t", p=128) for sl in range(2)]
        VG_v = [VG.ap()[sl * 1024:(sl + 1) * 1024, :].rearrange("(h s) (l d) -> s l h d", s=128, d=128)
                for sl in range(2)]
        LG_v = [LG.ap()[sl * TPC:(sl + 1) * TPC, :].rearrange("(l p) h -> p l h", p=128) for sl in range(2)]
        win_v = wbf["win"].ap().rearrange("(c p) n -> p c n", p=128)
        win32_v = wfull["win"].ap().rearrange("(c p) n -> p c n", p=128)

        def w_in_chunk(ci):
            need_cast("win", ci)
            return wload(win_v[:, :, ci * 512:(ci + 1) * 512],
                         lambda t: t[:].rearrange("p (c f) -> p c f", c=8), r_w["win"][ci],
                         win32_v[:, :, ci * 512:(ci + 1) * 512])

        cact = {"i": 0}

        def evac(dst_ap, src_ap, reads, writes, func=None):
            if func is not None:
                return T.op("act", lambda e: e.activation(out=dst_ap, in_=src_ap, func=func), reads, writes)
            cact["i"] += 1
            if cact["i"] % 2:
                return T.op("act", lambda e: e.activation(out=dst_ap, in_=src_ap, func=AF.Copy), reads, writes)
            return T.op("dve", lambda e: e.tensor_copy(dst_ap, src_ap), reads, writes)

        xs = xbuf[0]
        r_xc = [Res(f"xc{dc}") for dc in range(8)]

        def load_x(ti):
            src = xT_v if ti < NT else xTp_v
            lt_ = ti % NT
            T.dma("sp", xs.ds, [(xs.t[:], src[:, :, lt_ * TT:(lt_ + 1) * TT])], writes=r_xc)

        load_x(0)
        for ti in range(2 * NT):
            slot, lt = ti // NT, ti % NT
            own = slot == 0
            wdirect["on"] = (ti == 0)
            x_t = xs.t
            tsl = slice(lt * TT, (lt + 1) * TT)
            prenorm(x_t, r_xc, 0)
            ffn(("wg1", "wu1", "wd1"), x_t, r_xc, 1)
            if own:
                T.dma("sp", xs.ds, [(X1_v[:, :, tsl], x_t[:])], reads=r_xc, writes=[r_X1[lt]])
            prenorm(x_t, r_xc, 2)
            if ti + 1 < 2 * NT:
                load_x(ti + 1)
            for which, (dst_v, rdst) in enumerate(((Q_v, r_Q[lt]), (KG_v[slot], r_KG[slot][lt]))):
                if which == 0 and not own:
                    continue
                sti = qk_ring.i % 2
                st = qk_ring.next()
                for half in range(2):
                    sw, wv = w_in_chunk(which * 2 + half)
                    for ci in range(4):
                        c = half * 4 + ci
                        b = next_mm()
                        mm_group(b, psum[b][:], [(wv[:, dc, ci * 128:(ci + 1) * 128], hbuf[:, dc, :]) for dc in range(8)],
                                 [sw.res, r_h])
                        evac(st.t[:, c, :], psum[b][:], [pres[b]], [qk_cres[sti][c]])
                T.dma("sp", st.ds, [(dst_v[:, :, tsl], st.t[:])], reads=qk_cres[sti], writes=[rdst])
            for half in range(2):
                sw, wv = w_in_chunk(4 + half)
                for blk in range(4):
                    b = next_mm()
                    mm_group(b, psum[b][:], [(hbuf[:, dc, blk * 128:(blk + 1) * 128], wv[:, dc, :]) for dc in range(8)],
                             [sw.res, r_h])
                    evac(vst.t[:, blk, half * 512:(half + 1) * 512], psum[b][:], [pres[b]], [v_cres[half * 4 + blk]])
            T.dma("sp", vst.ds, [(VG_v[slot][:, lt * 4 + blk, :, :], vst.t[:, blk, :].rearrange("p (h d) -> p h d", h=H))
                                 for blk in range(4)], reads=v_cres, writes=[r_VG[slot][lt]])
            b = next_mm()
            for blk in range(4):
                mm_group(b, psum[b][:, blk * H:(blk + 1) * H],
                         [(hbuf[:, dc, blk * 128:(blk + 1) * 128], wf_sb[:, dc, :]) for dc in range(8)], [r_wf, r_h])
            T.op("dve", lambda e, b=b: e.tensor_tensor(
                lf_z[:], psum[b][:, 0:4 * H].rearrange("p (b h) -> p b h", h=H),
                bf_bc[:].unsqueeze(1).to_broadcast([128, 4, H]), ALU.add), [pres[b], r_sgc], [r_lfz])
            T.op("act", lambda e: e.activation(out=lf_z[:], in_=lf_z[:], func=AF.Exp, scale=-1.0), [r_lfz], [r_lfz])
            T.op("act", lambda e: e.activation(out=lf_z[:], in_=lf_z[:], func=AF.Ln, bias=1.0), [r_lfz], [r_lfz])
            T.op("dve", lambda e: e.tensor_scalar(lf_st.t[:], lf_z[:], -1.0, None, ALU.mult), [r_lfz], [lf_st.res])
            T.dma("sp", lf_st.ds, [(LG_v[slot][:, lt * 4:(lt + 1) * 4, :], lf_st.t[:])], reads=[lf_st.res],
                  writes=[r_LG[slot][lt]])
            if not own:
                continue
            for half in range(2):
                sw, wv = w_in_chunk(8 + half)
                for blk in range(4):
                    b = next_mm()
                    mm_group(b, psum[b][:], [(hbuf[:, dc, blk * 128:(blk + 1) * 128], wv[:, dc, :]) for dc in range(8)],
                             [sw.res, r_h])
                    T.op("act", lambda e, b=b, blk=blk: e.activation(out=vs_ring[blk][:], in_=psum[b][:], func=AF.Gelu),
                         [pres[b]], [r_vs[blk]])
                for blk in range(4):
                    for gi in range(4):
                        T.op("dve", lambda e, blk=blk, gi=gi: e.bn_stats(
                            stats[:, blk * 4 + gi, :], vs_ring[blk][:, gi * 128:(gi + 1) * 128]), [r_vs[blk]], [r_st])
                for q_ in range(16):
                    T.op("dve", lambda e, q_=q_: e.bn_aggr(mv[:, q_, :], stats[:, q_, :]), [r_st], [r_st])
                T.op("act", lambda e: e.activation(out=lrs[:], in_=mv[:, :, 1], func=AF.Sqrt, bias=eps_col[:, 1:2]),
                     [r_st, r_pc], [r_st])
                T.op("dve", lambda e: e.reciprocal(lrs[:], lrs[:]), [r_st], [r_st])
                for blk in range(4):
                    for gi in range(4):
                        g = half * 4 + gi
                        q_ = blk * 4 + gi
                        T.op("dve", lambda e, gi=gi, g=g, blk=blk, q_=q_: e.tensor_scalar(
                            vn[:, blk, g * 128:(g + 1) * 128], vs_ring[blk][:, gi * 128:(gi + 1) * 128],
                            mv[:, q_, 0:1], lrs[:, q_:q_ + 1], ALU.subtract, ALU.mult), [r_vs[blk], r_st], [r_vn[blk]])
            GA_v = GAs.ap().rearrange("(c p) t -> p c t", p=128)
            MB_v = MBs.ap().rearrange("(c p) t -> p c t", p=128)
            for half in range(2):
                su_, wu_ = w_in_chunk(6 + half)
                sb_, wb_ = w_in_chunk(12 + half)
                sa_, wa_ = w_in_chunk(10 + half)
                for gi in range(4):
                    g = half * 4 + gi
                    k = g % 2
                    csl = slice(gi * 128, (gi + 1) * 128)
                    b = next_mm()
                    mm_group(b, psum[b][:], [(wu_[:, dc, csl], hbuf[:, dc, :]) for dc in range(8)], [su_.res, r_h])
                    T.op("act", lambda e, b=b, k=k: e.activation(out=u_ring[k][:], in_=psum[b][:], func=AF.Gelu),
                         [pres[b]], [r_u[k]])
                    b = next_mm()
                    mm_group(b, psum[b][:], [(wb_[:, dc, csl], hbuf[:, dc, :]) for dc in range(8)], [sb_.res, r_h])
                    T.op("act", lambda e, b=b, k=k: e.activation(out=gb_ring[k][:], in_=psum[b][:], func=AF.Sigmoid),
                         [pres[b]], [r_gb[k]])
                    b = next_mm()
                    mm_group(b, psum[b][:], [(wa_[:, dc, csl], hbuf[:, dc, :]) for dc in range(8)], [sa_.res, r_h])
                    gs = ga_ring.next()
                    T.op("act", lambda e, b=b, gs=gs: e.activation(out=gs.t[:], in_=psum[b][:], func=AF.Sigmoid),
                         [pres[b]], [gs.res])
                    T.dma("sp", gs.ds, [(GA_v[:, g, tsl], gs.t[:])], reads=[gs.res], writes=[r_GA[lt][g]])
                    b = next_mm()
                    for blk in range(4):
                        T.op("pe", lambda e, b=b, blk=blk, g=g: e.matmul(
                            psum[b][:, blk * 128:(blk + 1) * 128], vn[:, blk, g * 128:(g + 1) * 128], wsT_b[:, g, :],
                            start=True, stop=True), [r_vn[blk], r_sgc], [pres[b]], signal=(blk == 3))
                    ms = mb_ring.next()
                    T.op("dve", lambda e, b=b, ms=ms, g=g: e.scalar_tensor_tensor(
                        out=ms.t[:].rearrange("p (b t) -> p b t", b=4), in0=psum[b][:].rearrange("p (b t) -> p b t", b=4),
                        scalar=lncols[:, 0, g:g + 1], in1=e_g[:, g, :].unsqueeze(1).to_broadcast([128, 4, 128]),
                        op0=ALU.mult, op1=ALU.add), [pres[b], r_const, r_sgc], [ms.res])
                    T.op("dve", lambda e, ms=ms, k=k: e.tensor_tensor(ms.t[:], ms.t[:], u_ring[k][:], ALU.mult),
                         [r_u[k], ms.res], [ms.res])
                    T.op("dve", lambda e, ms=ms, k=k: e.tensor_tensor(ms.t[:], ms.t[:], gb_ring[k][:], ALU.mult),
                         [r_gb[k], ms.res], [ms.res])
                    T.dma("sp", ms.ds, [(MB_v[:, g, tsl], ms.t[:])], reads=[ms.res], writes=[r_MB[lt][g]])

        wdirect["on"] = False
        T.barrier()
        if STOP == 1:
            raise _StopBuild

        ar.reset(cmark)
        lg = ar.alloc([128, NIDX, H], F32, "lg")
        pre = ar.alloc([128, NIDX, H], F32, "pre")
        preq = [ar.alloc([128, NIDX, H], F32, "preq") for _ in range(2)]
        negc = ar.alloc([128, H, NIDX], F32, "negc")
        offcol = ar.alloc([128, 2], F32, "offcol")
        r_c = Res("cums")
        ds_lg = T.dsem("lg")
        T.dma("sp", ds_lg, [(lg[:, sl * NLB:(sl + 1) * NLB, :], LG_v[sl]) for sl in range(2)],
              reads=r_LG[0] + r_LG[1], writes=[r_c])

        def gidx(q, gblk):
            G, j = gblk // 4, gblk % 4
            if G in GT[q]:
                return GT[q].index(G) * 4 + j
            return NLB + GT[1 - q].index(G) * 4 + j

        lg2 = lg[:].rearrange("p i h -> p (i h)")
        NLG = NIDX * H
        T.op("pe", lambda e: e.matmul(psum[0][:, 0:NLG], tri_f[:], lg2, start=True, stop=True), [r_c, r_pc], [pres[0]])
        T.op("pe", lambda e: e.matmul(psum[1][:, 0:NLG], ones_f[:], lg2, start=True, stop=True), [r_c, r_pc], [pres[1]])
        tot = ar.alloc([128, NIDX, H], F32, "tot")
        T.op("dve", lambda e: e.tensor_copy(tot[:].rearrange("p i h -> p (i h)"), psum[1][:, 0:NLG]), [pres[1]], [r_c])
        for q in range(2):
            T.op("dve", lambda e, q=q: e.memset(preq[q][:, gidx(q, 0), :], 0.0), [r_c], [r_c])
            for gb_ in range(1, NIDX):
                a, bq = gidx(q, gb_), gidx(q, gb_ - 1)
                T.op("dve", lambda e, a=a, bq=bq, q=q: e.tensor_tensor(preq[q][:, a, :], preq[q][:, bq, :], tot[:, bq, :], ALU.add),
                     [r_c], [r_c])
        T.op("dve", lambda e: e.tensor_scalar(pre[:], preq[0][:], flags[:, 1:2], None, ALU.mult), [r_c, r_const], [r_c])
        T.op("dve", lambda e: e.scalar_tensor_tensor(out=pre[:], in0=preq[1][:], scalar=flags[:, 0:1], in1=pre[:],
                                                     op0=ALU.mult, op1=ALU.add), [r_c, r_const], [r_c])
        T.op("dve", lambda e: e.scalar_tensor_tensor(
            out=negc[:].rearrange("p h i -> p i h"), in0=psum[0][:, 0:NLG].rearrange("p (i h) -> p i h", h=H), scalar=-1.0,
            in1=pre[:], op0=ALU.mult, op1=ALU.subtract), [pres[0], r_c], [r_c])
        T.op("dve", lambda e: e.tensor_scalar(offcol[:, 0:1], flags[:, 1:2], -30000.0, None, ALU.mult), [r_const], [r_c])
        T.op("dve", lambda e: e.tensor_scalar(offcol[:, 1:2], flags[:, 0:1], -30000.0, None, ALU.mult), [r_const], [r_c])

        kbuf = [Slot(T, ar.alloc([128, 2 * TPC], BF16, "kb"), f"kb{i}") for i in range(2)]
        vbuf = [Slot(T, ar.alloc([128, NIDX, 128], BF16, "vb"), f"vb{i}") for i in range(2)]
        q_ring = Ring(T, ar, 2, [128, TT], BF16, "q")
        p_ring = [ar.alloc([128, TT], BF16, "p") for _ in range(3)]
        r_p = [Res(f"p{i}") for i in range(3)]
        bias_ring = [ar.alloc([128, NIDX], F32, "bias") for _ in range(2)]
        r_bias = [Res("bias0"), Res("bias1")]
        gat = Ring(T, ar, 2, [128, TT], F32, "gat")
        mbt = Ring(T, ar, 2, [128, TT], F32, "mbt")
        rden = ar.alloc([128, TT], F32, "rden")
        r_rden = Res("rden")
        otmp = ar.alloc([128, TT], F32, "otmp")
        r_otmp = Res("otmp")
        mg_ring = Ring(T, ar, 2, [128, TT], BF16, "mgst")

        PS_S = [0, 1, 2]
        PS_O = [3, 4]
        PS_D = [5, 6]
        GA_v = GAs.ap().rearrange("(c p) t -> p c t", p=128)
        MB_v = MBs.ap().rearrange("(c p) t -> p c t", p=128)
        MG_v = MGs.ap().rearrange("(c p) t -> p c t", p=128)
        scale = 1.0 / float(np.sqrt(HD))

        units = []
        grp = 0
        for h in range(H):
            for lt in range(NT):
                first = True
                for sl in (1, 0):
                    for klt in range(lt + 1):
                        for j in range(4):
                            units.append(dict(h=h, lt=lt, grp=grp, first=first, last=False, rk=sl, klt=klt, j=j,
                                              diag=(sl == 0 and klt == lt)))
                            first = False
                units[-1]["last"] = True
                grp += 1
        state = {}

        def load_head(h):
            ks, vsb = kbuf[h % 2], vbuf[h % 2]
            T.dma("sp", ks.ds, [(ks.t[:, rk * TPC:(rk + 1) * TPC], KG.ap()[rk * 1024 + h * 128: rk * 1024 + (h + 1) * 128, :])
                                for rk in range(2)], reads=r_KG[0] + r_KG[1], writes=[ks.res])
            T.dma("sp", vsb.ds, [(vsb.t[:, rk * NLB:(rk + 1) * NLB, :],
                                  VG.ap()[rk * 1024 + h * 128: rk * 1024 + (h + 1) * 128, :].rearrange("s (l d) -> s l d", d=128))
                                 for rk in range(2)], reads=r_VG[0] + r_VG[1], writes=[vsb.res])

        def group_prologue(u):
            h, lt, g = u["h"], u["lt"], u["grp"]
            if lt == 0 and h == 0:
                load_head(0)
            if lt == min(1, NT - 1) and h + 1 < H:
                load_head(h + 1)
            qs = q_ring.next()
            T.dma("sp", qs.ds, [(qs.t[:], Q_v[:, h, lt * TT:(lt + 1) * TT])], reads=[r_Q[lt]], writes=[qs.res])
            k = g % 2
            T.op("dve", lambda e, k=k, h=h, lt=lt: e.tensor_scalar(
                bias_ring[k][:], negc[:, h, :], pre[:, 4 * lt + 2, h:h + 1], BCLAMP, ALU.add, ALU.min), [r_c], [r_bias[k]])
            fsl = slice(NLB + 4 * lt, NLB + 4 * lt + 4)
            T.op("dve", lambda e, k=k, lt=lt, fsl=fsl: e.tensor_scalar(
                bias_ring[k][:, fsl], bias_ring[k][:, fsl], offcol[:, lt % 2:lt % 2 + 1], None, ALU.add),
                [r_c, r_bias[k]], [r_bias[k]])
            gs, ms = gat.next(), mbt.next()
            T.dma("sp", gs.ds, [(gs.t[:], GA_v[:, h, lt * TT:(lt + 1) * TT])], reads=[r_GA[lt][h]], writes=[gs.res])
            T.dma("sp", ms.ds, [(ms.t[:], MB_v[:, h, lt * TT:(lt + 1) * TT])], reads=[r_MB[lt][h]], writes=[ms.res])
            state[g] = dict(q=qs, ga=gs, mb=ms)

        def emit_S(i):
            u = units[i]
            if u["first"]:
                group_prologue(u)
            st = state[u["grp"]]
            ks = kbuf[u["h"] % 2]
            b = PS_S[i % 3]
            col = u["rk"] * TPC + u["klt"] * TT + u["j"] * 128
            T.op("pe", lambda e, b=b, ks=ks, col=col, q=st["q"]: e.matmul(
                psum[b][:], ks.t[:, col:col + 128], q.t[:], start=True, stop=True),
                [ks.res, st["q"].res], [pres[b]])

        def emit_rest(i):
            u = units[i]
            g, h, lt = u["grp"], u["h"], u["lt"]
            st = state[g]
            b = PS_S[i % 3]
            k3 = i % 3
            idx = u["rk"] * NLB + u["klt"] * 4 + u["j"]
            kb = g % 2
            T.op("act", lambda e, b=b, k3=k3, kb=kb, idx=idx: e.activation(
                out=p_ring[k3][:], in_=psum[b][:], func=AF.Exp, bias=bias_ring[kb][:, idx:idx + 1], scale=scale),
                [pres[b], r_bias[kb]], [r_p[k3]])
            if u["diag"]:
                T.op("dve", lambda e, k3=k3, j=u["j"]: e.tensor_tensor(p_ring[k3][:], p_ring[k3][:], diag[:, j, :], ALU.mult),
                     [r_pc, r_p[k3]], [r_p[k3]])
            vsb = vbuf[h % 2]
            bo, bd = PS_O[g % 2], PS_D[g % 2]
            T.op("pe", lambda e, bo=bo, vsb=vsb, idx=idx, k3=k3, f=u["first"], l=u["last"]: e.matmul(
                psum[bo][:], vsb.t[:, idx, :], p_ring[k3][:], start=f, stop=l), [vsb.res, r_p[k3]], [pres[bo]], signal=False)
            T.op("pe", lambda e, bd=bd, k3=k3, f=u["first"], l=u["last"]: e.matmul(
                psum[bd][:], ones_bf[:], p_ring[k3][:], start=f, stop=l), [r_p[k3], r_pc], [pres[bd]], signal=True)
            if u["last"]:
                T.op("dve", lambda e, bd=bd: e.reciprocal(rden[:], psum[bd][:]), [pres[bd]], [r_rden])
                T.op("dve", lambda e, bo=bo: e.tensor_tensor(otmp[:], psum[bo][:], rden[:], ALU.mult),
                     [pres[bo], r_rden], [r_otmp])
                T.op("dve", lambda e, gs=st["ga"]: e.tensor_tensor(otmp[:], otmp[:], gs.t[:], ALU.mult),
                     [r_otmp, st["ga"].res], [r_otmp])
                mg = mg_ring.next()
                T.op("dve", lambda e, mg=mg, ms=st["mb"]: e.tensor_tensor(mg.t[:], otmp[:], ms.t[:], ALU.add),
                     [r_otmp, st["mb"].res], [mg.res])
                T.dma("sp", mg.ds, [(MG_v[:, h, lt * TT:(lt + 1) * TT], mg.t[:])], reads=[mg.res], writes=[r_MG[lt][h]])
                del state[g]

        NU = len(units)
        LOOK = 2
        for i in range(min(LOOK, NU)):
            emit_S(i)
        for i in range(NU):
            if i + LOOK < NU:
                emit_S(i + LOOK)
            emit_rest(i)
        T.barrier()
        if STOP == 2:
            raise _StopBuild

        ar.reset(ffn_mark)
        wout_sb = ar.alloc([128, 8, D], BF16, "wout")
        r_wo = Res("wout")
        ds_wo = T.dsem("wout")
        need_cast("wout", 1)
        T.dma("pool", ds_wo, [(wout_sb[:], wbf["wout"].ap().rearrange("(c p) n -> p c n", p=128))],
              reads=r_w["wout"], writes=[r_wo])
        mgin = Ring(T, ar, 2, [128, 8, TT], BF16, "mgin")
        outT_v = outT.ap().rearrange("(c p) t -> p c t", p=128)
        r_out = Res("out")
        r_x2 = [[Res(f"x2_{i}_{dc}") for dc in range(8)] for i in range(2)]

        def load_b2(lt):
            xsl = xbuf[lt % 2]
            tsl = slice(lt * TT, (lt + 1) * TT)
            T.dma("sp", xsl.ds, [(xsl.t[:], X1_v[:, :, tsl])], reads=[r_X1[lt]], writes=r_x2[lt % 2])
            ms = mgin.next()
            T.dma("sp", ms.ds, [(ms.t[:], MG_v[:, :, tsl])], reads=r_MG[lt], writes=[ms.res])
            return ms

        nxt = load_b2(0)
        for lt in range(NT):
            ms = nxt
            xsl = xbuf[lt % 2]
            x_t = xsl.t
            rx = r_x2[lt % 2]
            tsl = slice(lt * TT, (lt + 1) * TT)
            if lt + 1 < NT:
                nxt = load_b2(lt + 1)
            for dc in range(8):
                b = next_mm()
                mm_group(b, psum[b][:], [(wout_sb[:, c, dc * 128:(dc + 1) * 128], ms.t[:, c, :]) for c in range(8)],
                         [r_wo, ms.res])
                T.op("act", lambda e, dc=dc, b=b: e.activation(out=ybuf[:, dc, :], in_=psum[b][:], func=AF.Copy),
                     [pres[b]], [r_y[dc]])
            rmsnorm_stats(ybuf, r_y)
            for dc in range(8):
                k = cnt["tmp"] % 2
                cnt["tmp"] += 1
                T.op("dve", lambda e, dc=dc, k=k: e.scalar_tensor_tensor(
                    out=tmpb[k][:], in0=ybuf[:, dc, :], scalar=gcols[:, 3, dc:dc + 1], in1=rstd[:],
                    op0=ALU.mult, op1=ALU.mult), [r_y[dc], r_rstd, r_const], [r_tmp[k]])
                T.op("dve", lambda e, dc=dc, k=k, x_t=x_t: e.tensor_tensor(x_t[:, dc, :], x_t[:, dc, :], tmpb[k][:], ALU.add),
                     [r_tmp[k], rx[dc]], [rx[dc]])
            prenorm(x_t, rx, 4)
            ffn(("wg2", "wu2", "wd2"), x_t, rx, 5)
            T.dma("sp", xsl.ds, [(outT_v[:, :, tsl], x_t[:])], reads=rx, writes=[r_out])
        T.barrier()


    try:
        body()
    except _StopBuild:
        T.barrier()

    from contextlib import ExitStack
    with ExitStack() as es:
        for E in T.eng.values():
            E.sem = es.enter_context(nc.semaphore("s_" + E.name))
        for d in T.dsems:
            d.sem = es.enter_context(nc.semaphore("d_" + d.name))
        block = es.enter_context(nc.Block())

        @block.tensor
        def _(e):
            T.emit(nc, "pe", e)

        @block.scalar
        def _(e):
            T.emit(nc, "act", e)

        @block.vector
        def _(e):
            T.emit(nc, "dve", e)

        @block.gpsimd
        def _(e):
            T.emit(nc, "pool", e)

        @block.sync
        def _(e):
            T.emit(nc, "sp", e)
    return nc


def _prep_inputs(inp):
    f = lambda a: np.ascontiguousarray(np.asarray(a, dtype=np.float32))
    x = f(inp["x"])
    w = {"wg1": f(inp["ffn1_w_gate"])[0], "wu1": f(inp["ffn1_w_up"])[0], "wd1": f(inp["ffn1_w_down"])[0],
         "wout": f(inp["w_out"])[0], "wg2": f(inp["ffn2_w_gate"])[0], "wu2": f(inp["ffn2_w_up"])[0],
         "wd2": f(inp["ffn2_w_down"])[0]}
    w_in = f(inp["w_in"])[0]
    w["win"] = np.ascontiguousarray(np.concatenate([w_in[:, :3072], w_in[:, 3080:]], axis=1))
    wf = np.ascontiguousarray(w_in[:, 3072:3080])
    gains = [inp["ffn1_pre_g"], inp["ffn1_post_g"], inp["mix_pre_g"], inp["mix_post_g"], inp["ffn2_pre_g"],
             inp["ffn2_post_g"]]
    gcols = np.stack([f(g)[0].reshape(8, 128).T for g in gains], axis=1).reshape(128, 48)
    lncols = np.stack([f(inp["sgu_ln_g"])[0].reshape(8, 128).T, f(inp["sgu_ln_b"])[0].reshape(8, 128).T],
                      axis=1).reshape(128, 16)
    wsT = np.ascontiguousarray(f(inp["sgu_w_s"])[0].transpose(2, 0, 1)).reshape(128, 1024)
    bs = f(inp["sgu_b_s"])[0].reshape(1, 1024)
    bfg = f(inp["b_forget"]).reshape(1, 8)
    maps = []
    for c in range(NCORE):
        b, p = c // 2, c % 2
        toks = np.concatenate([np.arange(G * TT, (G + 1) * TT) for G in GT[p]])
        tokp = np.concatenate([np.arange(G * TT, (G + 1) * TT) for G in GT[1 - p]])
        m = {"xT": np.ascontiguousarray(x[b][toks].T), "xTp": np.ascontiguousarray(x[b][tokp].T)}
        for (n, r, cc) in WEIGHTS:
            m[n + "_f"] = w[n]
        m["wf"] = wf
        m["gcols"] = np.ascontiguousarray(gcols)
        m["lncols"] = np.ascontiguousarray(lncols)
        m["wsT"] = wsT
        m["bs"] = bs
        m["bfg"] = bfg
        fl = np.zeros((128, 4), np.float32)
        fl[:, 0] = p
        fl[:, 1] = 1 - p
        m["flags"] = fl
        maps.append(m)
    return maps


_NC_CACHE = {}


def kernel(**inputs):
    maps = _prep_inputs(inputs)
    if "nc" not in _NC_CACHE:
        _NC_CACHE["nc"] = build()
    nc = _NC_CACHE["nc"]
    res = run_bass_kernel_spmd(nc, maps, core_ids=list(range(NCORE)))
    out = np.empty((NB, SEQ, D), np.float32)
    for c in range(NCORE):
        b, p = c // 2, c % 2
        o = np.asarray(res.results[c]["outT"]).T
        for i, G in enumerate(GT[p]):
            out[b, G * TT:(G + 1) * TT] = o[i * TT:(i + 1) * TT]
    return out
```

```python
import numpy as np
import concourse.bass as bass
import concourse.mybir as mybir
from concourse.bass_utils import run_bass_kernel_spmd

F32 = mybir.dt.float32
BF16 = mybir.dt.bfloat16
AF = mybir.ActivationFunctionType
ALU = mybir.AluOpType

D = 1024
DFF = 4096
SEQ = 8192
NB = 4
NCORE = 8
TT = 512
NT = 8
TPC = NT * TT
NLB = NT * 4
NIDX = 2 * NLB
H = 8
HD = 128
WIN = 7168
GT = [[0, 3, 4, 7, 8, 11, 12, 15], [1, 2, 5, 6, 9, 10, 13, 14]]


def _configure(nt, ncore):
    global NT, TPC, NLB, NIDX, NCORE, GT, SEQ, NB
    NT, NCORE = nt, ncore
    TPC, NLB, NIDX = NT * TT, NT * 4, 2 * NT * 4
    SEQ = 2 * TPC
    NB = ncore // 2
    g0 = [g for g in range(2 * NT) if g % 4 in (0, 3)]
    g1 = [g for g in range(2 * NT) if g % 4 in (1, 2)]
    GT = [g0, g1]
RMS_EPS = 1e-6
LN_EPS = 1e-5
BCLAMP = 64.0
NSLOT = 5
SAME_ENGINE_SYNC = True
STOP = 99


class _StopBuild(Exception):
    pass

WEIGHTS = [("wg1", D, DFF), ("wu1", D, DFF), ("wd1", DFF, D), ("win", D, WIN),
           ("wout", D, D), ("wg2", D, DFF), ("wu2", D, DFF), ("wd2", DFF, D)]


class Res:
    __slots__ = ("name", "w", "r")

    def __init__(self, name):
        self.name = name
        self.w = None
        self.r = {}


class Eng:
    def __init__(self, name):
        self.name = name
        self.sem = None
        self.cnt = 0
        self.waited = {}
        self.prog = []


class DSem:
    def __init__(self, name, step=16):
        self.name = name
        self.sem = None
        self.cnt = 0
        self.step = step
        self.last = None


class Tracker:
    def __init__(self):
        self.eng = {n: Eng(n) for n in ("pe", "act", "dve", "pool", "sp")}
        self.dsems = []
        self.nwait = 0

    def dsem(self, name, step=16):
        d = DSem(name, step)
        self.dsems.append(d)
        return d

    def _wait(self, E, deps):
        for tok in deps:
            if tok is None:
                continue
            key, val, en = tok
            if en == E.name:
                if E.name == "pe" or not SAME_ENGINE_SYNC:
                    continue
            if E.waited.get(id(key), 0) >= val:
                continue
            E.waited[id(key)] = val
            self.nwait += 1
            E.prog.append(("wait", key, val))

    def _deps(self, reads, writes):
        deps = []
        for r in reads:
            if r.w is not None:
                deps.append(r.w)
        for w in writes:
            if w.w is not None:
                deps.append(w.w)
            deps.extend(w.r.values())
        return deps

    def _reg(self, tok, reads, writes):
        for r in reads:
            r.r[id(tok[0])] = tok
        for w in writes:
            w.w = tok
            w.r = {}

    def op(self, en, fn, reads=(), writes=(), signal=True):
        E = self.eng[en]
        self._wait(E, self._deps(reads, writes))
        if signal:
            E.cnt += 1
            tok = (E, E.cnt, en)
        else:
            tok = (E, E.cnt + 1, en)
        E.prog.append(("op", fn, signal))
        self._reg(tok, reads, writes)
        return tok

    def dma(self, q, ds, pairs, reads=(), writes=()):
        E = self.eng[q]
        deps = self._deps(reads, writes)
        deps.append(ds.last)
        self._wait(E, deps)
        for (o, i) in pairs:
            E.prog.append(("dma", o, i, ds))
        ds.cnt += ds.step * len(pairs)
        tok = (ds, ds.cnt, None)
        ds.last = tok
        self._reg(tok, reads, writes)
        return tok

    def cc(self, ds, kind, groups, in_ap, out_ap, reads=(), writes=()):
        E = self.eng["pool"]
        deps = self._deps(reads, writes)
        deps.append(ds.last)
        self._wait(E, deps)
        E.prog.append(("cc", kind, groups, in_ap, out_ap, ds))
        ds.cnt += ds.step
        tok = (ds, ds.cnt, None)
        ds.last = tok
        self._reg(tok, reads, writes)
        return tok

    def barrier(self):
        toks = [(E, E.cnt, E.name) for E in self.eng.values() if E.cnt > 0]
        toks += [d.last for d in self.dsems if d.last is not None]
        for E in self.eng.values():
            self._wait(E, [t for t in toks if t[2] != E.name])

    def emit(self, nc, en, h):
        E = self.eng[en]
        for item in E.prog:
            k = item[0]
            if k == "wait":
                h.wait_ge(item[1].sem, item[2])
            elif k == "op":
                ins = item[1](h)
                if item[2]:
                    ins.then_inc(E.sem, 1)
            elif k == "dma":
                h.dma_start(out=item[1], in_=item[2]).then_inc(item[3].sem, 16)
            elif k == "cc":
                h.collective_compute(item[1], ALU.bypass, replica_groups=item[2],
                                     ins=[item[3]], outs=[item[4]]).then_inc(item[5].sem, item[5].step)


class Arena:
    def __init__(self, nc, base=16512, limit=229376):
        self.nc = nc
        self.base = base
        self.off = base
        self.limit = limit
        self.n = 0

    def alloc(self, shape, dt, name=None):
        esz = 4 if dt == F32 else 2
        nbytes = esz * int(np.prod(shape[1:]))
        nbytes = (nbytes + 31) // 32 * 32
        assert self.off + nbytes <= self.limit, f"SBUF overflow {self.off}+{nbytes}"
        self.n += 1
        t = self.nc.alloc_sbuf_tensor_at(f"{name or 't'}_{self.n}", list(shape), dt, offset=self.off)
        self.off += nbytes
        return t

    def mark(self):
        return self.off

    def reset(self, m):
        self.off = m


class Slot:
    def __init__(self, T, t, name):
        self.t = t
        self.res = Res(name)
        self.ds = T.dsem(name)


class Ring:
    def __init__(self, T, arena, n, shape, dt, name):
        self.slots = [Slot(T, arena.alloc(shape, dt, name), f"{name}{i}") for i in range(n)]
        self.i = 0

    def next(self):
        s = self.slots[self.i % len(self.slots)]
        self.i += 1
        return s


def build():
    nc = bass.Bass("TRN2", target_bir_lowering=False)
    T = Tracker()

    def body():
        def din(name, shape, dt=F32):
            return nc.dram_tensor(name, list(shape), dt, kind="ExternalInput")

        xT = din("xT", [D, TPC])
        xTp = din("xTp", [D, TPC])
        wfull = {n: din(n + "_f", [r, c]) for (n, r, c) in WEIGHTS}
        wf_in = din("wf", [D, H])
        gcols_in = din("gcols", [128, 48])
        lncols_in = din("lncols", [128, 16])
        wsT_in = din("wsT", [128, 8 * 128])
        bs_in = din("bs", [1, 8 * 128])
        bf_in = din("bfg", [1, H])
        flags_in = din("flags", [128, 4])
        outT = nc.dram_tensor("outT", [D, TPC], F32, kind="ExternalOutput")

        wbf = {n: nc.dram_tensor(n + "_bf", [r, c], BF16) for (n, r, c) in WEIGHTS}
        Qs = nc.dram_tensor("Qs", [H * 128, TPC], BF16)
        KG = nc.dram_tensor("KG", [2 * H * 128, TPC], BF16)
        VG = nc.dram_tensor("VG", [2 * H * 128, NLB * 128], BF16)
        LG = nc.dram_tensor("LG", [2 * TPC, H], F32)
        GAs = nc.dram_tensor("GAs", [D, TPC], F32)
        MBs = nc.dram_tensor("MBs", [D, TPC], F32)
        X1s = nc.dram_tensor("X1s", [D, TPC], F32)
        MGs = nc.dram_tensor("MGs", [D, TPC], BF16)

        ar = Arena(nc)
        psum = [nc.alloc_psum_tensor(f"ps{i}", [128, 512], F32) for i in range(8)]
        pres = [Res(f"ps{i}") for i in range(8)]

        gcols = ar.alloc([128, 6, 8], F32, "gcols")
        lncols = ar.alloc([128, 2, 8], F32, "lncols")
        flags = ar.alloc([128, 4], F32, "flags")
        ones_bf = ar.alloc([128, 128], BF16, "ones_bf")
        invd_bf = ar.alloc([128, 128], BF16, "invd_bf")
        ones_f = ar.alloc([128, 128], F32, "ones_f")
        tri_f = ar.alloc([128, 128], F32, "tri_f")
        diag = ar.alloc([128, 4, 512], BF16, "diag")
        eps_col = ar.alloc([128, 2], F32, "eps")
        r_const = Res("const")
        ds_const = T.dsem("const")
        cmark = ar.mark()

        def vop(en, fname, reads, writes, *args, **kw):
            return T.op(en, lambda e: getattr(e, fname)(*args, **kw), reads, writes)

        T.dma("sp", ds_const, [
            (gcols[:].rearrange("p a b -> p (a b)"), gcols_in.ap()),
            (lncols[:].rearrange("p a b -> p (a b)"), lncols_in.ap()),
            (flags[:], flags_in.ap()),
        ], writes=[r_const])
        if STOP == -3:
            raise _StopBuild
        r_pc = Res("poolconst")
        vop("pool", "memset", [], [r_pc], ones_bf[:], 1.0)
        vop("pool", "memset", [], [r_pc], invd_bf[:], 1.0 / D)
        vop("pool", "memset", [], [r_pc], ones_f[:], 1.0)
        vop("pool", "memset", [], [r_pc], eps_col[:, 0:1], RMS_EPS)
        vop("pool", "memset", [], [r_pc], eps_col[:, 1:2], LN_EPS)
        T.op("pool", lambda e: e.affine_select(tri_f[:], ones_f[:], [[1, 128]], ALU.is_ge, 0.0,
                                               base=0, channel_multiplier=-1), [r_pc], [r_pc])
        vop("pool", "memset", [], [r_pc], diag[:].rearrange("p a b -> p (a b)"), 1.0)
        for j in range(4):
            T.op("pool", lambda e, j=j: e.affine_select(diag[:, j, :], diag[:, j, :], [[1, 512]], ALU.is_ge, 0.0,
                                                        base=-128 * j, channel_multiplier=-1), [r_pc], [r_pc])
        if STOP == -2:
            raise _StopBuild
        for n in (1, 5):
            vop("dve", "tensor_scalar", [r_const], [r_const], gcols[:, n, :], gcols[:, n, :], 0.5, None, ALU.mult)

        if STOP == -1:
            raise _StopBuild
        ds_wc = [T.dsem(f"wcast{i}") for i in range(4)]
        r_w = {n: [Res(f"{n}_bf{b}") for b in range(c // 512)] for (n, r, c) in WEIGHTS}
        cast_order = []
        for f4 in range(8):
            cast_order += [("wg1", f4), ("wu1", f4)]
        cast_order += [("wd1", 0), ("wd1", 1)]
        cast_order += [("win", ci) for ci in (0, 1, 2, 3, 4, 5, 8, 9, 6, 12, 10, 7, 13, 11)]
        cast_order += [("wout", 0), ("wout", 1)]
        for f4 in range(8):
            cast_order += [("wg2", f4), ("wu2", f4)]
        cast_order += [("wd2", 0), ("wd2", 1)]
        cast_state = {"pos": 0}

        def cast_upto(pos):
            pos = min(pos, len(cast_order))
            while cast_state["pos"] < pos:
                n, b = cast_order[cast_state["pos"]]
                ds = ds_wc[cast_state["pos"] % 4]
                cast_state["pos"] += 1
                T.dma("pool", ds, [(wbf[n].ap()[:, b * 512:(b + 1) * 512], wfull[n].ap()[:, b * 512:(b + 1) * 512])],
                      writes=[r_w[n][b]])

        def need_cast(n, b):
            cast_upto(cast_order.index((n, b)) + 4)
            cast_upto(cast_state["pos"] + 1)

        if STOP == 0:
            raise _StopBuild
        wring = Ring(T, ar, NSLOT, [128, 4096], BF16, "w")

        wdirect = {"on": False}

        def wload(src_ap, view, rw, src32=None):
            s = wring.next()
            dst = view(s.t)
            if wdirect["on"] and src32 is not None:
                T.dma("pool", s.ds, [(dst, src32)], writes=[s.res])
            else:
                T.dma("pool", s.ds, [(dst, src_ap)], reads=[rw], writes=[s.res])
            return s, dst

        def w_gu(n, f0):
            need_cast(n, f0 // 512)
            src = wbf[n].ap().rearrange("(c p) f -> p c f", p=128)[:, :, f0:f0 + 512]
            src32 = wfull[n].ap().rearrange("(c p) f -> p c f", p=128)[:, :, f0:f0 + 512]
            return wload(src, lambda t: t[:].rearrange("p (c f) -> p c f", c=8), r_w[n][f0 // 512], src32)

        def w_dn(n, dc):
            need_cast(n, dc // 4)
            src = wbf[n].ap().rearrange("(c p) d -> p c d", p=128)[:, :, dc * 128:(dc + 1) * 128]
            src32 = wfull[n].ap().rearrange("(c p) d -> p c d", p=128)[:, :, dc * 128:(dc + 1) * 128]
            return wload(src, lambda t: t[:].rearrange("p (c d) -> p c d", c=32), r_w[n][dc // 4], src32)

        xbuf = [Slot(T, ar.alloc([128, 8, TT], F32, "x"), f"x{i}") for i in range(2)]
        r_xc = [[Res(f"xc{i}_{dc}") for dc in range(8)] for i in range(2)]
        hA = ar.alloc([128, 8, TT], BF16, "hA")
        r_hA = [Res(f"hA{dc}") for dc in range(8)]
        abuf = ar.alloc([128, 32, TT], BF16, "a")
        r_a = [Res(f"a{i}") for i in range(32)]
        ybuf = ar.alloc([128, 8, TT], F32, "y")
        r_y = [Res(f"y{i}") for i in range(8)]
        sq = [ar.alloc([128, TT], BF16, "sq") for _ in range(2)]
        r_sq = [Res("sq0"), Res("sq1")]
        sqP = [ar.alloc([128, TT], BF16, "sqP") for _ in range(2)]
        r_sqP = [Res("sqP0"), Res("sqP1")]
        rstdP = ar.alloc([128, TT], F32, "rstdP")
        r_rstdP = Res("rstdP")
        rstdD = ar.alloc([128, TT], F32, "rstdD")
        r_rstdD = Res("rstdD")
        rstdM = ar.alloc([128, TT], F32, "rstdM")
        r_rstdM = Res("rstdM")
        sil = [ar.alloc([128, TT], F32, "sil") for _ in range(2)]
        r_sil = [Res("sil0"), Res("sil1")]
        ffn_mark = ar.mark()
        cnt = {"sq": 0, "sqP": 0, "sil": 0, "mm": 0}

        PS_G = [0, 1]
        PS_U = [2, 3]
        PS_MM = [4, 5, 6]
        PS_SS = 7

        def mm_group(bank, out_ap, pairs, reads):
            n = len(pairs)
            for i, (l, r) in enumerate(pairs):
                T.op("pe", lambda e, l=l, r=r, i=i: e.matmul(out_ap, l, r, start=(i == 0), stop=(i == n - 1)),
                     reads, [pres[bank]], signal=(i == n - 1))

        def next_mm():
            b = PS_MM[cnt["mm"] % 3]
            cnt["mm"] += 1
            return b

        PS_SS2 = 0

        def stat_ops(src_fn, r_src_fn, bank, ring, r_ring, ckey):
            slotof = {}

            def act(dc):
                k = cnt[ckey] % 2
                cnt[ckey] += 1
                slotof[dc] = k
                T.op("act", lambda e: e.activation(out=ring[k][:], in_=src_fn(dc), func=AF.Square),
                     r_src_fn(dc), [r_ring[k]])

            def pe(dc):
                k = slotof[dc]
                T.op("pe", lambda e: e.matmul(psum[bank][:], invd_bf[:], ring[k][:], start=(dc == 0), stop=(dc == 7)),
                     [r_ring[k], r_pc], [pres[bank]], signal=True)
            return act, pe

        def finish_rstd(bank, rt, r_rt):
            T.op("act", lambda e: e.activation(out=rt[:], in_=psum[bank][:], func=AF.Sqrt, bias=eps_col[:, 0:1]),
                 [pres[bank], r_pc], [r_rt])
            T.op("dve", lambda e: e.reciprocal(rt[:], rt[:]), [r_rt], [r_rt])

        def h_ops(src, r_src_list, gi, rt, r_rt, hdst, r_hdst, dcs):
            for dc in dcs:
                T.op("dve", lambda e, dc=dc: e.scalar_tensor_tensor(
                    out=hdst[:, dc, :], in0=src[:, dc, :], scalar=gcols[:, gi, dc:dc + 1], in1=rt[:],
                    op0=ALU.mult, op1=ALU.mult), [r_src_list[dc], r_rt, r_const], [r_hdst[dc]])

        def prenorm_now(src, r_src_list, gi, hdst, r_hdst, bank, rt, r_rt):
            act, pe = stat_ops(lambda dc: src[:, dc, :], lambda dc: [r_src_list[dc]], bank, sqP, r_sqP, "sqP")
            for dc in range(8):
                act(dc)
                pe(dc)
            finish_rstd(bank, rt, r_rt)
            h_ops(src, r_src_list, gi, rt, r_rt, hdst, r_hdst, range(8))

        def p_side(src, r_src_list, gi, hdst, r_hdst):
            act, pe = stat_ops(lambda dc: src[:, dc, :], lambda dc: [r_src_list[dc]], PS_SS2, sqP, r_sqP, "sqP")
            side = [[] for _ in range(8)]
            side[0] = [lambda: act(0), lambda: act(1)]
            side[1] = [lambda: pe(0), lambda: pe(1), lambda: act(2), lambda: act(3)]
            side[2] = [lambda: pe(2), lambda: pe(3), lambda: act(4), lambda: act(5)]
            side[3] = [lambda: pe(4), lambda: pe(5), lambda: act(6), lambda: act(7)]
            side[4] = [lambda: pe(6), lambda: pe(7), lambda: finish_rstd(PS_SS2, rstdP, r_rstdP)]
            side[5] = [lambda: h_ops(src, r_src_list, gi, rstdP, r_rstdP, hdst, r_hdst, range(0, 4))]
            side[6] = [lambda: h_ops(src, r_src_list, gi, rstdP, r_rstdP, hdst, r_hdst, range(4, 8))]
            return side

        gu_state = {}

        def gateup(names, fcs, hsrc, r_hsrc, tasks):
            ng, nu = names
            for fc in fcs:
                fi = fc % 4
                if fi == 0:
                    gu_state["g"] = w_gu(ng, (fc // 4) * 512)
                    gu_state["u"] = w_gu(nu, (fc // 4) * 512)
                sg, wg = gu_state["g"]
                su, wu = gu_state["u"]
                bg = PS_G[fc % 2]
                bu = PS_U[fc % 2]
                mm_group(bg, psum[bg][:], [(wg[:, dc, fi * 128:(fi + 1) * 128], hsrc[:, dc, :]) for dc in range(8)],
                         [sg.res] + r_hsrc)
                mm_group(bu, psum[bu][:], [(wu[:, dc, fi * 128:(fi + 1) * 128], hsrc[:, dc, :]) for dc in range(8)],
                         [su.res] + r_hsrc)
                k = cnt["sil"] % 2
                cnt["sil"] += 1
                T.op("act", lambda e, bg=bg, k=k: e.activation(out=sil[k][:], in_=psum[bg][:], func=AF.Silu),
                     [pres[bg]], [r_sil[k]])
                T.op("dve", lambda e, bu=bu, k=k, fc=fc: e.tensor_tensor(abuf[:, fc, :], psum[bu][:], sil[k][:], ALU.mult),
                     [pres[bu], r_sil[k]], [r_a[fc]])
                if tasks:
                    tasks.pop(0)()

        def down_stage(nd, gpost, side):
            act, pe = stat_ops(lambda dc: psum[bank_of[dc]][:], lambda dc: [pres[bank_of[dc]]], PS_SS, sq, r_sq, "sq")
            bank_of = {}
            for dc in range(8):
                sd, wd = w_dn(nd, dc)
                b = next_mm()
                bank_of[dc] = b
                mm_group(b, psum[b][:], [(wd[:, fc, :], abuf[:, fc, :]) for fc in range(32)], [sd.res] + r_a)
                if dc > 0:
                    pe(dc - 1)
                T.op("act", lambda e, dc=dc, b=b: e.activation(out=ybuf[:, dc, :], in_=psum[b][:], func=AF.Copy,
                                                               scale=gcols[:, gpost, dc:dc + 1]),
                     [pres[b], r_const], [r_y[dc]])
                act(dc)
                for t in side[dc]:
                    t()
            pe(7)
            finish_rstd(PS_SS, rstdD, r_rstdD)

        def resid_tasks(ysrc, r_ysrc, rt, r_rt, x_t, rx, after=None):
            tasks = []
            for dc in range(8):
                def t(dc=dc):
                    T.op("dve", lambda e: e.tensor_tensor(ysrc[:, dc, :], ysrc[:, dc, :], rt[:], ALU.mult),
                         [r_ysrc[dc], r_rt], [r_ysrc[dc]])
                    T.op("dve", lambda e: e.tensor_tensor(x_t[:, dc, :], x_t[:, dc, :], ysrc[:, dc, :], ALU.add),
                         [r_ysrc[dc], rx[dc]], [rx[dc]])
                    if after is not None:
                        after(dc)
                tasks.append(t)
            return tasks

        def chain_tasks(x_t, rx, gi, hdst, r_hdst, bank, ring, r_ring, ckey, ysrc, r_ysrc, rt_in, r_rt_in,
                        store_fn=None, final_fn=None):
            act, pe = stat_ops(lambda dc: x_t[:, dc, :], lambda dc: [rx[dc]], bank, ring, r_ring, ckey)

            def after(dc):
                act(dc)
                if dc > 0:
                    pe(dc - 1)
            tasks = resid_tasks(ysrc, r_ysrc, rt_in, r_rt_in, x_t, rx, after)

            def t8():
                pe(7)
                if store_fn is not None:
                    store_fn()
            tasks.append(t8)
            tasks.append(lambda: finish_rstd(bank, rstdM, r_rstdM))
            tasks.append(lambda: h_ops(x_t, rx, gi, rstdM, r_rstdM, hdst, r_hdst, range(0, 4)))

            def t11():
                h_ops(x_t, rx, gi, rstdM, r_rstdM, hdst, r_hdst, range(4, 8))
                if final_fn is not None:
                    final_fn()
            tasks.append(t11)
            return tasks

        def flush(tasks):
            while tasks:
                tasks.pop(0)()

        wf_sb = ar.alloc([128, 8, H], BF16, "wf")
        r_wf = Res("wf")
        ds_wf = T.dsem("wf")
        T.dma("pool", ds_wf, [(wf_sb[:], wf_in.ap().rearrange("(c p) h -> p c h", p=128))], writes=[r_wf])
        wsT_f = ar.alloc([128, 8, 128], F32, "wsTf")
        wsT_b = ar.alloc([128, 8, 128], BF16, "wsTb")
        bs_bc = ar.alloc([128, 8, 128], F32, "bsbc")
        bf_bc = ar.alloc([128, H], F32, "bfbc")
        e_g = ar.alloc([128, 8, 128], F32, "eg")
        r_sgc = Res("sgc")
        T.dma("sp", ds_const, [
            (wsT_f[:].rearrange("p a b -> p (a b)"), wsT_in.ap()),
            (bs_bc[:].rearrange("p a b -> p (a b)"), bs_in.ap().partition_broadcast(128).rearrange("p a b -> p (a b)")),
            (bf_bc[:], bf_in.ap().partition_broadcast(128).rearrange("p a b -> p (a b)")),
        ], writes=[r_sgc])
        T.op("dve", lambda e: e.memset(wsT_f[64:128, :, 0:64], 0.0), [r_sgc], [r_sgc])
        T.op("dve", lambda e: e.tensor_copy(wsT_b[:], wsT_f[:]), [r_sgc], [r_sgc])
        for half in range(2):
            T.op("pe", lambda e, half=half: e.matmul(psum[half][:], ones_f[:],
                                                     wsT_f[:, half * 4:(half + 1) * 4, :].rearrange("p a b -> p (a b)"),
                                                     start=True, stop=True),
                 [r_sgc, r_pc], [pres[half]])
            for gi in range(4):
                g = half * 4 + gi
                T.op("dve", lambda e, half=half, gi=gi, g=g: e.scalar_tensor_tensor(
                    out=e_g[:, g, :], in0=psum[half][:, gi * 128:(gi + 1) * 128], scalar=lncols[:, 1, g:g + 1],
                    in1=bs_bc[:, g, :], op0=ALU.mult, op1=ALU.add), [pres[half], r_const, r_sgc], [r_sgc])

        a_mark = ar.mark()
        hM = ar.alloc([128, 8, TT], BF16, "hM")
        r_hM = [Res(f"hM{dc}") for dc in range(8)]
        qk_ring = Ring(T, ar, 2, [128, 4, TT], BF16, "qkst")
        ybase = nc.lookup_mloc(ybuf).addr
        vst = Slot(T, nc.alloc_sbuf_tensor_at("vst_alias", [128, 4, D], BF16, offset=ybase), "vst")
        vn = nc.alloc_sbuf_tensor_at("vn_alias", [128, 4, D], BF16, offset=ybase + 4 * D * 2)
        r_vn = [r_y[4 + b] for b in range(4)]
        vs_ring = [ar.alloc([128, 512], F32, "vs") for _ in range(4)]
        r_vs = [Res(f"vs{i}") for i in range(4)]
        stats = ar.alloc([128, 16, 6], F32, "stats")
        mv = ar.alloc([128, 16, 2], F32, "mv")
        lrs = ar.alloc([128, 16], F32, "lrs")
        r_st = Res("stats")
        u_ring = [ar.alloc([128, TT], F32, "u") for _ in range(2)]
        r_u = [Res("u0"), Res("u1")]
        gb_ring = [ar.alloc([128, TT], F32, "gb") for _ in range(2)]
        r_gb = [Res("gb0"), Res("gb1")]
        ga_ring = Ring(T, ar, 2, [128, TT], F32, "gast")
        mb_ring = Ring(T, ar, 3, [128, TT], F32, "mbst")
        lf_z = ar.alloc([128, 4, H], F32, "lfz")
        lf_st = Slot(T, ar.alloc([128, 4, H], F32, "lfst"), "lfst")
        r_lfz = Res("lfz")

        r_Q = [[Res(f"Q{i}_{hf}") for hf in range(2)] for i in range(NT)]
        r_KG = [[[Res(f"KG{sl}_{i}_{hf}") for hf in range(2)] for i in range(NT)] for sl in range(2)]
        r_VG = [[Res(f"VG{sl}_{i}") for i in range(NT)] for sl in range(2)]
        r_LG = [[Res(f"LG{sl}_{i}") for i in range(NT)] for sl in range(2)]
        r_GA = [[Res(f"GA{i}_{g}") for g in range(8)] for i in range(NT)]
        r_MB = [[Res(f"MB{i}_{g}") for g in range(8)] for i in range(NT)]
        r_X1 = [Res(f"X1{i}") for i in range(NT)]
        r_MG = [[Res(f"MG{i}_{g}") for g in range(8)] for i in range(NT)]
        qk_cres = [[Res(f"qkc{i}_{c}") for c in range(8)] for i in range(2)]
        v_cres = [r_y[i % 4] for i in range(8)]

        xT_v = xT.ap().rearrange("(c p) t -> p c t", p=128)
        X1_v = X1s.ap().rearrange("(c p) t -> p c t", p=128)
        Q_v = Qs.ap().rearrange("(c p) t -> p c t", p=128)
        xTp_v = xTp.ap().rearrange("(c p) t -> p c t", p=128)
        KG_v = [KG.ap()[sl * 1024:(sl + 1) * 1024, :].rearrange("(c p) t -> p c t", p=128) for sl in range(2)]
        VG_v = [VG.ap()[sl * 1024:(sl + 1) * 1024, :].rearrange("(h s) (l d) -> s l h d", s=128, d=128)
                for sl in range(2)]
        LG_v = [LG.ap()[sl * TPC:(sl + 1) * TPC, :].rearrange("(l p) h -> p l h", p=128) for sl in range(2)]
        win_v = wbf["win"].ap().rearrange("(c p) n -> p c n", p=128)
        win32_v = wfull["win"].ap().rearrange("(c p) n -> p c n", p=128)

        def w_in_chunk(ci):
            need_cast("win", ci)
            return wload(win_v[:, :, ci * 512:(ci + 1) * 512],
                         lambda t: t[:].rearrange("p (c f) -> p c f", c=8), r_w["win"][ci],
                         win32_v[:, :, ci * 512:(ci + 1) * 512])

        cact = {"i": 0}

        def evac(dst_ap, src_ap, reads, writes, func=None):
            if func is not None:
                return T.op("act", lambda e: e.activation(out=dst_ap, in_=src_ap, func=func), reads, writes)
            cact["i"] += 1
            if cact["i"] % 2:
                return T.op("act", lambda e: e.activation(out=dst_ap, in_=src_ap, func=AF.Copy), reads, writes)
            return T.op("dve", lambda e: e.tensor_copy(dst_ap, src_ap), reads, writes)

        def load_x(ti):
            xs_ = xbuf[ti % 2]
            src = xT_v if ti < NT else xTp_v
            lt_ = ti % NT
            T.dma("sp", xs_.ds, [(xs_.t[:], src[:, :, lt_ * TT:(lt_ + 1) * TT])], writes=r_xc[ti % 2])

        def w_stage(slot, lt, own):
            tsl = slice(lt * TT, (lt + 1) * TT)
            for which, (dst_v, rdst) in enumerate(((Q_v, r_Q[lt]), (KG_v[slot], r_KG[slot][lt]))):
                if which == 0 and not own:
                    continue
                for half in range(2):
                    sti = qk_ring.i % 2
                    st = qk_ring.next()
                    sw, wv = w_in_chunk(which * 2 + half)
                    for ci in range(4):
                        b = next_mm()
                        mm_group(b, psum[b][:], [(wv[:, dc, ci * 128:(ci + 1) * 128], hM[:, dc, :]) for dc in range(8)],
                                 [sw.res] + r_hM)
                        evac(st.t[:, ci, :], psum[b][:], [pres[b]], [qk_cres[sti][ci]])
                    T.dma("sp", st.ds, [(dst_v[:, half * 4:(half + 1) * 4, tsl], st.t[:])], reads=qk_cres[sti][0:4],
                          writes=[rdst[half]])
            for half in range(2):
                sw, wv = w_in_chunk(4 + half)
                for blk in range(4):
                    b = next_mm()
                    mm_group(b, psum[b][:], [(hM[:, dc, blk * 128:(blk + 1) * 128], wv[:, dc, :]) for dc in range(8)],
                             [sw.res] + r_hM)
                    evac(vst.t[:, blk, half * 512:(half + 1) * 512], psum[b][:], [pres[b]], [v_cres[half * 4 + blk]])
            T.dma("sp", vst.ds, [(VG_v[slot][:, lt * 4 + blk, :, :], vst.t[:, blk, :].rearrange("p (h d) -> p h d", h=H))
                                 for blk in range(4)], reads=v_cres, writes=[r_VG[slot][lt]])
            b = next_mm()
            for blk in range(4):
                mm_group(b, psum[b][:, blk * H:(blk + 1) * H],
                         [(hM[:, dc, blk * 128:(blk + 1) * 128], wf_sb[:, dc, :]) for dc in range(8)], [r_wf] + r_hM)
            T.op("dve", lambda e, b=b: e.tensor_tensor(
                lf_z[:], psum[b][:, 0:4 * H].rearrange("p (b h) -> p b h", h=H),
                bf_bc[:].unsqueeze(1).to_broadcast([128, 4, H]), ALU.add), [pres[b], r_sgc], [r_lfz])
            T.op("act", lambda e: e.activation(out=lf_z[:], in_=lf_z[:], func=AF.Exp, scale=-1.0), [r_lfz], [r_lfz])
            T.op("act", lambda e: e.activation(out=lf_z[:], in_=lf_z[:], func=AF.Ln, bias=1.0), [r_lfz], [r_lfz])
            T.op("dve", lambda e: e.tensor_scalar(lf_st.t[:], lf_z[:], -1.0, None, ALU.mult), [r_lfz], [lf_st.res])
            T.dma("sp", lf_st.ds, [(LG_v[slot][:, lt * 4:(lt + 1) * 4, :], lf_st.t[:])], reads=[lf_st.res],
                  writes=[r_LG[slot][lt]])
            if not own:
                return
            for half in range(2):
                sw, wv = w_in_chunk(8 + half)
                for blk in range(4):
                    b = next_mm()
                    mm_group(b, psum[b][:], [(hM[:, dc, blk * 128:(blk + 1) * 128], wv[:, dc, :]) for dc in range(8)],
                             [sw.res] + r_hM)
                    T.op("act", lambda e, b=b, blk=blk: e.activation(out=vs_ring[blk][:], in_=psum[b][:], func=AF.Gelu),
                         [pres[b]], [r_vs[blk]])
                for blk in range(4):
                    for gi in range(4):
                        T.op("dve", lambda e, blk=blk, gi=gi: e.bn_stats(
                            stats[:, blk * 4 + gi, :], vs_ring[blk][:, gi * 128:(gi + 1) * 128]), [r_vs[blk]], [r_st])
                for q_ in range(16):
                    T.op("dve", lambda e, q_=q_: e.bn_aggr(mv[:, q_, :], stats[:, q_, :]), [r_st], [r_st])
                T.op("act", lambda e: e.activation(out=lrs[:], in_=mv[:, :, 1], func=AF.Sqrt, bias=eps_col[:, 1:2]),
                     [r_st, r_pc], [r_st])
                T.op("dve", lambda e: e.reciprocal(lrs[:], lrs[:]), [r_st], [r_st])
                for blk in range(4):
                    for gi in range(4):
                        g = half * 4 + gi
                        q_ = blk * 4 + gi
                        T.op("dve", lambda e, gi=gi, g=g, blk=blk, q_=q_: e.tensor_scalar(
                            vn[:, blk, g * 128:(g + 1) * 128], vs_ring[blk][:, gi * 128:(gi + 1) * 128],
                            mv[:, q_, 0:1], lrs[:, q_:q_ + 1], ALU.subtract, ALU.mult), [r_vs[blk], r_st], [r_vn[blk]])
            GA_v = GAs.ap().rearrange("(c p) t -> p c t", p=128)
            MB_v = MBs.ap().rearrange("(c p) t -> p c t", p=128)
            for half in range(2):
                su_, wu_ = w_in_chunk(6 + half)
                sb_, wb_ = w_in_chunk(12 + half)
                sa_, wa_ = w_in_chunk(10 + half)
                for gi in range(4):
                    g = half * 4 + gi
                    k = g % 2
                    csl = slice(gi * 128, (gi + 1) * 128)
                    b = next_mm()
                    mm_group(b, psum[b][:], [(wu_[:, dc, csl], hM[:, dc, :]) for dc in range(8)], [su_.res] + r_hM)
                    T.op("act", lambda e, b=b, k=k: e.activation(out=u_ring[k][:], in_=psum[b][:], func=AF.Gelu),
                         [pres[b]], [r_u[k]])
                    b = next_mm()
                    mm_group(b, psum[b][:], [(wb_[:, dc, csl], hM[:, dc, :]) for dc in range(8)], [sb_.res] + r_hM)
                    T.op("act", lambda e, b=b, k=k: e.activation(out=gb_ring[k][:], in_=psum[b][:], func=AF.Sigmoid),
                         [pres[b]], [r_gb[k]])
                    b = next_mm()
                    mm_group(b, psum[b][:], [(wa_[:, dc, csl], hM[:, dc, :]) for dc in range(8)], [sa_.res] + r_hM)
                    gs = ga_ring.next()
                    T.op("act", lambda e, b=b, gs=gs: e.activation(out=gs.t[:], in_=psum[b][:], func=AF.Sigmoid),
                         [pres[b]], [gs.res])
                    T.dma("sp", gs.ds, [(GA_v[:, g, tsl], gs.t[:])], reads=[gs.res], writes=[r_GA[lt][g]])
                    b = next_mm()
                    for blk in range(4):
                        T.op("pe", lambda e, b=b, blk=blk, g=g: e.matmul(
                            psum[b][:, blk * 128:(blk + 1) * 128], vn[:, blk, g * 128:(g + 1) * 128], wsT_b[:, g, :],
                            start=True, stop=True), [r_vn[blk], r_sgc], [pres[b]], signal=(blk == 3))
                    ms = mb_ring.next()
                    T.op("dve", lambda e, b=b, ms=ms, g=g: e.scalar_tensor_tensor(
                        out=ms.t[:].rearrange("p (b t) -> p b t", b=4), in0=psum[b][:].rearrange("p (b t) -> p b t", b=4),
                        scalar=lncols[:, 0, g:g + 1], in1=e_g[:, g, :].unsqueeze(1).to_broadcast([128, 4, 128]),
                        op0=ALU.mult, op1=ALU.add), [pres[b], r_const, r_sgc], [ms.res])
                    T.op("dve", lambda e, ms=ms, k=k: e.tensor_tensor(ms.t[:], ms.t[:], u_ring[k][:], ALU.mult),
                         [r_u[k], ms.res], [ms.res])
                    T.op("dve", lambda e, ms=ms, k=k: e.tensor_tensor(ms.t[:], ms.t[:], gb_ring[k][:], ALU.mult),
                         [r_gb[k], ms.res], [ms.res])
                    T.dma("sp", ms.ds, [(MB_v[:, g, tsl], ms.t[:])], reads=[ms.res], writes=[r_MB[lt][g]])

        NTA = 2 * NT
        FFN1 = ("wg1", "wu1")
        wdirect["on"] = True
        load_x(0)
        load_x(1)
        prenorm_now(xbuf[0].t, r_xc[0], 0, hA, r_hA, PS_SS2, rstdP, r_rstdP)
        gateup(FFN1, range(32), hA, r_hA, [])
        for ti in range(NTA):
            slot, lt = ti // NT, ti % NT
            own = slot == 0
            xs_ = xbuf[ti % 2]
            x_t, rx = xs_.t, r_xc[ti % 2]
            tsl = slice(lt * TT, (lt + 1) * TT)
            if ti + 1 < NTA:
                side = p_side(xbuf[(ti + 1) % 2].t, r_xc[(ti + 1) % 2], 0, hA, r_hA)
            else:
                side = [[] for _ in range(8)]
            down_stage("wd1", 1, side)

            def store_fn(xs_=xs_, x_t=x_t, rx=rx, lt=lt, tsl=tsl, own=own):
                if own:
                    T.dma("sp", xs_.ds, [(X1_v[:, :, tsl], x_t[:])], reads=rx, writes=[r_X1[lt]])

            def final_fn(ti=ti):
                if ti + 2 < NTA:
                    load_x(ti + 2)
            chain = chain_tasks(x_t, rx, 2, hM, r_hM, PS_SS, sq, r_sq, "sq", ybuf, r_y, rstdD, r_rstdD,
                                store_fn, final_fn)
            if ti + 1 < NTA:
                gateup(FFN1, range(0, 12), hA, r_hA, chain)
            flush(chain)
            w_stage(slot, lt, own)
            wdirect["on"] = False
            if ti + 1 < NTA:
                gateup(FFN1, range(12, 32), hA, r_hA, [])
        T.barrier()
        if STOP == 1:
            raise _StopBuild

        ar.reset(cmark)
        lg = ar.alloc([128, NIDX, H], F32, "lg")
        pre = ar.alloc([128, NIDX, H], F32, "pre")
        preq = [ar.alloc([128, NIDX, H], F32, "preq") for _ in range(2)]
        negc = ar.alloc([128, H, NIDX], F32, "negc")
        offcol = ar.alloc([128, 2], F32, "offcol")
        r_c = Res("cums")
        ds_lg = T.dsem("lg")
        T.dma("sp", ds_lg, [(lg[:, sl * NLB:(sl + 1) * NLB, :], LG_v[sl]) for sl in range(2)],
              reads=r_LG[0] + r_LG[1], writes=[r_c])

        def gidx(q, gblk):
            G, j = gblk // 4, gblk % 4
            if G in GT[q]:
                return GT[q].index(G) * 4 + j
            return NLB + GT[1 - q].index(G) * 4 + j

        lg2 = lg[:].rearrange("p i h -> p (i h)")
        NLG = NIDX * H
        T.op("pe", lambda e: e.matmul(psum[0][:, 0:NLG], tri_f[:], lg2, start=True, stop=True), [r_c, r_pc], [pres[0]])
        T.op("pe", lambda e: e.matmul(psum[1][:, 0:NLG], ones_f[:], lg2, start=True, stop=True), [r_c, r_pc], [pres[1]])
        tot = ar.alloc([128, NIDX, H], F32, "tot")
        T.op("dve", lambda e: e.tensor_copy(tot[:].rearrange("p i h -> p (i h)"), psum[1][:, 0:NLG]), [pres[1]], [r_c])
        for q in range(2):
            T.op("dve", lambda e, q=q: e.memset(preq[q][:, gidx(q, 0), :], 0.0), [r_c], [r_c])
            for gb_ in range(1, NIDX):
                a, bq = gidx(q, gb_), gidx(q, gb_ - 1)
                T.op("dve", lambda e, a=a, bq=bq, q=q: e.tensor_tensor(preq[q][:, a, :], preq[q][:, bq, :], tot[:, bq, :], ALU.add),
                     [r_c], [r_c])
        T.op("dve", lambda e: e.tensor_scalar(pre[:], preq[0][:], flags[:, 1:2], None, ALU.mult), [r_c, r_const], [r_c])
        T.op("dve", lambda e: e.scalar_tensor_tensor(out=pre[:], in0=preq[1][:], scalar=flags[:, 0:1], in1=pre[:],
                                                     op0=ALU.mult, op1=ALU.add), [r_c, r_const], [r_c])
        T.op("dve", lambda e: e.scalar_tensor_tensor(
            out=negc[:].rearrange("p h i -> p i h"), in0=psum[0][:, 0:NLG].rearrange("p (i h) -> p i h", h=H), scalar=-1.0,
            in1=pre[:], op0=ALU.mult, op1=ALU.subtract), [pres[0], r_c], [r_c])
        T.op("dve", lambda e: e.tensor_scalar(offcol[:, 0:1], flags[:, 1:2], -30000.0, None, ALU.mult), [r_const], [r_c])
        T.op("dve", lambda e: e.tensor_scalar(offcol[:, 1:2], flags[:, 0:1], -30000.0, None, ALU.mult), [r_const], [r_c])

        kbuf = [Slot(T, ar.alloc([128, 2 * TPC], BF16, "kb"), f"kb{i}") for i in range(2)]
        vbuf = [Slot(T, ar.alloc([128, NIDX, 128], BF16, "vb"), f"vb{i}") for i in range(2)]
        q_ring = Ring(T, ar, 2, [128, TT], BF16, "q")
        p_ring = [ar.alloc([128, TT], BF16, "p") for _ in range(3)]
        r_p = [Res(f"p{i}") for i in range(3)]
        bias_ring = [ar.alloc([128, NIDX], F32, "bias") for _ in range(2)]
        r_bias = [Res("bias0"), Res("bias1")]
        gat = Ring(T, ar, 2, [128, TT], F32, "gat")
        mbt = Ring(T, ar, 2, [128, TT], F32, "mbt")
        rden = ar.alloc([128, TT], F32, "rden")
        r_rden = Res("rden")
        otmp = ar.alloc([128, TT], F32, "otmp")
        r_otmp = Res("otmp")
        mg_ring = Ring(T, ar, 2, [128, TT], BF16, "mgst")

        PS_S = [0, 1, 2]
        PS_O = [3, 4]
        PS_D = [5, 6]
        GA_v = GAs.ap().rearrange("(c p) t -> p c t", p=128)
        MB_v = MBs.ap().rearrange("(c p) t -> p c t", p=128)
        MG_v = MGs.ap().rearrange("(c p) t -> p c t", p=128)
        scale = 1.0 / float(np.sqrt(HD))

        units = []
        grp = 0
        for h in range(H):
            for lt in range(NT):
                first = True
                for sl in (1, 0):
                    for klt in range(lt + 1):
                        for j in range(4):
                            units.append(dict(h=h, lt=lt, grp=grp, first=first, last=False, rk=sl, klt=klt, j=j,
                                              diag=(sl == 0 and klt == lt)))
                            first = False
                units[-1]["last"] = True
                grp += 1
        state = {}

        def load_head(h):
            ks, vsb = kbuf[h % 2], vbuf[h % 2]
            T.dma("sp", ks.ds, [(ks.t[:, rk * TPC:(rk + 1) * TPC], KG.ap()[rk * 1024 + h * 128: rk * 1024 + (h + 1) * 128, :])
                                for rk in range(2)], reads=[r for sl in range(2) for t_ in r_KG[sl] for r in t_], writes=[ks.res])
            T.dma("sp", vsb.ds, [(vsb.t[:, rk * NLB:(rk + 1) * NLB, :],
                                  VG.ap()[rk * 1024 + h * 128: rk * 1024 + (h + 1) * 128, :].rearrange("s (l d) -> s l d", d=128))
                                 for rk in range(2)], reads=r_VG[0] + r_VG[1], writes=[vsb.res])

        def group_prologue(u):
            h, lt, g = u["h"], u["lt"], u["grp"]
            if lt == 0 and h == 0:
                load_head(0)
            if lt == min(1, NT - 1) and h + 1 < H:
                load_head(h + 1)
            qs = q_ring.next()
            T.dma("sp", qs.ds, [(qs.t[:], Q_v[:, h, lt * TT:(lt + 1) * TT])], reads=[r_Q[lt][h // 4]], writes=[qs.res])
            k = g % 2
            T.op("dve", lambda e, k=k, h=h, lt=lt: e.tensor_scalar(
                bias_ring[k][:], negc[:, h, :], pre[:, 4 * lt + 2, h:h + 1], BCLAMP, ALU.add, ALU.min), [r_c], [r_bias[k]])
            fsl = slice(NLB + 4 * lt, NLB + 4 * lt + 4)
            T.op("dve", lambda e, k=k, lt=lt, fsl=fsl: e.tensor_scalar(
                bias_ring[k][:, fsl], bias_ring[k][:, fsl], offcol[:, lt % 2:lt % 2 + 1], None, ALU.add),
                [r_c, r_bias[k]], [r_bias[k]])
            gs, ms = gat.next(), mbt.next()
            T.dma("sp", gs.ds, [(gs.t[:], GA_v[:, h, lt * TT:(lt + 1) * TT])], reads=[r_GA[lt][h]], writes=[gs.res])
            T.dma("sp", ms.ds, [(ms.t[:], MB_v[:, h, lt * TT:(lt + 1) * TT])], reads=[r_MB[lt][h]], writes=[ms.res])
            state[g] = dict(q=qs, ga=gs, mb=ms)

        def emit_S(i):
            u = units[i]
            if u["first"]:
                group_prologue(u)
            st = state[u["grp"]]
            ks = kbuf[u["h"] % 2]
            b = PS_S[i % 3]
            col = u["rk"] * TPC + u["klt"] * TT + u["j"] * 128
            T.op("pe", lambda e, b=b, ks=ks, col=col, q=st["q"]: e.matmul(
                psum[b][:], ks.t[:, col:col + 128], q.t[:], start=True, stop=True),
                [ks.res, st["q"].res], [pres[b]])

        def emit_rest(i):
            u = units[i]
            g, h, lt = u["grp"], u["h"], u["lt"]
            st = state[g]
            b = PS_S[i % 3]
            k3 = i % 3
            idx = u["rk"] * NLB + u["klt"] * 4 + u["j"]
            kb = g % 2
            T.op("act", lambda e, b=b, k3=k3, kb=kb, idx=idx: e.activation(
                out=p_ring[k3][:], in_=psum[b][:], func=AF.Exp, bias=bias_ring[kb][:, idx:idx + 1], scale=scale),
                [pres[b], r_bias[kb]], [r_p[k3]])
            if u["diag"]:
                T.op("dve", lambda e, k3=k3, j=u["j"]: e.tensor_tensor(p_ring[k3][:], p_ring[k3][:], diag[:, j, :], ALU.mult),
                     [r_pc, r_p[k3]], [r_p[k3]])
            vsb = vbuf[h % 2]
            bo, bd = PS_O[g % 2], PS_D[g % 2]
            T.op("pe", lambda e, bo=bo, vsb=vsb, idx=idx, k3=k3, f=u["first"], l=u["last"]: e.matmul(
                psum[bo][:], vsb.t[:, idx, :], p_ring[k3][:], start=f, stop=l), [vsb.res, r_p[k3]], [pres[bo]], signal=False)
            T.op("pe", lambda e, bd=bd, k3=k3, f=u["first"], l=u["last"]: e.matmul(
                psum[bd][:], ones_bf[:], p_ring[k3][:], start=f, stop=l), [r_p[k3], r_pc], [pres[bd]], signal=True)
            if u["last"]:
                T.op("dve", lambda e, bd=bd: e.reciprocal(rden[:], psum[bd][:]), [pres[bd]], [r_rden])
                T.op("dve", lambda e, bo=bo: e.tensor_tensor(otmp[:], psum[bo][:], rden[:], ALU.mult),
                     [pres[bo], r_rden], [r_otmp])
                T.op("dve", lambda e, gs=st["ga"]: e.tensor_tensor(otmp[:], otmp[:], gs.t[:], ALU.mult),
                     [r_otmp, st["ga"].res], [r_otmp])
                mg = mg_ring.next()
                T.op("dve", lambda e, mg=mg, ms=st["mb"]: e.tensor_tensor(mg.t[:], otmp[:], ms.t[:], ALU.add),
                     [r_otmp, st["mb"].res], [mg.res])
                T.dma("sp", mg.ds, [(MG_v[:, h, lt * TT:(lt + 1) * TT], mg.t[:])], reads=[mg.res], writes=[r_MG[lt][h]])
                del state[g]

        NU = len(units)
        LOOK = 2
        for i in range(min(LOOK, NU)):
            emit_S(i)
        for i in range(NU):
            if i + LOOK < NU:
                emit_S(i + LOOK)
            emit_rest(i)
        T.barrier()
        if STOP == 2:
            raise _StopBuild

        ar.reset(ffn_mark)
        wout_sb = ar.alloc([128, 8, D], BF16, "wout")
        r_wo = Res("wout")
        ds_wo = T.dsem("wout")
        need_cast("wout", 1)
        T.dma("pool", ds_wo, [(wout_sb[:], wbf["wout"].ap().rearrange("(c p) n -> p c n", p=128))],
              reads=r_w["wout"], writes=[r_wo])
        mgin = Ring(T, ar, 2, [128, 8, TT], BF16, "mgin")
        y2 = ar.alloc([128, 8, TT], F32, "y2")
        r_y2 = [Res(f"y2_{dc}") for dc in range(8)]
        outT_v = outT.ap().rearrange("(c p) t -> p c t", p=128)
        r_out = Res("out")
        FFN2 = ("wg2", "wu2")

        def load_b2(lt):
            xsl = xbuf[lt % 2]
            tsl_ = slice(lt * TT, (lt + 1) * TT)
            T.dma("sp", xsl.ds, [(xsl.t[:], X1_v[:, :, tsl_])], reads=[r_X1[lt]], writes=r_xc[lt % 2])
            ms_ = mgin.next()
            T.dma("sp", ms_.ds, [(ms_.t[:], MG_v[:, :, tsl_])], reads=r_MG[lt], writes=[ms_.res])
            return ms_

        def wout_stage(ms_):
            bank_of = {}
            act, pe = stat_ops(lambda dc: psum[bank_of[dc]][:], lambda dc: [pres[bank_of[dc]]], PS_SS2, sqP, r_sqP, "sqP")
            for dc in range(8):
                b = next_mm()
                bank_of[dc] = b
                mm_group(b, psum[b][:], [(wout_sb[:, c, dc * 128:(dc + 1) * 128], ms_.t[:, c, :]) for c in range(8)],
                         [r_wo, ms_.res])
                if dc > 0:
                    pe(dc - 1)
                T.op("act", lambda e, dc=dc, b=b: e.activation(out=y2[:, dc, :], in_=psum[b][:], func=AF.Copy,
                                                               scale=gcols[:, 3, dc:dc + 1]),
                     [pres[b], r_const], [r_y2[dc]])
                act(dc)
            pe(7)

        def b2_chain(lt):
            x_t, rx = xbuf[lt % 2].t, r_xc[lt % 2]
            tasks = [lambda: finish_rstd(PS_SS2, rstdP, r_rstdP)]
            tasks += chain_tasks(x_t, rx, 4, hA, r_hA, PS_SS2, sqP, r_sqP, "sqP", y2, r_y2, rstdP, r_rstdP)
            return tasks

        mss = {0: load_b2(0)}
        if NT > 1:
            mss[1] = load_b2(1)
        wout_stage(mss[0])
        flush(b2_chain(0))
        gateup(FFN2, range(32), hA, r_hA, [])
        for lt in range(NT):
            xsl = xbuf[lt % 2]
            x_t, rx = xsl.t, r_xc[lt % 2]
            tsl = slice(lt * TT, (lt + 1) * TT)
            side = [[] for _ in range(8)]
            if lt + 1 < NT:
                wout_stage(mss[lt + 1])
                ch = b2_chain(lt + 1)
                per = [2, 2, 2, 2, 2, 1, 1, 1]
                for dc in range(8):
                    for _ in range(per[dc]):
                        if ch:
                            side[dc].append(ch.pop(0))
                assert not ch
            down_stage("wd2", 5, side)

            def out_store(xsl=xsl, x_t=x_t, rx=rx, tsl=tsl, lt=lt):
                T.dma("sp", xsl.ds, [(outT_v[:, :, tsl], x_t[:])], reads=rx, writes=[r_out])
                if lt + 2 < NT:
                    mss[lt + 2] = load_b2(lt + 2)
            otasks = resid_tasks(ybuf, r_y, rstdD, r_rstdD, x_t, rx)
            otasks.append(out_store)
            if lt + 1 < NT:
                gateup(FFN2, range(32), hA, r_hA, otasks)
            flush(otasks)
        T.barrier()


    try:
        body()
    except _StopBuild:
        T.barrier()

    from contextlib import ExitStack
    with ExitStack() as es:
        for E in T.eng.values():
            E.sem = es.enter_context(nc.semaphore("s_" + E.name))
        for d in T.dsems:
            d.sem = es.enter_context(nc.semaphore("d_" + d.name))
        block = es.enter_context(nc.Block())

        @block.tensor
        def _(e):
            T.emit(nc, "pe", e)

        @block.scalar
        def _(e):
            T.emit(nc, "act", e)

        @block.vector
        def _(e):
            T.emit(nc, "dve", e)

        @block.gpsimd
        def _(e):
            T.emit(nc, "pool", e)

        @block.sync
        def _(e):
            T.emit(nc, "sp", e)
    return nc


def _prep_inputs(inp):
    f = lambda a: np.ascontiguousarray(np.asarray(a, dtype=np.float32))
    x = f(inp["x"])
    w = {"wg1": f(inp["ffn1_w_gate"])[0], "wu1": f(inp["ffn1_w_up"])[0], "wd1": f(inp["ffn1_w_down"])[0],
         "wout": f(inp["w_out"])[0], "wg2": f(inp["ffn2_w_gate"])[0], "wu2": f(inp["ffn2_w_up"])[0],
         "wd2": f(inp["ffn2_w_down"])[0]}
    w_in = f(inp["w_in"])[0]
    w["win"] = np.ascontiguousarray(np.concatenate([w_in[:, :3072], w_in[:, 3080:]], axis=1))
    wf = np.ascontiguousarray(w_in[:, 3072:3080])
    gains = [inp["ffn1_pre_g"], inp["ffn1_post_g"], inp["mix_pre_g"], inp["mix_post_g"], inp["ffn2_pre_g"],
             inp["ffn2_post_g"]]
    gcols = np.stack([f(g)[0].reshape(8, 128).T for g in gains], axis=1).reshape(128, 48)
    lncols = np.stack([f(inp["sgu_ln_g"])[0].reshape(8, 128).T, f(inp["sgu_ln_b"])[0].reshape(8, 128).T],
                      axis=1).reshape(128, 16)
    wsT = np.ascontiguousarray(f(inp["sgu_w_s"])[0].transpose(2, 0, 1)).reshape(128, 1024)
    bs = f(inp["sgu_b_s"])[0].reshape(1, 1024)
    bfg = f(inp["b_forget"]).reshape(1, 8)
    maps = []
    for c in range(NCORE):
        b, p = c // 2, c % 2
        toks = np.concatenate([np.arange(G * TT, (G + 1) * TT) for G in GT[p]])
        tokp = np.concatenate([np.arange(G * TT, (G + 1) * TT) for G in GT[1 - p]])
        m = {"xT": np.ascontiguousarray(x[b][toks].T), "xTp": np.ascontiguousarray(x[b][tokp].T)}
        for (n, r, cc) in WEIGHTS:
            m[n + "_f"] = w[n]
        m["wf"] = wf
        m["gcols"] = np.ascontiguousarray(gcols)
        m["lncols"] = np.ascontiguousarray(lncols)
        m["wsT"] = wsT
        m["bs"] = bs
        m["bfg"] = bfg
        fl = np.zeros((128, 4), np.float32)
        fl[:, 0] = p
        fl[:, 1] = 1 - p
        m["flags"] = fl
        maps.append(m)
    return maps


_NC_CACHE = {}


def kernel(**inputs):
    maps = _prep_inputs(inputs)
    if "nc" not in _NC_CACHE:
        _NC_CACHE["nc"] = build()
    nc = _NC_CACHE["nc"]
    res = run_bass_kernel_spmd(nc, maps, core_ids=list(range(NCORE)))
    out = np.empty((NB, SEQ, D), np.float32)
    for c in range(NCORE):
        b, p = c // 2, c % 2
        o = np.asarray(res.results[c]["outT"]).T
        for i, G in enumerate(GT[p]):
            out[b, G * TT:(G + 1) * TT] = o[i * TT:(i + 1) * TT]
    return out
```

```python
import numpy as np
import concourse.bass as bass
import concourse.mybir as mybir
from concourse.bass_utils import run_bass_kernel_spmd

F32 = mybir.dt.float32
BF16 = mybir.dt.bfloat16
AF = mybir.ActivationFunctionType
ALU = mybir.AluOpType

D = 1024
DFF = 4096
SEQ = 8192
NB = 4
NCORE = 8
TT = 512
NT = 8
TPC = NT * TT
NLB = NT * 4
NIDX = 2 * NLB
H = 8
HD = 128
WIN = 7168
GT = [[0, 3, 4, 7, 8, 11, 12, 15], [1, 2, 5, 6, 9, 10, 13, 14]]


def _configure(nt, ncore):
    global NT, TPC, NLB, NIDX, NCORE, GT, SEQ, NB
    NT, NCORE = nt, ncore
    TPC, NLB, NIDX = NT * TT, NT * 4, 2 * NT * 4
    SEQ = 2 * TPC
    NB = ncore // 2
    g0 = [g for g in range(2 * NT) if g % 4 in (0, 3)]
    g1 = [g for g in range(2 * NT) if g % 4 in (1, 2)]
    GT = [g0, g1]
RMS_EPS = 1e-6
LN_EPS = 1e-5
BCLAMP = 64.0
NSLOT = 5
SAME_ENGINE_SYNC = True
STOP = 99


class _StopBuild(Exception):
    pass

WEIGHTS = [("wg1", D, DFF), ("wu1", D, DFF), ("wd1", DFF, D), ("win", D, WIN),
           ("wout", D, D), ("wg2", D, DFF), ("wu2", D, DFF), ("wd2", DFF, D)]


class Res:
    __slots__ = ("name", "w", "r")

    def __init__(self, name):
        self.name = name
        self.w = None
        self.r = {}


class Eng:
    def __init__(self, name):
        self.name = name
        self.sem = None
        self.cnt = 0
        self.waited = {}
        self.prog = []


class DSem:
    def __init__(self, name, step=16):
        self.name = name
        self.sem = None
        self.cnt = 0
        self.step = step
        self.last = None


class Tracker:
    def __init__(self):
        self.eng = {n: Eng(n) for n in ("pe", "act", "dve", "pool", "sp")}
        self.dsems = []
        self.nwait = 0

    def dsem(self, name, step=16):
        d = DSem(name, step)
        self.dsems.append(d)
        return d

    def _wait(self, E, deps):
        for tok in deps:
            if tok is None:
                continue
            key, val, en = tok
            if en == E.name:
                if E.name == "pe" or not SAME_ENGINE_SYNC:
                    continue
            if E.waited.get(id(key), 0) >= val:
                continue
            E.waited[id(key)] = val
            self.nwait += 1
            E.prog.append(("wait", key, val))

    def _deps(self, reads, writes):
        deps = []
        for r in reads:
            if r.w is not None:
                deps.append(r.w)
        for w in writes:
            if w.w is not None:
                deps.append(w.w)
            deps.extend(w.r.values())
        return deps

    def _reg(self, tok, reads, writes):
        for r in reads:
            r.r[id(tok[0])] = tok
        for w in writes:
            w.w = tok
            w.r = {}

    def op(self, en, fn, reads=(), writes=(), signal=True):
        E = self.eng[en]
        self._wait(E, self._deps(reads, writes))
        if signal:
            E.cnt += 1
            tok = (E, E.cnt, en)
        else:
            tok = (E, E.cnt + 1, en)
        E.prog.append(("op", fn, signal))
        self._reg(tok, reads, writes)
        return tok

    def dma(self, q, ds, pairs, reads=(), writes=()):
        E = self.eng[q]
        deps = self._deps(reads, writes)
        deps.append(ds.last)
        self._wait(E, deps)
        for (o, i) in pairs:
            E.prog.append(("dma", o, i, ds))
        ds.cnt += ds.step * len(pairs)
        tok = (ds, ds.cnt, None)
        ds.last = tok
        self._reg(tok, reads, writes)
        return tok

    def cc(self, ds, kind, groups, in_ap, out_ap, reads=(), writes=()):
        E = self.eng["pool"]
        deps = self._deps(reads, writes)
        deps.append(ds.last)
        self._wait(E, deps)
        E.prog.append(("cc", kind, groups, in_ap, out_ap, ds))
        ds.cnt += ds.step
        tok = (ds, ds.cnt, None)
        ds.last = tok
        self._reg(tok, reads, writes)
        return tok

    def barrier(self):
        toks = [(E, E.cnt, E.name) for E in self.eng.values() if E.cnt > 0]
        toks += [d.last for d in self.dsems if d.last is not None]
        for E in self.eng.values():
            self._wait(E, [t for t in toks if t[2] != E.name])

    def emit(self, nc, en, h):
        E = self.eng[en]
        for item in E.prog:
            k = item[0]
            if k == "wait":
                h.wait_ge(item[1].sem, item[2])
            elif k == "op":
                ins = item[1](h)
                if item[2]:
                    ins.then_inc(E.sem, 1)
            elif k == "dma":
                h.dma_start(out=item[1], in_=item[2]).then_inc(item[3].sem, 16)
            elif k == "cc":
                h.collective_compute(item[1], ALU.bypass, replica_groups=item[2],
                                     ins=[item[3]], outs=[item[4]]).then_inc(item[5].sem, item[5].step)


class Arena:
    def __init__(self, nc, base=16512, limit=229376):
        self.nc = nc
        self.base = base
        self.off = base
        self.limit = limit
        self.n = 0

    def alloc(self, shape, dt, name=None):
        esz = 4 if dt == F32 else 2
        nbytes = esz * int(np.prod(shape[1:]))
        nbytes = (nbytes + 31) // 32 * 32
        assert self.off + nbytes <= self.limit, f"SBUF overflow {self.off}+{nbytes}"
        self.n += 1
        t = self.nc.alloc_sbuf_tensor_at(f"{name or 't'}_{self.n}", list(shape), dt, offset=self.off)
        self.off += nbytes
        return t

    def mark(self):
        return self.off

    def reset(self, m):
        self.off = m


class Slot:
    def __init__(self, T, t, name):
        self.t = t
        self.res = Res(name)
        self.ds = T.dsem(name)


class Ring:
    def __init__(self, T, arena, n, shape, dt, name):
        self.slots = [Slot(T, arena.alloc(shape, dt, name), f"{name}{i}") for i in range(n)]
        self.i = 0

    def next(self):
        s = self.slots[self.i % len(self.slots)]
        self.i += 1
        return s


def build():
    nc = bass.Bass("TRN2", target_bir_lowering=False)
    T = Tracker()

    def body():
        def din(name, shape, dt=F32):
            return nc.dram_tensor(name, list(shape), dt, kind="ExternalInput")

        xT = din("xT", [D, TPC])
        xTp = din("xTp", [D, TPC])
        wfull = {n: din(n + "_f", [r, c]) for (n, r, c) in WEIGHTS}
        wf_in = din("wf", [D, H])
        gcols_in = din("gcols", [128, 48])
        lncols_in = din("lncols", [128, 16])
        wsT_in = din("wsT", [128, 8 * 128])
        bs_in = din("bs", [1, 8 * 128])
        bf_in = din("bfg", [1, H])
        flags_in = din("flags", [128, 4])
        outT = nc.dram_tensor("outT", [D, TPC], F32, kind="ExternalOutput")

        wbf = {n: nc.dram_tensor(n + "_bf", [r, c], BF16) for (n, r, c) in WEIGHTS}
        Qs = nc.dram_tensor("Qs", [H * 128, TPC], BF16)
        KG = nc.dram_tensor("KG", [2 * H * 128, TPC], BF16)
        VG = nc.dram_tensor("VG", [2 * H * 128, NLB * 128], BF16)
        LG = nc.dram_tensor("LG", [2 * TPC, H], F32)
        GAs = nc.dram_tensor("GAs", [D, TPC], F32)
        MBs = nc.dram_tensor("MBs", [D, TPC], F32)
        X1s = nc.dram_tensor("X1s", [D, TPC], F32)
        MGs = nc.dram_tensor("MGs", [D, TPC], BF16)

        ar = Arena(nc)
        psum = [nc.alloc_psum_tensor(f"ps{i}", [128, 512], F32) for i in range(8)]
        pres = [Res(f"ps{i}") for i in range(8)]

        gcols = ar.alloc([128, 6, 8], F32, "gcols")
        lncols = ar.alloc([128, 2, 8], F32, "lncols")
        flags = ar.alloc([128, 4], F32, "flags")
        ones_bf = ar.alloc([128, 128], BF16, "ones_bf")
        invd_bf = ar.alloc([128, 128], BF16, "invd_bf")
        ones_f = ar.alloc([128, 128], F32, "ones_f")
        tri_f = ar.alloc([128, 128], F32, "tri_f")
        diag = ar.alloc([128, 4, 512], BF16, "diag")
        eps_col = ar.alloc([128, 2], F32, "eps")
        r_const = Res("const")
        ds_const = T.dsem("const")
        cmark = ar.mark()

        def vop(en, fname, reads, writes, *args, **kw):
            return T.op(en, lambda e: getattr(e, fname)(*args, **kw), reads, writes)

        T.dma("sp", ds_const, [
            (gcols[:].rearrange("p a b -> p (a b)"), gcols_in.ap()),
            (lncols[:].rearrange("p a b -> p (a b)"), lncols_in.ap()),
            (flags[:], flags_in.ap()),
        ], writes=[r_const])
        if STOP == -3:
            raise _StopBuild
        r_pc = Res("poolconst")
        vop("pool", "memset", [], [r_pc], ones_bf[:], 1.0)
        vop("pool", "memset", [], [r_pc], invd_bf[:], 1.0 / D)
        vop("pool", "memset", [], [r_pc], ones_f[:], 1.0)
        vop("pool", "memset", [], [r_pc], eps_col[:, 0:1], RMS_EPS)
        vop("pool", "memset", [], [r_pc], eps_col[:, 1:2], LN_EPS)
        T.op("pool", lambda e: e.affine_select(tri_f[:], ones_f[:], [[1, 128]], ALU.is_ge, 0.0,
                                               base=0, channel_multiplier=-1), [r_pc], [r_pc])
        vop("pool", "memset", [], [r_pc], diag[:].rearrange("p a b -> p (a b)"), 1.0)
        for j in range(4):
            T.op("pool", lambda e, j=j: e.affine_select(diag[:, j, :], diag[:, j, :], [[1, 512]], ALU.is_ge, 0.0,
                                                        base=-128 * j, channel_multiplier=-1), [r_pc], [r_pc])
        if STOP == -2:
            raise _StopBuild
        for n in (1, 5):
            vop("dve", "tensor_scalar", [r_const], [r_const], gcols[:, n, :], gcols[:, n, :], 0.5, None, ALU.mult)

        if STOP == -1:
            raise _StopBuild
        ds_wc = [T.dsem(f"wcast{i}") for i in range(4)]
        wdirect = {"on": False}
        r_w = {n: [Res(f"{n}_bf{b}") for b in range(c // (128 if n in ("wd1", "wd2") else 512))]
               for (n, r, c) in WEIGHTS}
        ds_wb = [T.dsem(f"wback{i}") for i in range(4)]
        wb_state = {"i": 0}
        cast_order = [("wout", 0), ("wout", 1)]
        for f4 in range(8):
            cast_order += [("wg2", f4), ("wu2", f4)]
        cast_order += [("wd2", 0), ("wd2", 1)]
        cast_state = {"pos": 0}

        def cast_upto(pos):
            pos = min(pos, len(cast_order))
            while cast_state["pos"] < pos:
                n, b = cast_order[cast_state["pos"]]
                ds = ds_wc[cast_state["pos"] % 4]
                cast_state["pos"] += 1
                wr = r_w[n][4 * b:4 * b + 4] if n in ("wd1", "wd2") else [r_w[n][b]]
                T.dma("pool", ds, [(wbf[n].ap()[:, b * 512:(b + 1) * 512], wfull[n].ap()[:, b * 512:(b + 1) * 512])],
                      writes=wr)

        def need_cast(n, b):
            if (n, b) in cast_order:
                cast_upto(cast_order.index((n, b)) + 4)
            if not wdirect["on"]:
                cast_upto(cast_state["pos"] + 1)

        if STOP == 0:
            raise _StopBuild
        wring = Ring(T, ar, NSLOT, [128, 4096], BF16, "w")

        def wload(src_ap, view, rw, src32=None):
            s = wring.next()
            dst = view(s.t)
            if wdirect["on"] and src32 is not None:
                T.dma("pool", s.ds, [(dst, src32)], writes=[s.res])
                if rw.w is None:
                    ds = ds_wb[wb_state["i"] % 4]
                    wb_state["i"] += 1
                    T.dma("sp", ds, [(src_ap, dst)], reads=[s.res], writes=[rw])
            else:
                T.dma("pool", s.ds, [(dst, src_ap)], reads=[rw], writes=[s.res])
            return s, dst

        def w_gu(n, f0):
            need_cast(n, f0 // 512)
            src = wbf[n].ap().rearrange("(c p) f -> p c f", p=128)[:, :, f0:f0 + 512]
            src32 = wfull[n].ap().rearrange("(c p) f -> p c f", p=128)[:, :, f0:f0 + 512]
            return wload(src, lambda t: t[:].rearrange("p (c f) -> p c f", c=8), r_w[n][f0 // 512], src32)

        def w_dn(n, dc):
            need_cast(n, dc // 4)
            src = wbf[n].ap().rearrange("(c p) d -> p c d", p=128)[:, :, dc * 128:(dc + 1) * 128]
            src32 = wfull[n].ap().rearrange("(c p) d -> p c d", p=128)[:, :, dc * 128:(dc + 1) * 128]
            return wload(src, lambda t: t[:].rearrange("p (c d) -> p c d", c=32), r_w[n][dc], src32)

        xbuf = [Slot(T, ar.alloc([128, 8, TT], F32, "x"), f"x{i}") for i in range(2)]
        r_xc = [[Res(f"xc{i}_{dc}") for dc in range(8)] for i in range(2)]
        hA = ar.alloc([128, 8, TT], BF16, "hA")
        r_hA = [Res(f"hA{dc}") for dc in range(8)]
        abuf = ar.alloc([128, 32, TT], BF16, "a")
        r_a = [Res(f"a{i}") for i in range(32)]
        ybuf = ar.alloc([128, 8, TT], F32, "y")
        r_y = [Res(f"y{i}") for i in range(8)]
        sq = [ar.alloc([128, TT], BF16, "sq") for _ in range(2)]
        r_sq = [Res("sq0"), Res("sq1")]
        sqP = [ar.alloc([128, TT], BF16, "sqP") for _ in range(2)]
        r_sqP = [Res("sqP0"), Res("sqP1")]
        rstdP = ar.alloc([128, TT], F32, "rstdP")
        r_rstdP = Res("rstdP")
        rstdD = ar.alloc([128, TT], F32, "rstdD")
        r_rstdD = Res("rstdD")
        rstdM = ar.alloc([128, TT], F32, "rstdM")
        r_rstdM = Res("rstdM")
        sil = [ar.alloc([128, TT], F32, "sil") for _ in range(2)]
        r_sil = [Res("sil0"), Res("sil1")]
        ffn_mark = ar.mark()
        cnt = {"sq": 0, "sqP": 0, "sil": 0, "mm": 0}

        PS_G = [0, 1]
        PS_U = [2, 3]
        PS_MM = [4, 5, 6]
        PS_SS = 7

        def mm_group(bank, out_ap, pairs, reads):
            n = len(pairs)
            for i, (l, r) in enumerate(pairs):
                T.op("pe", lambda e, l=l, r=r, i=i: e.matmul(out_ap, l, r, start=(i == 0), stop=(i == n - 1)),
                     reads, [pres[bank]], signal=(i == n - 1))

        def next_mm():
            b = PS_MM[cnt["mm"] % 3]
            cnt["mm"] += 1
            return b

        PS_SS2 = 0

        def stat_ops(src_fn, r_src_fn, bank, ring, r_ring, ckey):
            slotof = {}

            def act(dc):
                k = cnt[ckey] % 2
                cnt[ckey] += 1
                slotof[dc] = k
                T.op("act", lambda e: e.activation(out=ring[k][:], in_=src_fn(dc), func=AF.Square),
                     r_src_fn(dc), [r_ring[k]])

            def pe(dc):
                k = slotof[dc]
                T.op("pe", lambda e: e.matmul(psum[bank][:], invd_bf[:], ring[k][:], start=(dc == 0), stop=(dc == 7)),
                     [r_ring[k], r_pc], [pres[bank]], signal=True)
            return act, pe

        def finish_rstd(bank, rt, r_rt):
            T.op("act", lambda e: e.activation(out=rt[:], in_=psum[bank][:], func=AF.Sqrt, bias=eps_col[:, 0:1]),
                 [pres[bank], r_pc], [r_rt])
            T.op("dve", lambda e: e.reciprocal(rt[:], rt[:]), [r_rt], [r_rt])

        def h_ops(src, r_src_list, gi, rt, r_rt, hdst, r_hdst, dcs):
            for dc in dcs:
                T.op("dve", lambda e, dc=dc: e.scalar_tensor_tensor(
                    out=hdst[:, dc, :], in0=src[:, dc, :], scalar=gcols[:, gi, dc:dc + 1], in1=rt[:],
                    op0=ALU.mult, op1=ALU.mult), [r_src_list[dc], r_rt, r_const], [r_hdst[dc]])

        def prenorm_now(src, r_src_list, gi, hdst, r_hdst, bank, rt, r_rt):
            act, pe = stat_ops(lambda dc: src[:, dc, :], lambda dc: [r_src_list[dc]], bank, sqP, r_sqP, "sqP")
            for dc in range(8):
                act(dc)
                pe(dc)
            finish_rstd(bank, rt, r_rt)
            h_ops(src, r_src_list, gi, rt, r_rt, hdst, r_hdst, range(8))

        def p_side(src, r_src_list, gi, hdst, r_hdst):
            act, pe = stat_ops(lambda dc: src[:, dc, :], lambda dc: [r_src_list[dc]], PS_SS2, sqP, r_sqP, "sqP")
            side = [[] for _ in range(8)]
            side[0] = [lambda: act(0), lambda: act(1)]
            side[1] = [lambda: pe(0), lambda: pe(1), lambda: act(2), lambda: act(3)]
            side[2] = [lambda: pe(2), lambda: pe(3), lambda: act(4), lambda: act(5)]
            side[3] = [lambda: pe(4), lambda: pe(5), lambda: act(6), lambda: act(7)]
            side[4] = [lambda: pe(6), lambda: pe(7), lambda: finish_rstd(PS_SS2, rstdP, r_rstdP)]
            side[5] = [lambda: h_ops(src, r_src_list, gi, rstdP, r_rstdP, hdst, r_hdst, range(0, 4))]
            side[6] = [lambda: h_ops(src, r_src_list, gi, rstdP, r_rstdP, hdst, r_hdst, range(4, 8))]
            return side

        gu_state = {}

        def gateup(names, fcs, hsrc, r_hsrc, tasks):
            ng, nu = names
            for fc in fcs:
                fi = fc % 4
                if fi == 0:
                    gu_state["g"] = w_gu(ng, (fc // 4) * 512)
                    gu_state["u"] = w_gu(nu, (fc // 4) * 512)
                sg, wg = gu_state["g"]
                su, wu = gu_state["u"]
                bg = PS_G[fc % 2]
                bu = PS_U[fc % 2]
                mm_group(bg, psum[bg][:], [(wg[:, dc, fi * 128:(fi + 1) * 128], hsrc[:, dc, :]) for dc in range(8)],
                         [sg.res] + r_hsrc)
                mm_group(bu, psum[bu][:], [(wu[:, dc, fi * 128:(fi + 1) * 128], hsrc[:, dc, :]) for dc in range(8)],
                         [su.res] + r_hsrc)
                k = cnt["sil"] % 2
                cnt["sil"] += 1
                T.op("act", lambda e, bg=bg, k=k: e.activation(out=sil[k][:], in_=psum[bg][:], func=AF.Silu),
                     [pres[bg]], [r_sil[k]])
                T.op("dve", lambda e, bu=bu, k=k, fc=fc: e.tensor_tensor(abuf[:, fc, :], psum[bu][:], sil[k][:], ALU.mult),
                     [pres[bu], r_sil[k]], [r_a[fc]])
                if tasks:
                    tasks.pop(0)()

        def down_stage(nd, gpost, side):
            act, pe = stat_ops(lambda dc: psum[bank_of[dc]][:], lambda dc: [pres[bank_of[dc]]], PS_SS, sq, r_sq, "sq")
            bank_of = {}
            for dc in range(8):
                sd, wd = w_dn(nd, dc)
                b = next_mm()
                bank_of[dc] = b
                mm_group(b, psum[b][:], [(wd[:, fc, :], abuf[:, fc, :]) for fc in range(32)], [sd.res] + r_a)
                if dc > 0:
                    pe(dc - 1)
                T.op("act", lambda e, dc=dc, b=b: e.activation(out=ybuf[:, dc, :], in_=psum[b][:], func=AF.Copy,
                                                               scale=gcols[:, gpost, dc:dc + 1]),
                     [pres[b], r_const], [r_y[dc]])
                act(dc)
                for t in side[dc]:
                    t()
            pe(7)
            finish_rstd(PS_SS, rstdD, r_rstdD)

        def resid_tasks(ysrc, r_ysrc, rt, r_rt, x_t, rx, after=None):
            tasks = []
            for dc in range(8):
                def t(dc=dc):
                    T.op("dve", lambda e: e.tensor_tensor(ysrc[:, dc, :], ysrc[:, dc, :], rt[:], ALU.mult),
                         [r_ysrc[dc], r_rt], [r_ysrc[dc]])
                    T.op("dve", lambda e: e.tensor_tensor(x_t[:, dc, :], x_t[:, dc, :], ysrc[:, dc, :], ALU.add),
                         [r_ysrc[dc], rx[dc]], [rx[dc]])
                    if after is not None:
                        after(dc)
                tasks.append(t)
            return tasks

        def chain_tasks(x_t, rx, gi, hdst, r_hdst, bank, ring, r_ring, ckey, ysrc, r_ysrc, rt_in, r_rt_in,
                        store_fn=None, final_fn=None):
            act, pe = stat_ops(lambda dc: x_t[:, dc, :], lambda dc: [rx[dc]], bank, ring, r_ring, ckey)

            def after(dc):
                act(dc)
                if dc > 0:
                    pe(dc - 1)
            tasks = resid_tasks(ysrc, r_ysrc, rt_in, r_rt_in, x_t, rx, after)

            def t8():
                pe(7)
                if store_fn is not None:
                    store_fn()
            tasks.append(t8)
            tasks.append(lambda: finish_rstd(bank, rstdM, r_rstdM))
            tasks.append(lambda: h_ops(x_t, rx, gi, rstdM, r_rstdM, hdst, r_hdst, range(0, 4)))

            def t11():
                h_ops(x_t, rx, gi, rstdM, r_rstdM, hdst, r_hdst, range(4, 8))
                if final_fn is not None:
                    final_fn()
            tasks.append(t11)
            return tasks

        def flush(tasks):
            while tasks:
                tasks.pop(0)()

        wf_sb = ar.alloc([128, 8, H], BF16, "wf")
        r_wf = Res("wf")
        ds_wf = T.dsem("wf")
        T.dma("pool", ds_wf, [(wf_sb[:], wf_in.ap().rearrange("(c p) h -> p c h", p=128))], writes=[r_wf])
        wsT_f = ar.alloc([128, 8, 128], F32, "wsTf")
        wsT_b = ar.alloc([128, 8, 128], BF16, "wsTb")
        bs_bc = ar.alloc([128, 8, 128], F32, "bsbc")
        bf_bc = ar.alloc([128, H], F32, "bfbc")
        e_g = ar.alloc([128, 8, 128], F32, "eg")
        r_sgc = Res("sgc")
        T.dma("sp", ds_const, [
            (wsT_f[:].rearrange("p a b -> p (a b)"), wsT_in.ap()),
            (bs_bc[:].rearrange("p a b -> p (a b)"), bs_in.ap().partition_broadcast(128).rearrange("p a b -> p (a b)")),
            (bf_bc[:], bf_in.ap().partition_broadcast(128).rearrange("p a b -> p (a b)")),
        ], writes=[r_sgc])
        T.op("dve", lambda e: e.memset(wsT_f[64:128, :, 0:64], 0.0), [r_sgc], [r_sgc])
        T.op("dve", lambda e: e.tensor_copy(wsT_b[:], wsT_f[:]), [r_sgc], [r_sgc])
        for half in range(2):
            T.op("pe", lambda e, half=half: e.matmul(psum[half][:], ones_f[:],
                                                     wsT_f[:, half * 4:(half + 1) * 4, :].rearrange("p a b -> p (a b)"),
                                                     start=True, stop=True),
                 [r_sgc, r_pc], [pres[half]])
            for gi in range(4):
                g = half * 4 + gi
                T.op("dve", lambda e, half=half, gi=gi, g=g: e.scalar_tensor_tensor(
                    out=e_g[:, g, :], in0=psum[half][:, gi * 128:(gi + 1) * 128], scalar=lncols[:, 1, g:g + 1],
                    in1=bs_bc[:, g, :], op0=ALU.mult, op1=ALU.add), [pres[half], r_const, r_sgc], [r_sgc])

        a_mark = ar.mark()
        hM = ar.alloc([128, 8, TT], BF16, "hM")
        r_hM = [Res(f"hM{dc}") for dc in range(8)]
        qk_ring = Ring(T, ar, 2, [128, 4, TT], BF16, "qkst")
        ybase = nc.lookup_mloc(ybuf).addr
        vst = Slot(T, nc.alloc_sbuf_tensor_at("vst_alias", [128, 4, D], BF16, offset=ybase), "vst")
        vn = nc.alloc_sbuf_tensor_at("vn_alias", [128, 4, D], BF16, offset=ybase + 4 * D * 2)
        r_vn = [r_y[4 + b] for b in range(4)]
        vs_ring = [ar.alloc([128, 512], F32, "vs") for _ in range(4)]
        r_vs = [Res(f"vs{i}") for i in range(4)]
        stats = ar.alloc([128, 16, 6], F32, "stats")
        mv = ar.alloc([128, 16, 2], F32, "mv")
        lrs = ar.alloc([128, 16], F32, "lrs")
        r_st = Res("stats")
        u_ring = [ar.alloc([128, TT], F32, "u") for _ in range(2)]
        r_u = [Res("u0"), Res("u1")]
        gb_ring = [ar.alloc([128, TT], F32, "gb") for _ in range(2)]
        r_gb = [Res("gb0"), Res("gb1")]
        ga_ring = Ring(T, ar, 2, [128, TT], F32, "gast")
        mb_ring = Ring(T, ar, 3, [128, TT], F32, "mbst")
        lf_z = ar.alloc([128, 4, H], F32, "lfz")
        lf_st = Slot(T, ar.alloc([128, 4, H], F32, "lfst"), "lfst")
        r_lfz = Res("lfz")

        r_Q = [[Res(f"Q{i}_{hf}") for hf in range(2)] for i in range(NT)]
        r_KG = [[[Res(f"KG{sl}_{i}_{hf}") for hf in range(2)] for i in range(NT)] for sl in range(2)]
        r_VG = [[Res(f"VG{sl}_{i}") for i in range(NT)] for sl in range(2)]
        r_LG = [[Res(f"LG{sl}_{i}") for i in range(NT)] for sl in range(2)]
        r_GA = [[Res(f"GA{i}_{g}") for g in range(8)] for i in range(NT)]
        r_MB = [[Res(f"MB{i}_{g}") for g in range(8)] for i in range(NT)]
        r_X1 = [Res(f"X1{i}") for i in range(NT)]
        r_MG = [[Res(f"MG{i}_{g}") for g in range(8)] for i in range(NT)]
        qk_cres = [[Res(f"qkc{i}_{c}") for c in range(8)] for i in range(2)]
        v_cres = [r_y[i % 4] for i in range(8)]

        xT_v = xT.ap().rearrange("(c p) t -> p c t", p=128)
        X1_v = X1s.ap().rearrange("(c p) t -> p c t", p=128)
        Q_v = Qs.ap().rearrange("(c p) t -> p c t", p=128)
        xTp_v = xTp.ap().rearrange("(c p) t -> p c t", p=128)
        KG_v = [KG.ap()[sl * 1024:(sl + 1) * 1024, :].rearrange("(c p) t -> p c t", p=128) for sl in range(2)]
        VG_v = [VG.ap()[sl * 1024:(sl + 1) * 1024, :].rearrange("(h s) (l d) -> s l h d", s=128, d=128)
                for sl in range(2)]
        LG_v = [LG.ap()[sl * TPC:(sl + 1) * TPC, :].rearrange("(l p) h -> p l h", p=128) for sl in range(2)]
        win_v = wbf["win"].ap().rearrange("(c p) n -> p c n", p=128)
        win32_v = wfull["win"].ap().rearrange("(c p) n -> p c n", p=128)

        def w_in_chunk(ci):
            need_cast("win", ci)
            return wload(win_v[:, :, ci * 512:(ci + 1) * 512],
                         lambda t: t[:].rearrange("p (c f) -> p c f", c=8), r_w["win"][ci],
                         win32_v[:, :, ci * 512:(ci + 1) * 512])

        cact = {"i": 0}

        def evac(dst_ap, src_ap, reads, writes, func=None):
            if func is not None:
                return T.op("act", lambda e: e.activation(out=dst_ap, in_=src_ap, func=func), reads, writes)
            cact["i"] += 1
            if cact["i"] % 2:
                return T.op("act", lambda e: e.activation(out=dst_ap, in_=src_ap, func=AF.Copy), reads, writes)
            return T.op("dve", lambda e: e.tensor_copy(dst_ap, src_ap), reads, writes)

        def load_x(ti):
            xs_ = xbuf[ti % 2]
            src = xT_v if ti < NT else xTp_v
            lt_ = ti % NT
            T.dma("sp", xs_.ds, [(xs_.t[:], src[:, :, lt_ * TT:(lt_ + 1) * TT])], writes=r_xc[ti % 2])

        def w_stage(slot, lt, own):
            tsl = slice(lt * TT, (lt + 1) * TT)
            for which, (dst_v, rdst) in enumerate(((Q_v, r_Q[lt]), (KG_v[slot], r_KG[slot][lt]))):
                if which == 0 and not own:
                    continue
                for half in range(2):
                    sti = qk_ring.i % 2
                    st = qk_ring.next()
                    sw, wv = w_in_chunk(which * 2 + half)
                    for ci in range(4):
                        b = next_mm()
                        mm_group(b, psum[b][:], [(wv[:, dc, ci * 128:(ci + 1) * 128], hM[:, dc, :]) for dc in range(8)],
                                 [sw.res] + r_hM)
                        evac(st.t[:, ci, :], psum[b][:], [pres[b]], [qk_cres[sti][ci]])
                    T.dma("sp", st.ds, [(dst_v[:, half * 4:(half + 1) * 4, tsl], st.t[:])], reads=qk_cres[sti][0:4],
                          writes=[rdst[half]])
            for half in range(2):
                sw, wv = w_in_chunk(4 + half)
                for blk in range(4):
                    b = next_mm()
                    mm_group(b, psum[b][:], [(hM[:, dc, blk * 128:(blk + 1) * 128], wv[:, dc, :]) for dc in range(8)],
                             [sw.res] + r_hM)
                    evac(vst.t[:, blk, half * 512:(half + 1) * 512], psum[b][:], [pres[b]], [v_cres[half * 4 + blk]])
            T.dma("sp", vst.ds, [(VG_v[slot][:, lt * 4 + blk, :, :], vst.t[:, blk, :].rearrange("p (h d) -> p h d", h=H))
                                 for blk in range(4)], reads=v_cres, writes=[r_VG[slot][lt]])
            b = next_mm()
            for blk in range(4):
                mm_group(b, psum[b][:, blk * H:(blk + 1) * H],
                         [(hM[:, dc, blk * 128:(blk + 1) * 128], wf_sb[:, dc, :]) for dc in range(8)], [r_wf] + r_hM)
            T.op("dve", lambda e, b=b: e.tensor_tensor(
                lf_z[:], psum[b][:, 0:4 * H].rearrange("p (b h) -> p b h", h=H),
                bf_bc[:].unsqueeze(1).to_broadcast([128, 4, H]), ALU.add), [pres[b], r_sgc], [r_lfz])
            T.op("act", lambda e: e.activation(out=lf_z[:], in_=lf_z[:], func=AF.Exp, scale=-1.0), [r_lfz], [r_lfz])
            T.op("act", lambda e: e.activation(out=lf_z[:], in_=lf_z[:], func=AF.Ln, bias=1.0), [r_lfz], [r_lfz])
            T.op("dve", lambda e: e.tensor_scalar(lf_st.t[:], lf_z[:], -1.0, None, ALU.mult), [r_lfz], [lf_st.res])
            T.dma("sp", lf_st.ds, [(LG_v[slot][:, lt * 4:(lt + 1) * 4, :], lf_st.t[:])], reads=[lf_st.res],
                  writes=[r_LG[slot][lt]])
            if not own:
                return
            for half in range(2):
                sw, wv = w_in_chunk(8 + half)
                for blk in range(4):
                    b = next_mm()
                    mm_group(b, psum[b][:], [(hM[:, dc, blk * 128:(blk + 1) * 128], wv[:, dc, :]) for dc in range(8)],
                             [sw.res] + r_hM)
                    T.op("act", lambda e, b=b, blk=blk: e.activation(out=vs_ring[blk][:], in_=psum[b][:], func=AF.Gelu),
                         [pres[b]], [r_vs[blk]])
                for blk in range(4):
                    for gi in range(4):
                        T.op("dve", lambda e, blk=blk, gi=gi: e.bn_stats(
                            stats[:, blk * 4 + gi, :], vs_ring[blk][:, gi * 128:(gi + 1) * 128]), [r_vs[blk]], [r_st])
                for q_ in range(16):
                    T.op("dve", lambda e, q_=q_: e.bn_aggr(mv[:, q_, :], stats[:, q_, :]), [r_st], [r_st])
                T.op("act", lambda e: e.activation(out=lrs[:], in_=mv[:, :, 1], func=AF.Sqrt, bias=eps_col[:, 1:2]),
                     [r_st, r_pc], [r_st])
                T.op("dve", lambda e: e.reciprocal(lrs[:], lrs[:]), [r_st], [r_st])
                for blk in range(4):
                    for gi in range(4):
                        g = half * 4 + gi
                        q_ = blk * 4 + gi
                        T.op("dve", lambda e, gi=gi, g=g, blk=blk, q_=q_: e.tensor_scalar(
                            vn[:, blk, g * 128:(g + 1) * 128], vs_ring[blk][:, gi * 128:(gi + 1) * 128],
                            mv[:, q_, 0:1], lrs[:, q_:q_ + 1], ALU.subtract, ALU.mult), [r_vs[blk], r_st], [r_vn[blk]])
            GA_v = GAs.ap().rearrange("(c p) t -> p c t", p=128)
            MB_v = MBs.ap().rearrange("(c p) t -> p c t", p=128)
            for half in range(2):
                su_, wu_ = w_in_chunk(6 + half)
                sb_, wb_ = w_in_chunk(12 + half)
                sa_, wa_ = w_in_chunk(10 + half)
                for gi in range(4):
                    g = half * 4 + gi
                    k = g % 2
                    csl = slice(gi * 128, (gi + 1) * 128)
                    b = next_mm()
                    mm_group(b, psum[b][:], [(wu_[:, dc, csl], hM[:, dc, :]) for dc in range(8)], [su_.res] + r_hM)
                    T.op("act", lambda e, b=b, k=k: e.activation(out=u_ring[k][:], in_=psum[b][:], func=AF.Gelu),
                         [pres[b]], [r_u[k]])
                    b = next_mm()
                    mm_group(b, psum[b][:], [(wb_[:, dc, csl], hM[:, dc, :]) for dc in range(8)], [sb_.res] + r_hM)
                    T.op("act", lambda e, b=b, k=k: e.activation(out=gb_ring[k][:], in_=psum[b][:], func=AF.Sigmoid),
                         [pres[b]], [r_gb[k]])
                    b = next_mm()
                    mm_group(b, psum[b][:], [(wa_[:, dc, csl], hM[:, dc, :]) for dc in range(8)], [sa_.res] + r_hM)
                    gs = ga_ring.next()
                    T.op("act", lambda e, b=b, gs=gs: e.activation(out=gs.t[:], in_=psum[b][:], func=AF.Sigmoid),
                         [pres[b]], [gs.res])
                    T.dma("sp", gs.ds, [(GA_v[:, g, tsl], gs.t[:])], reads=[gs.res], writes=[r_GA[lt][g]])
                    b = next_mm()
                    for blk in range(4):
                        T.op("pe", lambda e, b=b, blk=blk, g=g: e.matmul(
                            psum[b][:, blk * 128:(blk + 1) * 128], vn[:, blk, g * 128:(g + 1) * 128], wsT_b[:, g, :],
                            start=True, stop=True), [r_vn[blk], r_sgc], [pres[b]], signal=(blk == 3))
                    ms = mb_ring.next()
                    T.op("dve", lambda e, b=b, ms=ms, g=g: e.scalar_tensor_tensor(
                        out=ms.t[:].rearrange("p (b t) -> p b t", b=4), in0=psum[b][:].rearrange("p (b t) -> p b t", b=4),
                        scalar=lncols[:, 0, g:g + 1], in1=e_g[:, g, :].unsqueeze(1).to_broadcast([128, 4, 128]),
                        op0=ALU.mult, op1=ALU.add), [pres[b], r_const, r_sgc], [ms.res])
                    T.op("dve", lambda e, ms=ms, k=k: e.tensor_tensor(ms.t[:], ms.t[:], u_ring[k][:], ALU.mult),
                         [r_u[k], ms.res], [ms.res])
                    T.op("dve", lambda e, ms=ms, k=k: e.tensor_tensor(ms.t[:], ms.t[:], gb_ring[k][:], ALU.mult),
                         [r_gb[k], ms.res], [ms.res])
                    T.dma("sp", ms.ds, [(MB_v[:, g, tsl], ms.t[:])], reads=[ms.res], writes=[r_MB[lt][g]])

        NTA = 2 * NT
        FFN1 = ("wg1", "wu1")
        wdirect["on"] = True
        load_x(0)
        load_x(1)
        prenorm_now(xbuf[0].t, r_xc[0], 0, hA, r_hA, PS_SS2, rstdP, r_rstdP)
        gateup(FFN1, range(32), hA, r_hA, [])
        for ti in range(NTA):
            slot, lt = ti // NT, ti % NT
            own = slot == 0
            xs_ = xbuf[ti % 2]
            x_t, rx = xs_.t, r_xc[ti % 2]
            tsl = slice(lt * TT, (lt + 1) * TT)
            if ti + 1 < NTA:
                side = p_side(xbuf[(ti + 1) % 2].t, r_xc[(ti + 1) % 2], 0, hA, r_hA)
            else:
                side = [[] for _ in range(8)]
            down_stage("wd1", 1, side)

            def store_fn(xs_=xs_, x_t=x_t, rx=rx, lt=lt, tsl=tsl, own=own):
                if own:
                    T.dma("sp", xs_.ds, [(X1_v[:, :, tsl], x_t[:])], reads=rx, writes=[r_X1[lt]])

            def final_fn(ti=ti):
                if ti + 2 < NTA:
                    load_x(ti + 2)
            chain = chain_tasks(x_t, rx, 2, hM, r_hM, PS_SS, sq, r_sq, "sq", ybuf, r_y, rstdD, r_rstdD,
                                store_fn, final_fn)
            if ti + 1 < NTA:
                gateup(FFN1, range(0, 12), hA, r_hA, chain)
            flush(chain)
            w_stage(slot, lt, own)
            wdirect["on"] = False
            if ti + 1 < NTA:
                gateup(FFN1, range(12, 32), hA, r_hA, [])
        T.barrier()
        if STOP == 1:
            raise _StopBuild

        ar.reset(cmark)
        lg = ar.alloc([128, NIDX, H], F32, "lg")
        pre = ar.alloc([128, NIDX, H], F32, "pre")
        preq = [ar.alloc([128, NIDX, H], F32, "preq") for _ in range(2)]
        negc = ar.alloc([128, H, NIDX], F32, "negc")
        offcol = ar.alloc([128, 2], F32, "offcol")
        r_c = Res("cums")
        ds_lg = T.dsem("lg")
        T.dma("sp", ds_lg, [(lg[:, sl * NLB:(sl + 1) * NLB, :], LG_v[sl]) for sl in range(2)],
              reads=r_LG[0] + r_LG[1], writes=[r_c])

        def gidx(q, gblk):
            G, j = gblk // 4, gblk % 4
            if G in GT[q]:
                return GT[q].index(G) * 4 + j
            return NLB + GT[1 - q].index(G) * 4 + j

        lg2 = lg[:].rearrange("p i h -> p (i h)")
        NLG = NIDX * H
        T.op("pe", lambda e: e.matmul(psum[0][:, 0:NLG], tri_f[:], lg2, start=True, stop=True), [r_c, r_pc], [pres[0]])
        T.op("pe", lambda e: e.matmul(psum[1][:, 0:NLG], ones_f[:], lg2, start=True, stop=True), [r_c, r_pc], [pres[1]])
        tot = ar.alloc([128, NIDX, H], F32, "tot")
        T.op("dve", lambda e: e.tensor_copy(tot[:].rearrange("p i h -> p (i h)"), psum[1][:, 0:NLG]), [pres[1]], [r_c])
        for q in range(2):
            T.op("dve", lambda e, q=q: e.memset(preq[q][:, gidx(q, 0), :], 0.0), [r_c], [r_c])
            for gb_ in range(1, NIDX):
                a, bq = gidx(q, gb_), gidx(q, gb_ - 1)
                T.op("dve", lambda e, a=a, bq=bq, q=q: e.tensor_tensor(preq[q][:, a, :], preq[q][:, bq, :], tot[:, bq, :], ALU.add),
                     [r_c], [r_c])
        T.op("dve", lambda e: e.tensor_scalar(pre[:], preq[0][:], flags[:, 1:2], None, ALU.mult), [r_c, r_const], [r_c])
        T.op("dve", lambda e: e.scalar_tensor_tensor(out=pre[:], in0=preq[1][:], scalar=flags[:, 0:1], in1=pre[:],
                                                     op0=ALU.mult, op1=ALU.add), [r_c, r_const], [r_c])
        T.op("dve", lambda e: e.scalar_tensor_tensor(
            out=negc[:].rearrange("p h i -> p i h"), in0=psum[0][:, 0:NLG].rearrange("p (i h) -> p i h", h=H), scalar=-1.0,
            in1=pre[:], op0=ALU.mult, op1=ALU.subtract), [pres[0], r_c], [r_c])
        T.op("dve", lambda e: e.tensor_scalar(offcol[:, 0:1], flags[:, 1:2], -30000.0, None, ALU.mult), [r_const], [r_c])
        T.op("dve", lambda e: e.tensor_scalar(offcol[:, 1:2], flags[:, 0:1], -30000.0, None, ALU.mult), [r_const], [r_c])

        kbuf = [Slot(T, ar.alloc([128, 2 * TPC], BF16, "kb"), f"kb{i}") for i in range(2)]
        vbuf = [Slot(T, ar.alloc([128, NIDX, 128], BF16, "vb"), f"vb{i}") for i in range(2)]
        q_ring = Ring(T, ar, 2, [128, TT], BF16, "q")
        p_ring = [ar.alloc([128, TT], BF16, "p") for _ in range(3)]
        r_p = [Res(f"p{i}") for i in range(3)]
        bias_ring = [ar.alloc([128, NIDX], F32, "bias") for _ in range(2)]
        r_bias = [Res("bias0"), Res("bias1")]
        gat = Ring(T, ar, 2, [128, TT], F32, "gat")
        mbt = Ring(T, ar, 2, [128, TT], F32, "mbt")
        rden = ar.alloc([128, TT], F32, "rden")
        r_rden = Res("rden")
        otmp = ar.alloc([128, TT], F32, "otmp")
        r_otmp = Res("otmp")
        mg_ring = Ring(T, ar, 2, [128, TT], BF16, "mgst")

        PS_S = [0, 1, 2]
        PS_O = [3, 4]
        PS_D = [5, 6]
        GA_v = GAs.ap().rearrange("(c p) t -> p c t", p=128)
        MB_v = MBs.ap().rearrange("(c p) t -> p c t", p=128)
        MG_v = MGs.ap().rearrange("(c p) t -> p c t", p=128)
        scale = 1.0 / float(np.sqrt(HD))

        units = []
        grp = 0
        for h in range(H):
            for lt in range(NT):
                first = True
                for sl in (1, 0):
                    for klt in range(lt + 1):
                        for j in range(4):
                            units.append(dict(h=h, lt=lt, grp=grp, first=first, last=False, rk=sl, klt=klt, j=j,
                                              diag=(sl == 0 and klt == lt)))
                            first = False
                units[-1]["last"] = True
                grp += 1
        state = {}

        def load_head(h):
            ks, vsb = kbuf[h % 2], vbuf[h % 2]
            T.dma("sp", ks.ds, [(ks.t[:, rk * TPC:(rk + 1) * TPC], KG.ap()[rk * 1024 + h * 128: rk * 1024 + (h + 1) * 128, :])
                                for rk in range(2)], reads=[r for sl in range(2) for t_ in r_KG[sl] for r in t_], writes=[ks.res])
            T.dma("sp", vsb.ds, [(vsb.t[:, rk * NLB:(rk + 1) * NLB, :],
                                  VG.ap()[rk * 1024 + h * 128: rk * 1024 + (h + 1) * 128, :].rearrange("s (l d) -> s l d", d=128))
                                 for rk in range(2)], reads=r_VG[0] + r_VG[1], writes=[vsb.res])

        def group_prologue(u):
            h, lt, g = u["h"], u["lt"], u["grp"]
            if lt == 0 and h == 0:
                load_head(0)
            if lt == min(1, NT - 1) and h + 1 < H:
                load_head(h + 1)
            qs = q_ring.next()
            T.dma("sp", qs.ds, [(qs.t[:], Q_v[:, h, lt * TT:(lt + 1) * TT])], reads=[r_Q[lt][h // 4]], writes=[qs.res])
            k = g % 2
            T.op("dve", lambda e, k=k, h=h, lt=lt: e.tensor_scalar(
                bias_ring[k][:], negc[:, h, :], pre[:, 4 * lt + 2, h:h + 1], BCLAMP, ALU.add, ALU.min), [r_c], [r_bias[k]])
            fsl = slice(NLB + 4 * lt, NLB + 4 * lt + 4)
            T.op("dve", lambda e, k=k, lt=lt, fsl=fsl: e.tensor_scalar(
                bias_ring[k][:, fsl], bias_ring[k][:, fsl], offcol[:, lt % 2:lt % 2 + 1], None, ALU.add),
                [r_c, r_bias[k]], [r_bias[k]])
            gs, ms = gat.next(), mbt.next()
            T.dma("sp", gs.ds, [(gs.t[:], GA_v[:, h, lt * TT:(lt + 1) * TT])], reads=[r_GA[lt][h]], writes=[gs.res])
            T.dma("sp", ms.ds, [(ms.t[:], MB_v[:, h, lt * TT:(lt + 1) * TT])], reads=[r_MB[lt][h]], writes=[ms.res])
            state[g] = dict(q=qs, ga=gs, mb=ms)

        def emit_S(i):
            u = units[i]
            if u["first"]:
                group_prologue(u)
            st = state[u["grp"]]
            ks = kbuf[u["h"] % 2]
            b = PS_S[i % 3]
            col = u["rk"] * TPC + u["klt"] * TT + u["j"] * 128
            T.op("pe", lambda e, b=b, ks=ks, col=col, q=st["q"]: e.matmul(
                psum[b][:], ks.t[:, col:col + 128], q.t[:], start=True, stop=True),
                [ks.res, st["q"].res], [pres[b]])

        def emit_rest(i):
            u = units[i]
            g, h, lt = u["grp"], u["h"], u["lt"]
            st = state[g]
            b = PS_S[i % 3]
            k3 = i % 3
            idx = u["rk"] * NLB + u["klt"] * 4 + u["j"]
            kb = g % 2
            T.op("act", lambda e, b=b, k3=k3, kb=kb, idx=idx: e.activation(
                out=p_ring[k3][:], in_=psum[b][:], func=AF.Exp, bias=bias_ring[kb][:, idx:idx + 1], scale=scale),
                [pres[b], r_bias[kb]], [r_p[k3]])
            if u["diag"]:
                T.op("dve", lambda e, k3=k3, j=u["j"]: e.tensor_tensor(p_ring[k3][:], p_ring[k3][:], diag[:, j, :], ALU.mult),
                     [r_pc, r_p[k3]], [r_p[k3]])
            vsb = vbuf[h % 2]
            bo, bd = PS_O[g % 2], PS_D[g % 2]
            T.op("pe", lambda e, bo=bo, vsb=vsb, idx=idx, k3=k3, f=u["first"], l=u["last"]: e.matmul(
                psum[bo][:], vsb.t[:, idx, :], p_ring[k3][:], start=f, stop=l), [vsb.res, r_p[k3]], [pres[bo]], signal=False)
            T.op("pe", lambda e, bd=bd, k3=k3, f=u["first"], l=u["last"]: e.matmul(
                psum[bd][:], ones_bf[:], p_ring[k3][:], start=f, stop=l), [r_p[k3], r_pc], [pres[bd]], signal=True)
            if u["last"]:
                T.op("dve", lambda e, bd=bd: e.reciprocal(rden[:], psum[bd][:]), [pres[bd]], [r_rden])
                T.op("dve", lambda e, bo=bo: e.tensor_tensor(otmp[:], psum[bo][:], rden[:], ALU.mult),
                     [pres[bo], r_rden], [r_otmp])
                T.op("dve", lambda e, gs=st["ga"]: e.tensor_tensor(otmp[:], otmp[:], gs.t[:], ALU.mult),
                     [r_otmp, st["ga"].res], [r_otmp])
                mg = mg_ring.next()
                T.op("dve", lambda e, mg=mg, ms=st["mb"]: e.tensor_tensor(mg.t[:], otmp[:], ms.t[:], ALU.add),
                     [r_otmp, st["mb"].res], [mg.res])
                T.dma("sp", mg.ds, [(MG_v[:, h, lt * TT:(lt + 1) * TT], mg.t[:])], reads=[mg.res], writes=[r_MG[lt][h]])
                del state[g]

        NU = len(units)
        LOOK = 2
        for i in range(min(LOOK, NU)):
            emit_S(i)
        for i in range(NU):
            if i + LOOK < NU:
                emit_S(i + LOOK)
            emit_rest(i)
        T.barrier()
        if STOP == 2:
            raise _StopBuild

        ar.reset(ffn_mark)
        wout_sb = ar.alloc([128, 8, D], BF16, "wout")
        r_wo = Res("wout")
        ds_wo = T.dsem("wout")
        need_cast("wout", 1)
        T.dma("pool", ds_wo, [(wout_sb[:], wbf["wout"].ap().rearrange("(c p) n -> p c n", p=128))],
              reads=r_w["wout"], writes=[r_wo])
        mgin = Ring(T, ar, 2, [128, 8, TT], BF16, "mgin")
        y2 = ar.alloc([128, 8, TT], F32, "y2")
        r_y2 = [Res(f"y2_{dc}") for dc in range(8)]
        outT_v = outT.ap().rearrange("(c p) t -> p c t", p=128)
        r_out = Res("out")
        FFN2 = ("wg2", "wu2")

        def load_b2(lt):
            xsl = xbuf[lt % 2]
            tsl_ = slice(lt * TT, (lt + 1) * TT)
            T.dma("sp", xsl.ds, [(xsl.t[:], X1_v[:, :, tsl_])], reads=[r_X1[lt]], writes=r_xc[lt % 2])
            ms_ = mgin.next()
            T.dma("sp", ms_.ds, [(ms_.t[:], MG_v[:, :, tsl_])], reads=r_MG[lt], writes=[ms_.res])
            return ms_

        def wout_stage(ms_):
            bank_of = {}
            act, pe = stat_ops(lambda dc: psum[bank_of[dc]][:], lambda dc: [pres[bank_of[dc]]], PS_SS2, sqP, r_sqP, "sqP")
            for dc in range(8):
                b = next_mm()
                bank_of[dc] = b
                mm_group(b, psum[b][:], [(wout_sb[:, c, dc * 128:(dc + 1) * 128], ms_.t[:, c, :]) for c in range(8)],
                         [r_wo, ms_.res])
                if dc > 0:
                    pe(dc - 1)
                T.op("act", lambda e, dc=dc, b=b: e.activation(out=y2[:, dc, :], in_=psum[b][:], func=AF.Copy,
                                                               scale=gcols[:, 3, dc:dc + 1]),
                     [pres[b], r_const], [r_y2[dc]])
                act(dc)
            pe(7)

        def b2_chain(lt):
            x_t, rx = xbuf[lt % 2].t, r_xc[lt % 2]
            tasks = [lambda: finish_rstd(PS_SS2, rstdP, r_rstdP)]
            tasks += chain_tasks(x_t, rx, 4, hA, r_hA, PS_SS2, sqP, r_sqP, "sqP", y2, r_y2, rstdP, r_rstdP)
            return tasks

        mss = {0: load_b2(0)}
        if NT > 1:
            mss[1] = load_b2(1)
        wout_stage(mss[0])
        flush(b2_chain(0))
        gateup(FFN2, range(32), hA, r_hA, [])
        for lt in range(NT):
            xsl = xbuf[lt % 2]
            x_t, rx = xsl.t, r_xc[lt % 2]
            tsl = slice(lt * TT, (lt + 1) * TT)
            side = [[] for _ in range(8)]
            if lt + 1 < NT:
                wout_stage(mss[lt + 1])
                ch = b2_chain(lt + 1)
                per = [2, 2, 2, 2, 2, 1, 1, 1]
                for dc in range(8):
                    for _ in range(per[dc]):
                        if ch:
                            side[dc].append(ch.pop(0))
                assert not ch
            down_stage("wd2", 5, side)

            def out_store(xsl=xsl, x_t=x_t, rx=rx, tsl=tsl, lt=lt):
                T.dma("sp", xsl.ds, [(outT_v[:, :, tsl], x_t[:])], reads=rx, writes=[r_out])
                if lt + 2 < NT:
                    mss[lt + 2] = load_b2(lt + 2)
            otasks = resid_tasks(ybuf, r_y, rstdD, r_rstdD, x_t, rx)
            otasks.append(out_store)
            if lt + 1 < NT:
                gateup(FFN2, range(32), hA, r_hA, otasks)
            flush(otasks)
        T.barrier()


    try:
        body()
    except _StopBuild:
        T.barrier()

    from contextlib import ExitStack
    with ExitStack() as es:
        for E in T.eng.values():
            E.sem = es.enter_context(nc.semaphore("s_" + E.name))
        for d in T.dsems:
            d.sem = es.enter_context(nc.semaphore("d_" + d.name))
        block = es.enter_context(nc.Block())

        @block.tensor
        def _(e):
            T.emit(nc, "pe", e)

        @block.scalar
        def _(e):
            T.emit(nc, "act", e)

        @block.vector
        def _(e):
            T.emit(nc, "dve", e)

        @block.gpsimd
        def _(e):
            T.emit(nc, "pool", e)

        @block.sync
        def _(e):
            T.emit(nc, "sp", e)
    return nc


def _prep_inputs(inp):
    f = lambda a: np.ascontiguousarray(np.asarray(a, dtype=np.float32))
    x = f(inp["x"])
    w = {"wg1": f(inp["ffn1_w_gate"])[0], "wu1": f(inp["ffn1_w_up"])[0], "wd1": f(inp["ffn1_w_down"])[0],
         "wout": f(inp["w_out"])[0], "wg2": f(inp["ffn2_w_gate"])[0], "wu2": f(inp["ffn2_w_up"])[0],
         "wd2": f(inp["ffn2_w_down"])[0]}
    w_in = f(inp["w_in"])[0]
    w["win"] = np.ascontiguousarray(np.concatenate([w_in[:, :3072], w_in[:, 3080:]], axis=1))
    wf = np.ascontiguousarray(w_in[:, 3072:3080])
    gains = [inp["ffn1_pre_g"], inp["ffn1_post_g"], inp["mix_pre_g"], inp["mix_post_g"], inp["ffn2_pre_g"],
             inp["ffn2_post_g"]]
    gcols = np.stack([f(g)[0].reshape(8, 128).T for g in gains], axis=1).reshape(128, 48)
    lncols = np.stack([f(inp["sgu_ln_g"])[0].reshape(8, 128).T, f(inp["sgu_ln_b"])[0].reshape(8, 128).T],
                      axis=1).reshape(128, 16)
    wsT = np.ascontiguousarray(f(inp["sgu_w_s"])[0].transpose(2, 0, 1)).reshape(128, 1024)
    bs = f(inp["sgu_b_s"])[0].reshape(1, 1024)
    bfg = f(inp["b_forget"]).reshape(1, 8)
    maps = []
    for c in range(NCORE):
        b, p = c // 2, c % 2
        toks = np.concatenate([np.arange(G * TT, (G + 1) * TT) for G in GT[p]])
        tokp = np.concatenate([np.arange(G * TT, (G + 1) * TT) for G in GT[1 - p]])
        m = {"xT": np.ascontiguousarray(x[b][toks].T), "xTp": np.ascontiguousarray(x[b][tokp].T)}
        for (n, r, cc) in WEIGHTS:
            m[n + "_f"] = w[n]
        m["wf"] = wf
        m["gcols"] = np.ascontiguousarray(gcols)
        m["lncols"] = np.ascontiguousarray(lncols)
        m["wsT"] = wsT
        m["bs"] = bs
        m["bfg"] = bfg
        fl = np.zeros((128, 4), np.float32)
        fl[:, 0] = p
        fl[:, 1] = 1 - p
        m["flags"] = fl
        maps.append(m)
    return maps


_NC_CACHE = {}


def kernel(**inputs):
    maps = _prep_inputs(inputs)
    if "nc" not in _NC_CACHE:
        _NC_CACHE["nc"] = build()
    nc = _NC_CACHE["nc"]
    res = run_bass_kernel_spmd(nc, maps, core_ids=list(range(NCORE)))
    out = np.empty((NB, SEQ, D), np.float32)
    for c in range(NCORE):
        b, p = c // 2, c % 2
        o = np.asarray(res.results[c]["outT"]).T
        for i, G in enumerate(GT[p]):
            out[b, G * TT:(G + 1) * TT] = o[i * TT:(i + 1) * TT]
    return out
```

```python
import numpy as np
import concourse.bass as bass
import concourse.mybir as mybir
from concourse.bass_utils import run_bass_kernel_spmd

F32 = mybir.dt.float32
BF16 = mybir.dt.bfloat16
AF = mybir.ActivationFunctionType
ALU = mybir.AluOpType

D = 1024
DFF = 4096
SEQ = 8192
NB = 4
NCORE = 8
TT = 512
NT = 8
TPC = NT * TT
NLB = NT * 4
NIDX = 2 * NLB
H = 8
HD = 128
WIN = 7168
GT = [[0, 3, 4, 7, 8, 11, 12, 15], [1, 2, 5, 6, 9, 10, 13, 14]]


def _configure(nt, ncore):
    global NT, TPC, NLB, NIDX, NCORE, GT, SEQ, NB
    NT, NCORE = nt, ncore
    TPC, NLB, NIDX = NT * TT, NT * 4, 2 * NT * 4
    SEQ = 2 * TPC
    NB = ncore // 2
    g0 = [g for g in range(2 * NT) if g % 4 in (0, 3)]
    g1 = [g for g in range(2 * NT) if g % 4 in (1, 2)]
    GT = [g0, g1]
RMS_EPS = 1e-6
LN_EPS = 1e-5
BCLAMP = 64.0
NSLOT = 5
SAME_ENGINE_SYNC = True
STOP = 99


class _StopBuild(Exception):
    pass

WEIGHTS = [("wg1", D, DFF), ("wu1", D, DFF), ("wd1", DFF, D), ("win", D, WIN),
           ("wout", D, D), ("wg2", D, DFF), ("wu2", D, DFF), ("wd2", DFF, D)]


class Res:
    __slots__ = ("name", "w", "r")

    def __init__(self, name):
        self.name = name
        self.w = None
        self.r = {}


class Eng:
    def __init__(self, name):
        self.name = name
        self.sem = None
        self.cnt = 0
        self.waited = {}
        self.prog = []


class DSem:
    def __init__(self, name, step=16):
        self.name = name
        self.sem = None
        self.cnt = 0
        self.step = step
        self.last = None


class Tracker:
    def __init__(self):
        self.eng = {n: Eng(n) for n in ("pe", "act", "dve", "pool", "sp")}
        self.dsems = []
        self.nwait = 0

    def dsem(self, name, step=16):
        d = DSem(name, step)
        self.dsems.append(d)
        return d

    def _wait(self, E, deps):
        for tok in deps:
            if tok is None:
                continue
            key, val, en = tok
            if en == E.name:
                if E.name == "pe" or not SAME_ENGINE_SYNC:
                    continue
            if E.waited.get(id(key), 0) >= val:
                continue
            E.waited[id(key)] = val
            self.nwait += 1
            E.prog.append(("wait", key, val))

    def _deps(self, reads, writes):
        deps = []
        for r in reads:
            if r.w is not None:
                deps.append(r.w)
        for w in writes:
            if w.w is not None:
                deps.append(w.w)
            deps.extend(w.r.values())
        return deps

    def _reg(self, tok, reads, writes):
        for r in reads:
            r.r[id(tok[0])] = tok
        for w in writes:
            w.w = tok
            w.r = {}

    def op(self, en, fn, reads=(), writes=(), signal=True):
        E = self.eng[en]
        self._wait(E, self._deps(reads, writes))
        if signal:
            E.cnt += 1
            tok = (E, E.cnt, en)
        else:
            tok = (E, E.cnt + 1, en)
        E.prog.append(("op", fn, signal))
        self._reg(tok, reads, writes)
        return tok

    def dma(self, q, ds, pairs, reads=(), writes=()):
        E = self.eng[q]
        deps = self._deps(reads, writes)
        deps.append(ds.last)
        self._wait(E, deps)
        for (o, i) in pairs:
            E.prog.append(("dma", o, i, ds))
        ds.cnt += ds.step * len(pairs)
        tok = (ds, ds.cnt, None)
        ds.last = tok
        self._reg(tok, reads, writes)
        return tok

    def cc(self, ds, kind, groups, in_ap, out_ap, reads=(), writes=()):
        E = self.eng["pool"]
        deps = self._deps(reads, writes)
        deps.append(ds.last)
        self._wait(E, deps)
        E.prog.append(("cc", kind, groups, in_ap, out_ap, ds))
        ds.cnt += ds.step
        tok = (ds, ds.cnt, None)
        ds.last = tok
        self._reg(tok, reads, writes)
        return tok

    def barrier(self):
        toks = [(E, E.cnt, E.name) for E in self.eng.values() if E.cnt > 0]
        toks += [d.last for d in self.dsems if d.last is not None]
        for E in self.eng.values():
            self._wait(E, [t for t in toks if t[2] != E.name])

    def emit(self, nc, en, h):
        E = self.eng[en]
        for item in E.prog:
            k = item[0]
            if k == "wait":
                h.wait_ge(item[1].sem, item[2])
            elif k == "op":
                ins = item[1](h)
                if item[2]:
                    ins.then_inc(E.sem, 1)
            elif k == "dma":
                h.dma_start(out=item[1], in_=item[2]).then_inc(item[3].sem, 16)
            elif k == "cc":
                h.collective_compute(item[1], ALU.bypass, replica_groups=item[2],
                                     ins=[item[3]], outs=[item[4]]).then_inc(item[5].sem, item[5].step)


class Arena:
    def __init__(self, nc, base=16512, limit=229376):
        self.nc = nc
        self.base = base
        self.off = base
        self.limit = limit
        self.n = 0

    def alloc(self, shape, dt, name=None):
        esz = 4 if dt == F32 else 2
        nbytes = esz * int(np.prod(shape[1:]))
        nbytes = (nbytes + 31) // 32 * 32
        assert self.off + nbytes <= self.limit, f"SBUF overflow {self.off}+{nbytes}"
        self.n += 1
        t = self.nc.alloc_sbuf_tensor_at(f"{name or 't'}_{self.n}", list(shape), dt, offset=self.off)
        self.off += nbytes
        return t

    def mark(self):
        return self.off

    def reset(self, m):
        self.off = m


class Slot:
    def __init__(self, T, t, name):
        self.t = t
        self.res = Res(name)
        self.ds = T.dsem(name)


class Ring:
    def __init__(self, T, arena, n, shape, dt, name):
        self.slots = [Slot(T, arena.alloc(shape, dt, name), f"{name}{i}") for i in range(n)]
        self.i = 0

    def next(self):
        s = self.slots[self.i % len(self.slots)]
        self.i += 1
        return s


def build():
    nc = bass.Bass("TRN2", target_bir_lowering=False)
    T = Tracker()

    def body():
        def din(name, shape, dt=F32):
            return nc.dram_tensor(name, list(shape), dt, kind="ExternalInput")

        xT = din("xT", [D, TPC])
        xTp = din("xTp", [D, TPC])
        wfull = {n: din(n + "_f", [r, c]) for (n, r, c) in WEIGHTS}
        wf_in = din("wf", [D, H])
        gcols_in = din("gcols", [128, 48])
        lncols_in = din("lncols", [128, 16])
        wsT_in = din("wsT", [128, 8 * 128])
        bs_in = din("bs", [1, 8 * 128])
        bf_in = din("bfg", [1, H])
        flags_in = din("flags", [128, 4])
        outT = nc.dram_tensor("outT", [D, TPC], F32, kind="ExternalOutput")

        wbf = {n: nc.dram_tensor(n + "_bf", [r, c], BF16) for (n, r, c) in WEIGHTS}
        Qs = nc.dram_tensor("Qs", [H * 128, TPC], BF16)
        KG = nc.dram_tensor("KG", [2 * H * 128, TPC], BF16)
        VG = nc.dram_tensor("VG", [2 * H * 128, NLB * 128], BF16)
        LG = nc.dram_tensor("LG", [2 * TPC, H], F32)
        GAs = nc.dram_tensor("GAs", [D, TPC], F32)
        MBs = nc.dram_tensor("MBs", [D, TPC], F32)
        X1s = nc.dram_tensor("X1s", [D, TPC], F32)
        MGs = nc.dram_tensor("MGs", [D, TPC], BF16)

        ar = Arena(nc)
        psum = [nc.alloc_psum_tensor(f"ps{i}", [128, 512], F32) for i in range(8)]
        pres = [Res(f"ps{i}") for i in range(8)]

        gcols = ar.alloc([128, 6, 8], F32, "gcols")
        lncols = ar.alloc([128, 2, 8], F32, "lncols")
        flags = ar.alloc([128, 4], F32, "flags")
        ones_bf = ar.alloc([128, 128], BF16, "ones_bf")
        invd_bf = ar.alloc([128, 128], BF16, "invd_bf")
        ones_f = ar.alloc([128, 128], F32, "ones_f")
        tri_f = ar.alloc([128, 128], F32, "tri_f")
        diag = ar.alloc([128, 4, 512], BF16, "diag")
        eps_col = ar.alloc([128, 2], F32, "eps")
        r_const = Res("const")
        ds_const = T.dsem("const")
        cmark = ar.mark()

        def vop(en, fname, reads, writes, *args, **kw):
            return T.op(en, lambda e: getattr(e, fname)(*args, **kw), reads, writes)

        T.dma("sp", ds_const, [
            (gcols[:].rearrange("p a b -> p (a b)"), gcols_in.ap()),
            (lncols[:].rearrange("p a b -> p (a b)"), lncols_in.ap()),
            (flags[:], flags_in.ap()),
        ], writes=[r_const])
        if STOP == -3:
            raise _StopBuild
        r_pc = Res("poolconst")
        vop("pool", "memset", [], [r_pc], ones_bf[:], 1.0)
        vop("pool", "memset", [], [r_pc], invd_bf[:], 1.0 / D)
        vop("pool", "memset", [], [r_pc], ones_f[:], 1.0)
        vop("pool", "memset", [], [r_pc], eps_col[:, 0:1], RMS_EPS)
        vop("pool", "memset", [], [r_pc], eps_col[:, 1:2], LN_EPS)
        T.op("pool", lambda e: e.affine_select(tri_f[:], ones_f[:], [[1, 128]], ALU.is_ge, 0.0,
                                               base=0, channel_multiplier=-1), [r_pc], [r_pc])
        vop("pool", "memset", [], [r_pc], diag[:].rearrange("p a b -> p (a b)"), 1.0)
        for j in range(4):
            T.op("pool", lambda e, j=j: e.affine_select(diag[:, j, :], diag[:, j, :], [[1, 512]], ALU.is_ge, 0.0,
                                                        base=-128 * j, channel_multiplier=-1), [r_pc], [r_pc])
        if STOP == -2:
            raise _StopBuild
        for n in (1, 5):
            vop("dve", "tensor_scalar", [r_const], [r_const], gcols[:, n, :], gcols[:, n, :], 0.5, None, ALU.mult)

        if STOP == -1:
            raise _StopBuild
        ds_wc = [T.dsem(f"wcast{i}") for i in range(4)]
        wdirect = {"on": False}
        r_w = {n: [Res(f"{n}_bf{b}") for b in range(c // (128 if n in ("wd1", "wd2") else 512))]
               for (n, r, c) in WEIGHTS}
        ds_wb = [T.dsem(f"wback{i}") for i in range(4)]
        wb_state = {"i": 0}
        cast_order = [("wout", 0), ("wout", 1)]
        for f4 in range(8):
            cast_order += [("wg2", f4), ("wu2", f4)]
        cast_order += [("wd2", 0), ("wd2", 1)]
        cast_state = {"pos": 0}

        def cast_upto(pos):
            pos = min(pos, len(cast_order))
            while cast_state["pos"] < pos:
                n, b = cast_order[cast_state["pos"]]
                ds = ds_wc[cast_state["pos"] % 4]
                cast_state["pos"] += 1
                wr = r_w[n][4 * b:4 * b + 4] if n in ("wd1", "wd2") else [r_w[n][b]]
                T.dma("pool", ds, [(wbf[n].ap()[:, b * 512:(b + 1) * 512], wfull[n].ap()[:, b * 512:(b + 1) * 512])],
                      writes=wr)

        def need_cast(n, b):
            if (n, b) in cast_order:
                cast_upto(cast_order.index((n, b)) + 4)
            if not wdirect["on"]:
                cast_upto(cast_state["pos"] + 1)

        if STOP == 0:
            raise _StopBuild
        wring = Ring(T, ar, NSLOT, [128, 4096], BF16, "w")

        def wload(src_ap, view, rw, src32=None):
            s = wring.next()
            dst = view(s.t)
            if wdirect["on"] and src32 is not None:
                T.dma("pool", s.ds, [(dst, src32)], writes=[s.res])
                if rw.w is None:
                    ds = ds_wb[wb_state["i"] % 4]
                    wb_state["i"] += 1
                    T.dma("sp", ds, [(src_ap, dst)], reads=[s.res], writes=[rw])
            else:
                T.dma("pool", s.ds, [(dst, src_ap)], reads=[rw], writes=[s.res])
            return s, dst

        def w_gu(n, f0):
            need_cast(n, f0 // 512)
            src = wbf[n].ap().rearrange("(c p) f -> p c f", p=128)[:, :, f0:f0 + 512]
            src32 = wfull[n].ap().rearrange("(c p) f -> p c f", p=128)[:, :, f0:f0 + 512]
            return wload(src, lambda t: t[:].rearrange("p (c f) -> p c f", c=8), r_w[n][f0 // 512], src32)

        def w_dn(n, dc):
            need_cast(n, dc // 4)
            src = wbf[n].ap().rearrange("(c p) d -> p c d", p=128)[:, :, dc * 128:(dc + 1) * 128]
            src32 = wfull[n].ap().rearrange("(c p) d -> p c d", p=128)[:, :, dc * 128:(dc + 1) * 128]
            return wload(src, lambda t: t[:].rearrange("p (c d) -> p c d", c=32), r_w[n][dc], src32)

        xbuf = [Slot(T, ar.alloc([128, 8, TT], F32, "x"), f"x{i}") for i in range(2)]
        r_xc = [[Res(f"xc{i}_{dc}") for dc in range(8)] for i in range(2)]
        hA = ar.alloc([128, 8, TT], BF16, "hA")
        r_hA = [Res(f"hA{dc}") for dc in range(8)]
        abuf = ar.alloc([128, 32, TT], BF16, "a")
        r_a = [Res(f"a{i}") for i in range(32)]
        ybuf = ar.alloc([128, 8, TT], F32, "y")
        r_y = [Res(f"y{i}") for i in range(8)]
        sq = [ar.alloc([128, TT], BF16, "sq") for _ in range(2)]
        r_sq = [Res("sq0"), Res("sq1")]
        sqP = [ar.alloc([128, TT], BF16, "sqP") for _ in range(2)]
        r_sqP = [Res("sqP0"), Res("sqP1")]
        rstdP = ar.alloc([128, TT], F32, "rstdP")
        r_rstdP = Res("rstdP")
        rstdD = ar.alloc([128, TT], F32, "rstdD")
        r_rstdD = Res("rstdD")
        rstdM = ar.alloc([128, TT], F32, "rstdM")
        r_rstdM = Res("rstdM")
        sil = [ar.alloc([128, TT], F32, "sil") for _ in range(2)]
        r_sil = [Res("sil0"), Res("sil1")]
        ffn_mark = ar.mark()
        cnt = {"sq": 0, "sqP": 0, "sil": 0, "mm": 0}

        PS_G = [0, 1]
        PS_U = [2, 3]
        PS_MM = [4, 5, 6]
        PS_SS = 7

        def mm_group(bank, out_ap, pairs, reads):
            n = len(pairs)
            for i, (l, r) in enumerate(pairs):
                T.op("pe", lambda e, l=l, r=r, i=i: e.matmul(out_ap, l, r, start=(i == 0), stop=(i == n - 1)),
                     reads, [pres[bank]], signal=(i == n - 1))

        def next_mm():
            b = PS_MM[cnt["mm"] % 3]
            cnt["mm"] += 1
            return b

        PS_SS2 = 0

        def stat_ops(src_fn, r_src_fn, bank, ring, r_ring, ckey):
            slotof = {}

            def act(dc):
                k = cnt[ckey] % 2
                cnt[ckey] += 1
                slotof[dc] = k
                T.op("act", lambda e: e.activation(out=ring[k][:], in_=src_fn(dc), func=AF.Square),
                     r_src_fn(dc), [r_ring[k]])

            def pe(dc):
                k = slotof[dc]
                T.op("pe", lambda e: e.matmul(psum[bank][:], invd_bf[:], ring[k][:], start=(dc == 0), stop=(dc == 7)),
                     [r_ring[k], r_pc], [pres[bank]], signal=True)
            return act, pe

        def finish_rstd(bank, rt, r_rt):
            T.op("act", lambda e: e.activation(out=rt[:], in_=psum[bank][:], func=AF.Sqrt, bias=eps_col[:, 0:1]),
                 [pres[bank], r_pc], [r_rt])
            T.op("dve", lambda e: e.reciprocal(rt[:], rt[:]), [r_rt], [r_rt])

        def h_ops(src, r_src_list, gi, rt, r_rt, hdst, r_hdst, dcs):
            for dc in dcs:
                T.op("dve", lambda e, dc=dc: e.scalar_tensor_tensor(
                    out=hdst[:, dc, :], in0=src[:, dc, :], scalar=gcols[:, gi, dc:dc + 1], in1=rt[:],
                    op0=ALU.mult, op1=ALU.mult), [r_src_list[dc], r_rt, r_const], [r_hdst[dc]])

        def prenorm_now(src, r_src_list, gi, hdst, r_hdst, bank, rt, r_rt):
            act, pe = stat_ops(lambda dc: src[:, dc, :], lambda dc: [r_src_list[dc]], bank, sqP, r_sqP, "sqP")
            for dc in range(8):
                act(dc)
                pe(dc)
            finish_rstd(bank, rt, r_rt)
            h_ops(src, r_src_list, gi, rt, r_rt, hdst, r_hdst, range(8))

        def p_side(src, r_src_list, gi, hdst, r_hdst):
            act, pe = stat_ops(lambda dc: src[:, dc, :], lambda dc: [r_src_list[dc]], PS_SS2, sqP, r_sqP, "sqP")
            side = [[] for _ in range(8)]
            side[0] = [lambda: act(0), lambda: act(1)]
            side[1] = [lambda: pe(0), lambda: pe(1), lambda: act(2), lambda: act(3)]
            side[2] = [lambda: pe(2), lambda: pe(3), lambda: act(4), lambda: act(5)]
            side[3] = [lambda: pe(4), lambda: pe(5), lambda: act(6), lambda: act(7)]
            side[4] = [lambda: pe(6), lambda: pe(7), lambda: finish_rstd(PS_SS2, rstdP, r_rstdP)]
            side[5] = [lambda: h_ops(src, r_src_list, gi, rstdP, r_rstdP, hdst, r_hdst, range(0, 4))]
            side[6] = [lambda: h_ops(src, r_src_list, gi, rstdP, r_rstdP, hdst, r_hdst, range(4, 8))]
            return side

        gu_state = {}

        def gateup(names, fcs, hsrc, r_hsrc, tasks):
            ng, nu = names
            for fc in fcs:
                fi = fc % 4
                if fi == 0:
                    gu_state["g"] = w_gu(ng, (fc // 4) * 512)
                    gu_state["u"] = w_gu(nu, (fc // 4) * 512)
                sg, wg = gu_state["g"]
                su, wu = gu_state["u"]
                bg = PS_G[fc % 2]
                bu = PS_U[fc % 2]
                mm_group(bg, psum[bg][:], [(wg[:, dc, fi * 128:(fi + 1) * 128], hsrc[:, dc, :]) for dc in range(8)],
                         [sg.res] + r_hsrc)
                mm_group(bu, psum[bu][:], [(wu[:, dc, fi * 128:(fi + 1) * 128], hsrc[:, dc, :]) for dc in range(8)],
                         [su.res] + r_hsrc)
                k = cnt["sil"] % 2
                cnt["sil"] += 1
                T.op("act", lambda e, bg=bg, k=k: e.activation(out=sil[k][:], in_=psum[bg][:], func=AF.Silu),
                     [pres[bg]], [r_sil[k]])
                T.op("dve", lambda e, bu=bu, k=k, fc=fc: e.tensor_tensor(abuf[:, fc, :], psum[bu][:], sil[k][:], ALU.mult),
                     [pres[bu], r_sil[k]], [r_a[fc]])
                if tasks:
                    tasks.pop(0)()

        def down_stage(nd, gpost, side):
            act, pe = stat_ops(lambda dc: psum[bank_of[dc]][:], lambda dc: [pres[bank_of[dc]]], PS_SS, sq, r_sq, "sq")
            bank_of = {}
            for dc in range(8):
                sd, wd = w_dn(nd, dc)
                b = next_mm()
                bank_of[dc] = b
                mm_group(b, psum[b][:], [(wd[:, fc, :], abuf[:, fc, :]) for fc in range(32)], [sd.res] + r_a)
                if dc > 0:
                    pe(dc - 1)
                T.op("act", lambda e, dc=dc, b=b: e.activation(out=ybuf[:, dc, :], in_=psum[b][:], func=AF.Copy,
                                                               scale=gcols[:, gpost, dc:dc + 1]),
                     [pres[b], r_const], [r_y[dc]])
                act(dc)
                for t in side[dc]:
                    t()
            pe(7)
            finish_rstd(PS_SS, rstdD, r_rstdD)

        def resid_tasks(ysrc, r_ysrc, rt, r_rt, x_t, rx, after=None):
            tasks = []
            for dc in range(8):
                def t(dc=dc):
                    T.op("dve", lambda e: e.tensor_tensor(ysrc[:, dc, :], ysrc[:, dc, :], rt[:], ALU.mult),
                         [r_ysrc[dc], r_rt], [r_ysrc[dc]])
                    T.op("dve", lambda e: e.tensor_tensor(x_t[:, dc, :], x_t[:, dc, :], ysrc[:, dc, :], ALU.add),
                         [r_ysrc[dc], rx[dc]], [rx[dc]])
                    if after is not None:
                        after(dc)
                tasks.append(t)
            return tasks

        def chain_tasks(x_t, rx, gi, hdst, r_hdst, bank, ring, r_ring, ckey, ysrc, r_ysrc, rt_in, r_rt_in,
                        store_fn=None, final_fn=None):
            act, pe = stat_ops(lambda dc: x_t[:, dc, :], lambda dc: [rx[dc]], bank, ring, r_ring, ckey)

            def after(dc):
                act(dc)
                if dc > 0:
                    pe(dc - 1)
            tasks = resid_tasks(ysrc, r_ysrc, rt_in, r_rt_in, x_t, rx, after)

            def t8():
                pe(7)
                if store_fn is not None:
                    store_fn()
            tasks.append(t8)
            tasks.append(lambda: finish_rstd(bank, rstdM, r_rstdM))
            tasks.append(lambda: h_ops(x_t, rx, gi, rstdM, r_rstdM, hdst, r_hdst, range(0, 4)))

            def t11():
                h_ops(x_t, rx, gi, rstdM, r_rstdM, hdst, r_hdst, range(4, 8))
                if final_fn is not None:
                    final_fn()
            tasks.append(t11)
            return tasks

        def flush(tasks):
            while tasks:
                tasks.pop(0)()

        wf_sb = ar.alloc([128, 8, H], BF16, "wf")
        r_wf = Res("wf")
        ds_wf = T.dsem("wf")
        T.dma("pool", ds_wf, [(wf_sb[:], wf_in.ap().rearrange("(c p) h -> p c h", p=128))], writes=[r_wf])
        wsT_f = ar.alloc([128, 8, 128], F32, "wsTf")
        wsT_b = ar.alloc([128, 8, 128], BF16, "wsTb")
        bs_bc = ar.alloc([128, 8, 128], F32, "bsbc")
        bf_bc = ar.alloc([128, H], F32, "bfbc")
        e_g = ar.alloc([128, 8, 128], F32, "eg")
        r_sgc = Res("sgc")
        T.dma("sp", ds_const, [
            (wsT_f[:].rearrange("p a b -> p (a b)"), wsT_in.ap()),
            (bs_bc[:].rearrange("p a b -> p (a b)"), bs_in.ap().partition_broadcast(128).rearrange("p a b -> p (a b)")),
            (bf_bc[:], bf_in.ap().partition_broadcast(128).rearrange("p a b -> p (a b)")),
        ], writes=[r_sgc])
        T.op("dve", lambda e: e.memset(wsT_f[64:128, :, 0:64], 0.0), [r_sgc], [r_sgc])
        T.op("dve", lambda e: e.tensor_copy(wsT_b[:], wsT_f[:]), [r_sgc], [r_sgc])
        for half in range(2):
            T.op("pe", lambda e, half=half: e.matmul(psum[half][:], ones_f[:],
                                                     wsT_f[:, half * 4:(half + 1) * 4, :].rearrange("p a b -> p (a b)"),
                                                     start=True, stop=True),
                 [r_sgc, r_pc], [pres[half]])
            for gi in range(4):
                g = half * 4 + gi
                T.op("dve", lambda e, half=half, gi=gi, g=g: e.scalar_tensor_tensor(
                    out=e_g[:, g, :], in0=psum[half][:, gi * 128:(gi + 1) * 128], scalar=lncols[:, 1, g:g + 1],
                    in1=bs_bc[:, g, :], op0=ALU.mult, op1=ALU.add), [pres[half], r_const, r_sgc], [r_sgc])

        a_mark = ar.mark()
        hM = ar.alloc([128, 8, TT], BF16, "hM")
        r_hM = [Res(f"hM{dc}") for dc in range(8)]
        qk_ring = Ring(T, ar, 2, [128, 4, TT], BF16, "qkst")
        ybase = nc.lookup_mloc(ybuf).addr
        vst = Slot(T, nc.alloc_sbuf_tensor_at("vst_alias", [128, 4, D], BF16, offset=ybase), "vst")
        vn = nc.alloc_sbuf_tensor_at("vn_alias", [128, 4, D], BF16, offset=ybase + 4 * D * 2)
        r_vn = [r_y[4 + b] for b in range(4)]
        vs_ring = [ar.alloc([128, 512], F32, "vs") for _ in range(4)]
        r_vs = [Res(f"vs{i}") for i in range(4)]
        stats = ar.alloc([128, 16, 6], F32, "stats")
        mv = ar.alloc([128, 16, 2], F32, "mv")
        lrs = ar.alloc([128, 16], F32, "lrs")
        r_st = Res("stats")
        u_ring = [ar.alloc([128, TT], F32, "u") for _ in range(2)]
        r_u = [Res("u0"), Res("u1")]
        gb_ring = [ar.alloc([128, TT], F32, "gb") for _ in range(2)]
        r_gb = [Res("gb0"), Res("gb1")]
        ga_ring = Ring(T, ar, 2, [128, TT], F32, "gast")
        mb_ring = Ring(T, ar, 3, [128, TT], F32, "mbst")
        lf_z = ar.alloc([128, 4, H], F32, "lfz")
        lf_st = Slot(T, ar.alloc([128, 4, H], F32, "lfst"), "lfst")
        r_lfz = Res("lfz")

        r_Q = [[Res(f"Q{i}_{hf}") for hf in range(2)] for i in range(NT)]
        r_KG = [[[Res(f"KG{sl}_{i}_{hf}") for hf in range(2)] for i in range(NT)] for sl in range(2)]
        r_VG = [[Res(f"VG{sl}_{i}") for i in range(NT)] for sl in range(2)]
        r_LG = [[Res(f"LG{sl}_{i}") for i in range(NT)] for sl in range(2)]
        r_GA = [[Res(f"GA{i}_{g}") for g in range(8)] for i in range(NT)]
        r_MB = [[Res(f"MB{i}_{g}") for g in range(8)] for i in range(NT)]
        r_X1 = [Res(f"X1{i}") for i in range(NT)]
        r_MG = [[Res(f"MG{i}_{g}") for g in range(8)] for i in range(NT)]
        qk_cres = [[Res(f"qkc{i}_{c}") for c in range(8)] for i in range(2)]
        v_cres = [r_y[i % 4] for i in range(8)]

        xT_v = xT.ap().rearrange("(c p) t -> p c t", p=128)
        X1_v = X1s.ap().rearrange("(c p) t -> p c t", p=128)
        Q_v = Qs.ap().rearrange("(c p) t -> p c t", p=128)
        xTp_v = xTp.ap().rearrange("(c p) t -> p c t", p=128)
        KG_v = [KG.ap()[sl * 1024:(sl + 1) * 1024, :].rearrange("(c p) t -> p c t", p=128) for sl in range(2)]
        VG_v = [VG.ap()[sl * 1024:(sl + 1) * 1024, :].rearrange("(h s) (l d) -> s l h d", s=128, d=128)
                for sl in range(2)]
        LG_v = [LG.ap()[sl * TPC:(sl + 1) * TPC, :].rearrange("(l p) h -> p l h", p=128) for sl in range(2)]
        win_v = wbf["win"].ap().rearrange("(c p) n -> p c n", p=128)
        win32_v = wfull["win"].ap().rearrange("(c p) n -> p c n", p=128)

        def w_in_chunk(ci):
            need_cast("win", ci)
            return wload(win_v[:, :, ci * 512:(ci + 1) * 512],
                         lambda t: t[:].rearrange("p (c f) -> p c f", c=8), r_w["win"][ci],
                         win32_v[:, :, ci * 512:(ci + 1) * 512])

        cact = {"i": 0}

        def evac(dst_ap, src_ap, reads, writes, func=None):
            if func is not None:
                return T.op("act", lambda e: e.activation(out=dst_ap, in_=src_ap, func=func), reads, writes)
            cact["i"] += 1
            if cact["i"] % 2:
                return T.op("act", lambda e: e.activation(out=dst_ap, in_=src_ap, func=AF.Copy), reads, writes)
            return T.op("dve", lambda e: e.tensor_copy(dst_ap, src_ap), reads, writes)

        def load_x(ti):
            xs_ = xbuf[ti % 2]
            src = xT_v if ti < NT else xTp_v
            lt_ = ti % NT
            T.dma("sp", xs_.ds, [(xs_.t[:], src[:, :, lt_ * TT:(lt_ + 1) * TT])], writes=r_xc[ti % 2])

        def w_stage(slot, lt, own):
            tsl = slice(lt * TT, (lt + 1) * TT)

            def sec_qk(which, act_only):
                dst_v, rdst = ((Q_v, r_Q[lt]), (KG_v[slot], r_KG[slot][lt]))[which]
                for half in range(2):
                    sti = qk_ring.i % 2
                    st = qk_ring.next()
                    sw, wv = w_in_chunk(which * 2 + half)
                    for ci in range(4):
                        b = next_mm()
                        mm_group(b, psum[b][:], [(wv[:, dc, ci * 128:(ci + 1) * 128], hM[:, dc, :]) for dc in range(8)],
                                 [sw.res] + r_hM)
                        evac(st.t[:, ci, :], psum[b][:], [pres[b]], [qk_cres[sti][ci]],
                             func=(AF.Copy if act_only else None))
                    T.dma("sp", st.ds, [(dst_v[:, half * 4:(half + 1) * 4, tsl], st.t[:])], reads=qk_cres[sti][0:4],
                          writes=[rdst[half]])

            def sec_v():
                for half in range(2):
                    sw, wv = w_in_chunk(4 + half)
                    for blk in range(4):
                        b = next_mm()
                        mm_group(b, psum[b][:], [(hM[:, dc, blk * 128:(blk + 1) * 128], wv[:, dc, :]) for dc in range(8)],
                                 [sw.res] + r_hM)
                        evac(vst.t[:, blk, half * 512:(half + 1) * 512], psum[b][:], [pres[b]], [v_cres[half * 4 + blk]])
                T.dma("sp", vst.ds, [(VG_v[slot][:, lt * 4 + blk, :, :], vst.t[:, blk, :].rearrange("p (h d) -> p h d", h=H))
                                     for blk in range(4)], reads=v_cres, writes=[r_VG[slot][lt]])

            def sec_f():
                b = next_mm()
                for blk in range(4):
                    mm_group(b, psum[b][:, blk * H:(blk + 1) * H],
                             [(hM[:, dc, blk * 128:(blk + 1) * 128], wf_sb[:, dc, :]) for dc in range(8)], [r_wf] + r_hM)
                T.op("dve", lambda e, b=b: e.tensor_tensor(
                    lf_z[:], psum[b][:, 0:4 * H].rearrange("p (b h) -> p b h", h=H),
                    bf_bc[:].unsqueeze(1).to_broadcast([128, 4, H]), ALU.add), [pres[b], r_sgc], [r_lfz])
                T.op("act", lambda e: e.activation(out=lf_z[:], in_=lf_z[:], func=AF.Exp, scale=-1.0), [r_lfz], [r_lfz])
                T.op("act", lambda e: e.activation(out=lf_z[:], in_=lf_z[:], func=AF.Ln, bias=1.0), [r_lfz], [r_lfz])
                T.op("dve", lambda e: e.tensor_scalar(lf_st.t[:], lf_z[:], -1.0, None, ALU.mult), [r_lfz], [lf_st.res])
                T.dma("sp", lf_st.ds, [(LG_v[slot][:, lt * 4:(lt + 1) * 4, :], lf_st.t[:])], reads=[lf_st.res],
                      writes=[r_LG[slot][lt]])

            def sec_sv_mm(half):
                sw, wv = w_in_chunk(8 + half)
                for blk in range(4):
                    b = next_mm()
                    mm_group(b, psum[b][:], [(hM[:, dc, blk * 128:(blk + 1) * 128], wv[:, dc, :]) for dc in range(8)],
                             [sw.res] + r_hM)
                    T.op("act", lambda e, b=b, blk=blk: e.activation(out=vs_ring[blk][:], in_=psum[b][:], func=AF.Gelu),
                         [pres[b]], [r_vs[blk]])
                for blk in range(4):
                    for gi in range(4):
                        T.op("dve", lambda e, blk=blk, gi=gi: e.bn_stats(
                            stats[:, blk * 4 + gi, :], vs_ring[blk][:, gi * 128:(gi + 1) * 128]), [r_vs[blk]], [r_st])
                for q_ in range(16):
                    T.op("dve", lambda e, q_=q_: e.bn_aggr(mv[:, q_, :], stats[:, q_, :]), [r_st], [r_st])

            def sec_sv_fin(half):
                T.op("act", lambda e: e.activation(out=lrs[:], in_=mv[:, :, 1], func=AF.Sqrt, bias=eps_col[:, 1:2]),
                     [r_st, r_pc], [r_st])
                T.op("dve", lambda e: e.reciprocal(lrs[:], lrs[:]), [r_st], [r_st])
                for blk in range(4):
                    for gi in range(4):
                        g = half * 4 + gi
                        q_ = blk * 4 + gi
                        T.op("dve", lambda e, gi=gi, g=g, blk=blk, q_=q_: e.tensor_scalar(
                            vn[:, blk, g * 128:(g + 1) * 128], vs_ring[blk][:, gi * 128:(gi + 1) * 128],
                            mv[:, q_, 0:1], lrs[:, q_:q_ + 1], ALU.subtract, ALU.mult), [r_vs[blk], r_st], [r_vn[blk]])

            if not own:
                sec_qk(1, False)
                sec_v()
                sec_f()
                return
            sec_sv_mm(0)
            sec_qk(0, True)
            sec_sv_fin(0)
            sec_sv_mm(1)
            sec_qk(1, True)
            sec_sv_fin(1)
            sec_v()
            sec_f()
            GA_v = GAs.ap().rearrange("(c p) t -> p c t", p=128)
            MB_v = MBs.ap().rearrange("(c p) t -> p c t", p=128)
            for half in range(2):
                su_, wu_ = w_in_chunk(6 + half)
                sb_, wb_ = w_in_chunk(12 + half)
                sa_, wa_ = w_in_chunk(10 + half)
                for gi in range(4):
                    g = half * 4 + gi
                    k = g % 2
                    csl = slice(gi * 128, (gi + 1) * 128)
                    b = next_mm()
                    mm_group(b, psum[b][:], [(wu_[:, dc, csl], hM[:, dc, :]) for dc in range(8)], [su_.res] + r_hM)
                    T.op("act", lambda e, b=b, k=k: e.activation(out=u_ring[k][:], in_=psum[b][:], func=AF.Gelu),
                         [pres[b]], [r_u[k]])
                    b = next_mm()
                    mm_group(b, psum[b][:], [(wb_[:, dc, csl], hM[:, dc, :]) for dc in range(8)], [sb_.res] + r_hM)
                    T.op("act", lambda e, b=b, k=k: e.activation(out=gb_ring[k][:], in_=psum[b][:], func=AF.Sigmoid),
                         [pres[b]], [r_gb[k]])
                    b = next_mm()
                    mm_group(b, psum[b][:], [(wa_[:, dc, csl], hM[:, dc, :]) for dc in range(8)], [sa_.res] + r_hM)
                    gs = ga_ring.next()
                    T.op("act", lambda e, b=b, gs=gs: e.activation(out=gs.t[:], in_=psum[b][:], func=AF.Sigmoid),
                         [pres[b]], [gs.res])
                    T.dma("sp", gs.ds, [(GA_v[:, g, tsl], gs.t[:])], reads=[gs.res], writes=[r_GA[lt][g]])
                    b = next_mm()
                    for blk in range(4):
                        T.op("pe", lambda e, b=b, blk=blk, g=g: e.matmul(
                            psum[b][:, blk * 128:(blk + 1) * 128], vn[:, blk, g * 128:(g + 1) * 128], wsT_b[:, g, :],
                            start=True, stop=True), [r_vn[blk], r_sgc], [pres[b]], signal=(blk == 3))
                    ms = mb_ring.next()
                    T.op("dve", lambda e, b=b, ms=ms, g=g: e.scalar_tensor_tensor(
                        out=ms.t[:].rearrange("p (b t) -> p b t", b=4), in0=psum[b][:].rearrange("p (b t) -> p b t", b=4),
                        scalar=lncols[:, 0, g:g + 1], in1=e_g[:, g, :].unsqueeze(1).to_broadcast([128, 4, 128]),
                        op0=ALU.mult, op1=ALU.add), [pres[b], r_const, r_sgc], [ms.res])
                    T.op("dve", lambda e, ms=ms, k=k: e.tensor_tensor(ms.t[:], ms.t[:], u_ring[k][:], ALU.mult),
                         [r_u[k], ms.res], [ms.res])
                    T.op("dve", lambda e, ms=ms, k=k: e.tensor_tensor(ms.t[:], ms.t[:], gb_ring[k][:], ALU.mult),
                         [r_gb[k], ms.res], [ms.res])
                    T.dma("sp", ms.ds, [(MB_v[:, g, tsl], ms.t[:])], reads=[ms.res], writes=[r_MB[lt][g]])

        NTA = 2 * NT
        FFN1 = ("wg1", "wu1")
        wdirect["on"] = True
        load_x(0)
        load_x(1)
        prenorm_now(xbuf[0].t, r_xc[0], 0, hA, r_hA, PS_SS2, rstdP, r_rstdP)
        gateup(FFN1, range(32), hA, r_hA, [])
        for ti in range(NTA):
            slot, lt = ti // NT, ti % NT
            own = slot == 0
            xs_ = xbuf[ti % 2]
            x_t, rx = xs_.t, r_xc[ti % 2]
            tsl = slice(lt * TT, (lt + 1) * TT)
            if ti + 1 < NTA:
                side = p_side(xbuf[(ti + 1) % 2].t, r_xc[(ti + 1) % 2], 0, hA, r_hA)
            else:
                side = [[] for _ in range(8)]
            down_stage("wd1", 1, side)

            def store_fn(xs_=xs_, x_t=x_t, rx=rx, lt=lt, tsl=tsl, own=own):
                if own:
                    T.dma("sp", xs_.ds, [(X1_v[:, :, tsl], x_t[:])], reads=rx, writes=[r_X1[lt]])

            def final_fn(ti=ti):
                if ti + 2 < NTA:
                    load_x(ti + 2)
            chain = chain_tasks(x_t, rx, 2, hM, r_hM, PS_SS, sq, r_sq, "sq", ybuf, r_y, rstdD, r_rstdD,
                                store_fn, final_fn)
            if ti + 1 < NTA:
                gateup(FFN1, range(0, 12), hA, r_hA, chain)
            flush(chain)
            w_stage(slot, lt, own)
            wdirect["on"] = False
            if ti + 1 < NTA:
                gateup(FFN1, range(12, 32), hA, r_hA, [])
        T.barrier()
        if STOP == 1:
            raise _StopBuild

        ar.reset(cmark)
        lg = ar.alloc([128, NIDX, H], F32, "lg")
        pre = ar.alloc([128, NIDX, H], F32, "pre")
        preq = [ar.alloc([128, NIDX, H], F32, "preq") for _ in range(2)]
        negc = ar.alloc([128, H, NIDX], F32, "negc")
        offcol = ar.alloc([128, 2], F32, "offcol")
        r_c = Res("cums")
        ds_lg = T.dsem("lg")
        T.dma("sp", ds_lg, [(lg[:, sl * NLB:(sl + 1) * NLB, :], LG_v[sl]) for sl in range(2)],
              reads=r_LG[0] + r_LG[1], writes=[r_c])

        def gidx(q, gblk):
            G, j = gblk // 4, gblk % 4
            if G in GT[q]:
                return GT[q].index(G) * 4 + j
            return NLB + GT[1 - q].index(G) * 4 + j

        lg2 = lg[:].rearrange("p i h -> p (i h)")
        NLG = NIDX * H
        T.op("pe", lambda e: e.matmul(psum[0][:, 0:NLG], tri_f[:], lg2, start=True, stop=True), [r_c, r_pc], [pres[0]])
        T.op("pe", lambda e: e.matmul(psum[1][:, 0:NLG], ones_f[:], lg2, start=True, stop=True), [r_c, r_pc], [pres[1]])
        tot = ar.alloc([128, NIDX, H], F32, "tot")
        T.op("dve", lambda e: e.tensor_copy(tot[:].rearrange("p i h -> p (i h)"), psum[1][:, 0:NLG]), [pres[1]], [r_c])
        for q in range(2):
            T.op("dve", lambda e, q=q: e.memset(preq[q][:, gidx(q, 0), :], 0.0), [r_c], [r_c])
            for gb_ in range(1, NIDX):
                a, bq = gidx(q, gb_), gidx(q, gb_ - 1)
                T.op("dve", lambda e, a=a, bq=bq, q=q: e.tensor_tensor(preq[q][:, a, :], preq[q][:, bq, :], tot[:, bq, :], ALU.add),
                     [r_c], [r_c])
        T.op("dve", lambda e: e.tensor_scalar(pre[:], preq[0][:], flags[:, 1:2], None, ALU.mult), [r_c, r_const], [r_c])
        T.op("dve", lambda e: e.scalar_tensor_tensor(out=pre[:], in0=preq[1][:], scalar=flags[:, 0:1], in1=pre[:],
                                                     op0=ALU.mult, op1=ALU.add), [r_c, r_const], [r_c])
        T.op("dve", lambda e: e.scalar_tensor_tensor(
            out=negc[:].rearrange("p h i -> p i h"), in0=psum[0][:, 0:NLG].rearrange("p (i h) -> p i h", h=H), scalar=-1.0,
            in1=pre[:], op0=ALU.mult, op1=ALU.subtract), [pres[0], r_c], [r_c])
        T.op("dve", lambda e: e.tensor_scalar(offcol[:, 0:1], flags[:, 1:2], -30000.0, None, ALU.mult), [r_const], [r_c])
        T.op("dve", lambda e: e.tensor_scalar(offcol[:, 1:2], flags[:, 0:1], -30000.0, None, ALU.mult), [r_const], [r_c])

        kbuf = [Slot(T, ar.alloc([128, 2 * TPC], BF16, "kb"), f"kb{i}") for i in range(2)]
        vbuf = [Slot(T, ar.alloc([128, NIDX, 128], BF16, "vb"), f"vb{i}") for i in range(2)]
        q_ring = Ring(T, ar, 3, [128, TT], BF16, "q")
        p_ring = [ar.alloc([128, TT], BF16, "p") for _ in range(3)]
        r_p = [Res(f"p{i}") for i in range(3)]
        bias_ring = [ar.alloc([128, NIDX], F32, "bias") for _ in range(3)]
        r_bias = [Res("bias0"), Res("bias1"), Res("bias2")]
        gat = Ring(T, ar, 3, [128, TT], F32, "gat")
        mbt = Ring(T, ar, 3, [128, TT], F32, "mbt")
        rden = ar.alloc([128, TT], F32, "rden")
        r_rden = Res("rden")
        otmp = ar.alloc([128, TT], F32, "otmp")
        r_otmp = Res("otmp")
        mg_ring = Ring(T, ar, 2, [128, TT], BF16, "mgst")

        PS_S = [0, 1, 2]
        PS_O = [3, 4]
        PS_D = [5, 6]
        GA_v = GAs.ap().rearrange("(c p) t -> p c t", p=128)
        MB_v = MBs.ap().rearrange("(c p) t -> p c t", p=128)
        MG_v = MGs.ap().rearrange("(c p) t -> p c t", p=128)
        scale = 1.0 / float(np.sqrt(HD))

        units = []
        grp = 0
        for h in range(H):
            for lt in range(NT):
                first = True
                for sl in (1, 0):
                    for klt in range(lt + 1):
                        for j in range(4):
                            units.append(dict(h=h, lt=lt, grp=grp, first=first, last=False, rk=sl, klt=klt, j=j,
                                              diag=(sl == 0 and klt == lt)))
                            first = False
                units[-1]["last"] = True
                grp += 1
        state = {}

        def load_head(h):
            ks, vsb = kbuf[h % 2], vbuf[h % 2]
            T.dma("sp", ks.ds, [(ks.t[:, rk * TPC:(rk + 1) * TPC], KG.ap()[rk * 1024 + h * 128: rk * 1024 + (h + 1) * 128, :])
                                for rk in range(2)], reads=[r for sl in range(2) for t_ in r_KG[sl] for r in t_], writes=[ks.res])
            T.dma("sp", vsb.ds, [(vsb.t[:, rk * NLB:(rk + 1) * NLB, :],
                                  VG.ap()[rk * 1024 + h * 128: rk * 1024 + (h + 1) * 128, :].rearrange("s (l d) -> s l d", d=128))
                                 for rk in range(2)], reads=r_VG[0] + r_VG[1], writes=[vsb.res])

        def group_prologue(g):
            if g in state or g >= H * NT:
                return
            h, lt = divmod(g, NT)
            qs = q_ring.next()
            T.dma("sp", qs.ds, [(qs.t[:], Q_v[:, h, lt * TT:(lt + 1) * TT])], reads=[r_Q[lt][h // 4]], writes=[qs.res])
            k = g % 3
            T.op("dve", lambda e, k=k, h=h, lt=lt: e.tensor_scalar(
                bias_ring[k][:], negc[:, h, :], pre[:, 4 * lt + 2, h:h + 1], BCLAMP, ALU.add, ALU.min), [r_c], [r_bias[k]])
            fsl = slice(NLB + 4 * lt, NLB + 4 * lt + 4)
            T.op("dve", lambda e, k=k, lt=lt, fsl=fsl: e.tensor_scalar(
                bias_ring[k][:, fsl], bias_ring[k][:, fsl], offcol[:, lt % 2:lt % 2 + 1], None, ALU.add),
                [r_c, r_bias[k]], [r_bias[k]])
            gs, ms = gat.next(), mbt.next()
            T.dma("sp", gs.ds, [(gs.t[:], GA_v[:, h, lt * TT:(lt + 1) * TT])], reads=[r_GA[lt][h]], writes=[gs.res])
            T.dma("sp", ms.ds, [(ms.t[:], MB_v[:, h, lt * TT:(lt + 1) * TT])], reads=[r_MB[lt][h]], writes=[ms.res])
            state[g] = dict(q=qs, ga=gs, mb=ms)

        def emit_S(i):
            u = units[i]
            if u["first"]:
                h_, lt_ = u["h"], u["lt"]
                if lt_ == 0 and h_ == 0:
                    load_head(0)
                if lt_ == min(1, NT - 1) and h_ + 1 < H:
                    load_head(h_ + 1)
                group_prologue(u["grp"])
                group_prologue(u["grp"] + 1)
            st = state[u["grp"]]
            ks = kbuf[u["h"] % 2]
            b = PS_S[i % 3]
            col = u["rk"] * TPC + u["klt"] * TT + u["j"] * 128
            T.op("pe", lambda e, b=b, ks=ks, col=col, q=st["q"]: e.matmul(
                psum[b][:], ks.t[:, col:col + 128], q.t[:], start=True, stop=True),
                [ks.res, st["q"].res], [pres[b]])

        def emit_rest(i):
            u = units[i]
            g, h, lt = u["grp"], u["h"], u["lt"]
            st = state[g]
            b = PS_S[i % 3]
            k3 = i % 3
            idx = u["rk"] * NLB + u["klt"] * 4 + u["j"]
            kb = g % 3
            T.op("act", lambda e, b=b, k3=k3, kb=kb, idx=idx: e.activation(
                out=p_ring[k3][:], in_=psum[b][:], func=AF.Exp, bias=bias_ring[kb][:, idx:idx + 1], scale=scale),
                [pres[b], r_bias[kb]], [r_p[k3]])
            if u["diag"]:
                T.op("dve", lambda e, k3=k3, j=u["j"]: e.tensor_tensor(p_ring[k3][:], p_ring[k3][:], diag[:, j, :], ALU.mult),
                     [r_pc, r_p[k3]], [r_p[k3]])
            vsb = vbuf[h % 2]
            bo, bd = PS_O[g % 2], PS_D[g % 2]
            T.op("pe", lambda e, bo=bo, vsb=vsb, idx=idx, k3=k3, f=u["first"], l=u["last"]: e.matmul(
                psum[bo][:], vsb.t[:, idx, :], p_ring[k3][:], start=f, stop=l), [vsb.res, r_p[k3]], [pres[bo]], signal=False)
            T.op("pe", lambda e, bd=bd, k3=k3, f=u["first"], l=u["last"]: e.matmul(
                psum[bd][:], ones_bf[:], p_ring[k3][:], start=f, stop=l), [r_p[k3], r_pc], [pres[bd]], signal=True)
            if u["last"]:
                T.op("dve", lambda e, bd=bd: e.reciprocal(rden[:], psum[bd][:]), [pres[bd]], [r_rden])
                T.op("dve", lambda e, bo=bo: e.tensor_tensor(otmp[:], psum[bo][:], rden[:], ALU.mult),
                     [pres[bo], r_rden], [r_otmp])
                T.op("dve", lambda e, gs=st["ga"]: e.tensor_tensor(otmp[:], otmp[:], gs.t[:], ALU.mult),
                     [r_otmp, st["ga"].res], [r_otmp])
                mg = mg_ring.next()
                T.op("dve", lambda e, mg=mg, ms=st["mb"]: e.tensor_tensor(mg.t[:], otmp[:], ms.t[:], ALU.add),
                     [r_otmp, st["mb"].res], [mg.res])
                T.dma("sp", mg.ds, [(MG_v[:, h, lt * TT:(lt + 1) * TT], mg.t[:])], reads=[mg.res], writes=[r_MG[lt][h]])
                del state[g]

        NU = len(units)
        LOOK = 2
        for i in range(min(LOOK, NU)):
            emit_S(i)
        for i in range(NU):
            if i + LOOK < NU:
                emit_S(i + LOOK)
            emit_rest(i)
        T.barrier()
        if STOP == 2:
            raise _StopBuild

        ar.reset(ffn_mark)
        wout_sb = ar.alloc([128, 8, D], BF16, "wout")
        r_wo = Res("wout")
        ds_wo = T.dsem("wout")
        need_cast("wout", 1)
        T.dma("pool", ds_wo, [(wout_sb[:], wbf["wout"].ap().rearrange("(c p) n -> p c n", p=128))],
              reads=r_w["wout"], writes=[r_wo])
        mgin = Ring(T, ar, 2, [128, 8, TT], BF16, "mgin")
        y2 = ar.alloc([128, 8, TT], F32, "y2")
        r_y2 = [Res(f"y2_{dc}") for dc in range(8)]
        outT_v = outT.ap().rearrange("(c p) t -> p c t", p=128)
        r_out = Res("out")
        FFN2 = ("wg2", "wu2")

        def load_b2(lt):
            xsl = xbuf[lt % 2]
            tsl_ = slice(lt * TT, (lt + 1) * TT)
            T.dma("sp", xsl.ds, [(xsl.t[:], X1_v[:, :, tsl_])], reads=[r_X1[lt]], writes=r_xc[lt % 2])
            ms_ = mgin.next()
            T.dma("sp", ms_.ds, [(ms_.t[:], MG_v[:, :, tsl_])], reads=r_MG[lt], writes=[ms_.res])
            return ms_

        def wout_stage(ms_):
            bank_of = {}
            act, pe = stat_ops(lambda dc: psum[bank_of[dc]][:], lambda dc: [pres[bank_of[dc]]], PS_SS2, sqP, r_sqP, "sqP")
            for dc in range(8):
                b = next_mm()
                bank_of[dc] = b
                mm_group(b, psum[b][:], [(wout_sb[:, c, dc * 128:(dc + 1) * 128], ms_.t[:, c, :]) for c in range(8)],
                         [r_wo, ms_.res])
                if dc > 0:
                    pe(dc - 1)
                T.op("act", lambda e, dc=dc, b=b: e.activation(out=y2[:, dc, :], in_=psum[b][:], func=AF.Copy,
                                                               scale=gcols[:, 3, dc:dc + 1]),
                     [pres[b], r_const], [r_y2[dc]])
                act(dc)
            pe(7)

        def b2_chain(lt):
            x_t, rx = xbuf[lt % 2].t, r_xc[lt % 2]
            tasks = [lambda: finish_rstd(PS_SS2, rstdP, r_rstdP)]
            tasks += chain_tasks(x_t, rx, 4, hA, r_hA, PS_SS2, sqP, r_sqP, "sqP", y2, r_y2, rstdP, r_rstdP)
            return tasks

        mss = {0: load_b2(0)}
        if NT > 1:
            mss[1] = load_b2(1)
        wout_stage(mss[0])
        flush(b2_chain(0))
        gateup(FFN2, range(32), hA, r_hA, [])
        for lt in range(NT):
            xsl = xbuf[lt % 2]
            x_t, rx = xsl.t, r_xc[lt % 2]
            tsl = slice(lt * TT, (lt + 1) * TT)
            side = [[] for _ in range(8)]
            if lt + 1 < NT:
                wout_stage(mss[lt + 1])
                ch = b2_chain(lt + 1)
                per = [2, 2, 2, 2, 2, 1, 1, 1]
                for dc in range(8):
                    for _ in range(per[dc]):
                        if ch:
                            side[dc].append(ch.pop(0))
                assert not ch
            down_stage("wd2", 5, side)

            def out_store(xsl=xsl, x_t=x_t, rx=rx, tsl=tsl, lt=lt):
                T.dma("sp", xsl.ds, [(outT_v[:, :, tsl], x_t[:])], reads=rx, writes=[r_out])
                if lt + 2 < NT:
                    mss[lt + 2] = load_b2(lt + 2)
            otasks = resid_tasks(ybuf, r_y, rstdD, r_rstdD, x_t, rx)
            otasks.append(out_store)
            if lt + 1 < NT:
                gateup(FFN2, range(32), hA, r_hA, otasks)
            flush(otasks)
        T.barrier()


    try:
        body()
    except _StopBuild:
        T.barrier()

    from contextlib import ExitStack
    with ExitStack() as es:
        for E in T.eng.values():
            E.sem = es.enter_context(nc.semaphore("s_" + E.name))
        for d in T.dsems:
            d.sem = es.enter_context(nc.semaphore("d_" + d.name))
        block = es.enter_context(nc.Block())

        @block.tensor
        def _(e):
            T.emit(nc, "pe", e)

        @block.scalar
        def _(e):
            T.emit(nc, "act", e)

        @block.vector
        def _(e):
            T.emit(nc, "dve", e)

        @block.gpsimd
        def _(e):
            T.emit(nc, "pool", e)

        @block.sync
        def _(e):
            T.emit(nc, "sp", e)
    return nc


def _prep_inputs(inp):
    f = lambda a: np.ascontiguousarray(np.asarray(a, dtype=np.float32))
    x = f(inp["x"])
    w = {"wg1": f(inp["ffn1_w_gate"])[0], "wu1": f(inp["ffn1_w_up"])[0], "wd1": f(inp["ffn1_w_down"])[0],
         "wout": f(inp["w_out"])[0], "wg2": f(inp["ffn2_w_gate"])[0], "wu2": f(inp["ffn2_w_up"])[0],
         "wd2": f(inp["ffn2_w_down"])[0]}
    w_in = f(inp["w_in"])[0]
    w["win"] = np.ascontiguousarray(np.concatenate([w_in[:, :3072], w_in[:, 3080:]], axis=1))
    wf = np.ascontiguousarray(w_in[:, 3072:3080])
    gains = [inp["ffn1_pre_g"], inp["ffn1_post_g"], inp["mix_pre_g"], inp["mix_post_g"], inp["ffn2_pre_g"],
             inp["ffn2_post_g"]]
    gcols = np.stack([f(g)[0].reshape(8, 128).T for g in gains], axis=1).reshape(128, 48)
    lncols = np.stack([f(inp["sgu_ln_g"])[0].reshape(8, 128).T, f(inp["sgu_ln_b"])[0].reshape(8, 128).T],
                      axis=1).reshape(128, 16)
    wsT = np.ascontiguousarray(f(inp["sgu_w_s"])[0].transpose(2, 0, 1)).reshape(128, 1024)
    bs = f(inp["sgu_b_s"])[0].reshape(1, 1024)
    bfg = f(inp["b_forget"]).reshape(1, 8)
    maps = []
    for c in range(NCORE):
        b, p = c // 2, c % 2
        toks = np.concatenate([np.arange(G * TT, (G + 1) * TT) for G in GT[p]])
        tokp = np.concatenate([np.arange(G * TT, (G + 1) * TT) for G in GT[1 - p]])
        m = {"xT": np.ascontiguousarray(x[b][toks].T), "xTp": np.ascontiguousarray(x[b][tokp].T)}
        for (n, r, cc) in WEIGHTS:
            m[n + "_f"] = w[n]
        m["wf"] = wf
        m["gcols"] = np.ascontiguousarray(gcols)
        m["lncols"] = np.ascontiguousarray(lncols)
        m["wsT"] = wsT
        m["bs"] = bs
        m["bfg"] = bfg
        fl = np.zeros((128, 4), np.float32)
        fl[:, 0] = p
        fl[:, 1] = 1 - p
        m["flags"] = fl
        maps.append(m)
    return maps


_NC_CACHE = {}


def kernel(**inputs):
    maps = _prep_inputs(inputs)
    if "nc" not in _NC_CACHE:
        _NC_CACHE["nc"] = build()
    nc = _NC_CACHE["nc"]
    res = run_bass_kernel_spmd(nc, maps, core_ids=list(range(NCORE)))
    out = np.empty((NB, SEQ, D), np.float32)
    for c in range(NCORE):
        b, p = c // 2, c % 2
        o = np.asarray(res.results[c]["outT"]).T
        for i, G in enumerate(GT[p]):
            out[b, G * TT:(G + 1) * TT] = o[i * TT:(i + 1) * TT]
    return out
```

```python
import numpy as np
import concourse.bass as bass
import concourse.mybir as mybir
from concourse.bass_utils import run_bass_kernel_spmd

F32 = mybir.dt.float32
BF16 = mybir.dt.bfloat16
AF = mybir.ActivationFunctionType
ALU = mybir.AluOpType

D = 1024
DFF = 4096
SEQ = 8192
NB = 4
NCORE = 8
TT = 512
NT = 8
TPC = NT * TT
NLB = NT * 4
NIDX = 2 * NLB
H = 8
HD = 128
WIN = 7168
GT = [[0, 3, 4, 7, 8, 11, 12, 15], [1, 2, 5, 6, 9, 10, 13, 14]]


def _configure(nt, ncore):
    global NT, TPC, NLB, NIDX, NCORE, GT, SEQ, NB
    NT, NCORE = nt, ncore
    TPC, NLB, NIDX = NT * TT, NT * 4, 2 * NT * 4
    SEQ = 2 * TPC
    NB = ncore // 2
    g0 = [g for g in range(2 * NT) if g % 4 in (0, 3)]
    g1 = [g for g in range(2 * NT) if g % 4 in (1, 2)]
    GT = [g0, g1]
RMS_EPS = 1e-6
LN_EPS = 1e-5
BCLAMP = 64.0
NSLOT = 5
SAME_ENGINE_SYNC = True
STOP = 99


class _StopBuild(Exception):
    pass

WEIGHTS = [("wg1", D, DFF), ("wu1", D, DFF), ("wd1", DFF, D), ("win", D, WIN),
           ("wout", D, D), ("wg2", D, DFF), ("wu2", D, DFF), ("wd2", DFF, D)]


class Res:
    __slots__ = ("name", "w", "r")

    def __init__(self, name):
        self.name = name
        self.w = None
        self.r = {}


class Eng:
    def __init__(self, name):
        self.name = name
        self.sem = None
        self.cnt = 0
        self.waited = {}
        self.prog = []


class DSem:
    def __init__(self, name, step=16):
        self.name = name
        self.sem = None
        self.cnt = 0
        self.step = step
        self.last = None


class Tracker:
    def __init__(self):
        self.eng = {n: Eng(n) for n in ("pe", "act", "dve", "pool", "sp")}
        self.dsems = []
        self.nwait = 0

    def dsem(self, name, step=16):
        d = DSem(name, step)
        self.dsems.append(d)
        return d

    def _wait(self, E, deps):
        for tok in deps:
            if tok is None:
                continue
            key, val, en = tok
            if en == E.name:
                if E.name == "pe" or not SAME_ENGINE_SYNC:
                    continue
            if E.waited.get(id(key), 0) >= val:
                continue
            E.waited[id(key)] = val
            self.nwait += 1
            E.prog.append(("wait", key, val))

    def _deps(self, reads, writes):
        deps = []
        for r in reads:
            if r.w is not None:
                deps.append(r.w)
        for w in writes:
            if w.w is not None:
                deps.append(w.w)
            deps.extend(w.r.values())
        return deps

    def _reg(self, tok, reads, writes):
        for r in reads:
            r.r[id(tok[0])] = tok
        for w in writes:
            w.w = tok
            w.r = {}

    def op(self, en, fn, reads=(), writes=(), signal=True):
        E = self.eng[en]
        self._wait(E, self._deps(reads, writes))
        if signal:
            E.cnt += 1
            tok = (E, E.cnt, en)
        else:
            tok = (E, E.cnt + 1, en)
        E.prog.append(("op", fn, signal))
        self._reg(tok, reads, writes)
        return tok

    def dma(self, q, ds, pairs, reads=(), writes=()):
        E = self.eng[q]
        deps = self._deps(reads, writes)
        deps.append(ds.last)
        self._wait(E, deps)
        for (o, i) in pairs:
            E.prog.append(("dma", o, i, ds))
        ds.cnt += ds.step * len(pairs)
        tok = (ds, ds.cnt, None)
        ds.last = tok
        self._reg(tok, reads, writes)
        return tok

    def cc(self, ds, kind, groups, in_ap, out_ap, reads=(), writes=()):
        E = self.eng["pool"]
        deps = self._deps(reads, writes)
        deps.append(ds.last)
        self._wait(E, deps)
        E.prog.append(("cc", kind, groups, in_ap, out_ap, ds))
        ds.cnt += ds.step
        tok = (ds, ds.cnt, None)
        ds.last = tok
        self._reg(tok, reads, writes)
        return tok

    def barrier(self):
        toks = [(E, E.cnt, E.name) for E in self.eng.values() if E.cnt > 0]
        toks += [d.last for d in self.dsems if d.last is not None]
        for E in self.eng.values():
            self._wait(E, [t for t in toks if t[2] != E.name])

    def emit(self, nc, en, h):
        E = self.eng[en]
        for item in E.prog:
            k = item[0]
            if k == "wait":
                h.wait_ge(item[1].sem, item[2])
            elif k == "op":
                ins = item[1](h)
                if item[2]:
                    ins.then_inc(E.sem, 1)
            elif k == "dma":
                h.dma_start(out=item[1], in_=item[2]).then_inc(item[3].sem, 16)
            elif k == "cc":
                h.collective_compute(item[1], ALU.bypass, replica_groups=item[2],
                                     ins=[item[3]], outs=[item[4]]).then_inc(item[5].sem, item[5].step)


class Arena:
    def __init__(self, nc, base=16512, limit=229376):
        self.nc = nc
        self.base = base
        self.off = base
        self.limit = limit
        self.n = 0

    def alloc(self, shape, dt, name=None):
        esz = 4 if dt == F32 else 2
        nbytes = esz * int(np.prod(shape[1:]))
        nbytes = (nbytes + 31) // 32 * 32
        assert self.off + nbytes <= self.limit, f"SBUF overflow {self.off}+{nbytes}"
        self.n += 1
        t = self.nc.alloc_sbuf_tensor_at(f"{name or 't'}_{self.n}", list(shape), dt, offset=self.off)
        self.off += nbytes
        return t

    def mark(self):
        return self.off

    def reset(self, m):
        self.off = m


class Slot:
    def __init__(self, T, t, name):
        self.t = t
        self.res = Res(name)
        self.ds = T.dsem(name)


class Ring:
    def __init__(self, T, arena, n, shape, dt, name):
        self.slots = [Slot(T, arena.alloc(shape, dt, name), f"{name}{i}") for i in range(n)]
        self.i = 0

    def next(self):
        s = self.slots[self.i % len(self.slots)]
        self.i += 1
        return s


def build():
    nc = bass.Bass("TRN2", target_bir_lowering=False)
    T = Tracker()

    def body():
        def din(name, shape, dt=F32):
            return nc.dram_tensor(name, list(shape), dt, kind="ExternalInput")

        xT = din("xT", [D, TPC])
        xTp = din("xTp", [D, TPC])
        wfull = {n: din(n + "_f", [r, c]) for (n, r, c) in WEIGHTS}
        wf_in = din("wf", [D, H])
        gcols_in = din("gcols", [128, 48])
        lncols_in = din("lncols", [128, 16])
        wsT_in = din("wsT", [128, 8 * 128])
        bs_in = din("bs", [1, 8 * 128])
        bf_in = din("bfg", [1, H])
        flags_in = din("flags", [128, 4])
        outT = nc.dram_tensor("outT", [D, TPC], F32, kind="ExternalOutput")

        wbf = {n: nc.dram_tensor(n + "_bf", [r, c], BF16) for (n, r, c) in WEIGHTS}
        Qs = nc.dram_tensor("Qs", [H * 128, TPC], BF16)
        KG = nc.dram_tensor("KG", [2 * H * 128, TPC], BF16)
        VG = nc.dram_tensor("VG", [2 * H * 128, NLB * 128], BF16)
        LG = nc.dram_tensor("LG", [2 * TPC, H], F32)
        GAs = nc.dram_tensor("GAs", [D, TPC], F32)
        MBs = nc.dram_tensor("MBs", [D, TPC], F32)
        X1s = nc.dram_tensor("X1s", [D, TPC], F32)
        MGs = nc.dram_tensor("MGs", [D, TPC], BF16)

        ar = Arena(nc)
        psum = [nc.alloc_psum_tensor(f"ps{i}", [128, 512], F32) for i in range(8)]
        pres = [Res(f"ps{i}") for i in range(8)]

        gcols = ar.alloc([128, 6, 8], F32, "gcols")
        lncols = ar.alloc([128, 2, 8], F32, "lncols")
        flags = ar.alloc([128, 4], F32, "flags")
        ones_bf = ar.alloc([128, 128], BF16, "ones_bf")
        invd_bf = ar.alloc([128, 128], BF16, "invd_bf")
        ones_f = ar.alloc([128, 128], F32, "ones_f")
        tri_f = ar.alloc([128, 128], F32, "tri_f")
        diag = ar.alloc([128, 4, 512], BF16, "diag")
        eps_col = ar.alloc([128, 2], F32, "eps")
        r_const = Res("const")
        ds_const = T.dsem("const")
        cmark = ar.mark()

        def vop(en, fname, reads, writes, *args, **kw):
            return T.op(en, lambda e: getattr(e, fname)(*args, **kw), reads, writes)

        T.dma("sp", ds_const, [
            (gcols[:].rearrange("p a b -> p (a b)"), gcols_in.ap()),
            (lncols[:].rearrange("p a b -> p (a b)"), lncols_in.ap()),
            (flags[:], flags_in.ap()),
        ], writes=[r_const])
        if STOP == -3:
            raise _StopBuild
        r_pc = Res("poolconst")
        vop("pool", "memset", [], [r_pc], ones_bf[:], 1.0)
        vop("pool", "memset", [], [r_pc], invd_bf[:], 1.0 / D)
        vop("pool", "memset", [], [r_pc], ones_f[:], 1.0)
        vop("pool", "memset", [], [r_pc], eps_col[:, 0:1], RMS_EPS)
        vop("pool", "memset", [], [r_pc], eps_col[:, 1:2], LN_EPS)
        T.op("pool", lambda e: e.affine_select(tri_f[:], ones_f[:], [[1, 128]], ALU.is_ge, 0.0,
                                               base=0, channel_multiplier=-1), [r_pc], [r_pc])
        vop("pool", "memset", [], [r_pc], diag[:].rearrange("p a b -> p (a b)"), 1.0)
        for j in range(4):
            T.op("pool", lambda e, j=j: e.affine_select(diag[:, j, :], diag[:, j, :], [[1, 512]], ALU.is_ge, 0.0,
                                                        base=-128 * j, channel_multiplier=-1), [r_pc], [r_pc])
        if STOP == -2:
            raise _StopBuild
        for n in (1, 5):
            vop("dve", "tensor_scalar", [r_const], [r_const], gcols[:, n, :], gcols[:, n, :], 0.5, None, ALU.mult)

        if STOP == -1:
            raise _StopBuild
        ds_wc = [T.dsem(f"wcast{i}") for i in range(4)]
        wdirect = {"on": False}
        r_w = {n: [Res(f"{n}_bf{b}") for b in range(c // (128 if n in ("wd1", "wd2") else 512))]
               for (n, r, c) in WEIGHTS}
        ds_wb = [T.dsem(f"wback{i}") for i in range(4)]
        wb_state = {"i": 0}
        cast_order = [("wout", 0), ("wout", 1)]
        for f4 in range(8):
            cast_order += [("wg2", f4), ("wu2", f4)]
        cast_order += [("wd2", 0), ("wd2", 1)]
        cast_state = {"pos": 0}

        def cast_upto(pos):
            pos = min(pos, len(cast_order))
            while cast_state["pos"] < pos:
                n, b = cast_order[cast_state["pos"]]
                ds = ds_wc[cast_state["pos"] % 4]
                cast_state["pos"] += 1
                wr = r_w[n][4 * b:4 * b + 4] if n in ("wd1", "wd2") else [r_w[n][b]]
                T.dma("pool", ds, [(wbf[n].ap()[:, b * 512:(b + 1) * 512], wfull[n].ap()[:, b * 512:(b + 1) * 512])],
                      writes=wr)

        def need_cast(n, b):
            if (n, b) in cast_order:
                cast_upto(cast_order.index((n, b)) + 4)
            if not wdirect["on"]:
                cast_upto(cast_state["pos"] + 1)

        if STOP == 0:
            raise _StopBuild
        wring = Ring(T, ar, NSLOT, [128, 4096], BF16, "w")

        def wload(src_ap, view, rw, src32=None):
            s = wring.next()
            dst = view(s.t)
            if wdirect["on"] and src32 is not None:
                T.dma("pool", s.ds, [(dst, src32)], writes=[s.res])
                if rw.w is None:
                    ds = ds_wb[wb_state["i"] % 4]
                    wb_state["i"] += 1
                    T.dma("sp", ds, [(src_ap, dst)], reads=[s.res], writes=[rw])
            else:
                T.dma("pool", s.ds, [(dst, src_ap)], reads=[rw], writes=[s.res])
            return s, dst

        def w_gu(n, f0):
            need_cast(n, f0 // 512)
            src = wbf[n].ap().rearrange("(c p) f -> p c f", p=128)[:, :, f0:f0 + 512]
            src32 = wfull[n].ap().rearrange("(c p) f -> p c f", p=128)[:, :, f0:f0 + 512]
            return wload(src, lambda t: t[:].rearrange("p (c f) -> p c f", c=8), r_w[n][f0 // 512], src32)

        def w_dn(n, dc):
            need_cast(n, dc // 4)
            src = wbf[n].ap().rearrange("(c p) d -> p c d", p=128)[:, :, dc * 128:(dc + 1) * 128]
            src32 = wfull[n].ap().rearrange("(c p) d -> p c d", p=128)[:, :, dc * 128:(dc + 1) * 128]
            return wload(src, lambda t: t[:].rearrange("p (c d) -> p c d", c=32), r_w[n][dc], src32)

        xbuf = [Slot(T, ar.alloc([128, 8, TT], F32, "x"), f"x{i}") for i in range(2)]
        r_xc = [[Res(f"xc{i}_{dc}") for dc in range(8)] for i in range(2)]
        hA = ar.alloc([128, 8, TT], BF16, "hA")
        r_hA = [Res(f"hA{dc}") for dc in range(8)]
        abuf = ar.alloc([128, 32, TT], BF16, "a")
        r_a = [Res(f"a{i}") for i in range(32)]
        ybuf = ar.alloc([128, 8, TT], F32, "y")
        r_y = [Res(f"y{i}") for i in range(8)]
        sq = [ar.alloc([128, TT], BF16, "sq") for _ in range(2)]
        r_sq = [Res("sq0"), Res("sq1")]
        sqP = [ar.alloc([128, TT], BF16, "sqP") for _ in range(2)]
        r_sqP = [Res("sqP0"), Res("sqP1")]
        rstdP = ar.alloc([128, TT], F32, "rstdP")
        r_rstdP = Res("rstdP")
        rstdD = ar.alloc([128, TT], F32, "rstdD")
        r_rstdD = Res("rstdD")
        rstdM = ar.alloc([128, TT], F32, "rstdM")
        r_rstdM = Res("rstdM")
        sil = [ar.alloc([128, TT], F32, "sil") for _ in range(2)]
        r_sil = [Res("sil0"), Res("sil1")]
        ffn_mark = ar.mark()
        cnt = {"sq": 0, "sqP": 0, "sil": 0, "mm": 0}

        PS_G = [0, 1]
        PS_U = [2, 3]
        PS_MM = [4, 5, 6]
        PS_SS = 7

        def mm_group(bank, out_ap, pairs, reads):
            n = len(pairs)
            for i, (l, r) in enumerate(pairs):
                T.op("pe", lambda e, l=l, r=r, i=i: e.matmul(out_ap, l, r, start=(i == 0), stop=(i == n - 1)),
                     reads, [pres[bank]], signal=(i == n - 1))

        def next_mm():
            b = PS_MM[cnt["mm"] % 3]
            cnt["mm"] += 1
            return b

        PS_SS2 = 0

        def stat_ops(src_fn, r_src_fn, bank, ring, r_ring, ckey):
            slotof = {}

            def act(dc):
                k = cnt[ckey] % 2
                cnt[ckey] += 1
                slotof[dc] = k
                T.op("act", lambda e: e.activation(out=ring[k][:], in_=src_fn(dc), func=AF.Square),
                     r_src_fn(dc), [r_ring[k]])

            def pe(dc):
                k = slotof[dc]
                T.op("pe", lambda e: e.matmul(psum[bank][:], invd_bf[:], ring[k][:], start=(dc == 0), stop=(dc == 7)),
                     [r_ring[k], r_pc], [pres[bank]], signal=True)
            return act, pe

        def finish_rstd(bank, rt, r_rt):
            T.op("act", lambda e: e.activation(out=rt[:], in_=psum[bank][:], func=AF.Sqrt, bias=eps_col[:, 0:1]),
                 [pres[bank], r_pc], [r_rt])
            T.op("dve", lambda e: e.reciprocal(rt[:], rt[:]), [r_rt], [r_rt])

        def h_ops(src, r_src_list, gi, rt, r_rt, hdst, r_hdst, dcs):
            for dc in dcs:
                T.op("dve", lambda e, dc=dc: e.scalar_tensor_tensor(
                    out=hdst[:, dc, :], in0=src[:, dc, :], scalar=gcols[:, gi, dc:dc + 1], in1=rt[:],
                    op0=ALU.mult, op1=ALU.mult), [r_src_list[dc], r_rt, r_const], [r_hdst[dc]])

        def prenorm_now(src, r_src_list, gi, hdst, r_hdst, bank, rt, r_rt):
            act, pe = stat_ops(lambda dc: src[:, dc, :], lambda dc: [r_src_list[dc]], bank, sqP, r_sqP, "sqP")
            for dc in range(8):
                act(dc)
                pe(dc)
            finish_rstd(bank, rt, r_rt)
            h_ops(src, r_src_list, gi, rt, r_rt, hdst, r_hdst, range(8))

        def p_side(src, r_src_list, gi, hdst, r_hdst):
            act, pe = stat_ops(lambda dc: src[:, dc, :], lambda dc: [r_src_list[dc]], PS_SS2, sqP, r_sqP, "sqP")
            side = [[] for _ in range(8)]
            side[0] = [lambda: act(0), lambda: act(1)]
            side[1] = [lambda: pe(0), lambda: pe(1), lambda: act(2), lambda: act(3)]
            side[2] = [lambda: pe(2), lambda: pe(3), lambda: act(4), lambda: act(5)]
            side[3] = [lambda: pe(4), lambda: pe(5), lambda: act(6), lambda: act(7)]
            side[4] = [lambda: pe(6), lambda: pe(7), lambda: finish_rstd(PS_SS2, rstdP, r_rstdP)]
            side[5] = [lambda: h_ops(src, r_src_list, gi, rstdP, r_rstdP, hdst, r_hdst, range(0, 4))]
            side[6] = [lambda: h_ops(src, r_src_list, gi, rstdP, r_rstdP, hdst, r_hdst, range(4, 8))]
            return side

        gu_state = {}

        def gateup(names, fcs, hsrc, r_hsrc, tasks):
            ng, nu = names
            for fc in fcs:
                fi = fc % 4
                if fi == 0:
                    gu_state["g"] = w_gu(ng, (fc // 4) * 512)
                    gu_state["u"] = w_gu(nu, (fc // 4) * 512)
                sg, wg = gu_state["g"]
                su, wu = gu_state["u"]
                bg = PS_G[fc % 2]
                bu = PS_U[fc % 2]
                mm_group(bg, psum[bg][:], [(wg[:, dc, fi * 128:(fi + 1) * 128], hsrc[:, dc, :]) for dc in range(8)],
                         [sg.res] + r_hsrc)
                mm_group(bu, psum[bu][:], [(wu[:, dc, fi * 128:(fi + 1) * 128], hsrc[:, dc, :]) for dc in range(8)],
                         [su.res] + r_hsrc)
                k = cnt["sil"] % 2
                cnt["sil"] += 1
                T.op("act", lambda e, bg=bg, k=k: e.activation(out=sil[k][:], in_=psum[bg][:], func=AF.Silu),
                     [pres[bg]], [r_sil[k]])
                T.op("dve", lambda e, bu=bu, k=k, fc=fc: e.tensor_tensor(abuf[:, fc, :], psum[bu][:], sil[k][:], ALU.mult),
                     [pres[bu], r_sil[k]], [r_a[fc]])
                if tasks:
                    tasks.pop(0)()

        def down_stage(nd, gpost, side):
            act, pe = stat_ops(lambda dc: psum[bank_of[dc]][:], lambda dc: [pres[bank_of[dc]]], PS_SS, sq, r_sq, "sq")
            bank_of = {}
            for dc in range(8):
                sd, wd = w_dn(nd, dc)
                b = next_mm()
                bank_of[dc] = b
                mm_group(b, psum[b][:], [(wd[:, fc, :], abuf[:, fc, :]) for fc in range(32)], [sd.res] + r_a)
                if dc > 0:
                    pe(dc - 1)
                T.op("act", lambda e, dc=dc, b=b: e.activation(out=ybuf[:, dc, :], in_=psum[b][:], func=AF.Copy,
                                                               scale=gcols[:, gpost, dc:dc + 1]),
                     [pres[b], r_const], [r_y[dc]])
                act(dc)
                for t in side[dc]:
                    t()
            pe(7)
            finish_rstd(PS_SS, rstdD, r_rstdD)

        def resid_tasks(ysrc, r_ysrc, rt, r_rt, x_t, rx, after=None):
            tasks = []
            for dc in range(8):
                def t(dc=dc):
                    T.op("dve", lambda e: e.tensor_tensor(ysrc[:, dc, :], ysrc[:, dc, :], rt[:], ALU.mult),
                         [r_ysrc[dc], r_rt], [r_ysrc[dc]])
                    T.op("dve", lambda e: e.tensor_tensor(x_t[:, dc, :], x_t[:, dc, :], ysrc[:, dc, :], ALU.add),
                         [r_ysrc[dc], rx[dc]], [rx[dc]])
                    if after is not None:
                        after(dc)
                tasks.append(t)
            return tasks

        def chain_tasks(x_t, rx, gi, hdst, r_hdst, bank, ring, r_ring, ckey, ysrc, r_ysrc, rt_in, r_rt_in,
                        store_fn=None, final_fn=None):
            act, pe = stat_ops(lambda dc: x_t[:, dc, :], lambda dc: [rx[dc]], bank, ring, r_ring, ckey)

            def after(dc):
                act(dc)
                if dc > 0:
                    pe(dc - 1)
            tasks = resid_tasks(ysrc, r_ysrc, rt_in, r_rt_in, x_t, rx, after)

            def t8():
                pe(7)
                if store_fn is not None:
                    store_fn()
            tasks.append(t8)
            tasks.append(lambda: finish_rstd(bank, rstdM, r_rstdM))
            tasks.append(lambda: h_ops(x_t, rx, gi, rstdM, r_rstdM, hdst, r_hdst, range(0, 4)))

            def t11():
                h_ops(x_t, rx, gi, rstdM, r_rstdM, hdst, r_hdst, range(4, 8))
                if final_fn is not None:
                    final_fn()
            tasks.append(t11)
            return tasks

        def flush(tasks):
            while tasks:
                tasks.pop(0)()

        wf_sb = ar.alloc([128, 8, H], BF16, "wf")
        r_wf = Res("wf")
        ds_wf = T.dsem("wf")
        T.dma("pool", ds_wf, [(wf_sb[:], wf_in.ap().rearrange("(c p) h -> p c h", p=128))], writes=[r_wf])
        wsT_f = ar.alloc([128, 8, 128], F32, "wsTf")
        wsT_b = ar.alloc([128, 8, 128], BF16, "wsTb")
        bs_bc = ar.alloc([128, 8, 128], F32, "bsbc")
        bf_bc = ar.alloc([128, H], F32, "bfbc")
        e_g = ar.alloc([128, 8, 128], F32, "eg")
        r_sgc = Res("sgc")
        T.dma("sp", ds_const, [
            (wsT_f[:].rearrange("p a b -> p (a b)"), wsT_in.ap()),
            (bs_bc[:].rearrange("p a b -> p (a b)"), bs_in.ap().partition_broadcast(128).rearrange("p a b -> p (a b)")),
            (bf_bc[:], bf_in.ap().partition_broadcast(128).rearrange("p a b -> p (a b)")),
        ], writes=[r_sgc])
        T.op("dve", lambda e: e.memset(wsT_f[64:128, :, 0:64], 0.0), [r_sgc], [r_sgc])
        T.op("dve", lambda e: e.tensor_copy(wsT_b[:], wsT_f[:]), [r_sgc], [r_sgc])
        for half in range(2):
            T.op("pe", lambda e, half=half: e.matmul(psum[half][:], ones_f[:],
                                                     wsT_f[:, half * 4:(half + 1) * 4, :].rearrange("p a b -> p (a b)"),
                                                     start=True, stop=True),
                 [r_sgc, r_pc], [pres[half]])
            for gi in range(4):
                g = half * 4 + gi
                T.op("dve", lambda e, half=half, gi=gi, g=g: e.scalar_tensor_tensor(
                    out=e_g[:, g, :], in0=psum[half][:, gi * 128:(gi + 1) * 128], scalar=lncols[:, 1, g:g + 1],
                    in1=bs_bc[:, g, :], op0=ALU.mult, op1=ALU.add), [pres[half], r_const, r_sgc], [r_sgc])

        a_mark = ar.mark()
        hM = ar.alloc([128, 8, TT], BF16, "hM")
        r_hM = [Res(f"hM{dc}") for dc in range(8)]
        qk_ring = Ring(T, ar, 2, [128, 4, TT], BF16, "qkst")
        ybase = nc.lookup_mloc(ybuf).addr
        vst = Slot(T, nc.alloc_sbuf_tensor_at("vst_alias", [128, 4, D], BF16, offset=ybase), "vst")
        vn = nc.alloc_sbuf_tensor_at("vn_alias", [128, 4, D], BF16, offset=ybase + 4 * D * 2)
        r_vn = [r_y[4 + b] for b in range(4)]
        vs_ring = [ar.alloc([128, 512], F32, "vs") for _ in range(4)]
        r_vs = [Res(f"vs{i}") for i in range(4)]
        stats = ar.alloc([128, 16, 6], F32, "stats")
        mv = ar.alloc([128, 16, 2], F32, "mv")
        lrs = ar.alloc([128, 16], F32, "lrs")
        r_st = Res("stats")
        u_ring = [ar.alloc([128, TT], F32, "u") for _ in range(2)]
        r_u = [Res("u0"), Res("u1")]
        gb_ring = [ar.alloc([128, TT], F32, "gb") for _ in range(2)]
        r_gb = [Res("gb0"), Res("gb1")]
        ga_ring = Ring(T, ar, 2, [128, TT], F32, "gast")
        mb_ring = Ring(T, ar, 3, [128, TT], F32, "mbst")
        lf_z = ar.alloc([128, 4, H], F32, "lfz")
        lf_st = Slot(T, ar.alloc([128, 4, H], F32, "lfst"), "lfst")
        r_lfz = Res("lfz")

        r_Q = [[Res(f"Q{i}_{hf}") for hf in range(2)] for i in range(NT)]
        r_KG = [[[Res(f"KG{sl}_{i}_{hf}") for hf in range(2)] for i in range(NT)] for sl in range(2)]
        r_VG = [[Res(f"VG{sl}_{i}") for i in range(NT)] for sl in range(2)]
        r_LG = [[Res(f"LG{sl}_{i}") for i in range(NT)] for sl in range(2)]
        r_GA = [[Res(f"GA{i}_{g}") for g in range(8)] for i in range(NT)]
        r_MB = [[Res(f"MB{i}_{g}") for g in range(8)] for i in range(NT)]
        r_X1 = [Res(f"X1{i}") for i in range(NT)]
        r_MG = [[Res(f"MG{i}_{g}") for g in range(8)] for i in range(NT)]
        qk_cres = [[Res(f"qkc{i}_{c}") for c in range(8)] for i in range(2)]
        v_cres = [r_y[i % 4] for i in range(8)]

        xT_v = xT.ap().rearrange("(c p) t -> p c t", p=128)
        X1_v = X1s.ap().rearrange("(c p) t -> p c t", p=128)
        Q_v = Qs.ap().rearrange("(c p) t -> p c t", p=128)
        xTp_v = xTp.ap().rearrange("(c p) t -> p c t", p=128)
        KG_v = [KG.ap()[sl * 1024:(sl + 1) * 1024, :].rearrange("(c p) t -> p c t", p=128) for sl in range(2)]
        VG_v = [VG.ap()[sl * 1024:(sl + 1) * 1024, :].rearrange("(h s) (l d) -> s l h d", s=128, d=128)
                for sl in range(2)]
        LG_v = [LG.ap()[sl * TPC:(sl + 1) * TPC, :].rearrange("(l p) h -> p l h", p=128) for sl in range(2)]
        win_v = wbf["win"].ap().rearrange("(c p) n -> p c n", p=128)
        win32_v = wfull["win"].ap().rearrange("(c p) n -> p c n", p=128)

        def w_in_chunk(ci):
            need_cast("win", ci)
            return wload(win_v[:, :, ci * 512:(ci + 1) * 512],
                         lambda t: t[:].rearrange("p (c f) -> p c f", c=8), r_w["win"][ci],
                         win32_v[:, :, ci * 512:(ci + 1) * 512])

        cact = {"i": 0}

        def evac(dst_ap, src_ap, reads, writes, func=None):
            if func is not None:
                return T.op("act", lambda e: e.activation(out=dst_ap, in_=src_ap, func=func), reads, writes)
            cact["i"] += 1
            if cact["i"] % 2:
                return T.op("act", lambda e: e.activation(out=dst_ap, in_=src_ap, func=AF.Copy), reads, writes)
            return T.op("dve", lambda e: e.tensor_copy(dst_ap, src_ap), reads, writes)

        def load_x(ti):
            xs_ = xbuf[ti % 2]
            src = xT_v if ti < NT else xTp_v
            lt_ = ti % NT
            T.dma("sp", xs_.ds, [(xs_.t[:], src[:, :, lt_ * TT:(lt_ + 1) * TT])], writes=r_xc[ti % 2])

        def w_stage(slot, lt, own):
            tsl = slice(lt * TT, (lt + 1) * TT)

            def sec_qk(which, act_only):
                dst_v, rdst = ((Q_v, r_Q[lt]), (KG_v[slot], r_KG[slot][lt]))[which]
                for half in range(2):
                    sti = qk_ring.i % 2
                    st = qk_ring.next()
                    sw, wv = w_in_chunk(which * 2 + half)
                    for ci in range(4):
                        b = next_mm()
                        mm_group(b, psum[b][:], [(wv[:, dc, ci * 128:(ci + 1) * 128], hM[:, dc, :]) for dc in range(8)],
                                 [sw.res] + r_hM)
                        evac(st.t[:, ci, :], psum[b][:], [pres[b]], [qk_cres[sti][ci]],
                             func=(AF.Copy if act_only else None))
                    T.dma("sp", st.ds, [(dst_v[:, half * 4:(half + 1) * 4, tsl], st.t[:])], reads=qk_cres[sti][0:4],
                          writes=[rdst[half]])

            def sec_v():
                for half in range(2):
                    sw, wv = w_in_chunk(4 + half)
                    for blk in range(4):
                        b = next_mm()
                        mm_group(b, psum[b][:], [(hM[:, dc, blk * 128:(blk + 1) * 128], wv[:, dc, :]) for dc in range(8)],
                                 [sw.res] + r_hM)
                        evac(vst.t[:, blk, half * 512:(half + 1) * 512], psum[b][:], [pres[b]], [v_cres[half * 4 + blk]])
                T.dma("sp", vst.ds, [(VG_v[slot][:, lt * 4 + blk, :, :], vst.t[:, blk, :].rearrange("p (h d) -> p h d", h=H))
                                     for blk in range(4)], reads=v_cres, writes=[r_VG[slot][lt]])

            def sec_f():
                b = next_mm()
                for blk in range(4):
                    mm_group(b, psum[b][:, blk * H:(blk + 1) * H],
                             [(hM[:, dc, blk * 128:(blk + 1) * 128], wf_sb[:, dc, :]) for dc in range(8)], [r_wf] + r_hM)
                T.op("dve", lambda e, b=b: e.tensor_tensor(
                    lf_z[:], psum[b][:, 0:4 * H].rearrange("p (b h) -> p b h", h=H),
                    bf_bc[:].unsqueeze(1).to_broadcast([128, 4, H]), ALU.add), [pres[b], r_sgc], [r_lfz])
                T.op("act", lambda e: e.activation(out=lf_z[:], in_=lf_z[:], func=AF.Exp, scale=-1.0), [r_lfz], [r_lfz])
                T.op("act", lambda e: e.activation(out=lf_z[:], in_=lf_z[:], func=AF.Ln, bias=1.0), [r_lfz], [r_lfz])
                T.op("dve", lambda e: e.tensor_scalar(lf_st.t[:], lf_z[:], -1.0, None, ALU.mult), [r_lfz], [lf_st.res])
                T.dma("sp", lf_st.ds, [(LG_v[slot][:, lt * 4:(lt + 1) * 4, :], lf_st.t[:])], reads=[lf_st.res],
                      writes=[r_LG[slot][lt]])

            def sec_sv_mm(half):
                sw, wv = w_in_chunk(8 + half)
                for blk in range(4):
                    b = next_mm()
                    mm_group(b, psum[b][:], [(hM[:, dc, blk * 128:(blk + 1) * 128], wv[:, dc, :]) for dc in range(8)],
                             [sw.res] + r_hM)
                    T.op("act", lambda e, b=b, blk=blk: e.activation(out=vs_ring[blk][:], in_=psum[b][:], func=AF.Gelu),
                         [pres[b]], [r_vs[blk]])
                for blk in range(4):
                    for gi in range(4):
                        T.op("dve", lambda e, blk=blk, gi=gi: e.bn_stats(
                            stats[:, blk * 4 + gi, :], vs_ring[blk][:, gi * 128:(gi + 1) * 128]), [r_vs[blk]], [r_st])
                for q_ in range(16):
                    T.op("dve", lambda e, q_=q_: e.bn_aggr(mv[:, q_, :], stats[:, q_, :]), [r_st], [r_st])

            def sec_sv_fin(half):
                T.op("act", lambda e: e.activation(out=lrs[:], in_=mv[:, :, 1], func=AF.Sqrt, bias=eps_col[:, 1:2]),
                     [r_st, r_pc], [r_st])
                T.op("dve", lambda e: e.reciprocal(lrs[:], lrs[:]), [r_st], [r_st])
                for blk in range(4):
                    for gi in range(4):
                        g = half * 4 + gi
                        q_ = blk * 4 + gi
                        T.op("dve", lambda e, gi=gi, g=g, blk=blk, q_=q_: e.tensor_scalar(
                            vn[:, blk, g * 128:(g + 1) * 128], vs_ring[blk][:, gi * 128:(gi + 1) * 128],
                            mv[:, q_, 0:1], lrs[:, q_:q_ + 1], ALU.subtract, ALU.mult), [r_vs[blk], r_st], [r_vn[blk]])

            if not own:
                sec_qk(1, False)
                sec_v()
                sec_f()
                return
            sec_sv_mm(0)
            sec_qk(0, True)
            sec_sv_fin(0)
            sec_sv_mm(1)
            sec_qk(1, True)
            sec_sv_fin(1)
            sec_v()
            sec_f()
            GA_v = GAs.ap().rearrange("(c p) t -> p c t", p=128)
            MB_v = MBs.ap().rearrange("(c p) t -> p c t", p=128)
            for half in range(2):
                su_, wu_ = w_in_chunk(6 + half)
                sb_, wb_ = w_in_chunk(12 + half)
                sa_, wa_ = w_in_chunk(10 + half)
                for gi in range(4):
                    g = half * 4 + gi
                    k = g % 2
                    csl = slice(gi * 128, (gi + 1) * 128)
                    b = next_mm()
                    mm_group(b, psum[b][:], [(wu_[:, dc, csl], hM[:, dc, :]) for dc in range(8)], [su_.res] + r_hM)
                    T.op("act", lambda e, b=b, k=k: e.activation(out=u_ring[k][:], in_=psum[b][:], func=AF.Gelu),
                         [pres[b]], [r_u[k]])
                    b = next_mm()
                    mm_group(b, psum[b][:], [(wb_[:, dc, csl], hM[:, dc, :]) for dc in range(8)], [sb_.res] + r_hM)
                    T.op("act", lambda e, b=b, k=k: e.activation(out=gb_ring[k][:], in_=psum[b][:], func=AF.Sigmoid),
                         [pres[b]], [r_gb[k]])
                    b = next_mm()
                    mm_group(b, psum[b][:], [(wa_[:, dc, csl], hM[:, dc, :]) for dc in range(8)], [sa_.res] + r_hM)
                    gs = ga_ring.next()
                    T.op("act", lambda e, b=b, gs=gs: e.activation(out=gs.t[:], in_=psum[b][:], func=AF.Sigmoid),
                         [pres[b]], [gs.res])
                    T.dma("sp", gs.ds, [(GA_v[:, g, tsl], gs.t[:])], reads=[gs.res], writes=[r_GA[lt][g]])
                    b = next_mm()
                    for blk in range(4):
                        T.op("pe", lambda e, b=b, blk=blk, g=g: e.matmul(
                            psum[b][:, blk * 128:(blk + 1) * 128], vn[:, blk, g * 128:(g + 1) * 128], wsT_b[:, g, :],
                            start=True, stop=True), [r_vn[blk], r_sgc], [pres[b]], signal=(blk == 3))
                    ms = mb_ring.next()
                    T.op("dve", lambda e, b=b, ms=ms, g=g: e.scalar_tensor_tensor(
                        out=ms.t[:].rearrange("p (b t) -> p b t", b=4), in0=psum[b][:].rearrange("p (b t) -> p b t", b=4),
                        scalar=lncols[:, 0, g:g + 1], in1=e_g[:, g, :].unsqueeze(1).to_broadcast([128, 4, 128]),
                        op0=ALU.mult, op1=ALU.add), [pres[b], r_const, r_sgc], [ms.res])
                    T.op("dve", lambda e, ms=ms, k=k: e.tensor_tensor(ms.t[:], ms.t[:], u_ring[k][:], ALU.mult),
                         [r_u[k], ms.res], [ms.res])
                    T.op("dve", lambda e, ms=ms, k=k: e.tensor_tensor(ms.t[:], ms.t[:], gb_ring[k][:], ALU.mult),
                         [r_gb[k], ms.res], [ms.res])
                    T.dma("sp", ms.ds, [(MB_v[:, g, tsl], ms.t[:])], reads=[ms.res], writes=[r_MB[lt][g]])

        NTA = 2 * NT
        FFN1 = ("wg1", "wu1")
        wdirect["on"] = True
        load_x(0)
        load_x(1)
        prenorm_now(xbuf[0].t, r_xc[0], 0, hA, r_hA, PS_SS2, rstdP, r_rstdP)
        gateup(FFN1, range(32), hA, r_hA, [])
        for ti in range(NTA):
            slot, lt = ti // NT, ti % NT
            own = slot == 0
            xs_ = xbuf[ti % 2]
            x_t, rx = xs_.t, r_xc[ti % 2]
            tsl = slice(lt * TT, (lt + 1) * TT)
            if ti + 1 < NTA:
                side = p_side(xbuf[(ti + 1) % 2].t, r_xc[(ti + 1) % 2], 0, hA, r_hA)
            else:
                side = [[] for _ in range(8)]
            down_stage("wd1", 1, side)

            def store_fn(xs_=xs_, x_t=x_t, rx=rx, lt=lt, tsl=tsl, own=own):
                if own:
                    T.dma("sp", xs_.ds, [(X1_v[:, :, tsl], x_t[:])], reads=rx, writes=[r_X1[lt]])

            def final_fn(ti=ti):
                if ti + 2 < NTA:
                    load_x(ti + 2)
            chain = chain_tasks(x_t, rx, 2, hM, r_hM, PS_SS, sq, r_sq, "sq", ybuf, r_y, rstdD, r_rstdD,
                                store_fn, final_fn)
            if ti + 1 < NTA:
                gateup(FFN1, range(0, 12), hA, r_hA, chain)
            flush(chain)
            w_stage(slot, lt, own)
            wdirect["on"] = False
            if ti + 1 < NTA:
                gateup(FFN1, range(12, 32), hA, r_hA, [])
        T.barrier()
        if STOP == 1:
            raise _StopBuild

        ar.reset(cmark)
        lg = ar.alloc([128, NIDX, H], F32, "lg")
        pre = ar.alloc([128, NIDX, H], F32, "pre")
        preq = [ar.alloc([128, NIDX, H], F32, "preq") for _ in range(2)]
        negc = ar.alloc([128, H, NIDX], F32, "negc")
        offcol = ar.alloc([128, 2], F32, "offcol")
        r_c = Res("cums")
        ds_lg = T.dsem("lg")
        T.dma("sp", ds_lg, [(lg[:, sl * NLB:(sl + 1) * NLB, :], LG_v[sl]) for sl in range(2)],
              reads=r_LG[0] + r_LG[1], writes=[r_c])

        def gidx(q, gblk):
            G, j = gblk // 4, gblk % 4
            if G in GT[q]:
                return GT[q].index(G) * 4 + j
            return NLB + GT[1 - q].index(G) * 4 + j

        lg2 = lg[:].rearrange("p i h -> p (i h)")
        NLG = NIDX * H
        T.op("pe", lambda e: e.matmul(psum[0][:, 0:NLG], tri_f[:], lg2, start=True, stop=True), [r_c, r_pc], [pres[0]])
        T.op("pe", lambda e: e.matmul(psum[1][:, 0:NLG], ones_f[:], lg2, start=True, stop=True), [r_c, r_pc], [pres[1]])
        tot = ar.alloc([128, NIDX, H], F32, "tot")
        T.op("dve", lambda e: e.tensor_copy(tot[:].rearrange("p i h -> p (i h)"), psum[1][:, 0:NLG]), [pres[1]], [r_c])
        for q in range(2):
            T.op("dve", lambda e, q=q: e.memset(preq[q][:, gidx(q, 0), :], 0.0), [r_c], [r_c])
            for gb_ in range(1, NIDX):
                a, bq = gidx(q, gb_), gidx(q, gb_ - 1)
                T.op("dve", lambda e, a=a, bq=bq, q=q: e.tensor_tensor(preq[q][:, a, :], preq[q][:, bq, :], tot[:, bq, :], ALU.add),
                     [r_c], [r_c])
        T.op("dve", lambda e: e.tensor_scalar(pre[:], preq[0][:], flags[:, 1:2], None, ALU.mult), [r_c, r_const], [r_c])
        T.op("dve", lambda e: e.scalar_tensor_tensor(out=pre[:], in0=preq[1][:], scalar=flags[:, 0:1], in1=pre[:],
                                                     op0=ALU.mult, op1=ALU.add), [r_c, r_const], [r_c])
        T.op("dve", lambda e: e.scalar_tensor_tensor(
            out=negc[:].rearrange("p h i -> p i h"), in0=psum[0][:, 0:NLG].rearrange("p (i h) -> p i h", h=H), scalar=-1.0,
            in1=pre[:], op0=ALU.mult, op1=ALU.subtract), [pres[0], r_c], [r_c])
        T.op("dve", lambda e: e.tensor_scalar(offcol[:, 0:1], flags[:, 1:2], -30000.0, None, ALU.mult), [r_const], [r_c])
        T.op("dve", lambda e: e.tensor_scalar(offcol[:, 1:2], flags[:, 0:1], -30000.0, None, ALU.mult), [r_const], [r_c])

        kbuf = [Slot(T, ar.alloc([128, 2 * TPC], BF16, "kb"), f"kb{i}") for i in range(2)]
        vbuf = [Slot(T, ar.alloc([128, NIDX, 128], BF16, "vb"), f"vb{i}") for i in range(2)]
        q_ring = Ring(T, ar, 3, [128, TT], BF16, "q")
        p_ring = [ar.alloc([128, TT], BF16, "p") for _ in range(4)]
        r_p = [Res(f"p{i}") for i in range(4)]
        bias_ring = [ar.alloc([128, NIDX], F32, "bias") for _ in range(3)]
        r_bias = [Res("bias0"), Res("bias1"), Res("bias2")]
        gat = Ring(T, ar, 3, [128, TT], F32, "gat")
        mbt = Ring(T, ar, 3, [128, TT], F32, "mbt")
        rden = ar.alloc([128, TT], F32, "rden")
        r_rden = Res("rden")
        otmp = ar.alloc([128, TT], F32, "otmp")
        r_otmp = Res("otmp")
        mg_ring = Ring(T, ar, 2, [128, TT], BF16, "mgst")

        PS_S = [0, 1, 2, 7]
        PS_O = [3, 4]
        PS_D = [5, 6]
        GA_v = GAs.ap().rearrange("(c p) t -> p c t", p=128)
        MB_v = MBs.ap().rearrange("(c p) t -> p c t", p=128)
        MG_v = MGs.ap().rearrange("(c p) t -> p c t", p=128)
        scale = 1.0 / float(np.sqrt(HD))

        units = []
        grp = 0
        for h in range(H):
            for lt in range(NT):
                first = True
                for sl in (1, 0):
                    for klt in range(lt + 1):
                        for j in range(4):
                            units.append(dict(h=h, lt=lt, grp=grp, first=first, last=False, rk=sl, klt=klt, j=j,
                                              diag=(sl == 0 and klt == lt)))
                            first = False
                units[-1]["last"] = True
                grp += 1
        state = {}

        def load_head(h):
            ks, vsb = kbuf[h % 2], vbuf[h % 2]
            T.dma("sp", ks.ds, [(ks.t[:, rk * TPC:(rk + 1) * TPC], KG.ap()[rk * 1024 + h * 128: rk * 1024 + (h + 1) * 128, :])
                                for rk in range(2)], reads=[r for sl in range(2) for t_ in r_KG[sl] for r in t_], writes=[ks.res])
            T.dma("sp", vsb.ds, [(vsb.t[:, rk * NLB:(rk + 1) * NLB, :],
                                  VG.ap()[rk * 1024 + h * 128: rk * 1024 + (h + 1) * 128, :].rearrange("s (l d) -> s l d", d=128))
                                 for rk in range(2)], reads=r_VG[0] + r_VG[1], writes=[vsb.res])

        def group_prologue(g):
            if g in state or g >= H * NT:
                return
            h, lt = divmod(g, NT)
            qs = q_ring.next()
            T.dma("sp", qs.ds, [(qs.t[:], Q_v[:, h, lt * TT:(lt + 1) * TT])], reads=[r_Q[lt][h // 4]], writes=[qs.res])
            k = g % 3
            T.op("dve", lambda e, k=k, h=h, lt=lt: e.tensor_scalar(
                bias_ring[k][:], negc[:, h, :], pre[:, 4 * lt + 2, h:h + 1], BCLAMP, ALU.add, ALU.min), [r_c], [r_bias[k]])
            fsl = slice(NLB + 4 * lt, NLB + 4 * lt + 4)
            T.op("dve", lambda e, k=k, lt=lt, fsl=fsl: e.tensor_scalar(
                bias_ring[k][:, fsl], bias_ring[k][:, fsl], offcol[:, lt % 2:lt % 2 + 1], None, ALU.add),
                [r_c, r_bias[k]], [r_bias[k]])
            gs, ms = gat.next(), mbt.next()
            T.dma("sp", gs.ds, [(gs.t[:], GA_v[:, h, lt * TT:(lt + 1) * TT])], reads=[r_GA[lt][h]], writes=[gs.res])
            T.dma("sp", ms.ds, [(ms.t[:], MB_v[:, h, lt * TT:(lt + 1) * TT])], reads=[r_MB[lt][h]], writes=[ms.res])
            state[g] = dict(q=qs, ga=gs, mb=ms)

        def emit_S(i):
            u = units[i]
            if u["first"]:
                h_, lt_ = u["h"], u["lt"]
                if lt_ == 0 and h_ == 0:
                    load_head(0)
                if lt_ == min(1, NT - 1) and h_ + 1 < H:
                    load_head(h_ + 1)
                group_prologue(u["grp"])
                group_prologue(u["grp"] + 1)
            st = state[u["grp"]]
            ks = kbuf[u["h"] % 2]
            b = PS_S[i % 4]
            col = u["rk"] * TPC + u["klt"] * TT + u["j"] * 128
            T.op("pe", lambda e, b=b, ks=ks, col=col, q=st["q"]: e.matmul(
                psum[b][:], ks.t[:, col:col + 128], q.t[:], start=True, stop=True),
                [ks.res, st["q"].res], [pres[b]])

        def emit_rest(i):
            u = units[i]
            g, h, lt = u["grp"], u["h"], u["lt"]
            st = state[g]
            b = PS_S[i % 4]
            k3 = i % 4
            idx = u["rk"] * NLB + u["klt"] * 4 + u["j"]
            kb = g % 3
            T.op("act", lambda e, b=b, k3=k3, kb=kb, idx=idx: e.activation(
                out=p_ring[k3][:], in_=psum[b][:], func=AF.Exp, bias=bias_ring[kb][:, idx:idx + 1], scale=scale),
                [pres[b], r_bias[kb]], [r_p[k3]])
            if u["diag"]:
                T.op("dve", lambda e, k3=k3, j=u["j"]: e.tensor_tensor(p_ring[k3][:], p_ring[k3][:], diag[:, j, :], ALU.mult),
                     [r_pc, r_p[k3]], [r_p[k3]])
            vsb = vbuf[h % 2]
            bo, bd = PS_O[g % 2], PS_D[g % 2]
            T.op("pe", lambda e, bo=bo, vsb=vsb, idx=idx, k3=k3, f=u["first"], l=u["last"]: e.matmul(
                psum[bo][:], vsb.t[:, idx, :], p_ring[k3][:], start=f, stop=l), [vsb.res, r_p[k3]], [pres[bo]], signal=False)
            T.op("pe", lambda e, bd=bd, k3=k3, f=u["first"], l=u["last"]: e.matmul(
                psum[bd][:], ones_bf[:], p_ring[k3][:], start=f, stop=l), [r_p[k3], r_pc], [pres[bd]], signal=True)
            if u["last"]:
                T.op("dve", lambda e, bd=bd: e.reciprocal(rden[:], psum[bd][:]), [pres[bd]], [r_rden])
                T.op("dve", lambda e, bo=bo: e.tensor_tensor(otmp[:], psum[bo][:], rden[:], ALU.mult),
                     [pres[bo], r_rden], [r_otmp])
                T.op("dve", lambda e, gs=st["ga"]: e.tensor_tensor(otmp[:], otmp[:], gs.t[:], ALU.mult),
                     [r_otmp, st["ga"].res], [r_otmp])
                mg = mg_ring.next()
                T.op("dve", lambda e, mg=mg, ms=st["mb"]: e.tensor_tensor(mg.t[:], otmp[:], ms.t[:], ALU.add),
                     [r_otmp, st["mb"].res], [mg.res])
                T.dma("sp", mg.ds, [(MG_v[:, h, lt * TT:(lt + 1) * TT], mg.t[:])], reads=[mg.res], writes=[r_MG[lt][h]])
                del state[g]

        NU = len(units)
        LOOK = 3
        for i in range(min(LOOK, NU)):
            emit_S(i)
        for i in range(NU):
            if i + LOOK < NU:
                emit_S(i + LOOK)
            emit_rest(i)
        T.barrier()
        if STOP == 2:
            raise _StopBuild

        ar.reset(ffn_mark)
        wout_sb = ar.alloc([128, 8, D], BF16, "wout")
        r_wo = Res("wout")
        ds_wo = T.dsem("wout")
        need_cast("wout", 1)
        T.dma("pool", ds_wo, [(wout_sb[:], wbf["wout"].ap().rearrange("(c p) n -> p c n", p=128))],
              reads=r_w["wout"], writes=[r_wo])
        mgin = Ring(T, ar, 2, [128, 8, TT], BF16, "mgin")
        y2 = ar.alloc([128, 8, TT], F32, "y2")
        r_y2 = [Res(f"y2_{dc}") for dc in range(8)]
        outT_v = outT.ap().rearrange("(c p) t -> p c t", p=128)
        r_out = Res("out")
        FFN2 = ("wg2", "wu2")

        def load_b2(lt):
            xsl = xbuf[lt % 2]
            tsl_ = slice(lt * TT, (lt + 1) * TT)
            T.dma("sp", xsl.ds, [(xsl.t[:], X1_v[:, :, tsl_])], reads=[r_X1[lt]], writes=r_xc[lt % 2])
            ms_ = mgin.next()
            T.dma("sp", ms_.ds, [(ms_.t[:], MG_v[:, :, tsl_])], reads=r_MG[lt], writes=[ms_.res])
            return ms_

        def wout_stage(ms_):
            bank_of = {}
            act, pe = stat_ops(lambda dc: psum[bank_of[dc]][:], lambda dc: [pres[bank_of[dc]]], PS_SS2, sqP, r_sqP, "sqP")
            for dc in range(8):
                b = next_mm()
                bank_of[dc] = b
                mm_group(b, psum[b][:], [(wout_sb[:, c, dc * 128:(dc + 1) * 128], ms_.t[:, c, :]) for c in range(8)],
                         [r_wo, ms_.res])
                if dc > 0:
                    pe(dc - 1)
                T.op("act", lambda e, dc=dc, b=b: e.activation(out=y2[:, dc, :], in_=psum[b][:], func=AF.Copy,
                                                               scale=gcols[:, 3, dc:dc + 1]),
                     [pres[b], r_const], [r_y2[dc]])
                act(dc)
            pe(7)

        def b2_chain(lt):
            x_t, rx = xbuf[lt % 2].t, r_xc[lt % 2]
            tasks = [lambda: finish_rstd(PS_SS2, rstdP, r_rstdP)]
            tasks += chain_tasks(x_t, rx, 4, hA, r_hA, PS_SS2, sqP, r_sqP, "sqP", y2, r_y2, rstdP, r_rstdP)
            return tasks

        mss = {0: load_b2(0)}
        if NT > 1:
            mss[1] = load_b2(1)
        wout_stage(mss[0])
        flush(b2_chain(0))
        gateup(FFN2, range(32), hA, r_hA, [])
        for lt in range(NT):
            xsl = xbuf[lt % 2]
            x_t, rx = xsl.t, r_xc[lt % 2]
            tsl = slice(lt * TT, (lt + 1) * TT)
            side = [[] for _ in range(8)]
            if lt + 1 < NT:
                wout_stage(mss[lt + 1])
                ch = b2_chain(lt + 1)
                per = [2, 2, 2, 2, 2, 1, 1, 1]
                for dc in range(8):
                    for _ in range(per[dc]):
                        if ch:
                            side[dc].append(ch.pop(0))
                assert not ch
            down_stage("wd2", 5, side)

            def out_store(xsl=xsl, x_t=x_t, rx=rx, tsl=tsl, lt=lt):
                T.dma("sp", xsl.ds, [(outT_v[:, :, tsl], x_t[:])], reads=rx, writes=[r_out])
                if lt + 2 < NT:
                    mss[lt + 2] = load_b2(lt + 2)
            otasks = resid_tasks(ybuf, r_y, rstdD, r_rstdD, x_t, rx)
            otasks.append(out_store)
            if lt + 1 < NT:
                gateup(FFN2, range(32), hA, r_hA, otasks)
            flush(otasks)
        T.barrier()


    try:
        body()
    except _StopBuild:
        T.barrier()

    from contextlib import ExitStack
    with ExitStack() as es:
        for E in T.eng.values():
            E.sem = es.enter_context(nc.semaphore("s_" + E.name))
        for d in T.dsems:
            d.sem = es.enter_context(nc.semaphore("d_" + d.name))
        block = es.enter_context(nc.Block())

        @block.tensor
        def _(e):
            T.emit(nc, "pe", e)

        @block.scalar
        def _(e):
            T.emit(nc, "act", e)

        @block.vector
        def _(e):
            T.emit(nc, "dve", e)

        @block.gpsimd
        def _(e):
            T.emit(nc, "pool", e)

        @block.sync
        def _(e):
            T.emit(nc, "sp", e)
    return nc


def _prep_inputs(inp):
    f = lambda a: np.ascontiguousarray(np.asarray(a, dtype=np.float32))
    x = f(inp["x"])
    w = {"wg1": f(inp["ffn1_w_gate"])[0], "wu1": f(inp["ffn1_w_up"])[0], "wd1": f(inp["ffn1_w_down"])[0],
         "wout": f(inp["w_out"])[0], "wg2": f(inp["ffn2_w_gate"])[0], "wu2": f(inp["ffn2_w_up"])[0],
         "wd2": f(inp["ffn2_w_down"])[0]}
    w_in = f(inp["w_in"])[0]
    w["win"] = np.ascontiguousarray(np.concatenate([w_in[:, :3072], w_in[:, 3080:]], axis=1))
    wf = np.ascontiguousarray(w_in[:, 3072:3080])
    gains = [inp["ffn1_pre_g"], inp["ffn1_post_g"], inp["mix_pre_g"], inp["mix_post_g"], inp["ffn2_pre_g"],
             inp["ffn2_post_g"]]
    gcols = np.stack([f(g)[0].reshape(8, 128).T for g in gains], axis=1).reshape(128, 48)
    lncols = np.stack([f(inp["sgu_ln_g"])[0].reshape(8, 128).T, f(inp["sgu_ln_b"])[0].reshape(8, 128).T],
                      axis=1).reshape(128, 16)
    wsT = np.ascontiguousarray(f(inp["sgu_w_s"])[0].transpose(2, 0, 1)).reshape(128, 1024)
    bs = f(inp["sgu_b_s"])[0].reshape(1, 1024)
    bfg = f(inp["b_forget"]).reshape(1, 8)
    maps = []
    for c in range(NCORE):
        b, p = c // 2, c % 2
        toks = np.concatenate([np.arange(G * TT, (G + 1) * TT) for G in GT[p]])
        tokp = np.concatenate([np.arange(G * TT, (G + 1) * TT) for G in GT[1 - p]])
        m = {"xT": np.ascontiguousarray(x[b][toks].T), "xTp": np.ascontiguousarray(x[b][tokp].T)}
        for (n, r, cc) in WEIGHTS:
            m[n + "_f"] = w[n]
        m["wf"] = wf
        m["gcols"] = np.ascontiguousarray(gcols)
        m["lncols"] = np.ascontiguousarray(lncols)
        m["wsT"] = wsT
        m["bs"] = bs
        m["bfg"] = bfg
        fl = np.zeros((128, 4), np.float32)
        fl[:, 0] = p
        fl[:, 1] = 1 - p
        m["flags"] = fl
        maps.append(m)
    return maps


_NC_CACHE = {}


def kernel(**inputs):
    maps = _prep_inputs(inputs)
    if "nc" not in _NC_CACHE:
        _NC_CACHE["nc"] = build()
    nc = _NC_CACHE["nc"]
    res = run_bass_kernel_spmd(nc, maps, core_ids=list(range(NCORE)))
    out = np.empty((NB, SEQ, D), np.float32)
    for c in range(NCORE):
        b, p = c // 2, c % 2
        o = np.asarray(res.results[c]["outT"]).T
        for i, G in enumerate(GT[p]):
            out[b, G * TT:(G + 1) * TT] = o[i * TT:(i + 1) * TT]
    return out
```

```python
import numpy as np
import concourse.bass as bass
import concourse.mybir as mybir
from concourse.bass_utils import run_bass_kernel_spmd

F32 = mybir.dt.float32
BF16 = mybir.dt.bfloat16
AF = mybir.ActivationFunctionType
ALU = mybir.AluOpType

D = 1024
DFF = 4096
SEQ = 8192
NB = 4
NCORE = 8
TT = 512
NT = 8
TPC = NT * TT
NLB = NT * 4
NIDX = 2 * NLB
H = 8
HD = 128
WIN = 7168
GT = [[0, 3, 4, 7, 8, 11, 12, 15], [1, 2, 5, 6, 9, 10, 13, 14]]


def _configure(nt, ncore):
    global NT, TPC, NLB, NIDX, NCORE, GT, SEQ, NB
    NT, NCORE = nt, ncore
    TPC, NLB, NIDX = NT * TT, NT * 4, 2 * NT * 4
    SEQ = 2 * TPC
    NB = ncore // 2
    g0 = [g for g in range(2 * NT) if g % 4 in (0, 3)]
    g1 = [g for g in range(2 * NT) if g % 4 in (1, 2)]
    GT = [g0, g1]
RMS_EPS = 1e-6
LN_EPS = 1e-5
BCLAMP = 64.0
NSLOT = 5
SAME_ENGINE_SYNC = True
STOP = 99


class _StopBuild(Exception):
    pass

WEIGHTS = [("wg1", D, DFF), ("wu1", D, DFF), ("wd1", DFF, D), ("win", D, WIN),
           ("wout", D, D), ("wg2", D, DFF), ("wu2", D, DFF), ("wd2", DFF, D)]


class Res:
    __slots__ = ("name", "w", "r")

    def __init__(self, name):
        self.name = name
        self.w = None
        self.r = {}


class Eng:
    def __init__(self, name):
        self.name = name
        self.sem = None
        self.cnt = 0
        self.waited = {}
        self.prog = []


class DSem:
    def __init__(self, name, step=16):
        self.name = name
        self.sem = None
        self.cnt = 0
        self.step = step
        self.last = None


class Tracker:
    def __init__(self):
        self.eng = {n: Eng(n) for n in ("pe", "act", "dve", "pool", "sp")}
        self.dsems = []
        self.nwait = 0

    def dsem(self, name, step=16):
        d = DSem(name, step)
        self.dsems.append(d)
        return d

    def _wait(self, E, deps):
        for tok in deps:
            if tok is None:
                continue
            key, val, en = tok
            if en == E.name:
                if E.name == "pe" or not SAME_ENGINE_SYNC:
                    continue
            if E.waited.get(id(key), 0) >= val:
                continue
            E.waited[id(key)] = val
            self.nwait += 1
            E.prog.append(("wait", key, val))

    def _deps(self, reads, writes):
        deps = []
        for r in reads:
            if r.w is not None:
                deps.append(r.w)
        for w in writes:
            if w.w is not None:
                deps.append(w.w)
            deps.extend(w.r.values())
        return deps

    def _reg(self, tok, reads, writes):
        for r in reads:
            r.r[id(tok[0])] = tok
        for w in writes:
            w.w = tok
            w.r = {}

    def op(self, en, fn, reads=(), writes=(), signal=True):
        E = self.eng[en]
        self._wait(E, self._deps(reads, writes))
        if signal:
            E.cnt += 1
            tok = (E, E.cnt, en)
        else:
            tok = (E, E.cnt + 1, en)
        E.prog.append(("op", fn, signal))
        self._reg(tok, reads, writes)
        return tok

    def dma(self, q, ds, pairs, reads=(), writes=()):
        E = self.eng[q]
        deps = self._deps(reads, writes)
        deps.append(ds.last)
        self._wait(E, deps)
        for (o, i) in pairs:
            E.prog.append(("dma", o, i, ds))
        ds.cnt += ds.step * len(pairs)
        tok = (ds, ds.cnt, None)
        ds.last = tok
        self._reg(tok, reads, writes)
        return tok

    def cc(self, ds, kind, groups, in_ap, out_ap, reads=(), writes=()):
        E = self.eng["pool"]
        deps = self._deps(reads, writes)
        deps.append(ds.last)
        self._wait(E, deps)
        E.prog.append(("cc", kind, groups, in_ap, out_ap, ds))
        ds.cnt += ds.step
        tok = (ds, ds.cnt, None)
        ds.last = tok
        self._reg(tok, reads, writes)
        return tok

    def barrier(self):
        toks = [(E, E.cnt, E.name) for E in self.eng.values() if E.cnt > 0]
        toks += [d.last for d in self.dsems if d.last is not None]
        for E in self.eng.values():
            self._wait(E, [t for t in toks if t[2] != E.name])

    def emit(self, nc, en, h):
        E = self.eng[en]
        for item in E.prog:
            k = item[0]
            if k == "wait":
                h.wait_ge(item[1].sem, item[2])
            elif k == "op":
                ins = item[1](h)
                if item[2]:
                    ins.then_inc(E.sem, 1)
            elif k == "dma":
                h.dma_start(out=item[1], in_=item[2]).then_inc(item[3].sem, 16)
            elif k == "cc":
                h.collective_compute(item[1], ALU.bypass, replica_groups=item[2],
                                     ins=[item[3]], outs=[item[4]]).then_inc(item[5].sem, item[5].step)


class Arena:
    def __init__(self, nc, base=16512, limit=229376):
        self.nc = nc
        self.base = base
        self.off = base
        self.limit = limit
        self.n = 0

    def alloc(self, shape, dt, name=None):
        esz = 4 if dt == F32 else 2
        nbytes = esz * int(np.prod(shape[1:]))
        nbytes = (nbytes + 31) // 32 * 32
        assert self.off + nbytes <= self.limit, f"SBUF overflow {self.off}+{nbytes}"
        self.n += 1
        t = self.nc.alloc_sbuf_tensor_at(f"{name or 't'}_{self.n}", list(shape), dt, offset=self.off)
        self.off += nbytes
        return t

    def mark(self):
        return self.off

    def reset(self, m):
        self.off = m


class Slot:
    def __init__(self, T, t, name):
        self.t = t
        self.res = Res(name)
        self.ds = T.dsem(name)


class Ring:
    def __init__(self, T, arena, n, shape, dt, name):
        self.slots = [Slot(T, arena.alloc(shape, dt, name), f"{name}{i}") for i in range(n)]
        self.i = 0

    def next(self):
        s = self.slots[self.i % len(self.slots)]
        self.i += 1
        return s


def build():
    nc = bass.Bass("TRN2", target_bir_lowering=False)
    T = Tracker()

    def body():
        def din(name, shape, dt=F32):
            return nc.dram_tensor(name, list(shape), dt, kind="ExternalInput")

        xT = din("xT", [D, TPC])
        xTp = din("xTp", [D, TPC])
        wfull = {n: din(n + "_f", [r, c]) for (n, r, c) in WEIGHTS}
        wf_in = din("wf", [D, H])
        gcols_in = din("gcols", [128, 48])
        lncols_in = din("lncols", [128, 16])
        wsT_in = din("wsT", [128, 8 * 128])
        bs_in = din("bs", [1, 8 * 128])
        bf_in = din("bfg", [1, H])
        flags_in = din("flags", [128, 4])
        outT = nc.dram_tensor("outT", [D, TPC], F32, kind="ExternalOutput")

        wbf = {n: nc.dram_tensor(n + "_bf", [r, c], BF16) for (n, r, c) in WEIGHTS}
        Qs = nc.dram_tensor("Qs", [H * 128, TPC], BF16)
        KG = nc.dram_tensor("KG", [2 * H * 128, TPC], BF16)
        VG = nc.dram_tensor("VG", [2 * H * 128, NLB * 128], BF16)
        LG = nc.dram_tensor("LG", [2 * TPC, H], F32)
        GAs = nc.dram_tensor("GAs", [D, TPC], F32)
        MBs = nc.dram_tensor("MBs", [D, TPC], F32)
        X1s = nc.dram_tensor("X1s", [D, TPC], F32)
        MGs = nc.dram_tensor("MGs", [D, TPC], BF16)

        ar = Arena(nc)
        psum = [nc.alloc_psum_tensor(f"ps{i}", [128, 512], F32) for i in range(8)]
        pres = [Res(f"ps{i}") for i in range(8)]

        gcols = ar.alloc([128, 6, 8], F32, "gcols")
        lncols = ar.alloc([128, 2, 8], F32, "lncols")
        flags = ar.alloc([128, 4], F32, "flags")
        ones_bf = ar.alloc([128, 128], BF16, "ones_bf")
        invd_bf = ar.alloc([128, 128], BF16, "invd_bf")
        ones_f = ar.alloc([128, 128], F32, "ones_f")
        tri_f = ar.alloc([128, 128], F32, "tri_f")
        diag = ar.alloc([128, 4, 512], BF16, "diag")
        eps_col = ar.alloc([128, 2], F32, "eps")
        r_const = Res("const")
        ds_const = T.dsem("const")
        cmark = ar.mark()

        def vop(en, fname, reads, writes, *args, **kw):
            return T.op(en, lambda e: getattr(e, fname)(*args, **kw), reads, writes)

        T.dma("sp", ds_const, [
            (gcols[:].rearrange("p a b -> p (a b)"), gcols_in.ap()),
            (lncols[:].rearrange("p a b -> p (a b)"), lncols_in.ap()),
            (flags[:], flags_in.ap()),
        ], writes=[r_const])
        if STOP == -3:
            raise _StopBuild
        r_pc = Res("poolconst")
        vop("pool", "memset", [], [r_pc], ones_bf[:], 1.0)
        vop("pool", "memset", [], [r_pc], invd_bf[:], 1.0 / D)
        vop("pool", "memset", [], [r_pc], ones_f[:], 1.0)
        vop("pool", "memset", [], [r_pc], eps_col[:, 0:1], RMS_EPS)
        vop("pool", "memset", [], [r_pc], eps_col[:, 1:2], LN_EPS)
        T.op("pool", lambda e: e.affine_select(tri_f[:], ones_f[:], [[1, 128]], ALU.is_ge, 0.0,
                                               base=0, channel_multiplier=-1), [r_pc], [r_pc])
        vop("pool", "memset", [], [r_pc], diag[:].rearrange("p a b -> p (a b)"), 1.0)
        for j in range(4):
            T.op("pool", lambda e, j=j: e.affine_select(diag[:, j, :], diag[:, j, :], [[1, 512]], ALU.is_ge, 0.0,
                                                        base=-128 * j, channel_multiplier=-1), [r_pc], [r_pc])
        if STOP == -2:
            raise _StopBuild
        for n in (1, 5):
            vop("dve", "tensor_scalar", [r_const], [r_const], gcols[:, n, :], gcols[:, n, :], 0.5, None, ALU.mult)

        if STOP == -1:
            raise _StopBuild
        ds_wc = [T.dsem(f"wcast{i}") for i in range(4)]
        wdirect = {"on": False}
        r_w = {n: [Res(f"{n}_bf{b}") for b in range(c // (128 if n in ("wd1", "wd2") else 512))]
               for (n, r, c) in WEIGHTS}
        ds_wb = [T.dsem(f"wback{i}") for i in range(4)]
        wb_state = {"i": 0}
        cast_order = [("wout", 0), ("wout", 1)]
        for f4 in range(8):
            cast_order += [("wg2", f4), ("wu2", f4)]
        cast_order += [("wd2", 0), ("wd2", 1)]
        cast_state = {"pos": 0}

        def cast_upto(pos):
            pos = min(pos, len(cast_order))
            while cast_state["pos"] < pos:
                n, b = cast_order[cast_state["pos"]]
                ds = ds_wc[cast_state["pos"] % 4]
                cast_state["pos"] += 1
                wr = r_w[n][4 * b:4 * b + 4] if n in ("wd1", "wd2") else [r_w[n][b]]
                T.dma("pool", ds, [(wbf[n].ap()[:, b * 512:(b + 1) * 512], wfull[n].ap()[:, b * 512:(b + 1) * 512])],
                      writes=wr)

        def need_cast(n, b):
            if (n, b) in cast_order:
                cast_upto(cast_order.index((n, b)) + 4)
            if not wdirect["on"]:
                cast_upto(cast_state["pos"] + 1)

        if STOP == 0:
            raise _StopBuild
        wring = Ring(T, ar, NSLOT, [128, 4096], BF16, "w")

        def wload(src_ap, view, rw, src32=None):
            s = wring.next()
            dst = view(s.t)
            if wdirect["on"] and src32 is not None:
                T.dma("pool", s.ds, [(dst, src32)], writes=[s.res])
                if rw.w is None:
                    ds = ds_wb[wb_state["i"] % 4]
                    wb_state["i"] += 1
                    T.dma("sp", ds, [(src_ap, dst)], reads=[s.res], writes=[rw])
            else:
                T.dma("pool", s.ds, [(dst, src_ap)], reads=[rw], writes=[s.res])
            return s, dst

        def w_gu(n, f0):
            need_cast(n, f0 // 512)
            src = wbf[n].ap().rearrange("(c p) f -> p c f", p=128)[:, :, f0:f0 + 512]
            src32 = wfull[n].ap().rearrange("(c p) f -> p c f", p=128)[:, :, f0:f0 + 512]
            return wload(src, lambda t: t[:].rearrange("p (c f) -> p c f", c=8), r_w[n][f0 // 512], src32)

        def w_dn(n, dc):
            need_cast(n, dc // 4)
            src = wbf[n].ap().rearrange("(c p) d -> p c d", p=128)[:, :, dc * 128:(dc + 1) * 128]
            src32 = wfull[n].ap().rearrange("(c p) d -> p c d", p=128)[:, :, dc * 128:(dc + 1) * 128]
            return wload(src, lambda t: t[:].rearrange("p (c d) -> p c d", c=32), r_w[n][dc], src32)

        xbuf = [Slot(T, ar.alloc([128, 8, TT], F32, "x"), f"x{i}") for i in range(2)]
        r_xc = [[Res(f"xc{i}_{dc}") for dc in range(8)] for i in range(2)]
        hA = ar.alloc([128, 8, TT], BF16, "hA")
        r_hA = [Res(f"hA{dc}") for dc in range(8)]
        abuf = ar.alloc([128, 32, TT], BF16, "a")
        r_a = [Res(f"a{i}") for i in range(32)]
        ybuf = ar.alloc([128, 8, TT], F32, "y")
        r_y = [Res(f"y{i}") for i in range(8)]
        sq = [ar.alloc([128, TT], BF16, "sq") for _ in range(2)]
        r_sq = [Res("sq0"), Res("sq1")]
        sqP = [ar.alloc([128, TT], BF16, "sqP") for _ in range(2)]
        r_sqP = [Res("sqP0"), Res("sqP1")]
        rstdP = ar.alloc([128, TT], F32, "rstdP")
        r_rstdP = Res("rstdP")
        rstdD = ar.alloc([128, TT], F32, "rstdD")
        r_rstdD = Res("rstdD")
        rstdM = ar.alloc([128, TT], F32, "rstdM")
        r_rstdM = Res("rstdM")
        sil = [ar.alloc([128, TT], F32, "sil") for _ in range(2)]
        r_sil = [Res("sil0"), Res("sil1")]
        ffn_mark = ar.mark()
        cnt = {"sq": 0, "sqP": 0, "sil": 0, "mm": 0}

        PS_G = [0, 1]
        PS_U = [2, 3]
        PS_MM = [4, 5, 6]
        PS_SS = 7

        def mm_group(bank, out_ap, pairs, reads):
            n = len(pairs)
            for i, (l, r) in enumerate(pairs):
                T.op("pe", lambda e, l=l, r=r, i=i: e.matmul(out_ap, l, r, start=(i == 0), stop=(i == n - 1)),
                     reads, [pres[bank]], signal=(i == n - 1))

        def next_mm():
            b = PS_MM[cnt["mm"] % 3]
            cnt["mm"] += 1
            return b

        PS_SS2 = 0

        def stat_ops(src_fn, r_src_fn, bank, ring, r_ring, ckey):
            slotof = {}

            def act(dc):
                k = cnt[ckey] % 2
                cnt[ckey] += 1
                slotof[dc] = k
                T.op("act", lambda e: e.activation(out=ring[k][:], in_=src_fn(dc), func=AF.Square),
                     r_src_fn(dc), [r_ring[k]])

            def pe(dc):
                k = slotof[dc]
                T.op("pe", lambda e: e.matmul(psum[bank][:], invd_bf[:], ring[k][:], start=(dc == 0), stop=(dc == 7)),
                     [r_ring[k], r_pc], [pres[bank]], signal=True)
            return act, pe

        def finish_rstd(bank, rt, r_rt):
            T.op("act", lambda e: e.activation(out=rt[:], in_=psum[bank][:], func=AF.Sqrt, bias=eps_col[:, 0:1]),
                 [pres[bank], r_pc], [r_rt])
            T.op("dve", lambda e: e.reciprocal(rt[:], rt[:]), [r_rt], [r_rt])

        def h_ops(src, r_src_list, gi, rt, r_rt, hdst, r_hdst, dcs):
            for dc in dcs:
                T.op("dve", lambda e, dc=dc: e.scalar_tensor_tensor(
                    out=hdst[:, dc, :], in0=src[:, dc, :], scalar=gcols[:, gi, dc:dc + 1], in1=rt[:],
                    op0=ALU.mult, op1=ALU.mult), [r_src_list[dc], r_rt, r_const], [r_hdst[dc]])

        def prenorm_now(src, r_src_list, gi, hdst, r_hdst, bank, rt, r_rt):
            act, pe = stat_ops(lambda dc: src[:, dc, :], lambda dc: [r_src_list[dc]], bank, sqP, r_sqP, "sqP")
            for dc in range(8):
                act(dc)
                pe(dc)
            finish_rstd(bank, rt, r_rt)
            h_ops(src, r_src_list, gi, rt, r_rt, hdst, r_hdst, range(8))

        def p_side(src, r_src_list, gi, hdst, r_hdst):
            act, pe = stat_ops(lambda dc: src[:, dc, :], lambda dc: [r_src_list[dc]], PS_SS2, sqP, r_sqP, "sqP")
            side = [[] for _ in range(8)]
            side[0] = [lambda: act(0), lambda: act(1)]
            side[1] = [lambda: pe(0), lambda: pe(1), lambda: act(2), lambda: act(3)]
            side[2] = [lambda: pe(2), lambda: pe(3), lambda: act(4), lambda: act(5)]
            side[3] = [lambda: pe(4), lambda: pe(5), lambda: act(6), lambda: act(7)]
            side[4] = [lambda: pe(6), lambda: pe(7), lambda: finish_rstd(PS_SS2, rstdP, r_rstdP)]
            side[5] = [lambda: h_ops(src, r_src_list, gi, rstdP, r_rstdP, hdst, r_hdst, range(0, 4))]
            side[6] = [lambda: h_ops(src, r_src_list, gi, rstdP, r_rstdP, hdst, r_hdst, range(4, 8))]
            return side

        gu_state = {}

        def gateup(names, fcs, hsrc, r_hsrc, tasks):
            ng, nu = names
            for fc in fcs:
                fi = fc % 4
                if fi == 0:
                    gu_state["g"] = w_gu(ng, (fc // 4) * 512)
                    gu_state["u"] = w_gu(nu, (fc // 4) * 512)
                sg, wg = gu_state["g"]
                su, wu = gu_state["u"]
                bg = PS_G[fc % 2]
                bu = PS_U[fc % 2]
                mm_group(bg, psum[bg][:], [(wg[:, dc, fi * 128:(fi + 1) * 128], hsrc[:, dc, :]) for dc in range(8)],
                         [sg.res] + r_hsrc)
                mm_group(bu, psum[bu][:], [(wu[:, dc, fi * 128:(fi + 1) * 128], hsrc[:, dc, :]) for dc in range(8)],
                         [su.res] + r_hsrc)
                k = cnt["sil"] % 2
                cnt["sil"] += 1
                T.op("act", lambda e, bg=bg, k=k: e.activation(out=sil[k][:], in_=psum[bg][:], func=AF.Silu),
                     [pres[bg]], [r_sil[k]])
                T.op("dve", lambda e, bu=bu, k=k, fc=fc: e.tensor_tensor(abuf[:, fc, :], psum[bu][:], sil[k][:], ALU.mult),
                     [pres[bu], r_sil[k]], [r_a[fc]])
                if tasks:
                    tasks.pop(0)()

        def down_stage(nd, gpost, side):
            act, pe = stat_ops(lambda dc: psum[bank_of[dc]][:], lambda dc: [pres[bank_of[dc]]], PS_SS, sq, r_sq, "sq")
            bank_of = {}
            for dc in range(8):
                sd, wd = w_dn(nd, dc)
                b = next_mm()
                bank_of[dc] = b
                mm_group(b, psum[b][:], [(wd[:, fc, :], abuf[:, fc, :]) for fc in range(32)], [sd.res] + r_a)
                if dc > 0:
                    pe(dc - 1)
                T.op("act", lambda e, dc=dc, b=b: e.activation(out=ybuf[:, dc, :], in_=psum[b][:], func=AF.Copy,
                                                               scale=gcols[:, gpost, dc:dc + 1]),
                     [pres[b], r_const], [r_y[dc]])
                act(dc)
                for t in side[dc]:
                    t()
            pe(7)
            finish_rstd(PS_SS, rstdD, r_rstdD)

        def resid_tasks(ysrc, r_ysrc, rt, r_rt, x_t, rx, after=None):
            tasks = []
            for dc in range(8):
                def t(dc=dc):
                    T.op("dve", lambda e: e.tensor_tensor(ysrc[:, dc, :], ysrc[:, dc, :], rt[:], ALU.mult),
                         [r_ysrc[dc], r_rt], [r_ysrc[dc]])
                    T.op("dve", lambda e: e.tensor_tensor(x_t[:, dc, :], x_t[:, dc, :], ysrc[:, dc, :], ALU.add),
                         [r_ysrc[dc], rx[dc]], [rx[dc]])
                    if after is not None:
                        after(dc)
                tasks.append(t)
            return tasks

        def chain_tasks(x_t, rx, gi, hdst, r_hdst, bank, ring, r_ring, ckey, ysrc, r_ysrc, rt_in, r_rt_in,
                        store_fn=None, final_fn=None):
            act, pe = stat_ops(lambda dc: x_t[:, dc, :], lambda dc: [rx[dc]], bank, ring, r_ring, ckey)

            def after(dc):
                act(dc)
                if dc > 0:
                    pe(dc - 1)
            tasks = resid_tasks(ysrc, r_ysrc, rt_in, r_rt_in, x_t, rx, after)

            def t8():
                pe(7)
                if store_fn is not None:
                    store_fn()
            tasks.append(t8)
            tasks.append(lambda: finish_rstd(bank, rstdM, r_rstdM))
            tasks.append(lambda: h_ops(x_t, rx, gi, rstdM, r_rstdM, hdst, r_hdst, range(0, 4)))

            def t11():
                h_ops(x_t, rx, gi, rstdM, r_rstdM, hdst, r_hdst, range(4, 8))
                if final_fn is not None:
                    final_fn()
            tasks.append(t11)
            return tasks

        def flush(tasks):
            while tasks:
                tasks.pop(0)()

        wf_sb = ar.alloc([128, 8, H], BF16, "wf")
        r_wf = Res("wf")
        ds_wf = T.dsem("wf")
        T.dma("pool", ds_wf, [(wf_sb[:], wf_in.ap().rearrange("(c p) h -> p c h", p=128))], writes=[r_wf])
        wsT_f = ar.alloc([128, 8, 128], F32, "wsTf")
        wsT_b = ar.alloc([128, 8, 128], BF16, "wsTb")
        bs_bc = ar.alloc([128, 8, 128], F32, "bsbc")
        bf_bc = ar.alloc([128, H], F32, "bfbc")
        e_g = ar.alloc([128, 8, 128], F32, "eg")
        r_sgc = Res("sgc")
        T.dma("sp", ds_const, [
            (wsT_f[:].rearrange("p a b -> p (a b)"), wsT_in.ap()),
            (bs_bc[:].rearrange("p a b -> p (a b)"), bs_in.ap().partition_broadcast(128).rearrange("p a b -> p (a b)")),
            (bf_bc[:], bf_in.ap().partition_broadcast(128).rearrange("p a b -> p (a b)")),
        ], writes=[r_sgc])
        T.op("dve", lambda e: e.memset(wsT_f[64:128, :, 0:64], 0.0), [r_sgc], [r_sgc])
        T.op("dve", lambda e: e.tensor_copy(wsT_b[:], wsT_f[:]), [r_sgc], [r_sgc])
        for half in range(2):
            T.op("pe", lambda e, half=half: e.matmul(psum[half][:], ones_f[:],
                                                     wsT_f[:, half * 4:(half + 1) * 4, :].rearrange("p a b -> p (a b)"),
                                                     start=True, stop=True),
                 [r_sgc, r_pc], [pres[half]])
            for gi in range(4):
                g = half * 4 + gi
                T.op("dve", lambda e, half=half, gi=gi, g=g: e.scalar_tensor_tensor(
                    out=e_g[:, g, :], in0=psum[half][:, gi * 128:(gi + 1) * 128], scalar=lncols[:, 1, g:g + 1],
                    in1=bs_bc[:, g, :], op0=ALU.mult, op1=ALU.add), [pres[half], r_const, r_sgc], [r_sgc])

        a_mark = ar.mark()
        hM = ar.alloc([128, 8, TT], BF16, "hM")
        r_hM = [Res(f"hM{dc}") for dc in range(8)]
        qk_ring = Ring(T, ar, 2, [128, 4, TT], BF16, "qkst")
        ybase = nc.lookup_mloc(ybuf).addr
        vst = Slot(T, nc.alloc_sbuf_tensor_at("vst_alias", [128, 4, D], BF16, offset=ybase), "vst")
        vn = nc.alloc_sbuf_tensor_at("vn_alias", [128, 4, D], BF16, offset=ybase + 4 * D * 2)
        r_vn = [r_y[4 + b] for b in range(4)]
        vs_ring = [ar.alloc([128, 512], F32, "vs") for _ in range(4)]
        r_vs = [Res(f"vs{i}") for i in range(4)]
        stats = ar.alloc([128, 16, 6], F32, "stats")
        mv = ar.alloc([128, 16, 2], F32, "mv")
        lrs = ar.alloc([128, 16], F32, "lrs")
        r_st = Res("stats")
        u_ring = [ar.alloc([128, TT], F32, "u") for _ in range(2)]
        r_u = [Res("u0"), Res("u1")]
        gb_ring = [ar.alloc([128, TT], F32, "gb") for _ in range(2)]
        r_gb = [Res("gb0"), Res("gb1")]
        ga_ring = Ring(T, ar, 2, [128, TT], F32, "gast")
        mb_ring = Ring(T, ar, 3, [128, TT], F32, "mbst")
        lf_z = ar.alloc([128, 4, H], F32, "lfz")
        lf_st = Slot(T, ar.alloc([128, 4, H], F32, "lfst"), "lfst")
        r_lfz = Res("lfz")

        r_Q = [[Res(f"Q{i}_{hf}") for hf in range(2)] for i in range(NT)]
        r_KG = [[[Res(f"KG{sl}_{i}_{hf}") for hf in range(2)] for i in range(NT)] for sl in range(2)]
        r_VG = [[Res(f"VG{sl}_{i}") for i in range(NT)] for sl in range(2)]
        r_LG = [[Res(f"LG{sl}_{i}") for i in range(NT)] for sl in range(2)]
        r_GA = [[Res(f"GA{i}_{g}") for g in range(8)] for i in range(NT)]
        r_MB = [[Res(f"MB{i}_{g}") for g in range(8)] for i in range(NT)]
        r_X1 = [Res(f"X1{i}") for i in range(NT)]
        r_MG = [[Res(f"MG{i}_{g}") for g in range(8)] for i in range(NT)]
        qk_cres = [[Res(f"qkc{i}_{c}") for c in range(8)] for i in range(2)]
        v_cres = [r_y[i % 4] for i in range(8)]

        xT_v = xT.ap().rearrange("(c p) t -> p c t", p=128)
        X1_v = X1s.ap().rearrange("(c p) t -> p c t", p=128)
        Q_v = Qs.ap().rearrange("(c p) t -> p c t", p=128)
        xTp_v = xTp.ap().rearrange("(c p) t -> p c t", p=128)
        KG_v = [KG.ap()[sl * 1024:(sl + 1) * 1024, :].rearrange("(c p) t -> p c t", p=128) for sl in range(2)]
        VG_v = [VG.ap()[sl * 1024:(sl + 1) * 1024, :].rearrange("(h s) (l d) -> s l h d", s=128, d=128)
                for sl in range(2)]
        LG_v = [LG.ap()[sl * TPC:(sl + 1) * TPC, :].rearrange("(l p) h -> p l h", p=128) for sl in range(2)]
        win_v = wbf["win"].ap().rearrange("(c p) n -> p c n", p=128)
        win32_v = wfull["win"].ap().rearrange("(c p) n -> p c n", p=128)

        def w_in_chunk(ci):
            need_cast("win", ci)
            return wload(win_v[:, :, ci * 512:(ci + 1) * 512],
                         lambda t: t[:].rearrange("p (c f) -> p c f", c=8), r_w["win"][ci],
                         win32_v[:, :, ci * 512:(ci + 1) * 512])

        cact = {"i": 0}

        def evac(dst_ap, src_ap, reads, writes, func=None):
            if func is not None:
                return T.op("act", lambda e: e.activation(out=dst_ap, in_=src_ap, func=func), reads, writes)
            cact["i"] += 1
            if cact["i"] % 2:
                return T.op("act", lambda e: e.activation(out=dst_ap, in_=src_ap, func=AF.Copy), reads, writes)
            return T.op("dve", lambda e: e.tensor_copy(dst_ap, src_ap), reads, writes)

        def load_x(ti):
            xs_ = xbuf[ti % 2]
            src = xT_v if ti < NT else xTp_v
            lt_ = ti % NT
            T.dma("sp", xs_.ds, [(xs_.t[:], src[:, :, lt_ * TT:(lt_ + 1) * TT])], writes=r_xc[ti % 2])

        def w_stage(slot, lt, own):
            tsl = slice(lt * TT, (lt + 1) * TT)

            def sec_qk(which, act_only):
                dst_v, rdst = ((Q_v, r_Q[lt]), (KG_v[slot], r_KG[slot][lt]))[which]
                for half in range(2):
                    sti = qk_ring.i % 2
                    st = qk_ring.next()
                    sw, wv = w_in_chunk(which * 2 + half)
                    for ci in range(4):
                        b = next_mm()
                        mm_group(b, psum[b][:], [(wv[:, dc, ci * 128:(ci + 1) * 128], hM[:, dc, :]) for dc in range(8)],
                                 [sw.res] + r_hM)
                        evac(st.t[:, ci, :], psum[b][:], [pres[b]], [qk_cres[sti][ci]],
                             func=(AF.Copy if act_only else None))
                    T.dma("sp", st.ds, [(dst_v[:, half * 4:(half + 1) * 4, tsl], st.t[:])], reads=qk_cres[sti][0:4],
                          writes=[rdst[half]])

            def sec_v():
                for half in range(2):
                    sw, wv = w_in_chunk(4 + half)
                    for blk in range(4):
                        b = next_mm()
                        mm_group(b, psum[b][:], [(hM[:, dc, blk * 128:(blk + 1) * 128], wv[:, dc, :]) for dc in range(8)],
                                 [sw.res] + r_hM)
                        evac(vst.t[:, blk, half * 512:(half + 1) * 512], psum[b][:], [pres[b]], [v_cres[half * 4 + blk]])
                T.dma("sp", vst.ds, [(VG_v[slot][:, lt * 4 + blk, :, :], vst.t[:, blk, :].rearrange("p (h d) -> p h d", h=H))
                                     for blk in range(4)], reads=v_cres, writes=[r_VG[slot][lt]])

            def sec_f():
                b = next_mm()
                for blk in range(4):
                    mm_group(b, psum[b][:, blk * H:(blk + 1) * H],
                             [(hM[:, dc, blk * 128:(blk + 1) * 128], wf_sb[:, dc, :]) for dc in range(8)], [r_wf] + r_hM)
                T.op("dve", lambda e, b=b: e.tensor_tensor(
                    lf_z[:], psum[b][:, 0:4 * H].rearrange("p (b h) -> p b h", h=H),
                    bf_bc[:].unsqueeze(1).to_broadcast([128, 4, H]), ALU.add), [pres[b], r_sgc], [r_lfz])
                T.op("act", lambda e: e.activation(out=lf_z[:], in_=lf_z[:], func=AF.Exp, scale=-1.0), [r_lfz], [r_lfz])
                T.op("act", lambda e: e.activation(out=lf_z[:], in_=lf_z[:], func=AF.Ln, bias=1.0), [r_lfz], [r_lfz])
                T.op("dve", lambda e: e.tensor_scalar(lf_st.t[:], lf_z[:], -1.0, None, ALU.mult), [r_lfz], [lf_st.res])
                T.dma("sp", lf_st.ds, [(LG_v[slot][:, lt * 4:(lt + 1) * 4, :], lf_st.t[:])], reads=[lf_st.res],
                      writes=[r_LG[slot][lt]])

            def sec_sv_mm(half):
                sw, wv = w_in_chunk(8 + half)
                for blk in range(4):
                    b = next_mm()
                    mm_group(b, psum[b][:], [(hM[:, dc, blk * 128:(blk + 1) * 128], wv[:, dc, :]) for dc in range(8)],
                             [sw.res] + r_hM)
                    T.op("act", lambda e, b=b, blk=blk: e.activation(out=vs_ring[blk][:], in_=psum[b][:], func=AF.Gelu),
                         [pres[b]], [r_vs[blk]])
                for blk in range(4):
                    for gi in range(4):
                        T.op("dve", lambda e, blk=blk, gi=gi: e.bn_stats(
                            stats[:, blk * 4 + gi, :], vs_ring[blk][:, gi * 128:(gi + 1) * 128]), [r_vs[blk]], [r_st])
                for q_ in range(16):
                    T.op("dve", lambda e, q_=q_: e.bn_aggr(mv[:, q_, :], stats[:, q_, :]), [r_st], [r_st])

            def sec_sv_fin(half):
                T.op("act", lambda e: e.activation(out=lrs[:], in_=mv[:, :, 1], func=AF.Sqrt, bias=eps_col[:, 1:2]),
                     [r_st, r_pc], [r_st])
                T.op("dve", lambda e: e.reciprocal(lrs[:], lrs[:]), [r_st], [r_st])
                for blk in range(4):
                    for gi in range(4):
                        g = half * 4 + gi
                        q_ = blk * 4 + gi
                        T.op("dve", lambda e, gi=gi, g=g, blk=blk, q_=q_: e.tensor_scalar(
                            vn[:, blk, g * 128:(g + 1) * 128], vs_ring[blk][:, gi * 128:(gi + 1) * 128],
                            mv[:, q_, 0:1], lrs[:, q_:q_ + 1], ALU.subtract, ALU.mult), [r_vs[blk], r_st], [r_vn[blk]])

            if not own:
                sec_qk(1, False)
                sec_v()
                sec_f()
                return
            sec_sv_mm(0)
            sec_qk(0, True)
            sec_sv_fin(0)
            sec_sv_mm(1)
            sec_qk(1, True)
            sec_sv_fin(1)
            sec_v()
            sec_f()
            GA_v = GAs.ap().rearrange("(c p) t -> p c t", p=128)
            MB_v = MBs.ap().rearrange("(c p) t -> p c t", p=128)
            for half in range(2):
                su_, wu_ = w_in_chunk(6 + half)
                sb_, wb_ = w_in_chunk(12 + half)
                sa_, wa_ = w_in_chunk(10 + half)
                for gi in range(4):
                    g = half * 4 + gi
                    k = g % 2
                    csl = slice(gi * 128, (gi + 1) * 128)
                    b = next_mm()
                    mm_group(b, psum[b][:], [(wu_[:, dc, csl], hM[:, dc, :]) for dc in range(8)], [su_.res] + r_hM)
                    T.op("act", lambda e, b=b, k=k: e.activation(out=u_ring[k][:], in_=psum[b][:], func=AF.Gelu),
                         [pres[b]], [r_u[k]])
                    b = next_mm()
                    mm_group(b, psum[b][:], [(wb_[:, dc, csl], hM[:, dc, :]) for dc in range(8)], [sb_.res] + r_hM)
                    T.op("act", lambda e, b=b, k=k: e.activation(out=gb_ring[k][:], in_=psum[b][:], func=AF.Sigmoid),
                         [pres[b]], [r_gb[k]])
                    b = next_mm()
                    mm_group(b, psum[b][:], [(wa_[:, dc, csl], hM[:, dc, :]) for dc in range(8)], [sa_.res] + r_hM)
                    gs = ga_ring.next()
                    T.op("act", lambda e, b=b, gs=gs: e.activation(out=gs.t[:], in_=psum[b][:], func=AF.Sigmoid),
                         [pres[b]], [gs.res])
                    T.dma("sp", gs.ds, [(GA_v[:, g, tsl], gs.t[:])], reads=[gs.res], writes=[r_GA[lt][g]])
                    b = next_mm()
                    for blk in range(4):
                        T.op("pe", lambda e, b=b, blk=blk, g=g: e.matmul(
                            psum[b][:, blk * 128:(blk + 1) * 128], vn[:, blk, g * 128:(g + 1) * 128], wsT_b[:, g, :],
                            start=True, stop=True), [r_vn[blk], r_sgc], [pres[b]], signal=(blk == 3))
                    ms = mb_ring.next()
                    T.op("dve", lambda e, b=b, ms=ms, g=g: e.scalar_tensor_tensor(
                        out=ms.t[:].rearrange("p (b t) -> p b t", b=4), in0=psum[b][:].rearrange("p (b t) -> p b t", b=4),
                        scalar=lncols[:, 0, g:g + 1], in1=e_g[:, g, :].unsqueeze(1).to_broadcast([128, 4, 128]),
                        op0=ALU.mult, op1=ALU.add), [pres[b], r_const, r_sgc], [ms.res])
                    T.op("dve", lambda e, ms=ms, k=k: e.tensor_tensor(ms.t[:], ms.t[:], u_ring[k][:], ALU.mult),
                         [r_u[k], ms.res], [ms.res])
                    T.op("dve", lambda e, ms=ms, k=k: e.tensor_tensor(ms.t[:], ms.t[:], gb_ring[k][:], ALU.mult),
                         [r_gb[k], ms.res], [ms.res])
                    T.dma("sp", ms.ds, [(MB_v[:, g, tsl], ms.t[:])], reads=[ms.res], writes=[r_MB[lt][g]])

        NTA = 2 * NT
        FFN1 = ("wg1", "wu1")
        wdirect["on"] = True
        load_x(0)
        load_x(1)
        prenorm_now(xbuf[0].t, r_xc[0], 0, hA, r_hA, PS_SS2, rstdP, r_rstdP)
        gateup(FFN1, range(32), hA, r_hA, [])
        for ti in range(NTA):
            slot, lt = ti // NT, ti % NT
            own = slot == 0
            xs_ = xbuf[ti % 2]
            x_t, rx = xs_.t, r_xc[ti % 2]
            tsl = slice(lt * TT, (lt + 1) * TT)
            if ti + 1 < NTA:
                side = p_side(xbuf[(ti + 1) % 2].t, r_xc[(ti + 1) % 2], 0, hA, r_hA)
            else:
                side = [[] for _ in range(8)]
            down_stage("wd1", 1, side)

            def store_fn(xs_=xs_, x_t=x_t, rx=rx, lt=lt, tsl=tsl, own=own):
                if own:
                    T.dma("sp", xs_.ds, [(X1_v[:, :, tsl], x_t[:])], reads=rx, writes=[r_X1[lt]])

            def final_fn(ti=ti):
                if ti + 2 < NTA:
                    load_x(ti + 2)
            chain = chain_tasks(x_t, rx, 2, hM, r_hM, PS_SS, sq, r_sq, "sq", ybuf, r_y, rstdD, r_rstdD,
                                store_fn, final_fn)
            if ti + 1 < NTA:
                gateup(FFN1, range(0, 12), hA, r_hA, chain)
            flush(chain)
            w_stage(slot, lt, own)
            wdirect["on"] = False
            if ti + 1 < NTA:
                gateup(FFN1, range(12, 32), hA, r_hA, [])
        T.barrier()
        if STOP == 1:
            raise _StopBuild

        ar.reset(cmark)
        lg = ar.alloc([128, NIDX, H], F32, "lg")
        pre = ar.alloc([128, NIDX, H], F32, "pre")
        preq = [ar.alloc([128, NIDX, H], F32, "preq") for _ in range(2)]
        negc = ar.alloc([128, H, NIDX], F32, "negc")
        offcol = ar.alloc([128, 2], F32, "offcol")
        r_c = Res("cums")
        ds_lg = T.dsem("lg")
        T.dma("sp", ds_lg, [(lg[:, sl * NLB:(sl + 1) * NLB, :], LG_v[sl]) for sl in range(2)],
              reads=r_LG[0] + r_LG[1], writes=[r_c])

        def gidx(q, gblk):
            G, j = gblk // 4, gblk % 4
            if G in GT[q]:
                return GT[q].index(G) * 4 + j
            return NLB + GT[1 - q].index(G) * 4 + j

        lg2 = lg[:].rearrange("p i h -> p (i h)")
        NLG = NIDX * H
        T.op("pe", lambda e: e.matmul(psum[0][:, 0:NLG], tri_f[:], lg2, start=True, stop=True), [r_c, r_pc], [pres[0]])
        T.op("pe", lambda e: e.matmul(psum[1][:, 0:NLG], ones_f[:], lg2, start=True, stop=True), [r_c, r_pc], [pres[1]])
        tot = ar.alloc([128, NIDX, H], F32, "tot")
        T.op("dve", lambda e: e.tensor_copy(tot[:].rearrange("p i h -> p (i h)"), psum[1][:, 0:NLG]), [pres[1]], [r_c])
        for q in range(2):
            T.op("dve", lambda e, q=q: e.memset(preq[q][:, gidx(q, 0), :], 0.0), [r_c], [r_c])
            for gb_ in range(1, NIDX):
                a, bq = gidx(q, gb_), gidx(q, gb_ - 1)
                T.op("dve", lambda e, a=a, bq=bq, q=q: e.tensor_tensor(preq[q][:, a, :], preq[q][:, bq, :], tot[:, bq, :], ALU.add),
                     [r_c], [r_c])
        T.op("dve", lambda e: e.tensor_scalar(pre[:], preq[0][:], flags[:, 1:2], None, ALU.mult), [r_c, r_const], [r_c])
        T.op("dve", lambda e: e.scalar_tensor_tensor(out=pre[:], in0=preq[1][:], scalar=flags[:, 0:1], in1=pre[:],
                                                     op0=ALU.mult, op1=ALU.add), [r_c, r_const], [r_c])
        T.op("dve", lambda e: e.scalar_tensor_tensor(
            out=negc[:].rearrange("p h i -> p i h"), in0=psum[0][:, 0:NLG].rearrange("p (i h) -> p i h", h=H), scalar=-1.0,
            in1=pre[:], op0=ALU.mult, op1=ALU.subtract), [pres[0], r_c], [r_c])
        T.op("dve", lambda e: e.tensor_scalar(offcol[:, 0:1], flags[:, 1:2], -30000.0, None, ALU.mult), [r_const], [r_c])
        T.op("dve", lambda e: e.tensor_scalar(offcol[:, 1:2], flags[:, 0:1], -30000.0, None, ALU.mult), [r_const], [r_c])

        kbuf = [Slot(T, ar.alloc([128, 2 * TPC], BF16, "kb"), f"kb{i}") for i in range(2)]
        vbuf = [Slot(T, ar.alloc([128, NIDX, 128], BF16, "vb"), f"vb{i}") for i in range(2)]
        q_ring = Ring(T, ar, 3, [128, TT], BF16, "q")
        p_ring = [ar.alloc([128, TT], BF16, "p") for _ in range(4)]
        r_p = [Res(f"p{i}") for i in range(4)]
        bias_ring = [ar.alloc([128, NIDX], F32, "bias") for _ in range(3)]
        r_bias = [Res("bias0"), Res("bias1"), Res("bias2")]
        gat = Ring(T, ar, 3, [128, TT], F32, "gat")
        mbt = Ring(T, ar, 3, [128, TT], F32, "mbt")
        rden = ar.alloc([128, TT], F32, "rden")
        r_rden = Res("rden")
        otmp = ar.alloc([128, TT], F32, "otmp")
        r_otmp = Res("otmp")
        mg_ring = Ring(T, ar, 2, [128, TT], BF16, "mgst")

        PS_S = [0, 1, 2, 7]
        PS_O = [3, 4]
        PS_D = [5, 6]
        GA_v = GAs.ap().rearrange("(c p) t -> p c t", p=128)
        MB_v = MBs.ap().rearrange("(c p) t -> p c t", p=128)
        MG_v = MGs.ap().rearrange("(c p) t -> p c t", p=128)
        scale = 1.0 / float(np.sqrt(HD))

        units = []
        grp = 0
        for h in range(H):
            for lt in range(NT):
                first = True
                for sl in (1, 0):
                    for klt in range(lt + 1):
                        for j in range(4):
                            units.append(dict(h=h, lt=lt, grp=grp, first=first, last=False, rk=sl, klt=klt, j=j,
                                              diag=(sl == 0 and klt == lt)))
                            first = False
                units[-1]["last"] = True
                grp += 1
        state = {}

        def load_head(h):
            ks, vsb = kbuf[h % 2], vbuf[h % 2]
            T.dma("sp", ks.ds, [(ks.t[:, rk * TPC:(rk + 1) * TPC], KG.ap()[rk * 1024 + h * 128: rk * 1024 + (h + 1) * 128, :])
                                for rk in range(2)], reads=[r for sl in range(2) for t_ in r_KG[sl] for r in t_], writes=[ks.res])
            T.dma("sp", vsb.ds, [(vsb.t[:, rk * NLB:(rk + 1) * NLB, :],
                                  VG.ap()[rk * 1024 + h * 128: rk * 1024 + (h + 1) * 128, :].rearrange("s (l d) -> s l d", d=128))
                                 for rk in range(2)], reads=r_VG[0] + r_VG[1], writes=[vsb.res])

        def group_prologue(g):
            if g in state or g >= H * NT:
                return
            h, lt = divmod(g, NT)
            qs = q_ring.next()
            T.dma("sp", qs.ds, [(qs.t[:], Q_v[:, h, lt * TT:(lt + 1) * TT])], reads=[r_Q[lt][h // 4]], writes=[qs.res])
            k = g % 3
            T.op("dve", lambda e, k=k, h=h, lt=lt: e.tensor_scalar(
                bias_ring[k][:], negc[:, h, :], pre[:, 4 * lt + 2, h:h + 1], BCLAMP, ALU.add, ALU.min), [r_c], [r_bias[k]])
            fsl = slice(NLB + 4 * lt, NLB + 4 * lt + 4)
            T.op("dve", lambda e, k=k, lt=lt, fsl=fsl: e.tensor_scalar(
                bias_ring[k][:, fsl], bias_ring[k][:, fsl], offcol[:, lt % 2:lt % 2 + 1], None, ALU.add),
                [r_c, r_bias[k]], [r_bias[k]])
            gs, ms = gat.next(), mbt.next()
            T.dma("sp", gs.ds, [(gs.t[:], GA_v[:, h, lt * TT:(lt + 1) * TT])], reads=[r_GA[lt][h]], writes=[gs.res])
            T.dma("sp", ms.ds, [(ms.t[:], MB_v[:, h, lt * TT:(lt + 1) * TT])], reads=[r_MB[lt][h]], writes=[ms.res])
            state[g] = dict(q=qs, ga=gs, mb=ms)

        def emit_S(i):
            u = units[i]
            if u["first"]:
                h_, lt_ = u["h"], u["lt"]
                if lt_ == 0 and h_ == 0:
                    load_head(0)
                if lt_ == min(1, NT - 1) and h_ + 1 < H:
                    load_head(h_ + 1)
                group_prologue(u["grp"])
                group_prologue(u["grp"] + 1)
            st = state[u["grp"]]
            ks = kbuf[u["h"] % 2]
            b = PS_S[i % 4]
            col = u["rk"] * TPC + u["klt"] * TT + u["j"] * 128
            c0 = u["j"] * 128 if u["diag"] else 0
            T.op("pe", lambda e, b=b, ks=ks, col=col, q=st["q"], c0=c0: e.matmul(
                psum[b][:, c0:], ks.t[:, col:col + 128], q.t[:, c0:], start=True, stop=True),
                [ks.res, st["q"].res], [pres[b]])

        def emit_rest(i):
            u = units[i]
            g, h, lt = u["grp"], u["h"], u["lt"]
            st = state[g]
            b = PS_S[i % 4]
            k3 = i % 4
            idx = u["rk"] * NLB + u["klt"] * 4 + u["j"]
            kb = g % 3
            c0 = u["j"] * 128 if u["diag"] else 0
            T.op("act", lambda e, b=b, k3=k3, kb=kb, idx=idx, c0=c0: e.activation(
                out=p_ring[k3][:, c0:], in_=psum[b][:, c0:], func=AF.Exp, bias=bias_ring[kb][:, idx:idx + 1], scale=scale),
                [pres[b], r_bias[kb]], [r_p[k3]])
            if u["diag"]:
                T.op("dve", lambda e, k3=k3, j=u["j"], c0=c0: e.tensor_tensor(
                    p_ring[k3][:, c0:c0 + 128], p_ring[k3][:, c0:c0 + 128], diag[:, j, c0:c0 + 128], ALU.mult),
                    [r_pc, r_p[k3]], [r_p[k3]])
            vsb = vbuf[h % 2]
            bo, bd = PS_O[g % 2], PS_D[g % 2]
            T.op("pe", lambda e, bo=bo, vsb=vsb, idx=idx, k3=k3, f=u["first"], l=u["last"], c0=c0: e.matmul(
                psum[bo][:, c0:], vsb.t[:, idx, :], p_ring[k3][:, c0:], start=f, stop=l),
                [vsb.res, r_p[k3]], [pres[bo]], signal=False)
            T.op("pe", lambda e, bd=bd, k3=k3, f=u["first"], l=u["last"], c0=c0: e.matmul(
                psum[bd][:, c0:], ones_bf[:], p_ring[k3][:, c0:], start=f, stop=l), [r_p[k3], r_pc], [pres[bd]], signal=True)
            if u["last"]:
                T.op("dve", lambda e, bd=bd: e.reciprocal(rden[:], psum[bd][:]), [pres[bd]], [r_rden])
                T.op("dve", lambda e, bo=bo: e.tensor_tensor(otmp[:], psum[bo][:], rden[:], ALU.mult),
                     [pres[bo], r_rden], [r_otmp])
                T.op("dve", lambda e, gs=st["ga"]: e.tensor_tensor(otmp[:], otmp[:], gs.t[:], ALU.mult),
                     [r_otmp, st["ga"].res], [r_otmp])
                mg = mg_ring.next()
                T.op("dve", lambda e, mg=mg, ms=st["mb"]: e.tensor_tensor(mg.t[:], otmp[:], ms.t[:], ALU.add),
                     [r_otmp, st["mb"].res], [mg.res])
                T.dma("sp", mg.ds, [(MG_v[:, h, lt * TT:(lt + 1) * TT], mg.t[:])], reads=[mg.res], writes=[r_MG[lt][h]])
                del state[g]

        NU = len(units)
        LOOK = 3
        for i in range(min(LOOK, NU)):
            emit_S(i)
        for i in range(NU):
            if i + LOOK < NU:
                emit_S(i + LOOK)
            emit_rest(i)
        T.barrier()
        if STOP == 2:
            raise _StopBuild

        ar.reset(ffn_mark)
        wout_sb = ar.alloc([128, 8, D], BF16, "wout")
        r_wo = Res("wout")
        ds_wo = T.dsem("wout")
        need_cast("wout", 1)
        T.dma("pool", ds_wo, [(wout_sb[:], wbf["wout"].ap().rearrange("(c p) n -> p c n", p=128))],
              reads=r_w["wout"], writes=[r_wo])
        mgin = Ring(T, ar, 2, [128, 8, TT], BF16, "mgin")
        y2 = ar.alloc([128, 8, TT], F32, "y2")
        r_y2 = [Res(f"y2_{dc}") for dc in range(8)]
        outT_v = outT.ap().rearrange("(c p) t -> p c t", p=128)
        r_out = Res("out")
        FFN2 = ("wg2", "wu2")

        def load_b2(lt):
            xsl = xbuf[lt % 2]
            tsl_ = slice(lt * TT, (lt + 1) * TT)
            T.dma("sp", xsl.ds, [(xsl.t[:], X1_v[:, :, tsl_])], reads=[r_X1[lt]], writes=r_xc[lt % 2])
            ms_ = mgin.next()
            T.dma("sp", ms_.ds, [(ms_.t[:], MG_v[:, :, tsl_])], reads=r_MG[lt], writes=[ms_.res])
            return ms_

        def wout_stage(ms_):
            bank_of = {}
            act, pe = stat_ops(lambda dc: psum[bank_of[dc]][:], lambda dc: [pres[bank_of[dc]]], PS_SS2, sqP, r_sqP, "sqP")
            for dc in range(8):
                b = next_mm()
                bank_of[dc] = b
                mm_group(b, psum[b][:], [(wout_sb[:, c, dc * 128:(dc + 1) * 128], ms_.t[:, c, :]) for c in range(8)],
                         [r_wo, ms_.res])
                if dc > 0:
                    pe(dc - 1)
                T.op("act", lambda e, dc=dc, b=b: e.activation(out=y2[:, dc, :], in_=psum[b][:], func=AF.Copy,
                                                               scale=gcols[:, 3, dc:dc + 1]),
                     [pres[b], r_const], [r_y2[dc]])
                act(dc)
            pe(7)

        def b2_chain(lt):
            x_t, rx = xbuf[lt % 2].t, r_xc[lt % 2]
            tasks = [lambda: finish_rstd(PS_SS2, rstdP, r_rstdP)]
            tasks += chain_tasks(x_t, rx, 4, hA, r_hA, PS_SS2, sqP, r_sqP, "sqP", y2, r_y2, rstdP, r_rstdP)
            return tasks

        mss = {0: load_b2(0)}
        if NT > 1:
            mss[1] = load_b2(1)
        wout_stage(mss[0])
        flush(b2_chain(0))
        gateup(FFN2, range(32), hA, r_hA, [])
        for lt in range(NT):
            xsl = xbuf[lt % 2]
            x_t, rx = xsl.t, r_xc[lt % 2]
            tsl = slice(lt * TT, (lt + 1) * TT)
            side = [[] for _ in range(8)]
            if lt + 1 < NT:
                wout_stage(mss[lt + 1])
                ch = b2_chain(lt + 1)
                per = [2, 2, 2, 2, 2, 1, 1, 1]
                for dc in range(8):
                    for _ in range(per[dc]):
                        if ch:
                            side[dc].append(ch.pop(0))
                assert not ch
            down_stage("wd2", 5, side)

            def out_store(xsl=xsl, x_t=x_t, rx=rx, tsl=tsl, lt=lt):
                T.dma("sp", xsl.ds, [(outT_v[:, :, tsl], x_t[:])], reads=rx, writes=[r_out])
                if lt + 2 < NT:
                    mss[lt + 2] = load_b2(lt + 2)
            otasks = resid_tasks(ybuf, r_y, rstdD, r_rstdD, x_t, rx)
            otasks.append(out_store)
            if lt + 1 < NT:
                gateup(FFN2, range(32), hA, r_hA, otasks)
            flush(otasks)
        T.barrier()


    try:
        body()
    except _StopBuild:
        T.barrier()

    from contextlib import ExitStack
    with ExitStack() as es:
        for E in T.eng.values():
            E.sem = es.enter_context(nc.semaphore("s_" + E.name))
        for d in T.dsems:
            d.sem = es.enter_context(nc.semaphore("d_" + d.name))
        block = es.enter_context(nc.Block())

        @block.tensor
        def _(e):
            T.emit(nc, "pe", e)

        @block.scalar
        def _(e):
            T.emit(nc, "act", e)

        @block.vector
        def _(e):
            T.emit(nc, "dve", e)

        @block.gpsimd
        def _(e):
            T.emit(nc, "pool", e)

        @block.sync
        def _(e):
            T.emit(nc, "sp", e)
    return nc


def _prep_inputs(inp):
    f = lambda a: np.ascontiguousarray(np.asarray(a, dtype=np.float32))
    x = f(inp["x"])
    w = {"wg1": f(inp["ffn1_w_gate"])[0], "wu1": f(inp["ffn1_w_up"])[0], "wd1": f(inp["ffn1_w_down"])[0],
         "wout": f(inp["w_out"])[0], "wg2": f(inp["ffn2_w_gate"])[0], "wu2": f(inp["ffn2_w_up"])[0],
         "wd2": f(inp["ffn2_w_down"])[0]}
    w_in = f(inp["w_in"])[0]
    w["win"] = np.ascontiguousarray(np.concatenate([w_in[:, :3072], w_in[:, 3080:]], axis=1))
    wf = np.ascontiguousarray(w_in[:, 3072:3080])
    gains = [inp["ffn1_pre_g"], inp["ffn1_post_g"], inp["mix_pre_g"], inp["mix_post_g"], inp["ffn2_pre_g"],
             inp["ffn2_post_g"]]
    gcols = np.stack([f(g)[0].reshape(8, 128).T for g in gains], axis=1).reshape(128, 48)
    lncols = np.stack([f(inp["sgu_ln_g"])[0].reshape(8, 128).T, f(inp["sgu_ln_b"])[0].reshape(8, 128).T],
                      axis=1).reshape(128, 16)
    wsT = np.ascontiguousarray(f(inp["sgu_w_s"])[0].transpose(2, 0, 1)).reshape(128, 1024)
    bs = f(inp["sgu_b_s"])[0].reshape(1, 1024)
    bfg = f(inp["b_forget"]).reshape(1, 8)
    maps = []
    for c in range(NCORE):
        b, p = c // 2, c % 2
        toks = np.concatenate([np.arange(G * TT, (G + 1) * TT) for G in GT[p]])
        tokp = np.concatenate([np.arange(G * TT, (G + 1) * TT) for G in GT[1 - p]])
        m = {"xT": np.ascontiguousarray(x[b][toks].T), "xTp": np.ascontiguousarray(x[b][tokp].T)}
        for (n, r, cc) in WEIGHTS:
            m[n + "_f"] = w[n]
        m["wf"] = wf
        m["gcols"] = np.ascontiguousarray(gcols)
        m["lncols"] = np.ascontiguousarray(lncols)
        m["wsT"] = wsT
        m["bs"] = bs
        m["bfg"] = bfg
        fl = np.zeros((128, 4), np.float32)
        fl[:, 0] = p
        fl[:, 1] = 1 - p
        m["flags"] = fl
        maps.append(m)
    return maps


_NC_CACHE = {}


def kernel(**inputs):
    maps = _prep_inputs(inputs)
    if "nc" not in _NC_CACHE:
        _NC_CACHE["nc"] = build()
    nc = _NC_CACHE["nc"]
    res = run_bass_kernel_spmd(nc, maps, core_ids=list(range(NCORE)))
    out = np.empty((NB, SEQ, D), np.float32)
    for c in range(NCORE):
        b, p = c // 2, c % 2
        o = np.asarray(res.results[c]["outT"]).T
        for i, G in enumerate(GT[p]):
            out[b, G * TT:(G + 1) * TT] = o[i * TT:(i + 1) * TT]
    return out
```

```python
import numpy as np
import concourse.bass as bass
import concourse.mybir as mybir
from concourse.bass_utils import run_bass_kernel_spmd

F32 = mybir.dt.float32
BF16 = mybir.dt.bfloat16
AF = mybir.ActivationFunctionType
ALU = mybir.AluOpType

D = 1024
DFF = 4096
SEQ = 8192
NB = 4
NCORE = 8
TT = 512
NT = 8
TPC = NT * TT
NLB = NT * 4
NIDX = 2 * NLB
H = 8
HD = 128
WIN = 7168
GT = [[0, 3, 4, 7, 8, 11, 12, 15], [1, 2, 5, 6, 9, 10, 13, 14]]


def _configure(nt, ncore):
    global NT, TPC, NLB, NIDX, NCORE, GT, SEQ, NB
    NT, NCORE = nt, ncore
    TPC, NLB, NIDX = NT * TT, NT * 4, 2 * NT * 4
    SEQ = 2 * TPC
    NB = ncore // 2
    g0 = [g for g in range(2 * NT) if g % 4 in (0, 3)]
    g1 = [g for g in range(2 * NT) if g % 4 in (1, 2)]
    GT = [g0, g1]
RMS_EPS = 1e-6
LN_EPS = 1e-5
BCLAMP = 64.0
NSLOT = 5
SAME_ENGINE_SYNC = True
STOP = 99


class _StopBuild(Exception):
    pass

WEIGHTS = [("wg1", D, DFF), ("wu1", D, DFF), ("wd1", DFF, D), ("win", D, WIN),
           ("wout", D, D), ("wg2", D, DFF), ("wu2", D, DFF), ("wd2", DFF, D)]


class Res:
    __slots__ = ("name", "w", "r")

    def __init__(self, name):
        self.name = name
        self.w = None
        self.r = {}


class Eng:
    def __init__(self, name):
        self.name = name
        self.sem = None
        self.cnt = 0
        self.waited = {}
        self.prog = []


class DSem:
    def __init__(self, name, step=16):
        self.name = name
        self.sem = None
        self.cnt = 0
        self.step = step
        self.last = None


class Tracker:
    def __init__(self):
        self.eng = {n: Eng(n) for n in ("pe", "act", "dve", "pool", "sp")}
        self.dsems = []
        self.nwait = 0

    def dsem(self, name, step=16):
        d = DSem(name, step)
        self.dsems.append(d)
        return d

    def _wait(self, E, deps):
        for tok in deps:
            if tok is None:
                continue
            key, val, en = tok
            if en == E.name:
                if E.name == "pe" or not SAME_ENGINE_SYNC:
                    continue
            if E.waited.get(id(key), 0) >= val:
                continue
            E.waited[id(key)] = val
            self.nwait += 1
            E.prog.append(("wait", key, val))

    def _deps(self, reads, writes):
        deps = []
        for r in reads:
            if r.w is not None:
                deps.append(r.w)
        for w in writes:
            if w.w is not None:
                deps.append(w.w)
            deps.extend(w.r.values())
        return deps

    def _reg(self, tok, reads, writes):
        for r in reads:
            r.r[id(tok[0])] = tok
        for w in writes:
            w.w = tok
            w.r = {}

    def op(self, en, fn, reads=(), writes=(), signal=True):
        E = self.eng[en]
        self._wait(E, self._deps(reads, writes))
        if signal:
            E.cnt += 1
            tok = (E, E.cnt, en)
        else:
            tok = (E, E.cnt + 1, en)
        E.prog.append(("op", fn, signal))
        self._reg(tok, reads, writes)
        return tok

    def dma(self, q, ds, pairs, reads=(), writes=()):
        E = self.eng[q]
        deps = self._deps(reads, writes)
        deps.append(ds.last)
        self._wait(E, deps)
        for (o, i) in pairs:
            E.prog.append(("dma", o, i, ds))
        ds.cnt += ds.step * len(pairs)
        tok = (ds, ds.cnt, None)
        ds.last = tok
        self._reg(tok, reads, writes)
        return tok

    def cc(self, ds, kind, groups, in_ap, out_ap, reads=(), writes=()):
        E = self.eng["pool"]
        deps = self._deps(reads, writes)
        deps.append(ds.last)
        self._wait(E, deps)
        E.prog.append(("cc", kind, groups, in_ap, out_ap, ds))
        ds.cnt += ds.step
        tok = (ds, ds.cnt, None)
        ds.last = tok
        self._reg(tok, reads, writes)
        return tok

    def barrier(self):
        toks = [(E, E.cnt, E.name) for E in self.eng.values() if E.cnt > 0]
        toks += [d.last for d in self.dsems if d.last is not None]
        for E in self.eng.values():
            self._wait(E, [t for t in toks if t[2] != E.name])

    def emit(self, nc, en, h):
        E = self.eng[en]
        for item in E.prog:
            k = item[0]
            if k == "wait":
                h.wait_ge(item[1].sem, item[2])
            elif k == "op":
                ins = item[1](h)
                if item[2]:
                    ins.then_inc(E.sem, 1)
            elif k == "dma":
                h.dma_start(out=item[1], in_=item[2]).then_inc(item[3].sem, 16)
            elif k == "cc":
                h.collective_compute(item[1], ALU.bypass, replica_groups=item[2],
                                     ins=[item[3]], outs=[item[4]]).then_inc(item[5].sem, item[5].step)


class Arena:
    def __init__(self, nc, base=16512, limit=229376):
        self.nc = nc
        self.base = base
        self.off = base
        self.limit = limit
        self.n = 0

    def alloc(self, shape, dt, name=None):
        esz = 4 if dt == F32 else 2
        nbytes = esz * int(np.prod(shape[1:]))
        nbytes = (nbytes + 31) // 32 * 32
        assert self.off + nbytes <= self.limit, f"SBUF overflow {self.off}+{nbytes}"
        self.n += 1
        t = self.nc.alloc_sbuf_tensor_at(f"{name or 't'}_{self.n}", list(shape), dt, offset=self.off)
        self.off += nbytes
        return t

    def mark(self):
        return self.off

    def reset(self, m):
        self.off = m


class Slot:
    def __init__(self, T, t, name):
        self.t = t
        self.res = Res(name)
        self.ds = T.dsem(name)


class Ring:
    def __init__(self, T, arena, n, shape, dt, name):
        self.slots = [Slot(T, arena.alloc(shape, dt, name), f"{name}{i}") for i in range(n)]
        self.i = 0

    def next(self):
        s = self.slots[self.i % len(self.slots)]
        self.i += 1
        return s


def build():
    nc = bass.Bass("TRN2", target_bir_lowering=False)
    T = Tracker()

    def body():
        def din(name, shape, dt=F32):
            return nc.dram_tensor(name, list(shape), dt, kind="ExternalInput")

        xT = din("xT", [D, TPC])
        xTp = din("xTp", [D, TPC])
        wfull = {n: din(n + "_f", [r, c]) for (n, r, c) in WEIGHTS}
        wf_in = din("wf", [D, H])
        gcols_in = din("gcols", [128, 48])
        lncols_in = din("lncols", [128, 16])
        wsT_in = din("wsT", [128, 8 * 128])
        bs_in = din("bs", [1, 8 * 128])
        bf_in = din("bfg", [1, H])
        flags_in = din("flags", [128, 4])
        outT = nc.dram_tensor("outT", [D, TPC], F32, kind="ExternalOutput")

        wbf = {n: nc.dram_tensor(n + "_bf", [r, c], BF16) for (n, r, c) in WEIGHTS}
        Qs = nc.dram_tensor("Qs", [H * 128, TPC], BF16)
        KG = nc.dram_tensor("KG", [2 * H * 128, TPC], BF16)
        VG = nc.dram_tensor("VG", [2 * H * 128, NLB * 128], BF16)
        LG = nc.dram_tensor("LG", [2 * TPC, H], F32)
        GAs = nc.dram_tensor("GAs", [D, TPC], F32)
        MBs = nc.dram_tensor("MBs", [D, TPC], F32)
        X1s = nc.dram_tensor("X1s", [D, TPC], F32)
        MGs = nc.dram_tensor("MGs", [D, TPC], BF16)

        ar = Arena(nc)
        psum = [nc.alloc_psum_tensor(f"ps{i}", [128, 512], F32) for i in range(8)]
        pres = [Res(f"ps{i}") for i in range(8)]

        gcols = ar.alloc([128, 6, 8], F32, "gcols")
        lncols = ar.alloc([128, 2, 8], F32, "lncols")
        flags = ar.alloc([128, 4], F32, "flags")
        ones_bf = ar.alloc([128, 128], BF16, "ones_bf")
        invd_bf = ar.alloc([128, 128], BF16, "invd_bf")
        ones_f = ar.alloc([128, 128], F32, "ones_f")
        tri_f = ar.alloc([128, 128], F32, "tri_f")
        diag = ar.alloc([128, 4, 512], BF16, "diag")
        eps_col = ar.alloc([128, 2], F32, "eps")
        r_const = Res("const")
        ds_const = T.dsem("const")
        cmark = ar.mark()

        def vop(en, fname, reads, writes, *args, **kw):
            return T.op(en, lambda e: getattr(e, fname)(*args, **kw), reads, writes)

        T.dma("sp", ds_const, [
            (gcols[:].rearrange("p a b -> p (a b)"), gcols_in.ap()),
            (lncols[:].rearrange("p a b -> p (a b)"), lncols_in.ap()),
            (flags[:], flags_in.ap()),
        ], writes=[r_const])
        if STOP == -3:
            raise _StopBuild
        r_pc = Res("poolconst")
        vop("pool", "memset", [], [r_pc], ones_bf[:], 1.0)
        vop("pool", "memset", [], [r_pc], invd_bf[:], 1.0 / D)
        vop("pool", "memset", [], [r_pc], ones_f[:], 1.0)
        vop("pool", "memset", [], [r_pc], eps_col[:, 0:1], RMS_EPS)
        vop("pool", "memset", [], [r_pc], eps_col[:, 1:2], LN_EPS)
        T.op("pool", lambda e: e.affine_select(tri_f[:], ones_f[:], [[1, 128]], ALU.is_ge, 0.0,
                                               base=0, channel_multiplier=-1), [r_pc], [r_pc])
        vop("pool", "memset", [], [r_pc], diag[:].rearrange("p a b -> p (a b)"), 1.0)
        for j in range(4):
            T.op("pool", lambda e, j=j: e.affine_select(diag[:, j, :], diag[:, j, :], [[1, 512]], ALU.is_ge, 0.0,
                                                        base=-128 * j, channel_multiplier=-1), [r_pc], [r_pc])
        if STOP == -2:
            raise _StopBuild
        for n in (1, 5):
            vop("dve", "tensor_scalar", [r_const], [r_const], gcols[:, n, :], gcols[:, n, :], 0.5, None, ALU.mult)

        if STOP == -1:
            raise _StopBuild
        ds_wc = [T.dsem(f"wcast{i}") for i in range(4)]
        wdirect = {"on": False}
        r_w = {n: [Res(f"{n}_bf{b}") for b in range(c // (128 if n in ("wd1", "wd2") else 512))]
               for (n, r, c) in WEIGHTS}
        ds_wb = [T.dsem(f"wback{i}") for i in range(4)]
        wb_state = {"i": 0}
        cast_order = [("wout", 0), ("wout", 1)]
        for f4 in range(8):
            cast_order += [("wg2", f4), ("wu2", f4)]
        cast_order += [("wd2", 0), ("wd2", 1)]
        cast_state = {"pos": 0}

        def cast_upto(pos):
            pos = min(pos, len(cast_order))
            while cast_state["pos"] < pos:
                n, b = cast_order[cast_state["pos"]]
                ds = ds_wc[cast_state["pos"] % 4]
                cast_state["pos"] += 1
                wr = r_w[n][4 * b:4 * b + 4] if n in ("wd1", "wd2") else [r_w[n][b]]
                T.dma("pool", ds, [(wbf[n].ap()[:, b * 512:(b + 1) * 512], wfull[n].ap()[:, b * 512:(b + 1) * 512])],
                      writes=wr)

        def need_cast(n, b):
            if (n, b) in cast_order:
                cast_upto(cast_order.index((n, b)) + 4)
            if not wdirect["on"]:
                cast_upto(cast_state["pos"] + 1)

        if STOP == 0:
            raise _StopBuild
        wring = Ring(T, ar, NSLOT, [128, 4096], BF16, "w")

        def wload(src_ap, view, rw, src32=None):
            s = wring.next()
            dst = view(s.t)
            if wdirect["on"] and src32 is not None:
                T.dma("pool", s.ds, [(dst, src32)], writes=[s.res])
                if rw.w is None:
                    ds = ds_wb[wb_state["i"] % 4]
                    wb_state["i"] += 1
                    T.dma("sp", ds, [(src_ap, dst)], reads=[s.res], writes=[rw])
            else:
                T.dma("pool", s.ds, [(dst, src_ap)], reads=[rw], writes=[s.res])
            return s, dst

        def w_gu(n, f0):
            need_cast(n, f0 // 512)
            src = wbf[n].ap().rearrange("(c p) f -> p c f", p=128)[:, :, f0:f0 + 512]
            src32 = wfull[n].ap().rearrange("(c p) f -> p c f", p=128)[:, :, f0:f0 + 512]
            return wload(src, lambda t: t[:].rearrange("p (c f) -> p c f", c=8), r_w[n][f0 // 512], src32)

        def w_dn(n, dc):
            need_cast(n, dc // 4)
            src = wbf[n].ap().rearrange("(c p) d -> p c d", p=128)[:, :, dc * 128:(dc + 1) * 128]
            src32 = wfull[n].ap().rearrange("(c p) d -> p c d", p=128)[:, :, dc * 128:(dc + 1) * 128]
            return wload(src, lambda t: t[:].rearrange("p (c d) -> p c d", c=32), r_w[n][dc], src32)

        xbuf = [Slot(T, ar.alloc([128, 8, TT], F32, "x"), f"x{i}") for i in range(2)]
        r_xc = [[Res(f"xc{i}_{dc}") for dc in range(8)] for i in range(2)]
        hA = ar.alloc([128, 8, TT], BF16, "hA")
        r_hA = [Res(f"hA{dc}") for dc in range(8)]
        abuf = ar.alloc([128, 32, TT], BF16, "a")
        r_a = [Res(f"a{i}") for i in range(32)]
        ybuf = ar.alloc([128, 8, TT], F32, "y")
        r_y = [Res(f"y{i}") for i in range(8)]
        sq = [ar.alloc([128, TT], BF16, "sq") for _ in range(2)]
        r_sq = [Res("sq0"), Res("sq1")]
        sqP = [ar.alloc([128, TT], BF16, "sqP") for _ in range(2)]
        r_sqP = [Res("sqP0"), Res("sqP1")]
        rstdP = ar.alloc([128, TT], F32, "rstdP")
        r_rstdP = Res("rstdP")
        rstdD = ar.alloc([128, TT], F32, "rstdD")
        r_rstdD = Res("rstdD")
        rstdM = ar.alloc([128, TT], F32, "rstdM")
        r_rstdM = Res("rstdM")
        sil = [ar.alloc([128, TT], F32, "sil") for _ in range(2)]
        r_sil = [Res("sil0"), Res("sil1")]
        ffn_mark = ar.mark()
        cnt = {"sq": 0, "sqP": 0, "sil": 0, "mm": 0}

        PS_G = [0, 1]
        PS_U = [2, 3]
        PS_MM = [4, 5, 6]
        PS_SS = 7

        def mm_group(bank, out_ap, pairs, reads):
            n = len(pairs)
            for i, (l, r) in enumerate(pairs):
                T.op("pe", lambda e, l=l, r=r, i=i: e.matmul(out_ap, l, r, start=(i == 0), stop=(i == n - 1)),
                     reads, [pres[bank]], signal=(i == n - 1))

        def next_mm():
            b = PS_MM[cnt["mm"] % 3]
            cnt["mm"] += 1
            return b

        PS_SS2 = 0

        def stat_ops(src_fn, r_src_fn, bank, ring, r_ring, ckey):
            slotof = {}

            def act(dc):
                k = cnt[ckey] % 2
                cnt[ckey] += 1
                slotof[dc] = k
                T.op("act", lambda e: e.activation(out=ring[k][:], in_=src_fn(dc), func=AF.Square),
                     r_src_fn(dc), [r_ring[k]])

            def pe(dc):
                k = slotof[dc]
                T.op("pe", lambda e: e.matmul(psum[bank][:], invd_bf[:], ring[k][:], start=(dc == 0), stop=(dc == 7)),
                     [r_ring[k], r_pc], [pres[bank]], signal=True)
            return act, pe

        def finish_rstd(bank, rt, r_rt):
            T.op("act", lambda e: e.activation(out=rt[:], in_=psum[bank][:], func=AF.Sqrt, bias=eps_col[:, 0:1]),
                 [pres[bank], r_pc], [r_rt])
            T.op("dve", lambda e: e.reciprocal(rt[:], rt[:]), [r_rt], [r_rt])

        def h_ops(src, r_src_list, gi, rt, r_rt, hdst, r_hdst, dcs):
            for dc in dcs:
                T.op("dve", lambda e, dc=dc: e.scalar_tensor_tensor(
                    out=hdst[:, dc, :], in0=src[:, dc, :], scalar=gcols[:, gi, dc:dc + 1], in1=rt[:],
                    op0=ALU.mult, op1=ALU.mult), [r_src_list[dc], r_rt, r_const], [r_hdst[dc]])

        def prenorm_now(src, r_src_list, gi, hdst, r_hdst, bank, rt, r_rt):
            act, pe = stat_ops(lambda dc: src[:, dc, :], lambda dc: [r_src_list[dc]], bank, sqP, r_sqP, "sqP")
            for dc in range(8):
                act(dc)
                pe(dc)
            finish_rstd(bank, rt, r_rt)
            h_ops(src, r_src_list, gi, rt, r_rt, hdst, r_hdst, range(8))

        def p_side(src, r_src_list, gi, hdst, r_hdst):
            act, pe = stat_ops(lambda dc: src[:, dc, :], lambda dc: [r_src_list[dc]], PS_SS2, sqP, r_sqP, "sqP")
            side = [[] for _ in range(8)]
            side[0] = [lambda: act(0), lambda: act(1)]
            side[1] = [lambda: pe(0), lambda: pe(1), lambda: act(2), lambda: act(3)]
            side[2] = [lambda: pe(2), lambda: pe(3), lambda: act(4), lambda: act(5)]
            side[3] = [lambda: pe(4), lambda: pe(5), lambda: act(6), lambda: act(7)]
            side[4] = [lambda: pe(6), lambda: pe(7), lambda: finish_rstd(PS_SS2, rstdP, r_rstdP)]
            side[5] = [lambda: h_ops(src, r_src_list, gi, rstdP, r_rstdP, hdst, r_hdst, range(0, 4))]
            side[6] = [lambda: h_ops(src, r_src_list, gi, rstdP, r_rstdP, hdst, r_hdst, range(4, 8))]
            return side

        gu_state = {}

        def gateup(names, fcs, hsrc, r_hsrc, tasks):
            ng, nu = names
            for fc in fcs:
                fi = fc % 4
                if fi == 0:
                    gu_state["g"] = w_gu(ng, (fc // 4) * 512)
                    gu_state["u"] = w_gu(nu, (fc // 4) * 512)
                sg, wg = gu_state["g"]
                su, wu = gu_state["u"]
                bg = PS_G[fc % 2]
                bu = PS_U[fc % 2]
                mm_group(bg, psum[bg][:], [(wg[:, dc, fi * 128:(fi + 1) * 128], hsrc[:, dc, :]) for dc in range(8)],
                         [sg.res] + r_hsrc)
                mm_group(bu, psum[bu][:], [(wu[:, dc, fi * 128:(fi + 1) * 128], hsrc[:, dc, :]) for dc in range(8)],
                         [su.res] + r_hsrc)
                k = cnt["sil"] % 2
                cnt["sil"] += 1
                T.op("act", lambda e, bg=bg, k=k: e.activation(out=sil[k][:], in_=psum[bg][:], func=AF.Silu),
                     [pres[bg]], [r_sil[k]])
                T.op("dve", lambda e, bu=bu, k=k, fc=fc: e.tensor_tensor(abuf[:, fc, :], psum[bu][:], sil[k][:], ALU.mult),
                     [pres[bu], r_sil[k]], [r_a[fc]])
                if tasks:
                    tasks.pop(0)()

        def down_stage(nd, gpost, side):
            act, pe = stat_ops(lambda dc: psum[bank_of[dc]][:], lambda dc: [pres[bank_of[dc]]], PS_SS, sq, r_sq, "sq")
            bank_of = {}
            for dc in range(8):
                sd, wd = w_dn(nd, dc)
                b = next_mm()
                bank_of[dc] = b
                mm_group(b, psum[b][:], [(wd[:, fc, :], abuf[:, fc, :]) for fc in range(32)], [sd.res] + r_a)
                if dc > 0:
                    pe(dc - 1)
                T.op("act", lambda e, dc=dc, b=b: e.activation(out=ybuf[:, dc, :], in_=psum[b][:], func=AF.Copy,
                                                               scale=gcols[:, gpost, dc:dc + 1]),
                     [pres[b], r_const], [r_y[dc]])
                act(dc)
                for t in side[dc]:
                    t()
            pe(7)
            finish_rstd(PS_SS, rstdD, r_rstdD)

        def resid_tasks(ysrc, r_ysrc, rt, r_rt, x_t, rx, after=None):
            tasks = []
            for dc in range(8):
                def t(dc=dc):
                    T.op("dve", lambda e: e.tensor_tensor(ysrc[:, dc, :], ysrc[:, dc, :], rt[:], ALU.mult),
                         [r_ysrc[dc], r_rt], [r_ysrc[dc]])
                    T.op("dve", lambda e: e.tensor_tensor(x_t[:, dc, :], x_t[:, dc, :], ysrc[:, dc, :], ALU.add),
                         [r_ysrc[dc], rx[dc]], [rx[dc]])
                    if after is not None:
                        after(dc)
                tasks.append(t)
            return tasks

        def chain_tasks(x_t, rx, gi, hdst, r_hdst, bank, ring, r_ring, ckey, ysrc, r_ysrc, rt_in, r_rt_in,
                        store_fn=None, final_fn=None):
            act, pe = stat_ops(lambda dc: x_t[:, dc, :], lambda dc: [rx[dc]], bank, ring, r_ring, ckey)

            def after(dc):
                act(dc)
                if dc > 0:
                    pe(dc - 1)
            tasks = resid_tasks(ysrc, r_ysrc, rt_in, r_rt_in, x_t, rx, after)

            def t8():
                pe(7)
                if store_fn is not None:
                    store_fn()
            tasks.append(t8)
            tasks.append(lambda: finish_rstd(bank, rstdM, r_rstdM))
            tasks.append(lambda: h_ops(x_t, rx, gi, rstdM, r_rstdM, hdst, r_hdst, range(0, 4)))

            def t11():
                h_ops(x_t, rx, gi, rstdM, r_rstdM, hdst, r_hdst, range(4, 8))
                if final_fn is not None:
                    final_fn()
            tasks.append(t11)
            return tasks

        def flush(tasks):
            while tasks:
                tasks.pop(0)()

        wf_sb = ar.alloc([128, 8, H], BF16, "wf")
        r_wf = Res("wf")
        ds_wf = T.dsem("wf")
        T.dma("pool", ds_wf, [(wf_sb[:], wf_in.ap().rearrange("(c p) h -> p c h", p=128))], writes=[r_wf])
        wsT_f = ar.alloc([128, 8, 128], F32, "wsTf")
        wsT_b = ar.alloc([128, 8, 128], BF16, "wsTb")
        bs_bc = ar.alloc([128, 8, 128], F32, "bsbc")
        bf_bc = ar.alloc([128, H], F32, "bfbc")
        e_g = ar.alloc([128, 8, 128], F32, "eg")
        r_sgc = Res("sgc")
        T.dma("sp", ds_const, [
            (wsT_f[:].rearrange("p a b -> p (a b)"), wsT_in.ap()),
            (bs_bc[:].rearrange("p a b -> p (a b)"), bs_in.ap().partition_broadcast(128).rearrange("p a b -> p (a b)")),
            (bf_bc[:], bf_in.ap().partition_broadcast(128).rearrange("p a b -> p (a b)")),
        ], writes=[r_sgc])
        T.op("dve", lambda e: e.memset(wsT_f[64:128, :, 0:64], 0.0), [r_sgc], [r_sgc])
        T.op("dve", lambda e: e.tensor_copy(wsT_b[:], wsT_f[:]), [r_sgc], [r_sgc])
        for half in range(2):
            T.op("pe", lambda e, half=half: e.matmul(psum[half][:], ones_f[:],
                                                     wsT_f[:, half * 4:(half + 1) * 4, :].rearrange("p a b -> p (a b)"),
                                                     start=True, stop=True),
                 [r_sgc, r_pc], [pres[half]])
            for gi in range(4):
                g = half * 4 + gi
                T.op("dve", lambda e, half=half, gi=gi, g=g: e.scalar_tensor_tensor(
                    out=e_g[:, g, :], in0=psum[half][:, gi * 128:(gi + 1) * 128], scalar=lncols[:, 1, g:g + 1],
                    in1=bs_bc[:, g, :], op0=ALU.mult, op1=ALU.add), [pres[half], r_const, r_sgc], [r_sgc])

        a_mark = ar.mark()
        hM = ar.alloc([128, 8, TT], BF16, "hM")
        r_hM = [Res(f"hM{dc}") for dc in range(8)]
        qk_ring = Ring(T, ar, 2, [128, 4, TT], BF16, "qkst")
        ybase = nc.lookup_mloc(ybuf).addr
        vst = Slot(T, nc.alloc_sbuf_tensor_at("vst_alias", [128, 4, D], BF16, offset=ybase), "vst")
        vn = nc.alloc_sbuf_tensor_at("vn_alias", [128, 4, D], BF16, offset=ybase + 4 * D * 2)
        r_vn = [r_y[4 + b] for b in range(4)]
        vs_ring = [ar.alloc([128, 512], F32, "vs") for _ in range(4)]
        r_vs = [Res(f"vs{i}") for i in range(4)]
        stats = ar.alloc([128, 16, 6], F32, "stats")
        mv = ar.alloc([128, 16, 2], F32, "mv")
        lrs = ar.alloc([128, 16], F32, "lrs")
        r_st = Res("stats")
        u_ring = [ar.alloc([128, TT], F32, "u") for _ in range(2)]
        r_u = [Res("u0"), Res("u1")]
        gb_ring = [ar.alloc([128, TT], F32, "gb") for _ in range(2)]
        r_gb = [Res("gb0"), Res("gb1")]
        ga_ring = Ring(T, ar, 2, [128, TT], F32, "gast")
        mb_ring = Ring(T, ar, 3, [128, TT], F32, "mbst")
        lf_z = ar.alloc([128, 4, H], F32, "lfz")
        lf_st = Slot(T, ar.alloc([128, 4, H], F32, "lfst"), "lfst")
        r_lfz = Res("lfz")

        r_Q = [[Res(f"Q{i}_{hf}") for hf in range(2)] for i in range(NT)]
        r_KG = [[[Res(f"KG{sl}_{i}_{hf}") for hf in range(2)] for i in range(NT)] for sl in range(2)]
        r_VG = [[Res(f"VG{sl}_{i}") for i in range(NT)] for sl in range(2)]
        r_LG = [[Res(f"LG{sl}_{i}") for i in range(NT)] for sl in range(2)]
        r_GA = [[Res(f"GA{i}_{g}") for g in range(8)] for i in range(NT)]
        r_MB = [[Res(f"MB{i}_{g}") for g in range(8)] for i in range(NT)]
        r_X1 = [Res(f"X1{i}") for i in range(NT)]
        r_MG = [[Res(f"MG{i}_{g}") for g in range(8)] for i in range(NT)]
        qk_cres = [[Res(f"qkc{i}_{c}") for c in range(8)] for i in range(2)]
        v_cres = [r_y[i % 4] for i in range(8)]

        xT_v = xT.ap().rearrange("(c p) t -> p c t", p=128)
        X1_v = X1s.ap().rearrange("(c p) t -> p c t", p=128)
        Q_v = Qs.ap().rearrange("(c p) t -> p c t", p=128)
        xTp_v = xTp.ap().rearrange("(c p) t -> p c t", p=128)
        KG_v = [KG.ap()[sl * 1024:(sl + 1) * 1024, :].rearrange("(c p) t -> p c t", p=128) for sl in range(2)]
        VG_v = [VG.ap()[sl * 1024:(sl + 1) * 1024, :].rearrange("(h s) (l d) -> s l h d", s=128, d=128)
                for sl in range(2)]
        LG_v = [LG.ap()[sl * TPC:(sl + 1) * TPC, :].rearrange("(l p) h -> p l h", p=128) for sl in range(2)]
        win_v = wbf["win"].ap().rearrange("(c p) n -> p c n", p=128)
        win32_v = wfull["win"].ap().rearrange("(c p) n -> p c n", p=128)

        def w_in_chunk(ci):
            need_cast("win", ci)
            return wload(win_v[:, :, ci * 512:(ci + 1) * 512],
                         lambda t: t[:].rearrange("p (c f) -> p c f", c=8), r_w["win"][ci],
                         win32_v[:, :, ci * 512:(ci + 1) * 512])

        cact = {"i": 0}

        def evac(dst_ap, src_ap, reads, writes, func=None):
            if func is not None:
                return T.op("act", lambda e: e.activation(out=dst_ap, in_=src_ap, func=func), reads, writes)
            cact["i"] += 1
            if cact["i"] % 2:
                return T.op("act", lambda e: e.activation(out=dst_ap, in_=src_ap, func=AF.Copy), reads, writes)
            return T.op("dve", lambda e: e.tensor_copy(dst_ap, src_ap), reads, writes)

        def load_x(ti):
            xs_ = xbuf[ti % 2]
            src = xT_v if ti < NT else xTp_v
            lt_ = ti % NT
            T.dma("sp", xs_.ds, [(xs_.t[:], src[:, :, lt_ * TT:(lt_ + 1) * TT])], writes=r_xc[ti % 2])

        def w_stage(slot, lt, own):
            tsl = slice(lt * TT, (lt + 1) * TT)

            def sec_qk(which, act_only):
                dst_v, rdst = ((Q_v, r_Q[lt]), (KG_v[slot], r_KG[slot][lt]))[which]
                for half in range(2):
                    sti = qk_ring.i % 2
                    st = qk_ring.next()
                    sw, wv = w_in_chunk(which * 2 + half)
                    for ci in range(4):
                        b = next_mm()
                        mm_group(b, psum[b][:], [(wv[:, dc, ci * 128:(ci + 1) * 128], hM[:, dc, :]) for dc in range(8)],
                                 [sw.res] + r_hM)
                        evac(st.t[:, ci, :], psum[b][:], [pres[b]], [qk_cres[sti][ci]],
                             func=(AF.Copy if act_only else None))
                    T.dma("sp", st.ds, [(dst_v[:, half * 4:(half + 1) * 4, tsl], st.t[:])], reads=qk_cres[sti][0:4],
                          writes=[rdst[half]])

            def sec_v():
                for half in range(2):
                    sw, wv = w_in_chunk(4 + half)
                    for blk in range(4):
                        b = next_mm()
                        mm_group(b, psum[b][:], [(hM[:, dc, blk * 128:(blk + 1) * 128], wv[:, dc, :]) for dc in range(8)],
                                 [sw.res] + r_hM)
                        evac(vst.t[:, blk, half * 512:(half + 1) * 512], psum[b][:], [pres[b]], [v_cres[half * 4 + blk]])
                T.dma("sp", vst.ds, [(VG_v[slot][:, lt * 4 + blk, :, :], vst.t[:, blk, :].rearrange("p (h d) -> p h d", h=H))
                                     for blk in range(4)], reads=v_cres, writes=[r_VG[slot][lt]])

            def sec_f():
                b = next_mm()
                for blk in range(4):
                    mm_group(b, psum[b][:, blk * H:(blk + 1) * H],
                             [(hM[:, dc, blk * 128:(blk + 1) * 128], wf_sb[:, dc, :]) for dc in range(8)], [r_wf] + r_hM)
                T.op("dve", lambda e, b=b: e.tensor_tensor(
                    lf_z[:], psum[b][:, 0:4 * H].rearrange("p (b h) -> p b h", h=H),
                    bf_bc[:].unsqueeze(1).to_broadcast([128, 4, H]), ALU.add), [pres[b], r_sgc], [r_lfz])
                T.op("act", lambda e: e.activation(out=lf_z[:], in_=lf_z[:], func=AF.Exp, scale=-1.0), [r_lfz], [r_lfz])
                T.op("act", lambda e: e.activation(out=lf_z[:], in_=lf_z[:], func=AF.Ln, bias=1.0), [r_lfz], [r_lfz])
                T.op("dve", lambda e: e.tensor_scalar(lf_st.t[:], lf_z[:], -1.0, None, ALU.mult), [r_lfz], [lf_st.res])
                T.dma("sp", lf_st.ds, [(LG_v[slot][:, lt * 4:(lt + 1) * 4, :], lf_st.t[:])], reads=[lf_st.res],
                      writes=[r_LG[slot][lt]])

            def sec_sv_mm(half):
                sw, wv = w_in_chunk(8 + half)
                for blk in range(4):
                    b = next_mm()
                    mm_group(b, psum[b][:], [(hM[:, dc, blk * 128:(blk + 1) * 128], wv[:, dc, :]) for dc in range(8)],
                             [sw.res] + r_hM)
                    T.op("act", lambda e, b=b, blk=blk: e.activation(out=vs_ring[blk][:], in_=psum[b][:], func=AF.Gelu),
                         [pres[b]], [r_vs[blk]])
                for blk in range(4):
                    for gi in range(4):
                        T.op("dve", lambda e, blk=blk, gi=gi: e.bn_stats(
                            stats[:, blk * 4 + gi, :], vs_ring[blk][:, gi * 128:(gi + 1) * 128]), [r_vs[blk]], [r_st])
                for q_ in range(16):
                    T.op("dve", lambda e, q_=q_: e.bn_aggr(mv[:, q_, :], stats[:, q_, :]), [r_st], [r_st])

            def sec_sv_fin(half):
                T.op("act", lambda e: e.activation(out=lrs[:], in_=mv[:, :, 1], func=AF.Sqrt, bias=eps_col[:, 1:2]),
                     [r_st, r_pc], [r_st])
                T.op("dve", lambda e: e.reciprocal(lrs[:], lrs[:]), [r_st], [r_st])
                for blk in range(4):
                    for gi in range(4):
                        g = half * 4 + gi
                        q_ = blk * 4 + gi
                        T.op("dve", lambda e, gi=gi, g=g, blk=blk, q_=q_: e.tensor_scalar(
                            vn[:, blk, g * 128:(g + 1) * 128], vs_ring[blk][:, gi * 128:(gi + 1) * 128],
                            mv[:, q_, 0:1], lrs[:, q_:q_ + 1], ALU.subtract, ALU.mult), [r_vs[blk], r_st], [r_vn[blk]])

            if not own:
                sec_qk(1, False)
                sec_v()
                sec_f()
                return
            sec_sv_mm(0)
            sec_qk(0, True)
            sec_sv_fin(0)
            sec_sv_mm(1)
            sec_qk(1, True)
            sec_sv_fin(1)
            sec_v()
            sec_f()
            GA_v = GAs.ap().rearrange("(c p) t -> p c t", p=128)
            MB_v = MBs.ap().rearrange("(c p) t -> p c t", p=128)
            for half in range(2):
                su_, wu_ = w_in_chunk(6 + half)
                sb_, wb_ = w_in_chunk(12 + half)
                sa_, wa_ = w_in_chunk(10 + half)
                for gi in range(4):
                    g = half * 4 + gi
                    k = g % 2
                    csl = slice(gi * 128, (gi + 1) * 128)
                    b = next_mm()
                    mm_group(b, psum[b][:], [(wu_[:, dc, csl], hM[:, dc, :]) for dc in range(8)], [su_.res] + r_hM)
                    T.op("act", lambda e, b=b, k=k: e.activation(out=u_ring[k][:], in_=psum[b][:], func=AF.Gelu),
                         [pres[b]], [r_u[k]])
                    b = next_mm()
                    mm_group(b, psum[b][:], [(wb_[:, dc, csl], hM[:, dc, :]) for dc in range(8)], [sb_.res] + r_hM)
                    T.op("act", lambda e, b=b, k=k: e.activation(out=gb_ring[k][:], in_=psum[b][:], func=AF.Sigmoid),
                         [pres[b]], [r_gb[k]])
                    b = next_mm()
                    mm_group(b, psum[b][:], [(wa_[:, dc, csl], hM[:, dc, :]) for dc in range(8)], [sa_.res] + r_hM)
                    gs = ga_ring.next()
                    T.op("act", lambda e, b=b, gs=gs: e.activation(out=gs.t[:], in_=psum[b][:], func=AF.Sigmoid),
                         [pres[b]], [gs.res])
                    T.dma("sp", gs.ds, [(GA_v[:, g, tsl], gs.t[:])], reads=[gs.res], writes=[r_GA[lt][g]])
                    b = next_mm()
                    for blk in range(4):
                        T.op("pe", lambda e, b=b, blk=blk, g=g: e.matmul(
                            psum[b][:, blk * 128:(blk + 1) * 128], vn[:, blk, g * 128:(g + 1) * 128], wsT_b[:, g, :],
                            start=True, stop=True), [r_vn[blk], r_sgc], [pres[b]], signal=(blk == 3))
                    ms = mb_ring.next()
                    T.op("dve", lambda e, b=b, ms=ms, g=g: e.scalar_tensor_tensor(
                        out=ms.t[:].rearrange("p (b t) -> p b t", b=4), in0=psum[b][:].rearrange("p (b t) -> p b t", b=4),
                        scalar=lncols[:, 0, g:g + 1], in1=e_g[:, g, :].unsqueeze(1).to_broadcast([128, 4, 128]),
                        op0=ALU.mult, op1=ALU.add), [pres[b], r_const, r_sgc], [ms.res])
                    T.op("dve", lambda e, ms=ms, k=k: e.tensor_tensor(ms.t[:], ms.t[:], u_ring[k][:], ALU.mult),
                         [r_u[k], ms.res], [ms.res])
                    T.op("dve", lambda e, ms=ms, k=k: e.tensor_tensor(ms.t[:], ms.t[:], gb_ring[k][:], ALU.mult),
                         [r_gb[k], ms.res], [ms.res])
                    T.dma("sp", ms.ds, [(MB_v[:, g, tsl], ms.t[:])], reads=[ms.res], writes=[r_MB[lt][g]])

        NTA = 2 * NT
        FFN1 = ("wg1", "wu1")
        wdirect["on"] = True
        load_x(0)
        load_x(1)
        prenorm_now(xbuf[0].t, r_xc[0], 0, hA, r_hA, PS_SS2, rstdP, r_rstdP)
        gateup(FFN1, range(32), hA, r_hA, [])
        for ti in range(NTA):
            slot, lt = ti // NT, ti % NT
            own = slot == 0
            xs_ = xbuf[ti % 2]
            x_t, rx = xs_.t, r_xc[ti % 2]
            tsl = slice(lt * TT, (lt + 1) * TT)
            if ti + 1 < NTA:
                side = p_side(xbuf[(ti + 1) % 2].t, r_xc[(ti + 1) % 2], 0, hA, r_hA)
            else:
                side = [[] for _ in range(8)]
            down_stage("wd1", 1, side)

            def store_fn(xs_=xs_, x_t=x_t, rx=rx, lt=lt, tsl=tsl, own=own):
                if own:
                    T.dma("sp", xs_.ds, [(X1_v[:, :, tsl], x_t[:])], reads=rx, writes=[r_X1[lt]])

            def final_fn(ti=ti):
                if ti + 2 < NTA:
                    load_x(ti + 2)
            chain = chain_tasks(x_t, rx, 2, hM, r_hM, PS_SS, sq, r_sq, "sq", ybuf, r_y, rstdD, r_rstdD,
                                store_fn, final_fn)
            if ti + 1 < NTA:
                gateup(FFN1, range(0, 12), hA, r_hA, chain)
            flush(chain)
            w_stage(slot, lt, own)
            wdirect["on"] = False
            if ti + 1 < NTA:
                gateup(FFN1, range(12, 32), hA, r_hA, [])
        T.barrier()
        if STOP == 1:
            raise _StopBuild

        ar.reset(cmark)
        lg = ar.alloc([128, NIDX, H], F32, "lg")
        pre = ar.alloc([128, NIDX, H], F32, "pre")
        negc = ar.alloc([128, H, NIDX], F32, "negc")
        offcol = ar.alloc([128, 2], F32, "offcol")
        r_c = Res("cums")
        ds_lg = T.dsem("lg")
        T.dma("sp", ds_lg, [(lg[:, sl * NLB:(sl + 1) * NLB, :], LG_v[sl]) for sl in range(2)],
              reads=r_LG[0] + r_LG[1], writes=[r_c])

        def gidx(q, gblk):
            G, j = gblk // 4, gblk % 4
            if G in GT[q]:
                return GT[q].index(G) * 4 + j
            return NLB + GT[1 - q].index(G) * 4 + j

        lg2 = lg[:].rearrange("p i h -> p (i h)")
        NLG = NIDX * H
        T.op("pe", lambda e: e.matmul(psum[0][:, 0:NLG], tri_f[:], lg2, start=True, stop=True), [r_c, r_pc], [pres[0]])
        T.op("pe", lambda e: e.matmul(psum[1][:, 0:NLG], ones_f[:], lg2, start=True, stop=True), [r_c, r_pc], [pres[1]])
        tot = ar.alloc([128, NIDX, H], F32, "tot")
        T.op("dve", lambda e: e.tensor_copy(tot[:].rearrange("p i h -> p (i h)"), psum[1][:, 0:NLG]), [pres[1]], [r_c])
        NTT = 2 * NT
        totv = tot[:].rearrange("p (t j) h -> p t j h", j=4)
        prev4 = pre[:].rearrange("p (t j) h -> p t j h", j=4)
        wint = ar.alloc([128, NTT, 4, H], F32, "wint")
        ttile = ar.alloc([128, NTT, H], F32, "ttile")
        tq = [ar.alloc([128, NTT, H], F32, "tq") for _ in range(2)]
        tqb = ar.alloc([128, NTT, H], F32, "tqb")
        T.op("dve", lambda e: e.memset(wint[:, :, 0, :], 0.0), [r_c], [r_c])
        T.op("dve", lambda e: e.tensor_copy(wint[:, :, 1, :], totv[:, :, 0, :]), [r_c], [r_c])
        T.op("dve", lambda e: e.tensor_tensor(wint[:, :, 2, :], wint[:, :, 1, :], totv[:, :, 1, :], ALU.add), [r_c], [r_c])
        T.op("dve", lambda e: e.tensor_tensor(wint[:, :, 3, :], wint[:, :, 2, :], totv[:, :, 2, :], ALU.add), [r_c], [r_c])
        T.op("dve", lambda e: e.tensor_tensor(ttile[:], wint[:, :, 3, :], totv[:, :, 3, :], ALU.add), [r_c], [r_c])

        def tau(q, G):
            return GT[q].index(G) if G in GT[q] else NT + GT[1 - q].index(G)
        for q in range(2):
            T.op("dve", lambda e, q=q: e.memset(tq[q][:, tau(q, 0), :], 0.0), [r_c], [r_c])
            for G in range(1, NTT):
                a, bq = tau(q, G), tau(q, G - 1)
                T.op("dve", lambda e, a=a, bq=bq, q=q: e.tensor_tensor(tq[q][:, a, :], tq[q][:, bq, :], ttile[:, bq, :], ALU.add),
                     [r_c], [r_c])
        T.op("dve", lambda e: e.tensor_scalar(tqb[:], tq[0][:], flags[:, 1:2], None, ALU.mult), [r_c, r_const], [r_c])
        T.op("dve", lambda e: e.scalar_tensor_tensor(out=tqb[:], in0=tq[1][:], scalar=flags[:, 0:1], in1=tqb[:],
                                                     op0=ALU.mult, op1=ALU.add), [r_c, r_const], [r_c])
        T.op("dve", lambda e: e.tensor_tensor(prev4, wint[:], tqb[:].unsqueeze(2).to_broadcast([128, NTT, 4, H]), ALU.add),
             [r_c], [r_c])
        T.op("dve", lambda e: e.scalar_tensor_tensor(
            out=negc[:].rearrange("p h i -> p i h"), in0=psum[0][:, 0:NLG].rearrange("p (i h) -> p i h", h=H), scalar=-1.0,
            in1=pre[:], op0=ALU.mult, op1=ALU.subtract), [pres[0], r_c], [r_c])
        T.op("dve", lambda e: e.tensor_scalar(offcol[:, 0:1], flags[:, 1:2], -30000.0, None, ALU.mult), [r_const], [r_c])
        T.op("dve", lambda e: e.tensor_scalar(offcol[:, 1:2], flags[:, 0:1], -30000.0, None, ALU.mult), [r_const], [r_c])

        kbuf = [Slot(T, ar.alloc([128, 2 * TPC], BF16, "kb"), f"kb{i}") for i in range(2)]
        vbuf = [Slot(T, ar.alloc([128, NIDX, 128], BF16, "vb"), f"vb{i}") for i in range(2)]
        q_ring = Ring(T, ar, 3, [128, TT], BF16, "q")
        p_ring = [ar.alloc([128, TT], BF16, "p") for _ in range(4)]
        r_p = [Res(f"p{i}") for i in range(4)]
        bias_ring = [ar.alloc([128, NIDX], F32, "bias") for _ in range(3)]
        r_bias = [Res("bias0"), Res("bias1"), Res("bias2")]
        gat = Ring(T, ar, 3, [128, TT], F32, "gat")
        mbt = Ring(T, ar, 3, [128, TT], F32, "mbt")
        rden = ar.alloc([128, TT], F32, "rden")
        r_rden = Res("rden")
        otmp = ar.alloc([128, TT], F32, "otmp")
        r_otmp = Res("otmp")
        mg_ring = Ring(T, ar, 2, [128, TT], BF16, "mgst")

        PS_S = [0, 1, 2, 7]
        PS_O = [3, 4]
        PS_D = [5, 6]
        GA_v = GAs.ap().rearrange("(c p) t -> p c t", p=128)
        MB_v = MBs.ap().rearrange("(c p) t -> p c t", p=128)
        MG_v = MGs.ap().rearrange("(c p) t -> p c t", p=128)
        scale = 1.0 / float(np.sqrt(HD))

        units = []
        grp = 0
        for h in range(H):
            for lt in range(NT):
                first = True
                for sl in (1, 0):
                    for klt in range(lt + 1):
                        for j in range(4):
                            units.append(dict(h=h, lt=lt, grp=grp, first=first, last=False, rk=sl, klt=klt, j=j,
                                              diag=(sl == 0 and klt == lt)))
                            first = False
                units[-1]["last"] = True
                grp += 1
        state = {}

        def load_head(h):
            ks, vsb = kbuf[h % 2], vbuf[h % 2]
            T.dma("sp", ks.ds, [(ks.t[:, rk * TPC:(rk + 1) * TPC], KG.ap()[rk * 1024 + h * 128: rk * 1024 + (h + 1) * 128, :])
                                for rk in range(2)], reads=[r for sl in range(2) for t_ in r_KG[sl] for r in t_], writes=[ks.res])
            T.dma("sp", vsb.ds, [(vsb.t[:, rk * NLB:(rk + 1) * NLB, :],
                                  VG.ap()[rk * 1024 + h * 128: rk * 1024 + (h + 1) * 128, :].rearrange("s (l d) -> s l d", d=128))
                                 for rk in range(2)], reads=r_VG[0] + r_VG[1], writes=[vsb.res])

        def group_prologue(g):
            if g in state or g >= H * NT:
                return
            h, lt = divmod(g, NT)
            qs = q_ring.next()
            T.dma("sp", qs.ds, [(qs.t[:], Q_v[:, h, lt * TT:(lt + 1) * TT])], reads=[r_Q[lt][h // 4]], writes=[qs.res])
            k = g % 3
            T.op("dve", lambda e, k=k, h=h, lt=lt: e.tensor_scalar(
                bias_ring[k][:], negc[:, h, :], pre[:, 4 * lt + 2, h:h + 1], BCLAMP, ALU.add, ALU.min), [r_c], [r_bias[k]])
            fsl = slice(NLB + 4 * lt, NLB + 4 * lt + 4)
            T.op("dve", lambda e, k=k, lt=lt, fsl=fsl: e.tensor_scalar(
                bias_ring[k][:, fsl], bias_ring[k][:, fsl], offcol[:, lt % 2:lt % 2 + 1], None, ALU.add),
                [r_c, r_bias[k]], [r_bias[k]])
            gs, ms = gat.next(), mbt.next()
            T.dma("sp", gs.ds, [(gs.t[:], GA_v[:, h, lt * TT:(lt + 1) * TT])], reads=[r_GA[lt][h]], writes=[gs.res])
            T.dma("sp", ms.ds, [(ms.t[:], MB_v[:, h, lt * TT:(lt + 1) * TT])], reads=[r_MB[lt][h]], writes=[ms.res])
            state[g] = dict(q=qs, ga=gs, mb=ms)

        def emit_S(i):
            u = units[i]
            if u["first"]:
                h_, lt_ = u["h"], u["lt"]
                if lt_ == 0 and h_ == 0:
                    load_head(0)
                if lt_ == min(1, NT - 1) and h_ + 1 < H:
                    load_head(h_ + 1)
                group_prologue(u["grp"])
                group_prologue(u["grp"] + 1)
            st = state[u["grp"]]
            ks = kbuf[u["h"] % 2]
            b = PS_S[i % 4]
            col = u["rk"] * TPC + u["klt"] * TT + u["j"] * 128
            c0 = u["j"] * 128 if u["diag"] else 0
            T.op("pe", lambda e, b=b, ks=ks, col=col, q=st["q"], c0=c0: e.matmul(
                psum[b][:, c0:], ks.t[:, col:col + 128], q.t[:, c0:], start=True, stop=True),
                [ks.res, st["q"].res], [pres[b]])

        def emit_rest(i):
            u = units[i]
            g, h, lt = u["grp"], u["h"], u["lt"]
            st = state[g]
            b = PS_S[i % 4]
            k3 = i % 4
            idx = u["rk"] * NLB + u["klt"] * 4 + u["j"]
            kb = g % 3
            c0 = u["j"] * 128 if u["diag"] else 0
            T.op("act", lambda e, b=b, k3=k3, kb=kb, idx=idx, c0=c0: e.activation(
                out=p_ring[k3][:, c0:], in_=psum[b][:, c0:], func=AF.Exp, bias=bias_ring[kb][:, idx:idx + 1], scale=scale),
                [pres[b], r_bias[kb]], [r_p[k3]])
            if u["diag"]:
                T.op("dve", lambda e, k3=k3, j=u["j"], c0=c0: e.tensor_tensor(
                    p_ring[k3][:, c0:c0 + 128], p_ring[k3][:, c0:c0 + 128], diag[:, j, c0:c0 + 128], ALU.mult),
                    [r_pc, r_p[k3]], [r_p[k3]])
            vsb = vbuf[h % 2]
            bo, bd = PS_O[g % 2], PS_D[g % 2]
            T.op("pe", lambda e, bo=bo, vsb=vsb, idx=idx, k3=k3, f=u["first"], l=u["last"], c0=c0: e.matmul(
                psum[bo][:, c0:], vsb.t[:, idx, :], p_ring[k3][:, c0:], start=f, stop=l),
                [vsb.res, r_p[k3]], [pres[bo]], signal=False)
            T.op("pe", lambda e, bd=bd, k3=k3, f=u["first"], l=u["last"], c0=c0: e.matmul(
                psum[bd][:, c0:], ones_bf[:], p_ring[k3][:, c0:], start=f, stop=l), [r_p[k3], r_pc], [pres[bd]], signal=True)
            if u["last"]:
                T.op("dve", lambda e, bd=bd: e.reciprocal(rden[:], psum[bd][:]), [pres[bd]], [r_rden])
                T.op("dve", lambda e, bo=bo: e.tensor_tensor(otmp[:], psum[bo][:], rden[:], ALU.mult),
                     [pres[bo], r_rden], [r_otmp])
                T.op("dve", lambda e, gs=st["ga"]: e.tensor_tensor(otmp[:], otmp[:], gs.t[:], ALU.mult),
                     [r_otmp, st["ga"].res], [r_otmp])
                mg = mg_ring.next()
                T.op("dve", lambda e, mg=mg, ms=st["mb"]: e.tensor_tensor(mg.t[:], otmp[:], ms.t[:], ALU.add),
                     [r_otmp, st["mb"].res], [mg.res])
                T.dma("sp", mg.ds, [(MG_v[:, h, lt * TT:(lt + 1) * TT], mg.t[:])], reads=[mg.res], writes=[r_MG[lt][h]])
                del state[g]

        NU = len(units)
        LOOK = 3
        for i in range(min(LOOK, NU)):
            emit_S(i)
        for i in range(NU):
            if i + LOOK < NU:
                emit_S(i + LOOK)
            emit_rest(i)
        T.barrier()
        if STOP == 2:
            raise _StopBuild

        ar.reset(ffn_mark)
        wout_sb = ar.alloc([128, 8, D], BF16, "wout")
        r_wo = Res("wout")
        ds_wo = T.dsem("wout")
        need_cast("wout", 1)
        T.dma("pool", ds_wo, [(wout_sb[:], wbf["wout"].ap().rearrange("(c p) n -> p c n", p=128))],
              reads=r_w["wout"], writes=[r_wo])
        mgin = Ring(T, ar, 2, [128, 8, TT], BF16, "mgin")
        y2 = ar.alloc([128, 8, TT], F32, "y2")
        r_y2 = [Res(f"y2_{dc}") for dc in range(8)]
        outT_v = outT.ap().rearrange("(c p) t -> p c t", p=128)
        r_out = Res("out")
        FFN2 = ("wg2", "wu2")

        def load_b2(lt):
            xsl = xbuf[lt % 2]
            tsl_ = slice(lt * TT, (lt + 1) * TT)
            T.dma("sp", xsl.ds, [(xsl.t[:], X1_v[:, :, tsl_])], reads=[r_X1[lt]], writes=r_xc[lt % 2])
            ms_ = mgin.next()
            T.dma("sp", ms_.ds, [(ms_.t[:], MG_v[:, :, tsl_])], reads=r_MG[lt], writes=[ms_.res])
            return ms_

        def wout_stage(ms_):
            bank_of = {}
            act, pe = stat_ops(lambda dc: psum[bank_of[dc]][:], lambda dc: [pres[bank_of[dc]]], PS_SS2, sqP, r_sqP, "sqP")
            for dc in range(8):
                b = next_mm()
                bank_of[dc] = b
                mm_group(b, psum[b][:], [(wout_sb[:, c, dc * 128:(dc + 1) * 128], ms_.t[:, c, :]) for c in range(8)],
                         [r_wo, ms_.res])
                if dc > 0:
                    pe(dc - 1)
                T.op("act", lambda e, dc=dc, b=b: e.activation(out=y2[:, dc, :], in_=psum[b][:], func=AF.Copy,
                                                               scale=gcols[:, 3, dc:dc + 1]),
                     [pres[b], r_const], [r_y2[dc]])
                act(dc)
            pe(7)

        def b2_chain(lt):
            x_t, rx = xbuf[lt % 2].t, r_xc[lt % 2]
            tasks = [lambda: finish_rstd(PS_SS2, rstdP, r_rstdP)]
            tasks += chain_tasks(x_t, rx, 4, hA, r_hA, PS_SS2, sqP, r_sqP, "sqP", y2, r_y2, rstdP, r_rstdP)
            return tasks

        mss = {0: load_b2(0)}
        if NT > 1:
            mss[1] = load_b2(1)
        wout_stage(mss[0])
        flush(b2_chain(0))
        gateup(FFN2, range(32), hA, r_hA, [])
        for lt in range(NT):
            xsl = xbuf[lt % 2]
            x_t, rx = xsl.t, r_xc[lt % 2]
            tsl = slice(lt * TT, (lt + 1) * TT)
            side = [[] for _ in range(8)]
            if lt + 1 < NT:
                wout_stage(mss[lt + 1])
                ch = b2_chain(lt + 1)
                per = [2, 2, 2, 2, 2, 1, 1, 1]
                for dc in range(8):
                    for _ in range(per[dc]):
                        if ch:
                            side[dc].append(ch.pop(0))
                assert not ch
            down_stage("wd2", 5, side)

            def out_store(xsl=xsl, x_t=x_t, rx=rx, tsl=tsl, lt=lt):
                T.dma("sp", xsl.ds, [(outT_v[:, :, tsl], x_t[:])], reads=rx, writes=[r_out])
                if lt + 2 < NT:
                    mss[lt + 2] = load_b2(lt + 2)
            otasks = resid_tasks(ybuf, r_y, rstdD, r_rstdD, x_t, rx)
            otasks.append(out_store)
            if lt + 1 < NT:
                gateup(FFN2, range(32), hA, r_hA, otasks)
            flush(otasks)
        T.barrier()


    try:
        body()
    except _StopBuild:
        T.barrier()

    from contextlib import ExitStack
    with ExitStack() as es:
        for E in T.eng.values():
            E.sem = es.enter_context(nc.semaphore("s_" + E.name))
        for d in T.dsems:
            d.sem = es.enter_context(nc.semaphore("d_" + d.name))
        block = es.enter_context(nc.Block())

        @block.tensor
        def _(e):
            T.emit(nc, "pe", e)

        @block.scalar
        def _(e):
            T.emit(nc, "act", e)

        @block.vector
        def _(e):
            T.emit(nc, "dve", e)

        @block.gpsimd
        def _(e):
            T.emit(nc, "pool", e)

        @block.sync
        def _(e):
            T.emit(nc, "sp", e)
    return nc


def _prep_inputs(inp):
    f = lambda a: np.ascontiguousarray(np.asarray(a, dtype=np.float32))
    x = f(inp["x"])
    w = {"wg1": f(inp["ffn1_w_gate"])[0], "wu1": f(inp["ffn1_w_up"])[0], "wd1": f(inp["ffn1_w_down"])[0],
         "wout": f(inp["w_out"])[0], "wg2": f(inp["ffn2_w_gate"])[0], "wu2": f(inp["ffn2_w_up"])[0],
         "wd2": f(inp["ffn2_w_down"])[0]}
    w_in = f(inp["w_in"])[0]
    w["win"] = np.ascontiguousarray(np.concatenate([w_in[:, :3072], w_in[:, 3080:]], axis=1))
    wf = np.ascontiguousarray(w_in[:, 3072:3080])
    gains = [inp["ffn1_pre_g"], inp["ffn1_post_g"], inp["mix_pre_g"], inp["mix_post_g"], inp["ffn2_pre_g"],
             inp["ffn2_post_g"]]
    gcols = np.stack([f(g)[0].reshape(8, 128).T for g in gains], axis=1).reshape(128, 48)
    lncols = np.stack([f(inp["sgu_ln_g"])[0].reshape(8, 128).T, f(inp["sgu_ln_b"])[0].reshape(8, 128).T],
                      axis=1).reshape(128, 16)
    wsT = np.ascontiguousarray(f(inp["sgu_w_s"])[0].transpose(2, 0, 1)).reshape(128, 1024)
    bs = f(inp["sgu_b_s"])[0].reshape(1, 1024)
    bfg = f(inp["b_forget"]).reshape(1, 8)
    maps = []
    for c in range(NCORE):
        b, p = c // 2, c % 2
        toks = np.concatenate([np.arange(G * TT, (G + 1) * TT) for G in GT[p]])
        tokp = np.concatenate([np.arange(G * TT, (G + 1) * TT) for G in GT[1 - p]])
        m = {"xT": np.ascontiguousarray(x[b][toks].T), "xTp": np.ascontiguousarray(x[b][tokp].T)}
        for (n, r, cc) in WEIGHTS:
            m[n + "_f"] = w[n]
        m["wf"] = wf
        m["gcols"] = np.ascontiguousarray(gcols)
        m["lncols"] = np.ascontiguousarray(lncols)
        m["wsT"] = wsT
        m["bs"] = bs
        m["bfg"] = bfg
        fl = np.zeros((128, 4), np.float32)
        fl[:, 0] = p
        fl[:, 1] = 1 - p
        m["flags"] = fl
        maps.append(m)
    return maps


_NC_CACHE = {}


def kernel(**inputs):
    maps = _prep_inputs(inputs)
    if "nc" not in _NC_CACHE:
        _NC_CACHE["nc"] = build()
    nc = _NC_CACHE["nc"]
    res = run_bass_kernel_spmd(nc, maps, core_ids=list(range(NCORE)))
    out = np.empty((NB, SEQ, D), np.float32)
    for c in range(NCORE):
        b, p = c // 2, c % 2
        o = np.asarray(res.results[c]["outT"]).T
        for i, G in enumerate(GT[p]):
            out[b, G * TT:(G + 1) * TT] = o[i * TT:(i + 1) * TT]
    return out
```
